# Optimizing a Trainium2 kernel written in Bass

```python
import math
import jax, jax.numpy as jnp
from jax import lax
import numpy as np

D_MODEL = 1024
BATCH = 32
SEQ = 2048
DEPTH = 1
DEC_BATCH = 128
DEC_SEQ = 1
PAST_LEN = 8192
PAGE_SIZE = 128

N_HEADS_A = 8
HEAD_DIM_A = 64
D_ATTN = N_HEADS_A * HEAD_DIM_A
DILATED_PATTERNS = ((128, 1), (512, 4), (2048, 16))
MAX_WINDOW = 2048
NUM_BUCKETS = 32
MAX_DISTANCE = 2048
N_HEADS_B = 16
HEAD_DIM_B = 64
D_INNER = N_HEADS_B * HEAD_DIM_B
N_SSM_GROUPS = 2
HEADS_PER_GROUP = N_HEADS_B // N_SSM_GROUPS
D_STATE = 128
CONV_WIDTH = 4
SSD_CHUNK = 128
D_XBC = D_INNER + 2 * N_SSM_GROUPS * D_STATE
D_MIX = D_ATTN + D_INNER
D_IN_PROJ = 3 * D_ATTN + D_INNER + D_XBC + N_HEADS_B
N_MEM = 256
N_HEADS_X = 4
HEAD_DIM_X = D_MODEL // N_HEADS_X
D_FF = ((8 * D_MODEL // 3 + 127) // 128) * 128
EPS = 1e-6
NEG_INF = -1e30

kernel_name = "hymba_dilated_ssd_macaron_step"


def rms_norm(x, g):
    x32 = x.astype(jnp.float32)
    y = x32 * lax.rsqrt(jnp.mean(x32 * x32, axis=-1, keepdims=True) + EPS)
    return (y * g.astype(jnp.float32)).astype(x.dtype)


def swiglu(x, w_gate, w_up, w_down):
    return (jax.nn.silu(x @ w_gate) * (x @ w_up)) @ w_down


def t5_bucket(dist):
    max_exact = NUM_BUCKETS // 2
    d = jnp.maximum(dist, 0)
    df = jnp.maximum(d, 1).astype(jnp.float32)
    large = max_exact + (jnp.log(df / max_exact) / math.log(MAX_DISTANCE / max_exact)
                         * (NUM_BUCKETS - max_exact)).astype(jnp.int32)
    large = jnp.minimum(large, NUM_BUCKETS - 1)
    return jnp.where(d < max_exact, d, large)


def dilated_attn_prompt(q, k, v, rel_bias, window, dilation):
    b, s, h, dh = q.shape
    n_back = window // dilation
    blk = n_back
    ls = s // dilation
    nb = -(-ls // blk)
    lp = nb * blk
    n = b * dilation

    def by_residue(a):
        a = a.reshape(b, ls, dilation, h, dh).transpose(0, 2, 1, 3, 4).reshape(n, ls, h, dh)
        a = jnp.pad(a, ((0, 0), (0, lp - ls), (0, 0), (0, 0)))
        return a.reshape(n, nb, blk, h, dh)

    def with_prev_block(a):
        prev = jnp.concatenate([jnp.zeros_like(a[:, :1]), a[:, :-1]], axis=1)
        return jnp.concatenate([prev, a], axis=2)

    qb = by_residue(q)
    kb = with_prev_block(by_residue(k))
    vb = with_prev_block(by_residue(v))
    qi = jnp.arange(blk)[:, None]
    kj = jnp.arange(2 * blk)[None, :]
    delta = qi + blk - kj
    band = (delta >= 0) & (delta <= n_back)
    exists = (jnp.arange(nb)[:, None, None] > 0) | (kj[None] >= blk)
    mask = band[None] & exists
    bias = jnp.transpose(rel_bias[t5_bucket(delta * dilation)], (2, 0, 1)).astype(jnp.float32)
    scores = jnp.einsum('nbqhd,nbkhd->nbhqk', qb, kb).astype(jnp.float32) * (HEAD_DIM_A ** -0.5)
    scores = jnp.where(mask[None, :, None], scores + bias[None, None], NEG_INF)
    lse = jax.nn.logsumexp(scores, axis=-1)
    p = jnp.exp(scores - lse[..., None])
    o = jnp.einsum('nbhqk,nbkhd->nbqhd', p, vb.astype(jnp.float32))
    o = o.reshape(n, lp, h, dh)[:, :ls]
    lse = jnp.transpose(lse, (0, 1, 3, 2)).reshape(n, lp, h)[:, :ls]
    o = o.reshape(b, dilation, ls, h, dh).transpose(0, 2, 1, 3, 4).reshape(b, s, h, dh)
    lse = lse.reshape(b, dilation, ls, h).transpose(0, 2, 1, 3).reshape(b, s, h)
    return o, lse


def dilated_attn_sample(q, k_all, v_all, rel_bias, window, dilation):
    t = q.shape[1]
    w = k_all.shape[1] - t
    offs = jnp.arange(window // dilation + 1) * dilation
    idx = w + jnp.arange(t)[:, None] - offs[None, :]
    valid = idx >= 0
    idx = jnp.maximum(idx, 0)
    kg = k_all[:, idx]
    vg = v_all[:, idx]
    bias = jnp.transpose(rel_bias[t5_bucket(offs)], (1, 0)).astype(jnp.float32)
    scores = jnp.einsum('bthd,btkhd->bhtk', q, kg).astype(jnp.float32) * (HEAD_DIM_A ** -0.5)
    scores = jnp.where(valid[None, None], scores + bias[None, :, None, :], NEG_INF)
    lse = jax.nn.logsumexp(scores, axis=-1)
    p = jnp.exp(scores - lse[..., None])
    o = jnp.einsum('bhtk,btkhd->bthd', p, vg.astype(jnp.float32))
    return o, jnp.transpose(lse, (0, 2, 1))


def denominator_mixture(outs, lses):
    wts = jax.nn.softmax(jnp.stack(lses, 0), axis=0)
    return jnp.einsum('pblh,pblhd->blhd', wts, jnp.stack(outs, 0))


def ssd_chunked(xs, dt, a, bm, cm, h0):
    b, L, G, E, P = xs.shape
    N = bm.shape[-1]
    Q = min(SSD_CHUNK, L)
    lp = -(-L // Q) * Q
    pad = lp - L
    xs = xs.astype(jnp.float32)
    bm = bm.astype(jnp.float32)
    cm = cm.astype(jnp.float32)
    if pad:
        xs = jnp.pad(xs, ((0, 0), (0, pad), (0, 0), (0, 0), (0, 0)))
        dt = jnp.pad(dt, ((0, 0), (0, pad), (0, 0), (0, 0)))
        bm = jnp.pad(bm, ((0, 0), (0, pad), (0, 0), (0, 0)))
        cm = jnp.pad(cm, ((0, 0), (0, pad), (0, 0), (0, 0)))
    nc = lp // Q
    xdt = (xs * dt[..., None]).reshape(b, nc, Q, G, E, P)
    la = (dt * a[None, None]).reshape(b, nc, Q, G, E).transpose(0, 1, 3, 4, 2)
    a_cs = jnp.cumsum(la, axis=-1)
    bc = bm.reshape(b, nc, Q, G, N)
    cc = cm.reshape(b, nc, Q, G, N)
    causal = jnp.tril(jnp.ones((Q, Q), dtype=bool))
    seg = a_cs[..., :, None] - a_cs[..., None, :]
    lmat = jnp.exp(jnp.where(causal, seg, -jnp.inf))
    cb = jnp.einsum('bclgn,bcsgn->bcgls', cc, bc)
    y_diag = jnp.einsum('bcgls,bcgels,bcsgep->bclgep', cb, lmat, xdt)
    decay_s = jnp.exp(a_cs[..., -1:] - a_cs)
    st = jnp.einsum('bcsgn,bcges,bcsgep->bcgepn', bc, decay_s, xdt)
    chunk_decay = jnp.exp(a_cs[..., -1])

    def step(h, inp):
        st_c, dec_c = inp
        return dec_c[..., None, None] * h + st_c, h

    h_last, h_in = lax.scan(step, h0, (jnp.transpose(st, (1, 0, 2, 3, 4, 5)),
                                       jnp.transpose(chunk_decay, (1, 0, 2, 3))))
    h_in = jnp.transpose(h_in, (1, 0, 2, 3, 4, 5))
    y_off = jnp.einsum('bclgn,bcgepn,bcgel->bclgep', cc, h_in, jnp.exp(a_cs))
    y = (y_diag + y_off).reshape(b, lp, G, E, P)[:, :L]
    return y, h_last


def mamba2_mixer(z, xbc_raw, dt_raw, conv_prev, ssm_prev, conv_w, conv_b, dt_bias, a_log, d_skip, g_ssm):
    b, l, _ = z.shape
    xin = jnp.concatenate([conv_prev.astype(jnp.float32), xbc_raw.astype(jnp.float32)], axis=1)
    new_conv = xin[:, -(CONV_WIDTH - 1):].astype(xbc_raw.dtype)
    xbc = lax.conv_general_dilated(xin, conv_w.astype(jnp.float32)[:, None, :], window_strides=(1,),
                                   padding='VALID', dimension_numbers=('NWC', 'WIO', 'NWC'),
                                   feature_group_count=D_XBC)
    xbc = jax.nn.silu(xbc + conv_b.astype(jnp.float32))
    xs = xbc[..., :D_INNER].reshape(b, l, N_SSM_GROUPS, HEADS_PER_GROUP, HEAD_DIM_B)
    bm = xbc[..., D_INNER:D_INNER + N_SSM_GROUPS * D_STATE].reshape(b, l, N_SSM_GROUPS, D_STATE)
    cm = xbc[..., D_INNER + N_SSM_GROUPS * D_STATE:].reshape(b, l, N_SSM_GROUPS, D_STATE)
    dt = jax.nn.softplus(dt_raw.astype(jnp.float32) + dt_bias.astype(jnp.float32))
    dt = dt.reshape(b, l, N_SSM_GROUPS, HEADS_PER_GROUP)
    a = -jnp.exp(a_log.astype(jnp.float32)).reshape(N_SSM_GROUPS, HEADS_PER_GROUP)
    h0 = ssm_prev.astype(jnp.float32).reshape(b, N_SSM_GROUPS, HEADS_PER_GROUP, HEAD_DIM_B, D_STATE)
    y, h_last = ssd_chunked(xs, dt, a, bm, cm, h0)
    y = y + d_skip.astype(jnp.float32).reshape(N_SSM_GROUPS, HEADS_PER_GROUP)[..., None] * xs
    gate = jax.nn.silu(z.astype(jnp.float32)).reshape(b, l, N_SSM_GROUPS, D_INNER // N_SSM_GROUPS)
    yg = y.reshape(b, l, N_SSM_GROUPS, D_INNER // N_SSM_GROUPS) * gate
    yg = yg * lax.rsqrt(jnp.mean(yg * yg, axis=-1, keepdims=True) + EPS)
    y = yg.reshape(b, l, D_INNER) * g_ssm.astype(jnp.float32)
    return y, new_conv, h_last.reshape(b, N_HEADS_B, HEAD_DIM_B, D_STATE)


def memory_kv(mem, g_mem, w_ck, w_cv):
    b = mem.shape[0]
    m = rms_norm(mem, g_mem)
    mk = (m @ w_ck).reshape(b, N_MEM, N_HEADS_X, HEAD_DIM_X)
    mv = (m @ w_cv).reshape(b, N_MEM, N_HEADS_X, HEAD_DIM_X)
    return mk, mv


def memory_attend(h, mk, mv, w_cq, w_co):
    b, l, _ = h.shape
    q = (h @ w_cq).reshape(b, l, N_HEADS_X, HEAD_DIM_X)
    s = jnp.einsum('blhd,bmhd->bhlm', q, mk).astype(jnp.float32) * (HEAD_DIM_X ** -0.5)
    p = jax.nn.softmax(s, axis=-1)
    o = jnp.einsum('bhlm,bmhd->blhd', p, mv.astype(jnp.float32))
    return o.reshape(b, l, N_HEADS_X * HEAD_DIM_X).astype(h.dtype) @ w_co


def decoder_layer(x, k_prev, v_prev, conv_prev, ssm_prev, mem_k, mem_v, rel_bias,
                  g_ffn1, w1_gate, w1_up, w1_down, g_mix, w_in, conv_w, conv_b, dt_bias, a_log,
                  d_skip, g_ssm, w_out, g_cross, w_cq, w_co, g_ffn2, w2_gate, w2_up, w2_down):
    b, l, _ = x.shape
    x = x + (0.5 * swiglu(rms_norm(x, g_ffn1), w1_gate, w1_up, w1_down)).astype(x.dtype)
    h = rms_norm(x, g_mix)
    proj = h @ w_in
    cuts = [D_ATTN, 2 * D_ATTN, 3 * D_ATTN, 3 * D_ATTN + D_INNER, 3 * D_ATTN + D_INNER + D_XBC]
    q, k, v, z, xbc_raw, dt_raw = jnp.split(proj, cuts, axis=-1)
    q = q.reshape(b, l, N_HEADS_A, HEAD_DIM_A)
    k = k.reshape(b, l, N_HEADS_A, HEAD_DIM_A)
    v = v.reshape(b, l, N_HEADS_A, HEAD_DIM_A)
    if k_prev is None:
        res = [dilated_attn_prompt(q, k, v, rel_bias, wnd, dil) for (wnd, dil) in DILATED_PATTERNS]
        keep = min(MAX_WINDOW, l)
        new_k, new_v = k[:, l - keep:], v[:, l - keep:]
        conv_prev = jnp.zeros((b, CONV_WIDTH - 1, D_XBC), xbc_raw.dtype)
        ssm_prev = jnp.zeros((b, N_HEADS_B, HEAD_DIM_B, D_STATE), jnp.float32)
    else:
        k_all = jnp.concatenate([k_prev.astype(k.dtype), k], axis=1)
        v_all = jnp.concatenate([v_prev.astype(v.dtype), v], axis=1)
        res = [dilated_attn_sample(q, k_all, v_all, rel_bias, wnd, dil) for (wnd, dil) in DILATED_PATTERNS]
        new_k, new_v = k, v
    o_attn = denominator_mixture([r[0] for r in res], [r[1] for r in res]).reshape(b, l, D_ATTN)
    y_ssm, new_conv, new_ssm = mamba2_mixer(z, xbc_raw, dt_raw, conv_prev, ssm_prev, conv_w, conv_b,
                                            dt_bias, a_log, d_skip, g_ssm)
    mixed = jnp.concatenate([o_attn.astype(x.dtype), y_ssm.astype(x.dtype)], axis=-1)
    x = x + (mixed @ w_out).astype(x.dtype)
    x = x + memory_attend(rms_norm(x, g_cross), mem_k, mem_v, w_cq, w_co).astype(x.dtype)
    x = x + (0.5 * swiglu(rms_norm(x, g_ffn2), w2_gate, w2_up, w2_down)).astype(x.dtype)
    return x, new_k, new_v, new_conv, new_ssm


def setup_inputs(seed: int = 0) -> dict:
    key = jax.random.key(seed)
    ks = iter(jax.random.split(key, 48))

    def nrm(shape, scale=1.0):
        return jax.random.normal(next(ks), shape, jnp.float32) * scale

    def gain(shape):
        return 1.0 + nrm(shape, 0.05)

    w_buf = min(MAX_WINDOW, PAST_LEN)
    L = DEPTH
    inp = {}
    inp["x_prompt"] = nrm((BATCH, SEQ, D_MODEL))
    inp["x_sample"] = nrm((DEC_BATCH, DEC_SEQ, D_MODEL))
    inp["cache_win_k"] = nrm((L, DEC_BATCH, w_buf, N_HEADS_A, HEAD_DIM_A))
    inp["cache_win_v"] = nrm((L, DEC_BATCH, w_buf, N_HEADS_A, HEAD_DIM_A))
    inp["cache_conv"] = nrm((L, DEC_BATCH, CONV_WIDTH - 1, D_XBC))
    inp["state_ssm"] = nrm((L, DEC_BATCH, N_HEADS_B, HEAD_DIM_B, D_STATE), 0.1)
    inp["cache_mem_k"] = nrm((L, DEC_BATCH, N_MEM, N_HEADS_X, HEAD_DIM_X))
    inp["cache_mem_v"] = nrm((L, DEC_BATCH, N_MEM, N_HEADS_X, HEAD_DIM_X))
    inp["mem_prompt"] = nrm((BATCH, N_MEM, D_MODEL))
    inp["rel_bias"] = nrm((NUM_BUCKETS, N_HEADS_A), 0.5)
    inp["g_ffn1"] = gain((L, D_MODEL))
    inp["w1_gate"] = nrm((L, D_MODEL, D_FF), D_MODEL ** -0.5)
    inp["w1_up"] = nrm((L, D_MODEL, D_FF), D_MODEL ** -0.5)
    inp["w1_down"] = nrm((L, D_FF, D_MODEL), D_FF ** -0.5)
    inp["g_mix"] = gain((L, D_MODEL))
    inp["w_in"] = nrm((L, D_MODEL, D_IN_PROJ), D_MODEL ** -0.5)
    inp["conv_w"] = nrm((L, CONV_WIDTH, D_XBC), CONV_WIDTH ** -0.5)
    inp["conv_b"] = nrm((L, D_XBC), 0.02)
    u = jax.random.uniform(next(ks), (L, N_HEADS_B), jnp.float32)
    dt0 = jnp.exp(u * (math.log(0.1) - math.log(0.001)) + math.log(0.001))
    inp["dt_bias"] = dt0 + jnp.log(-jnp.expm1(-dt0))
    inp["a_log"] = jnp.log(jax.random.uniform(next(ks), (L, N_HEADS_B), jnp.float32, 1.0, 16.0))
    inp["d_skip"] = gain((L, N_HEADS_B))
    inp["g_ssm"] = gain((L, D_INNER))
    inp["w_out"] = nrm((L, D_MIX, D_MODEL), D_MIX ** -0.5)
    inp["g_mem"] = gain((L, D_MODEL))
    inp["w_ck"] = nrm((L, D_MODEL, N_HEADS_X * HEAD_DIM_X), D_MODEL ** -0.5)
    inp["w_cv"] = nrm((L, D_MODEL, N_HEADS_X * HEAD_DIM_X), D_MODEL ** -0.5)
    inp["g_cross"] = gain((L, D_MODEL))
    inp["w_cq"] = nrm((L, D_MODEL, N_HEADS_X * HEAD_DIM_X), D_MODEL ** -0.5)
    inp["w_co"] = nrm((L, N_HEADS_X * HEAD_DIM_X, D_MODEL), (N_HEADS_X * HEAD_DIM_X) ** -0.5)
    inp["g_ffn2"] = gain((L, D_MODEL))
    inp["w2_gate"] = nrm((L, D_MODEL, D_FF), D_MODEL ** -0.5)
    inp["w2_up"] = nrm((L, D_MODEL, D_FF), D_MODEL ** -0.5)
    inp["w2_down"] = nrm((L, D_FF, D_MODEL), D_FF ** -0.5)
    inp["g_final"] = gain((D_MODEL,))
    return inp


def reference(x_prompt, x_sample, cache_win_k, cache_win_v, cache_conv, state_ssm, cache_mem_k,
              cache_mem_v, mem_prompt, rel_bias, g_ffn1, w1_gate, w1_up, w1_down, g_mix, w_in,
              conv_w, conv_b, dt_bias, a_log, d_skip, g_ssm, w_out, g_mem, w_ck, w_cv, g_cross,
              w_cq, w_co, g_ffn2, w2_gate, w2_up, w2_down, g_final):
    yp, ys = x_prompt, x_sample
    pk, pv, pc, pss, pmk, pmv = [], [], [], [], [], []
    sk, sv, sc, sss = [], [], [], []
    for i in range(DEPTH):
        lw = (g_ffn1[i], w1_gate[i], w1_up[i], w1_down[i], g_mix[i], w_in[i], conv_w[i], conv_b[i],
              dt_bias[i], a_log[i], d_skip[i], g_ssm[i], w_out[i], g_cross[i], w_cq[i], w_co[i],
              g_ffn2[i], w2_gate[i], w2_up[i], w2_down[i])
        mk_p, mv_p = memory_kv(mem_prompt, g_mem[i], w_ck[i], w_cv[i])
        yp, k_p, v_p, c_p, s_p = decoder_layer(yp, None, None, None, None, mk_p, mv_p, rel_bias, *lw)
        ys, k_s, v_s, c_s, s_s = decoder_layer(ys, cache_win_k[i], cache_win_v[i], cache_conv[i],
                                               state_ssm[i], cache_mem_k[i], cache_mem_v[i], rel_bias, *lw)
        pk.append(k_p); pv.append(v_p); pc.append(c_p); pss.append(s_p); pmk.append(mk_p); pmv.append(mv_p)
        sk.append(k_s); sv.append(v_s); sc.append(c_s); sss.append(s_s)
    y_prompt = rms_norm(yp, g_final)
    y_sample = rms_norm(ys, g_final)
    return (y_prompt, y_sample, jnp.stack(pk), jnp.stack(pv), jnp.stack(pc), jnp.stack(pss),
            jnp.stack(pmk), jnp.stack(pmv), jnp.stack(sk), jnp.stack(sv), jnp.stack(sc), jnp.stack(sss))
```

```python
import numpy as np
import concourse.bass as bass
import concourse.mybir as mybir
from concourse.bass_utils import run_bass_kernel_spmd

F32 = mybir.dt.float32
BF16 = mybir.dt.bfloat16
AF = mybir.ActivationFunctionType
ALU = mybir.AluOpType
AX = mybir.AxisListType

D = 1024
DFF = 2816
NCORES = 8
EPS = 1e-6


class Buf:
    __slots__ = ("name", "w", "rs", "excl")

    def __init__(self, name="", excl=False):
        self.name = name
        self.w = None
        self.rs = {}
        self.excl = excl


class Sch:
    ND = 12

    def __init__(self, nc):
        self.nc = nc
        self.eng = {"pe": nc.tensor, "act": nc.scalar, "dve": nc.vector, "pool": nc.gpsimd, "sp": nc.sync}
        self.sem = {k: nc.alloc_semaphore("s_" + k) for k in self.eng}
        self.cnt = {k: 0 for k in self.eng}
        self.known = {k: {} for k in self.eng}
        self.dsem = {q: [[nc.alloc_semaphore(f"d_{q}{i}"), 0] for i in range(self.ND)]
                     for q in ("sp", "pool", "act")}
        self.dnext = {q: 0 for q in self.dsem}
        self.il = None

    def _deps(self, own, r, w):
        toks = []
        for b in r:
            if b.w is not None:
                toks.append(b.w)
        for b in w:
            if b.w is not None and b.w[0] is not own:
                toks.append(b.w)
            for sem, v in b.rs.items():
                if sem is not own:
                    toks.append((sem, v))
        return toks

    def _wait(self, e, toks):
        need = {}
        for sem, v in toks:
            if v > need.get(sem, 0):
                need[sem] = v
        kn = self.known[e]
        for sem, v in need.items():
            if kn.get(sem, 0) >= v:
                continue
            self.eng[e].wait_ge(sem, v)
            kn[sem] = v

    def _mark(self, tok, r, w):
        for b in r:
            if b.rs.get(tok[0], 0) < tok[1]:
                b.rs[tok[0]] = tok[1]
        for b in w:
            b.w = tok
            b.rs = {}

    def op(self, e, ins_fn, r=(), w=()):
        own = self.sem[e]
        if any(b.excl for b in r):
            w = list(w) + [b for b in r if b.excl]
            r = [b for b in r if not b.excl]
        self._wait(e, self._deps(own, r, w))
        ins = ins_fn()
        self.cnt[e] += 1
        ins.then_inc(own, 1)
        self._mark((own, self.cnt[e]), r, w)
        if self.il is not None:
            self.il.switch()
        return ins

    def dma(self, q, out, in_, r=(), w=(), slow=False):
        pool = self.dsem[q]
        i = self.dnext[q]
        self.dnext[q] = (i + 1) % len(pool)
        slot = pool[i]
        toks = self._deps(None, r, w)
        if slot[1] > 0:
            toks.append((slot[0], slot[1]))
        self._wait(q, toks)
        ins = (self.eng[q].dma_start(out=out, in_=in_, allow_slow_non_contiguous=True) if slow
               else self.eng[q].dma_start(out=out, in_=in_))
        slot[1] += 16
        ins.then_inc(slot[0], 16)
        self._mark((slot[0], slot[1]), r, w)
        if self.il is not None:
            self.il.switch()
        return ins

    def finish(self):
        toks = []
        for q in self.dsem:
            for sem, v in self.dsem[q]:
                if v > 0:
                    toks.append((sem, v))
        for k in self.eng:
            if self.cnt[k] > 0:
                toks.append((self.sem[k], self.cnt[k]))
        self._wait("sp", toks)


class Ring:
    def __init__(self, items):
        self.items = items
        self.i = 0

    def next(self):
        it = self.items[self.i]
        self.i = (self.i + 1) % len(self.items)
        return it


class Ctx:
    pass


def sb(nc, name, shape, dt):
    t = nc.alloc_sbuf_tensor(name, shape, dt)
    return t, Buf(name)


def sb_ring(nc, name, shape, dt, n):
    return Ring([sb(nc, f"{name}{i}", shape, dt) for i in range(n)])


import threading


class Interleave:
    def __init__(self, sch, fns):
        self.sch = sch
        self.fns = fns
        self.ev = [threading.Event() for _ in fns]
        self.alive = [True] * len(fns)
        self.idx = {}
        self.exc = None

    def _next(self, i):
        n = len(self.fns)
        for d in range(1, n + 1):
            j = (i + d) % n
            if j != i and self.alive[j]:
                return j
        return None

    def _wrap(self, i):
        self.idx[threading.get_ident()] = i
        self.ev[i].wait()
        self.ev[i].clear()
        try:
            if self.exc is None:
                self.fns[i]()
        except BaseException as e:
            self.exc = e
        finally:
            self.alive[i] = False
            j = self._next(i)
            if j is not None:
                self.ev[j].set()

    def switch(self):
        i = self.idx.get(threading.get_ident())
        if i is None:
            return
        if self.exc is not None:
            raise RuntimeError("sibling stream failed")
        j = self._next(i)
        if j is None:
            return
        self.ev[j].set()
        self.ev[i].wait()
        self.ev[i].clear()

    def run(self):
        ths = [threading.Thread(target=self._wrap, args=(i,)) for i in range(len(self.fns))]
        self.sch.il = self
        for t in ths:
            t.start()
        self.ev[0].set()
        for t in ths:
            t.join()
        self.sch.il = None
        if self.exc is not None:
            raise self.exc
from contextlib import ExitStack

DIN = 4112
PATTERNS = ((128, 1), (512, 4), (2048, 16))
NEG = -30000.0


def _barrier(self):
    toks = []
    for q in self.dsem:
        for sem, v in self.dsem[q]:
            if v > 0:
                toks.append((sem, v))
    for k in self.eng:
        if self.cnt[k] > 0:
            toks.append((self.sem[k], self.cnt[k]))
    for e in self.eng:
        self._wait(e, [t for t in toks if t[0] is not self.sem[e]])


Sch.barrier = _barrier


class Stage:
    _n = 0

    def __init__(self, c):
        self.c = c
        self.es = ExitStack()

    def __enter__(self):
        self.es.__enter__()
        return self

    def __exit__(self, *a):
        if a[0] is None:
            self.c.s.barrier()
        return self.es.__exit__(*a)

    def sb(self, name, shape, dt):
        Stage._n += 1
        t = self.es.enter_context(self.c.nc.sbuf_tensor(f"{name}_{Stage._n}", shape, dt))
        return t, Buf(name)

    def ring(self, name, shape, dt, n):
        return Ring([self.sb(f"{name}{i}", shape, dt) for i in range(n)])

    def ps(self, name, shape, dt=F32):
        Stage._n += 1
        t = self.es.enter_context(self.c.nc.psum_tensor(f"{name}_{Stage._n}", shape, dt))
        return t, Buf(name, excl=True)

    def psring(self, name, shape, dt, n):
        return Ring([self.ps(f"{name}{i}", shape, dt) for i in range(n)])


def load_weight(c, st, dram_w, dst, dst_buf, kc_n, f_n, piece=1024):
    s, nc = c.s, c.nc
    for kc in range(kc_n):
        for f0 in range(0, f_n, piece):
            fw = min(piece, f_n - f0)
            stg, stb = st.wstage.next()
            s.dma("sp", stg[:, 0:fw], dram_w[:, kc, f0:f0 + fw], w=[stb])
            s.op("pool", lambda: nc.gpsimd.tensor_copy(dst[:, kc, f0:f0 + fw], stg[:, 0:fw]),
                 r=[stb], w=[dst_buf])


def rstd_from_ss(c, ss, ssb, n):
    s, nc = c.s, c.nc
    s.op("dve", lambda: nc.vector.tensor_scalar(ss[:, 1:2], ss[:, 0:1], 1.0 / n, EPS, ALU.mult, ALU.add),
         r=[ssb], w=[ssb])
    s.op("act", lambda: nc.scalar.activation(out=ss[:, 3:4], in_=ss[:, 1:2], func=AF.Sqrt), r=[ssb], w=[ssb])
    s.op("dve", lambda: nc.vector.reciprocal(ss[:, 2:3], ss[:, 3:4]), r=[ssb], w=[ssb])


def norm_transpose(c, st, xt, xtb, nsub, gcol, hT, hTb):
    s, nc = c.s, c.nc
    for sub in range(nsub):
        ss, ssb = st.stat.next()
        jk, jkb = st.junk.next()
        s.op("act", lambda: nc.scalar.activation(out=jk[:, :], in_=xt[:, sub, :], func=AF.Square,
                                                 accum_out=ss[:, 0:1]), r=[xtb], w=[jkb, ssb])
        rstd_from_ss(c, ss, ssb, D)
        xn, xnb = st.xn.next()
        s.op("act", lambda: nc.scalar.activation(out=xn[:, :], in_=xt[:, sub, :], func=AF.Copy,
                                                 scale=ss[:, 2:3]), r=[xtb, ssb], w=[xnb])
        pt, ptb = st.pst.next()
        for kc in range(8):
            s.op("pe", lambda: nc.tensor.transpose(pt[:, kc, :], xn[:, kc * 128:(kc + 1) * 128], c.ident_bf[:, :]),
                 r=[xnb, c.constb], w=[ptb])
        s.op("dve", lambda: nc.vector.tensor_tensor(
            hT[:, :, sub * 128:(sub + 1) * 128], pt[:, :, :],
            gcol.unsqueeze(2).to_broadcast([128, 8, 128]), ALU.mult),
            r=[ptb, c.constb], w=[hTb])


def norm_bufs(st):
    st.stat = st.ring("stat", [128, 4], F32, 4)
    st.junk = st.ring("junk", [128, 1024], BF16, 2)
    st.xn = st.ring("xn", [128, 1024], BF16, 2)
    st.pst = st.psring("pst", [128, 8, 128], BF16, 2)


def mm_fm(c, W, Wb, f0, hT, hTb, N, ps, psb, KC=8):
    s, nc = c.s, c.nc
    for kc in range(KC):
        s.op("pe", lambda: nc.tensor.matmul(ps[:, 0:N], W[:, kc, f0:f0 + 128], hT[:, kc, 0:N],
                                            start=(kc == 0), stop=(kc == KC - 1)), r=[Wb, hTb], w=[psb])


def mm_tm(c, hT, hTb, sub, W, Wb, c0, ncols, ps, psb, KC=8):
    s, nc = c.s, c.nc
    for kc in range(KC):
        s.op("pe", lambda: nc.tensor.matmul(ps[:, 0:ncols], hT[:, kc, sub * 128:(sub + 1) * 128], W[:, kc, c0:c0 + ncols],
                                            start=(kc == 0), stop=(kc == KC - 1)), r=[Wb, hTb], w=[psb])


def evac(c, out, in_, r, w, scale=None):
    s, nc = c.s, c.nc
    c.flip = not getattr(c, "flip", False)
    if c.flip:
        if scale is None:
            s.op("act", lambda: nc.scalar.copy(out, in_), r=r, w=w)
        else:
            s.op("act", lambda: nc.scalar.mul(out, in_, scale), r=r, w=w)
    else:
        if scale is None:
            s.op("dve", lambda: nc.vector.tensor_copy(out, in_), r=r, w=w)
        else:
            s.op("dve", lambda: nc.vector.tensor_scalar(out, in_, scale, None, ALU.mult), r=r, w=w)


def ffn_stage(c, x_in, x_out, gcol, wg_d, wu_d, wd_d, hT_scr, tiles, tag, final=None):
    s, nc = c.s, c.nc
    H = DFF // 2
    HC = H // 128
    with Stage(c) as st:
        st.wstage = st.ring("wst", [128, 1024], F32, 6)
        wA, wAb = st.sb("wA", [128, 8, H], BF16)
        wB, wBb = st.sb("wB", [128, 8, H], BF16)
        wC, wCb = st.sb("wC", [128, HC, D], BF16)
        xtr = st.ring("xt", [128, 4, 1024], F32, 2)
        hTr = st.ring("hT", [128, 8, 512], BF16, 2)
        actr = st.ring("actT", [128, HC, 512], BF16, 2)
        sgr = st.ring("sg", [128, 512], F32, 2)
        norm_bufs(st)
        psr = st.psring("ps", [128, 512], F32, 6)
        for half in range(2):
            load_weight(c, st, wg_d[:, :, half * H:(half + 1) * H], wA, wAb, 8, H)
            load_weight(c, st, wu_d[:, :, half * H:(half + 1) * H], wB, wBb, 8, H)
            load_weight(c, st, wd_d[:, half * HC:(half + 1) * HC, :], wC, wCb, HC, D)
            src = x_in if half == 0 else x_out
            def prep(tile):
                t0, nsub = tile
                N = nsub * 128
                xt, xtb = xtr.next()
                xob = c.db((tag, "xo", t0))
                rd = [xob] if half == 1 else [c.db((tag, "xi", t0))]
                s.dma("sp", xt[:, 0:nsub, :], src[t0:t0 + N, :].rearrange("(s p) d -> p s d", p=128), r=rd, w=[xtb])
                hT, hTb = hTr.next()
                if half == 0:
                    norm_transpose(c, st, xt, xtb, nsub, gcol, hT, hTb)
                    s.dma("pool", hT_scr[:, :, t0:t0 + N], hT[:, :, 0:N], r=[hTb], w=[c.db((tag, "hs", t0))])
                else:
                    s.dma("sp", hT[:, :, 0:N], hT_scr[:, :, t0:t0 + N], r=[c.db((tag, "hs", t0))], w=[hTb])
                return (xt, xtb, hT, hTb, xob)

            nxt = prep(tiles[0])
            for ti, (t0, nsub) in enumerate(tiles):
                N = nsub * 128
                xt, xtb, hT, hTb, xob = nxt
                act, actb = actr.next()
                for j in range(HC):
                    pg, pgb = psr.next()
                    mm_fm(c, wA, wAb, j * 128, hT, hTb, N, pg, pgb)
                    pu, pub = psr.next()
                    mm_fm(c, wB, wBb, j * 128, hT, hTb, N, pu, pub)
                    sg, sgb = sgr.next()
                    s.op("act", lambda: nc.scalar.activation(out=sg[:, 0:N], in_=pg[:, 0:N], func=AF.Silu), r=[pgb], w=[sgb])
                    s.op("dve", lambda: nc.vector.tensor_tensor(act[:, j, 0:N], sg[:, 0:N], pu[:, 0:N], ALU.mult),
                         r=[sgb, pub], w=[actb])
                if ti + 1 < len(tiles):
                    nxt = prep(tiles[ti + 1])
                for sub in range(nsub):
                    for hf in range(2):
                        pd, pdb = psr.next()
                        mm_tm(c, act, actb, sub, wC, wCb, hf * 512, 512, pd, pdb, KC=HC)
                        s.op("dve", lambda: nc.vector.scalar_tensor_tensor(
                            xt[:, sub, hf * 512:(hf + 1) * 512], pd[:, :], 0.5, xt[:, sub, hf * 512:(hf + 1) * 512],
                            ALU.mult, ALU.add), r=[pdb, xtb], w=[xtb])
                if half == 1 and final is not None:
                    gfin, y_out = final
                    for sub in range(nsub):
                        ss, ssb = st.stat.next()
                        jk, jkb = st.junk.next()
                        s.op("act", lambda: nc.scalar.activation(out=jk[:, :], in_=xt[:, sub, :], func=AF.Square,
                                                                 accum_out=ss[:, 0:1]), r=[xtb], w=[jkb, ssb])
                        rstd_from_ss(c, ss, ssb, D)
                        s.op("dve", lambda: nc.vector.scalar_tensor_tensor(
                            xt[:, sub, :], xt[:, sub, :], ss[:, 2:3], gfin, ALU.mult, ALU.mult),
                            r=[xtb, ssb, c.constb], w=[xtb])
                    s.dma("pool", y_out[t0:t0 + N, :].rearrange("(s p) d -> p s d", p=128), xt[:, 0:nsub, :], r=[xtb])
                else:
                    s.dma("pool", x_out[t0:t0 + N, :].rearrange("(s p) d -> p s d", p=128), xt[:, 0:nsub, :], r=[xtb], w=[xob])


def memkv_stage(c, mem_in, gcol, wck_d, wcv_d, mk_out, mv_out, memKT, memV, NBM):
    s, nc = c.s, c.nc
    with Stage(c) as st:
        st.wstage = st.ring("wst", [128, 1024], F32, 3)
        wK, wKb = st.sb("wK", [128, 8, D], BF16)
        wV, wVb = st.sb("wV", [128, 8, D], BF16)
        xtr = st.ring("xt", [128, 4, 1024], F32, 2)
        hTr = st.ring("hT", [128, 8, 512], BF16, 2)
        tmr = st.ring("tm", [128, 1024], F32, 3)
        tbr = st.ring("tb", [128, 1024], BF16, 2)
        fmr = st.ring("fm", [128, 512], BF16, 3)
        norm_bufs(st)
        psr = st.psring("ps", [128, 512], F32, 6)
        load_weight(c, st, wck_d, wK, wKb, 8, D)
        load_weight(c, st, wcv_d, wV, wVb, 8, D)
        tiles = []
        t0 = 0
        while t0 < NBM:
            n = min(4, (NBM - t0) // 128)
            tiles.append((t0, n))
            t0 += n * 128
        for (t0, nsub) in tiles:
            N = nsub * 128
            xt, xtb = xtr.next()
            s.dma("sp", xt[:, 0:nsub, :], mem_in[t0:t0 + N, :].rearrange("(s p) d -> p s d", p=128), w=[xtb])
            hT, hTb = hTr.next()
            norm_transpose(c, st, xt, xtb, nsub, gcol, hT, hTb)
            for sub in range(nsub):
                r0 = t0 + sub * 128
                for (W, Wb, outd, bfd) in ((wK, wKb, mk_out, None), (wV, wVb, mv_out, memV)):
                    tm, tmb = tmr.next()
                    for hf in range(2):
                        ps, psb = psr.next()
                        mm_tm(c, hT, hTb, sub, W, Wb, hf * 512, 512, ps, psb)
                        evac(c, tm[:, hf * 512:(hf + 1) * 512], ps[:, :], [psb], [tmb])
                    s.dma("pool", outd[r0:r0 + 128, :], tm[:, :], r=[tmb])
                    if bfd is not None:
                        tb, tbb = tbr.next()
                        s.op("pool", lambda: nc.gpsimd.tensor_copy(tb[:, :], tm[:, :]), r=[tmb], w=[tbb])
                        s.dma("pool", bfd[r0:r0 + 128, :], tb[:, :], r=[tbb])
            for j in range(8):
                ps, psb = psr.next()
                mm_fm(c, wK, wKb, j * 128, hT, hTb, N, ps, psb)
                fm, fmb = fmr.next()
                evac(c, fm[:, 0:N], ps[:, 0:N], [psb], [fmb])
                s.dma("pool", memKT[:, j, t0:t0 + N], fm[:, 0:N], r=[fmb])


def inproj_stage(c, x1, gcol, win_d, qT, kT, kT_out, v_out, v_bf, z_scr, xbcT, dt_scr, tiles):
    s, nc = c.s, c.nc
    with Stage(c) as st:
        st.wstage = st.ring("wst", [128, 1024], F32, 3)
        Wa, Wab = st.sb("winA", [128, 8, 2560], BF16)
        Wc, Wcb = st.sb("winB", [128, 8, DIN - 2560], BF16)
        xtr = st.ring("xt", [128, 4, 1024], F32, 2)
        hTr = st.ring("hT", [128, 8, 512], BF16, 2)
        f32r = st.ring("f32", [128, 512], F32, 6)
        bfr = st.ring("bf", [128, 512], BF16, 6)
        zr = st.ring("zt", [128, 1024], F32, 2)
        dtr = st.ring("dtt", [128, 16], F32, 3)
        norm_bufs(st)
        psr = st.psring("ps", [128, 512], F32, 6)
        load_weight(c, st, win_d[:, :, 0:2560], Wa, Wab, 8, 2560)
        load_weight(c, st, win_d[:, :, 2560:DIN], Wc, Wcb, 8, DIN - 2560)
        for (t0, nsub) in tiles:
            N = nsub * 128
            xt, xtb = xtr.next()
            s.dma("sp", xt[:, 0:nsub, :], x1[t0:t0 + N, :].rearrange("(s p) d -> p s d", p=128), w=[xtb])
            hT, hTb = hTr.next()
            norm_transpose(c, st, xt, xtb, nsub, gcol, hT, hTb)
            for j in range(20):
                f0 = j * 128 if j < 8 else 2560 + (j - 8) * 128
                ps, psb = psr.next()
                if f0 < 2560:
                    mm_fm(c, Wa, Wab, f0, hT, hTb, N, ps, psb)
                else:
                    mm_fm(c, Wc, Wcb, f0 - 2560, hT, hTb, N, ps, psb)
                if j < 4:
                    o, ob = bfr.next()
                    evac(c, o[:, 0:N], ps[:, 0:N], [psb], [ob], scale=0.125)
                    s.dma("pool", qT[:, j, t0:t0 + N], o[:, 0:N], r=[ob])
                elif j < 8:
                    o, ob = bfr.next()
                    evac(c, o[:, 0:N], ps[:, 0:N], [psb], [ob])
                    s.dma("pool", kT[:, j - 4, t0:t0 + N], o[:, 0:N], r=[ob])
                    o2, o2b = f32r.next()
                    evac(c, o2[:, 0:N], ps[:, 0:N], [psb], [o2b])
                    s.dma("pool", kT_out[:, j - 4, t0:t0 + N], o2[:, 0:N], r=[o2b])
                else:
                    o2, o2b = f32r.next()
                    evac(c, o2[:, 0:N], ps[:, 0:N], [psb], [o2b])
                    s.dma("pool", xbcT[:, j - 8, t0:t0 + N], o2[:, 0:N], r=[o2b])
            for sub in range(nsub):
                r0 = t0 + sub * 128
                ps, psb = psr.next()
                mm_tm(c, hT, hTb, sub, Wa, Wab, 1024, 512, ps, psb)
                o2, o2b = f32r.next()
                evac(c, o2[:, :], ps[:, :], [psb], [o2b])
                s.dma("pool", v_out[r0:r0 + 128, :], o2[:, :], r=[o2b])
                o, ob = bfr.next()
                evac(c, o[:, :], ps[:, :], [psb], [ob])
                s.dma("pool", v_bf[r0:r0 + 128, :], o[:, :], r=[ob])
                zt, ztb = zr.next()
                for hf in range(2):
                    ps, psb = psr.next()
                    mm_tm(c, hT, hTb, sub, Wa, Wab, 1536 + hf * 512, 512, ps, psb)
                    evac(c, zt[:, hf * 512:(hf + 1) * 512], ps[:, :], [psb], [ztb])
                s.dma("pool", z_scr[r0:r0 + 128, :], zt[:, :], r=[ztb])
                ps, psb = psr.next()
                mm_tm(c, hT, hTb, sub, Wc, Wcb, 4096 - 2560, 16, ps, psb)
                dtt, dtb = dtr.next()
                evac(c, dtt[:, :], ps[:, 0:16], [psb], [dtb])
                s.dma("pool", dt_scr[r0:r0 + 128, :], dtt[:, :], r=[dtb])


def t5_bucket_np(dist):
    d = np.maximum(np.asarray(dist, np.int64), 0)
    df = np.maximum(d, 1).astype(np.float32)
    large = 16 + (np.log(df / np.float32(16.0)) / np.float32(np.log(2048.0 / 16.0)) * np.float32(16.0)).astype(np.int32)
    large = np.minimum(large, 31)
    return np.where(d < 16, d, large).astype(np.int64)


def bias_consts():
    k = np.arange(128)[:, None, None]
    kb = np.arange(2)[None, :, None]
    q = np.arange(128)[None, None, :]
    delta = q + 128 * kb - k
    valid = (delta >= 0) & (delta <= 128)
    pb, masks, negm = [], [], []
    for pi, (wnd, dil) in enumerate(PATTERNS):
        bk = t5_bucket_np(delta * dil)
        negm.append(np.where(valid, 0.0, NEG).astype(np.float32).reshape(128, 256))
        for b in range(32):
            m = (valid & (bk == b))
            if m.any():
                pb.append((pi, b))
                masks.append(m.astype(np.float32).reshape(128, 256))
    return pb, np.stack(masks), np.stack(negm)


def attn_stage(c, qT, kT, v_bf, mixedT, relb_d, bmask_d, negm_d, pb, cache_k, cache_v, NB, NTP):
    s, nc = c.s, c.nc
    with Stage(c) as st:
        BT, BTb = st.sb("BT", [128, 24, 256], F32)
        rb, rbb = st.sb("rb", [128, 256], F32)
        mr = st.ring("bm", [128, 256], F32, 3)
        s.dma("sp", rb[:, :], relb_d[0:1, :].partition_broadcast(128), w=[rbb])
        for pi in range(3):
            for h in range(8):
                s.dma("sp", BT[:, pi * 8 + h, :], negm_d[pi], w=[BTb])
        for n, (pi, b) in enumerate(pb):
            m, mb = mr.next()
            s.dma("sp", m[:, :], bmask_d[n], w=[mb])
            for h in range(8):
                s.op("dve", lambda: nc.vector.scalar_tensor_tensor(
                    BT[:, pi * 8 + h, :], m[:, :], rb[:, b * 8 + h:b * 8 + h + 1], BT[:, pi * 8 + h, :],
                    ALU.mult, ALU.add), r=[mb, rbb, BTb], w=[BTb])
        BT4 = BT[:, :, :].rearrange("p n (kb q) -> p n kb q", kb=2)
        BTh, BThb = st.sb("BTh", [128, 24, 256], BF16)
        BTl, BTlb = st.sb("BTl", [128, 24, 256], BF16)
        btmp, btmpb = st.sb("btmp", [128, 8, 256], F32)
        for pi in range(3):
            ps_ = slice(pi * 8, pi * 8 + 8)
            s.op("dve", lambda: nc.vector.tensor_copy(BTh[:, ps_, :], BT[:, ps_, :]), r=[BTb], w=[BThb])
            s.op("dve", lambda: nc.vector.tensor_tensor(btmp[:, :, :], BT[:, ps_, :], BTh[:, ps_, :], ALU.subtract),
                 r=[BTb, BThb], w=[btmpb])
            s.op("dve", lambda: nc.vector.tensor_copy(BTl[:, ps_, :], btmp[:, :, :]), r=[btmpb], w=[BTlb])

        qTb, qTbb = st.sb("qTb", [128, 4, 2048], BF16)
        kTb, kTbb = st.sb("kTb", [128, 4, 2048], BF16)
        acc, accb = st.sb("acc", [128, 2, 4, 2048], F32)
        Vr = st.ring("Vt", [128, 512], BF16, 5)
        sbr = st.ring("sbs", [128, 2, 128], F32, 4)
        ptr = st.ring("PT", [128, 2, 128], BF16, 4)
        rcr = st.ring("rc", [128, 1024], F32, 1)
        obr = st.ring("ob", [128, 1024], BF16, 2)
        psS = st.psring("psS", [128, 512], F32, 3)
        psO = st.psring("psO", [128, 512], F32, 3)
        for b in range(NB):
            tok0 = b * 2048
            s.dma("sp", qTb[:, :, :], qT[:, :, tok0:tok0 + 2048], w=[qTbb])
            s.dma("sp", kTb[:, :, :], kT[:, :, tok0:tok0 + 2048], w=[kTbb])
            units = []
            for pi, (wnd, dil) in enumerate(PATTERNS):
                nblk = 2048 // dil // 128
                for r in range(dil):
                    for blk in range(nblk):
                        for h in range(8):
                            units.append((pi, dil, r, blk, h))
            vstate = {}

            def phaseA(u):
                pi, dil, r, blk, h = u
                cols = slice(r + dil * 128 * blk, r + dil * 128 * blk + dil * 127 + 1, dil)
                pcols = slice(r + dil * 128 * (blk - 1), r + dil * 128 * (blk - 1) + dil * 127 + 1, dil)
                if h == 0:
                    Vt, Vtb = Vr.next()
                    row0 = tok0 + r + dil * 128 * blk
                    s.dma("sp", Vt[:, :], v_bf[row0:row0 + 127 * dil + 1:dil, :], w=[Vtb])
                    prev = vstate.get((pi, r, blk - 1))
                    vstate[(pi, r, blk)] = (Vt, Vtb)
                    vstate[("cur", pi, r, blk)] = [(Vt, Vtb)] + ([prev] if blk > 0 else [])
                nkb = 2 if blk > 0 else 1
                pair = h // 2
                rows = slice(64 * (h % 2), 64 * (h % 2) + 64)
                Sp, Spb = psS.next()
                S = Sp[:, 0:256].rearrange("p (kb q) -> p kb q", kb=2)
                s.op("pe", lambda: nc.tensor.matmul(Sp[:, 0:nkb * 128], c.ident_bf[:, :], BTh[:, pi * 8 + h, 0:nkb * 128],
                                                    start=True, stop=False), r=[c.constb, BThb], w=[Spb])
                s.op("pe", lambda: nc.tensor.matmul(Sp[:, 0:nkb * 128], c.ident_bf[:, :], BTl[:, pi * 8 + h, 0:nkb * 128],
                                                    start=False, stop=False), r=[c.constb, BTlb], w=[Spb])
                s.op("pe", lambda: nc.tensor.matmul(S[:, 0, :], kTb[rows, pair, cols], qTb[rows, pair, cols],
                                                    start=False, stop=(nkb == 1)), r=[kTbb, qTbb], w=[Spb])
                if blk > 0:
                    s.op("pe", lambda: nc.tensor.matmul(S[:, 1, :], kTb[rows, pair, pcols], qTb[rows, pair, cols],
                                                        start=False, stop=True), r=[kTbb, qTbb], w=[Spb])
                PT, PTb = ptr.next()
                s.op("act", lambda: nc.scalar.activation(out=PT[:, 0:nkb, :], in_=S[:, 0:nkb, :], func=AF.Exp),
                     r=[Spb], w=[PTb])
                return (PT, PTb, cols, nkb, vstate[("cur", pi, r, blk)])

            def phaseB(u, A):
                pi, dil, r, blk, h = u
                PT, PTb, cols, nkb, vs = A
                pair = h // 2
                rows = slice(64 * (h % 2), 64 * (h % 2) + 64)
                Op, Opb = psO.next()
                OL = Op[:, 0:256].rearrange("p (a q) -> p a q", a=2)
                for kb, (vt, vtb) in enumerate(vs):
                    s.op("pe", lambda: nc.tensor.matmul(OL[:, 0, :], vt[:, pair * 128:(pair + 1) * 128], PT[:, kb, :],
                                                        start=(kb == 0), stop=(kb == nkb - 1)), r=[vtb, PTb], w=[Opb])
                for kb in range(nkb):
                    s.op("pe", lambda: nc.tensor.matmul(OL[:, 1, :], c.ones_bf[:, :], PT[:, kb, :],
                                                        start=(kb == 0), stop=(kb == nkb - 1)), r=[c.constb, PTb], w=[Opb])
                dst = acc[rows, :, pair, cols]
                if pi == 0:
                    s.op("dve", lambda: nc.vector.tensor_copy(dst, OL[rows, :, :]), r=[Opb], w=[accb])
                else:
                    s.op("dve", lambda: nc.vector.tensor_tensor(dst, dst, OL[rows, :, :], ALU.add),
                         r=[Opb, accb], w=[accb])

            Acur = phaseA(units[0])
            for k, u in enumerate(units):
                Anext = phaseA(units[k + 1]) if k + 1 < len(units) else None
                phaseB(u, Acur)
                Acur = Anext
            for pair in range(4):
                for hh in range(2):
                    ts_ = slice(hh * 1024, (hh + 1) * 1024)
                    rc, rcb = rcr.next()
                    s.op("dve", lambda: nc.vector.reciprocal(rc[:, :], acc[:, 1, pair, ts_]), r=[accb], w=[rcb])
                    ob, obb = obr.next()
                    s.op("dve", lambda: nc.vector.tensor_tensor(ob[:, :], acc[:, 0, pair, ts_], rc[:, :], ALU.mult),
                         r=[accb, rcb], w=[obb])
                    s.dma("pool", mixedT[:, pair, tok0 + hh * 1024:tok0 + (hh + 1) * 1024], ob[:, :], r=[obb])

        qS, qSb = st.sb("qS", [128, 4, 16], BF16)
        kS, kSb = st.sb("kS", [128, 4, 16], BF16)
        oS, oSb = st.sb("oS", [128, 4, 16], BF16)
        s.dma("sp", qS[:, :, :], qT[:, :, NTP:NTP + 16], w=[qSb])
        s.dma("sp", kS[:, :, :], kT[:, :, NTP:NTP + 16], w=[kSb])
        vrr = st.ring("vrow", [1, 512], BF16, 3)
        kgr = st.ring("Kg", [128, 512], F32, 3)
        vgr = st.ring("Vg", [128, 512], F32, 3)
        kgbr = st.ring("Kgb", [128, 512], BF16, 2)
        vgbr = st.ring("Vgb", [128, 512], BF16, 4)
        kgtr = st.ring("KgT", [128, 4, 128], BF16, 2)
        s8r = st.ring("s8", [128, 16], F32, 3)
        p8r = st.ring("p8", [128, 16], BF16, 3)
        t8r = st.ring("t8", [128, 16], F32, 2)
        pstr = st.psring("pstA", [128, 8, 128], BF16, 1)
        smp, smb = st.ps("psm", [128, 512], F32)
        Sgb = Sob = OSb = smb
        Sg = smp[:, 0:8]
        So = smp[0:1, 8:16]
        OS = smp[:, 16:32].rearrange("p (a h) -> p a h", a=2)
        for i in range(16):
            vrow, vrowb = vrr.next()
            s.dma("sp", vrow[0:1, :], v_bf[NTP + i:NTP + i + 1, :], w=[vrowb])
            keep = []
            for pi, (wnd, dil) in enumerate(PATTERNS):
                Kg, Kgb_ = kgr.next()
                Vg, Vgb_ = vgr.next()
                s.dma("sp", Kg[:, :], cache_k[i, 2048 - 128 * dil:2048:dil, :], w=[Kgb_])
                s.dma("sp", Vg[:, :], cache_v[i, 2048 - 128 * dil:2048:dil, :], w=[Vgb_])
                Kb, Kbb = kgbr.next()
                Vb, Vbb = vgbr.next()
                s.op("pool", lambda: nc.gpsimd.tensor_copy(Kb[:, :], Kg[:, :]), r=[Kgb_], w=[Kbb])
                s.op("pool", lambda: nc.gpsimd.tensor_copy(Vb[:, :], Vg[:, :]), r=[Vgb_], w=[Vbb])
                pt, ptb = pstr.next()
                for pr in range(4):
                    s.op("pe", lambda: nc.tensor.transpose(pt[:, pr, :], Kb[:, pr * 128:(pr + 1) * 128], c.ident_bf[:, :]),
                         r=[Kbb, c.constb], w=[ptb])
                KT_, KTb_ = kgtr.next()
                evac(c, KT_[:, :, :], pt[:, 0:4, :], [ptb], [KTb_])
                for h in range(8):
                    pair = h // 2
                    rows = slice(64 * (h % 2), 64 * (h % 2) + 64)
                    s.op("pe", lambda: nc.tensor.matmul(Sg[:, h:h + 1], KT_[rows, pair, :], qS[rows, pair, i:i + 1],
                                                        start=True, stop=True), r=[KTb_, qSb], w=[Sgb])
                    s.op("pe", lambda: nc.tensor.matmul(So[0:1, h:h + 1], kS[rows, pair, i:i + 1], qS[rows, pair, i:i + 1],
                                                        start=True, stop=True), r=[kSb, qSb], w=[Sob])
                s8, s8b = s8r.next()
                s.op("dve", lambda: nc.vector.tensor_tensor(s8[:, 0:8], Sg, BT4[:, pi * 8:(pi + 1) * 8, 1, 0], ALU.add),
                     r=[Sgb, BTb], w=[s8b])
                s.op("dve", lambda: nc.vector.tensor_tensor(s8[0:1, 8:16], So, BT4[0:1, pi * 8:(pi + 1) * 8, 0, 0], ALU.add),
                     r=[Sob, BTb], w=[s8b])
                p8, p8b = p8r.next()
                s.op("act", lambda: nc.scalar.activation(out=p8[:, 0:8], in_=s8[:, 0:8], func=AF.Exp), r=[s8b], w=[p8b])
                s.op("act", lambda: nc.scalar.activation(out=p8[0:1, 8:16], in_=s8[0:1, 8:16], func=AF.Exp), r=[s8b], w=[p8b])
                keep.append((Vb, Vbb, p8, p8b))
            for a_ in range(2):
                for h in range(8):
                    pair = h // 2
                    for pi, (Vb, Vbb, p8, p8b) in enumerate(keep):
                        lh = Vb[:, pair * 128:(pair + 1) * 128] if a_ == 0 else c.ones_bf[:, :]
                        lo = vrow[0:1, pair * 128:(pair + 1) * 128] if a_ == 0 else c.ones_bf[0:1, :]
                        s.op("pe", lambda: nc.tensor.matmul(OS[:, a_, h:h + 1], lh, p8[:, h:h + 1],
                                                            start=(pi == 0), stop=False), r=[Vbb, c.constb, p8b], w=[OSb])
                        s.op("pe", lambda: nc.tensor.matmul(OS[:, a_, h:h + 1], lo, p8[0:1, 8 + h:9 + h],
                                                            start=False, stop=(pi == 2)), r=[vrowb, c.constb, p8b], w=[OSb])
            t8, t8b = t8r.next()
            s.op("dve", lambda: nc.vector.reciprocal(t8[:, 8:16], OS[:, 1, :]), r=[OSb], w=[t8b])
            s.op("dve", lambda: nc.vector.tensor_tensor(t8[:, 0:8], OS[:, 0, :], t8[:, 8:16], ALU.mult), r=[OSb, t8b], w=[t8b])
            for half in range(2):
                rows = slice(64 * half, 64 * half + 64)
                s.op("dve", lambda: nc.vector.tensor_copy(oS[rows, :, i], t8[rows, half:8:2]), r=[t8b], w=[oSb])
        s.dma("pool", mixedT[:, 0:4, NTP:NTP + 16], oS[:, :, :], r=[oSb])


def ssd_stage(c, xbcT, dt_scr, z_scr, mixedT, convw_d, convb_d, vec16_d, gssm_d, tri_d,
              cconv_d, state_d, conv_out, ssm_out, NB, NTP):
    s, nc = c.s, c.nc
    NSTREAM = 2
    with Stage(c) as st:
        cw, cwb = st.sb("cw", [128, 12, 4], F32)
        cb, cbb = st.sb("cb", [128, 12], F32)
        v16, v16b = st.sb("v16", [128, 3, 16], F32)
        gss, gssb = st.sb("gss", [128, 1024], F32)
        tri, trib = st.sb("tri", [128, 2, 128], F32)
        onesf, onesfb = st.sb("onesf", [128, 128], F32)
        s.dma("sp", cw[:, :, :], convw_d[:, :, :], w=[cwb])
        s.dma("sp", cb[:, :], convb_d[:, :], w=[cbb])
        s.dma("sp", v16[:, :, :], vec16_d[0:1, :, :].partition_broadcast(128), w=[v16b])
        s.dma("sp", gss[:, :], gssm_d[0:1, :].partition_broadcast(128), w=[gssb])
        s.dma("sp", tri[:, :, :], tri_d[:, :, :], w=[trib])
        s.op("pool", lambda: nc.gpsimd.memset(onesf[:, :], 1.0), w=[onesfb])
        s.op("act", lambda: nc.scalar.activation(out=v16[:, 1, :], in_=v16[:, 1, :], func=AF.Exp), r=[v16b], w=[v16b])
        s.op("dve", lambda: nc.vector.tensor_scalar(v16[:, 1, :], v16[:, 1, :], -1.0, None, ALU.mult), r=[v16b], w=[v16b])
        dtb_bc, a_bc, dsk_bc = v16[:, 0, :], v16[:, 1, :], v16[:, 2, :]
        triU, strictL = tri[:, 0, :], tri[:, 1, :]

        def make_stream(k):
            S = Ctx()
            n = f"s{k}"
            S.xin = st.sb(n + "xin", [128, 12, 131], F32)
            S.cacc = st.sb(n + "cacc", [128, 12, 128], F32)
            S.ctmp = st.sb(n + "ctmp", [128, 12, 128], F32)
            S.c2 = st.sb(n + "c2", [128, 12, 128], F32)
            S.a32 = st.sb(n + "a32", [128, 12, 128], F32)
            S.BCT = st.sb(n + "BCT", [128, 4, 128], BF16)
            S.Btok = st.sb(n + "Btok", [128, 2, 128], BF16)
            S.xdt = st.sb(n + "xdt", [128, 16, 64], BF16)
            S.xdts = st.sb(n + "xdts", [128, 16, 64], BF16)
            S.dsk = st.sb(n + "dsk", [128, 16, 64], F32)
            S.dtt = st.sb(n + "dtt", [128, 6, 16], F32)
            S.acs = st.sb(n + "acs", [128, 5, 16], F32)
            S.H = st.sb(n + "H", [128, 16, 64], F32)
            S.Hb = st.sb(n + "Hb", [128, 16, 64], BF16)
            S.CBm = st.sb(n + "CBm", [128, 2, 128], F32)
            S.rhsD = st.ring(n + "rhsD", [128, 4, 128], F32, 2)
            S.Es = st.ring(n + "Es", [128, 4, 128], F32, 1)
            S.MT = st.ring(n + "MT", [128, 4, 128], BF16, 2)
            S.YT = st.sb(n + "YTsb", [128, 2, 16, 64], F32)
            S.yt = st.sb(n + "yt", [128, 16, 64], F32)
            S.zt = st.sb(n + "zt", [128, 1024], F32)
            S.yn = st.sb(n + "yn", [128, 1024], F32)
            S.yT = st.sb(n + "yT", [128, 8, 128], BF16)
            S.stat = st.ring(n + "stat", [128, 4], F32, 2)
            S.junk = st.sb(n + "junk", [128, 512], BF16)
            S.YTp = st.ps(n + "YTp", [128, 512], F32)
            S.Fp = st.ps(n + "Fp", [128, 512], F32)
            S.Mr = st.psring(n + "M", [128, 512], F32, 2)
            return S

        def front(S, item):
            kind, idx, tok0, ch, nchunk = item
            t0 = tok0 + ch * 128
            sidx = idx if kind == "p" else NB + idx
            Mr = S.Mr
            xin, xinb = S.xin
            if kind == "p":
                if ch == 0:
                    s.op("pool", lambda: nc.gpsimd.memset(xin[:, :, 0:3], 0.0), w=[xinb])
                    s.dma("sp", xin[:, :, 3:131], xbcT[:, :, t0:t0 + 128], w=[xinb])
                else:
                    s.dma("sp", xin[:, :, :], xbcT[:, :, t0 - 3:t0 + 128], w=[xinb])
            else:
                s.op("pool", lambda: nc.gpsimd.memset(xin[:, :, :], 0.0), w=[xinb])
                s.dma("sp", xin[:, :, 0:3], cconv_d[idx], w=[xinb])
                s.dma("sp", xin[:, :, 3:4], xbcT[:, :, t0:t0 + 1], w=[xinb], slow=True)
            dtt, dttb = S.dtt
            if kind == "p":
                s.dma("sp", dtt[:, 0, :], dt_scr[t0:t0 + 128, :], w=[dttb])
            else:
                s.op("pool", lambda: nc.gpsimd.memset(dtt[:, 0, :], 0.0), w=[dttb])
                s.dma("sp", dtt[0:1, 0, :], dt_scr[t0:t0 + 1, :], w=[dttb])
            zt, ztb = S.zt
            if kind == "p":
                s.dma("sp", zt[:, :], z_scr[t0:t0 + 128, :], w=[ztb])
            else:
                s.op("pool", lambda: nc.gpsimd.memset(zt[:, :], 0.0), w=[ztb])
                s.dma("sp", zt[0:1, :], z_scr[t0:t0 + 1, :], w=[ztb])
            if ch == nchunk - 1:
                lo = 128 if kind == "p" else 1
                s.dma("sp", conv_out[sidx], xin[:, :, lo:lo + 3], r=[xinb])
            cacc, caccb = S.cacc
            ctmp, ctmpb = S.ctmp
            c2, c2b = S.c2
            a32, a32b = S.a32

            def wbc(i):
                return cw[:, :, i:i + 1].to_broadcast([128, 12, 128])
            s.op("pool", lambda: nc.gpsimd.tensor_tensor(cacc[:, :, :], xin[:, :, 3:131], wbc(3), ALU.mult), r=[xinb, cwb], w=[caccb])
            s.op("dve", lambda: nc.vector.tensor_tensor(c2[:, :, :], xin[:, :, 1:129], wbc(1), ALU.mult), r=[xinb, cwb], w=[c2b])
            s.op("pool", lambda: nc.gpsimd.tensor_tensor(ctmp[:, :, :], xin[:, :, 2:130], wbc(2), ALU.mult), r=[xinb, cwb], w=[ctmpb])
            s.op("dve", lambda: nc.vector.tensor_tensor(a32[:, :, :], xin[:, :, 0:128], wbc(0), ALU.mult), r=[xinb, cwb], w=[a32b])
            s.op("pool", lambda: nc.gpsimd.tensor_tensor(cacc[:, :, :], cacc[:, :, :], ctmp[:, :, :], ALU.add), r=[ctmpb, caccb], w=[caccb])
            s.op("dve", lambda: nc.vector.tensor_tensor(c2[:, :, :], c2[:, :, :], a32[:, :, :], ALU.add), r=[c2b, a32b], w=[c2b])
            s.op("dve", lambda: nc.vector.tensor_tensor(c2[:, :, :], c2[:, :, :], cacc[:, :, :], ALU.add), r=[c2b, caccb], w=[c2b])
            for j in range(12):
                s.op("act", lambda: nc.scalar.activation(out=a32[:, j, :], in_=c2[:, j, :], func=AF.Silu, bias=cb[:, j:j + 1]),
                     r=[c2b, cbb], w=[a32b])
            BCT, BCTb = S.BCT
            s.op("act", lambda: nc.scalar.copy(BCT[:, :, :], a32[:, 8:12, :]), r=[a32b], w=[BCTb])
            s.op("dve", lambda: nc.vector.tensor_tensor(dtt[:, 1, :], dtt[:, 0, :], dtb_bc, ALU.add), r=[dttb, v16b], w=[dttb])
            s.op("act", lambda: nc.scalar.activation(out=dtt[:, 1, :], in_=dtt[:, 1, :], func=AF.Exp), r=[dttb], w=[dttb])
            s.op("dve", lambda: nc.vector.tensor_scalar(dtt[:, 1, :], dtt[:, 1, :], 1.0, None, ALU.add), r=[dttb], w=[dttb])
            s.op("act", lambda: nc.scalar.activation(out=dtt[:, 2, :], in_=dtt[:, 1, :], func=AF.Ln), r=[dttb], w=[dttb])
            if kind == "s":
                s.op("dve", lambda: nc.vector.tensor_scalar(dtt[:, 2, :], dtt[:, 2, :], c.ident_f[:, 0:1], None, ALU.mult),
                     r=[dttb, c.constb], w=[dttb])
            s.op("dve", lambda: nc.vector.tensor_tensor(dtt[:, 3, :], dtt[:, 2, :], a_bc, ALU.mult), r=[dttb, v16b], w=[dttb])
            dt_, la = dtt[:, 2, :], dtt[:, 3, :]
            Mc, Mcb = Mr.next()
            s.op("pe", lambda: nc.tensor.matmul(Mc[:, 0:16], triU, la, start=True, stop=True), r=[trib, dttb], w=[Mcb])
            s.op("pe", lambda: nc.tensor.matmul(Mc[:, 16:32], onesf[:, :], la, start=True, stop=True), r=[onesfb, dttb], w=[Mcb])
            acs, acsb = S.acs
            s.op("dve", lambda: nc.vector.tensor_copy(acs[:, 0:2, :], Mc[:, 0:32].rearrange("p (a e) -> p a e", a=2)),
                 r=[Mcb], w=[acsb])
            s.op("dve", lambda: nc.vector.tensor_tensor(acs[:, 3, :], acs[:, 1, :], acs[:, 0, :], ALU.subtract), r=[acsb], w=[acsb])
            s.op("act", lambda: nc.scalar.activation(out=acs[:, 2, :], in_=acs[:, 0, :], func=AF.Exp), r=[acsb], w=[acsb])
            s.op("act", lambda: nc.scalar.activation(out=acs[:, 3, :], in_=acs[:, 3, :], func=AF.Exp), r=[acsb], w=[acsb])
            s.op("act", lambda: nc.scalar.activation(out=acs[:, 4, :], in_=acs[:, 1, :], func=AF.Exp), r=[acsb], w=[acsb])
            s.op("dve", lambda: nc.vector.tensor_tensor(dtt[:, 4, :], dt_, acs[:, 3, :], ALU.mult), r=[dttb, acsb], w=[dttb])
            dtd = dtt[:, 4, :]
            xdt, xdtb = S.xdt
            xdts, xdtsb = S.xdts
            dsk, dskb = S.dsk
            for g in range(2):
                m, mb = Mr.next()
                for j in range(g * 4, g * 4 + 4):
                    s.op("pe", lambda: nc.tensor.transpose(m[:, (j % 4) * 128:(j % 4 + 1) * 128], a32[:, j, :], c.ident_f[:, :]),
                         r=[a32b, c.constb], w=[mb])
                mv = m[:, :].rearrange("p (e q) -> p e q", e=8)
                hs = slice(g * 8, g * 8 + 8)
                s.op("dve", lambda: nc.vector.tensor_tensor(xdt[:, hs, :], mv, dt_[:, hs].unsqueeze(2).to_broadcast([128, 8, 64]), ALU.mult),
                     r=[mb, dttb], w=[xdtb])
                s.op("dve", lambda: nc.vector.tensor_tensor(xdts[:, hs, :], mv, dtd[:, hs].unsqueeze(2).to_broadcast([128, 8, 64]), ALU.mult),
                     r=[mb, dttb], w=[xdtsb])
                s.op("dve", lambda: nc.vector.tensor_tensor(dsk[:, hs, :], mv, dsk_bc[:, hs].unsqueeze(2).to_broadcast([128, 8, 64]), ALU.mult),
                     r=[mb, v16b], w=[dskb])
            Mb_, Mbb = Mr.next()
            for g in range(2):
                s.op("pe", lambda: nc.tensor.transpose(Mb_[:, g * 128:(g + 1) * 128], a32[:, 8 + g, :], c.ident_f[:, :]),
                     r=[a32b, c.constb], w=[Mbb])
            Btok, Btokb = S.Btok
            s.op("act", lambda: nc.scalar.copy(Btok[:, :, :], Mb_[:, 0:256].rearrange("p (g n) -> p g n", g=2)), r=[Mbb], w=[Btokb])
            CBm, CBmb = S.CBm
            Mg, Mgb = Mr.next()
            for g in range(2):
                s.op("pe", lambda: nc.tensor.matmul(Mg[:, g * 128:(g + 1) * 128], BCT[:, g, :], BCT[:, 2 + g, :], start=True, stop=True),
                     r=[BCTb], w=[Mgb])
            s.op("dve", lambda: nc.vector.tensor_tensor(CBm[:, :, :], Mg[:, 0:256].rearrange("p (g l) -> p g l", g=2),
                                                        triU.unsqueeze(1).to_broadcast([128, 2, 128]), ALU.mult), r=[Mgb, trib], w=[CBmb])
            YT, YTb = S.YT
            YTp, YTpb = S.YTp
            for g in range(2):
                for q4 in range(2):
                    e0 = g * 8 + q4 * 4
                    rhsD, rhsDb = S.rhsD.next()
                    s.op("pool", lambda: nc.gpsimd.tensor_tensor(rhsD[:, :, :], triU.unsqueeze(1).to_broadcast([128, 4, 128]),
                                                                 la[:, e0:e0 + 4].unsqueeze(2).to_broadcast([128, 4, 128]), ALU.mult),
                         r=[trib, dttb], w=[rhsDb])
                    Md, Mdb = Mr.next()
                    s.op("pe", lambda: nc.tensor.matmul(Md[:, :], strictL, rhsD[:, :, :].rearrange("p e l -> p (e l)"), start=True, stop=True),
                         r=[trib, rhsDb], w=[Mdb])
                    Es, Esb = S.Es.next()
                    s.op("act", lambda: nc.scalar.activation(out=Es[:, :, :].rearrange("p e l -> p (e l)"), in_=Md[:, :], func=AF.Exp),
                         r=[Mdb], w=[Esb])
                    MT, MTb = S.MT.next()
                    s.op("dve", lambda: nc.vector.tensor_tensor(MT[:, :, :], Es[:, :, :],
                                                                CBm[:, g:g + 1, :].to_broadcast([128, 4, 128]), ALU.mult),
                         r=[Esb, CBmb], w=[MTb])
                    for e in range(e0, e0 + 4):
                        cs = slice((e - e0) * 64, (e - e0) * 64 + 64)
                        cs2 = slice(256 + (e - e0) * 64, 256 + (e - e0) * 64 + 64)
                        s.op("pe", lambda: nc.tensor.matmul(YTp[:, cs], MT[:, e - e0, :], xdt[:, e, :], start=True, stop=True),
                             r=[MTb, xdtb], w=[YTpb])
                        s.op("pe", lambda: nc.tensor.matmul(YTp[:, cs2], Btok[:, g, :], xdts[:, e, :], start=True, stop=True),
                             r=[Btokb, xdtsb], w=[YTpb])
                    s.op("act", lambda: nc.scalar.copy(YT[:, :, e0:e0 + 4, :], YTp[:, :].rearrange("p (a e q) -> p a e q", a=2, e=4)),
                         r=[YTpb], w=[YTb])

        def back(S, item):
            kind, idx, tok0, ch, nchunk = item
            t0 = tok0 + ch * 128
            sidx = idx if kind == "p" else NB + idx
            Mr = S.Mr
            H, Hb_ = S.H
            Hbc, Hbcb = S.Hb
            acs, acsb = S.acs
            BCT, BCTb = S.BCT
            YT, YTb = S.YT
            Fp, Fpb = S.Fp
            eacs, cdec = acs[:, 2, :], acs[:, 4, :]
            if ch == 0:
                if kind == "p":
                    s.op("dve", lambda: nc.vector.memset(H[:, :, :], 0.0), w=[Hb_])
                else:
                    s.dma("sp", H[:, :, :], state_d[idx].rearrange("n (e p) -> n e p", e=16), w=[Hb_])
                s.op("act", lambda: nc.scalar.copy(Hbc[:, :, :], H[:, :, :]), r=[Hb_], w=[Hbcb])
            yt, ytb = S.yt
            for g in range(2):
                hs = slice(g * 8, g * 8 + 8)
                for e in range(g * 8, g * 8 + 8):
                    cs = slice((e % 8) * 64, (e % 8) * 64 + 64)
                    s.op("pe", lambda: nc.tensor.matmul(Fp[:, cs], BCT[:, 2 + g, :], Hbc[:, e, :], start=True, stop=True),
                         r=[BCTb, Hbcb], w=[Fpb])
                fv = Fp[:, :].rearrange("p (e q) -> p e q", e=8)
                s.op("dve", lambda: nc.vector.tensor_tensor(yt[:, hs, :], fv, eacs[:, hs].unsqueeze(2).to_broadcast([128, 8, 64]), ALU.mult),
                     r=[Fpb, acsb], w=[ytb])
                s.op("dve", lambda: nc.vector.tensor_tensor(H[:, hs, :], H[:, hs, :], cdec[:, hs].unsqueeze(2).to_broadcast([128, 8, 64]), ALU.mult),
                     r=[Hb_, acsb], w=[Hb_])
                s.op("dve", lambda: nc.vector.tensor_tensor(H[:, hs, :], H[:, hs, :], YT[:, 1, hs, :], ALU.add), r=[Hb_, YTb], w=[Hb_])
            s.op("act", lambda: nc.scalar.copy(Hbc[:, :, :], H[:, :, :]), r=[Hb_], w=[Hbcb])
            s.op("pool", lambda: nc.gpsimd.tensor_tensor(yt[:, :, :], yt[:, :, :], YT[:, 0, :, :], ALU.add), r=[ytb, YTb], w=[ytb])
            s.op("pool", lambda: nc.gpsimd.tensor_tensor(yt[:, :, :], yt[:, :, :], S.dsk[0][:, :, :], ALU.add), r=[ytb, S.dsk[1]], w=[ytb])
            zt, ztb = S.zt
            s.op("act", lambda: nc.scalar.activation(out=zt[:, :], in_=zt[:, :], func=AF.Silu), r=[ztb], w=[ztb])
            ytf = yt[:, :, :].rearrange("p e q -> p (e q)")
            s.op("dve", lambda: nc.vector.tensor_tensor(ytf, ytf, zt[:, :], ALU.mult), r=[ytb, ztb], w=[ytb])
            yn, ynb = S.yn
            jk, jkb = S.junk
            for g in range(2):
                gs = slice(g * 512, (g + 1) * 512)
                ss, ssb = S.stat.next()
                s.op("act", lambda: nc.scalar.activation(out=jk[:, 0:512], in_=ytf[:, gs], func=AF.Square, accum_out=ss[:, 0:1]),
                     r=[ytb], w=[jkb, ssb])
                rstd_from_ss(c, ss, ssb, 512)
                s.op("dve", lambda: nc.vector.scalar_tensor_tensor(yn[:, gs], ytf[:, gs], ss[:, 2:3], gss[:, gs], ALU.mult, ALU.mult),
                     r=[ytb, ssb, gssb], w=[ynb])
            yT, yTb = S.yT
            for g in range(2):
                m, mb = Mr.next()
                for j in range(g * 4, g * 4 + 4):
                    s.op("pe", lambda: nc.tensor.transpose(m[:, (j % 4) * 128:(j % 4 + 1) * 128], yn[:, j * 128:(j + 1) * 128], c.ident_f[:, :]),
                         r=[ynb, c.constb], w=[mb])
                evac(c, yT[:, g * 4:(g + 1) * 4, :], m[:, :].rearrange("p (j t) -> p j t", j=4), [mb], [yTb])
            if kind == "p":
                s.dma("sp", mixedT[:, 4:12, t0:t0 + 128], yT[:, :, :], r=[yTb])
            else:
                s.dma("sp", mixedT[:, 4:12, t0:t0 + 1], yT[:, :, 0:1], r=[yTb], slow=True)
            if ch == nchunk - 1:
                s.dma("sp", ssm_out[sidx].rearrange("n (e p) -> n e p", e=16), H[:, :, :], r=[Hb_])

        streams = [make_stream(k) for k in range(NSTREAM)]
        work = [[] for _ in range(NSTREAM)]
        for b in range(NB):
            for ch in range(16):
                work[b % NSTREAM].append(("p", b, b * 2048, ch, 16))
        for i in range(16):
            work[(i + NB) % NSTREAM].append(("s", i, NTP + i, 0, 1))

        def runner(k):
            def run():
                for item in work[k]:
                    front(streams[k], item)
                    back(streams[k], item)
            return run
        Interleave(s, [runner(k) for k in range(NSTREAM)]).run()


def outcross_stage(c, x1, mixedT, gcol, wout_d, wcq_d, wco_d, memKT, memV, cache_mk, cache_mv, x3, tiles, NTP):
    s, nc = c.s, c.nc
    with Stage(c) as st:
        st.wstage = st.ring("wst", [128, 1024], F32, 3)
        wO, wOb = st.sb("wO", [128, 12, D], BF16)
        wQ, wQb = st.sb("wQ", [128, 8, D], BF16)
        wC, wCb = st.sb("wCo", [128, 8, D], BF16)
        xtr = st.ring("xt", [128, 4, 1024], F32, 2)
        mTr = st.ring("mT", [128, 12, 512], BF16, 1)
        hTr = st.ring("hT", [128, 8, 512], BF16, 2)
        qxr = st.ring("qx", [128, 8, 512], BF16, 1)
        oTr = st.ring("oTn", [128, 8, 512], BF16, 1)
        ptr = st.ring("PT", [128, 512], BF16, 4)
        rlr = st.ring("rl", [128, 512], F32, 2)
        ktr = st.ring("KTm", [128, 8, 256], BF16, 2)
        vmr = st.ring("Vm", [128, 2, 1024], BF16, 2)
        ckr = st.ring("ck", [128, 2, 1024], F32, 2)
        cbr = st.ring("ckb", [128, 2, 1024], BF16, 1)
        norm_bufs(st)
        psr = st.psring("ps", [128, 512], F32, 6)
        load_weight(c, st, wout_d, wO, wOb, 12, D)
        load_weight(c, st, wcq_d, wQ, wQb, 8, D)
        load_weight(c, st, wco_d, wC, wCb, 8, D)

        def cross_core(KTm, KTmb, Vm, Vmb, qx, qxb, oT, oTb, cols, n):
            for h in range(4):
                PTs = []
                for mb in range(2):
                    ps, psb = psr.next()
                    for dc in range(2):
                        s.op("pe", lambda: nc.tensor.matmul(ps[:, 0:n], KTm[:, 2 * h + dc, mb * 128:(mb + 1) * 128],
                                                            qx[:, 2 * h + dc, cols], start=(dc == 0), stop=(dc == 1)),
                             r=[KTmb, qxb], w=[psb])
                    PT, PTb = ptr.next()
                    s.op("act", lambda: nc.scalar.activation(out=PT[:, 0:n], in_=ps[:, 0:n], func=AF.Exp), r=[psb], w=[PTb])
                    PTs.append((PT, PTb))
                ps, psb = psr.next()
                for mb in range(2):
                    s.op("pe", lambda: nc.tensor.matmul(ps[:, 0:n], c.ones_bf[:, :], PTs[mb][0][:, 0:n],
                                                        start=(mb == 0), stop=(mb == 1)), r=[c.constb, PTs[mb][1]], w=[psb])
                rl, rlb = rlr.next()
                s.op("dve", lambda: nc.vector.reciprocal(rl[:, 0:n], ps[:, 0:n]), r=[psb], w=[rlb])
                for dc in range(2):
                    ps, psb = psr.next()
                    for mb in range(2):
                        s.op("pe", lambda: nc.tensor.matmul(ps[:, 0:n], Vm[:, mb, (2 * h + dc) * 128:(2 * h + dc + 1) * 128],
                                                            PTs[mb][0][:, 0:n], start=(mb == 0), stop=(mb == 1)),
                             r=[Vmb, PTs[mb][1]], w=[psb])
                    s.op("dve", lambda: nc.vector.tensor_tensor(oT[:, 2 * h + dc, cols], ps[:, 0:n], rl[:, 0:n], ALU.mult),
                         r=[psb, rlb], w=[oTb])

        cur_b = -1
        KTm = Vm = None
        for (t0, nsub) in tiles:
            N = nsub * 128
            xt, xtb = xtr.next()
            s.dma("sp", xt[:, 0:nsub, :], x1[t0:t0 + N, :].rearrange("(s p) d -> p s d", p=128), w=[xtb])
            mT, mTb = mTr.next()
            s.dma("sp", mT[:, :, 0:N], mixedT[:, :, t0:t0 + N], w=[mTb])
            for sub in range(nsub):
                for hf in range(2):
                    ps, psb = psr.next()
                    mm_tm(c, mT, mTb, sub, wO, wOb, hf * 512, 512, ps, psb, KC=12)
                    xs_ = xt[:, sub, hf * 512:(hf + 1) * 512]
                    s.op("dve", lambda: nc.vector.tensor_tensor(xs_, xs_, ps[:, :], ALU.add), r=[psb, xtb], w=[xtb])
            hT, hTb = hTr.next()
            norm_transpose(c, st, xt, xtb, nsub, gcol, hT, hTb)
            qx, qxb = qxr.next()
            for j in range(8):
                ps, psb = psr.next()
                mm_fm(c, wQ, wQb, j * 128, hT, hTb, N, ps, psb)
                evac(c, qx[:, j, 0:N], ps[:, 0:N], [psb], [qxb], scale=1.0 / 16.0)
            oT, oTb = oTr.next()
            if t0 < NTP:
                b = t0 // 2048
                if b != cur_b:
                    cur_b = b
                    KTm, KTmb = ktr.next()
                    Vm, Vmb = vmr.next()
                    s.dma("sp", KTm[:, :, :], memKT[:, :, b * 256:(b + 1) * 256], w=[KTmb])
                    s.dma("sp", Vm[:, :, :], memV[b * 256:(b + 1) * 256, :].rearrange("(mb m) d -> m mb d", m=128), w=[Vmb])
                cross_core(KTm, KTmb, Vm, Vmb, qx, qxb, oT, oTb, slice(0, N), N)
            else:
                s.op("pool", lambda: nc.gpsimd.memset(oT[:, :, 0:N], 0.0), w=[oTb])
                for i in range(16):
                    ck, ckb_ = ckr.next()
                    s.dma("sp", ck[:, :, :], cache_mk[i].rearrange("(mb m) d -> m mb d", m=128), w=[ckb_])
                    cb_, cbb_ = cbr.next()
                    s.op("pool", lambda: nc.gpsimd.tensor_copy(cb_[:, :, :], ck[:, :, :]), r=[ckb_], w=[cbb_])
                    KTs, KTsb = ktr.next()
                    for mb in range(2):
                        pt, ptb = st.pst.next()
                        for j in range(8):
                            s.op("pe", lambda: nc.tensor.transpose(pt[:, j, :], cb_[:, mb, j * 128:(j + 1) * 128], c.ident_bf[:, :]),
                                 r=[cbb_, c.constb], w=[ptb])
                        evac(c, KTs[:, :, mb * 128:(mb + 1) * 128], pt[:, :, :], [ptb], [KTsb])
                    cv, cvb_ = ckr.next()
                    s.dma("sp", cv[:, :, :], cache_mv[i].rearrange("(mb m) d -> m mb d", m=128), w=[cvb_])
                    Vs, Vsb = vmr.next()
                    s.op("pool", lambda: nc.gpsimd.tensor_copy(Vs[:, :, :], cv[:, :, :]), r=[cvb_], w=[Vsb])
                    cross_core(KTs, KTsb, Vs, Vsb, qx, qxb, oT, oTb, slice(i, i + 1), 1)
            for sub in range(nsub):
                for hf in range(2):
                    ps, psb = psr.next()
                    mm_tm(c, oT, oTb, sub, wC, wCb, hf * 512, 512, ps, psb)
                    xs_ = xt[:, sub, hf * 512:(hf + 1) * 512]
                    s.op("dve", lambda: nc.vector.tensor_tensor(xs_, xs_, ps[:, :], ALU.add), r=[psb, xtb], w=[xtb])
            s.dma("pool", x3[t0:t0 + N, :].rearrange("(s p) d -> p s d", p=128), xt[:, 0:nsub, :], r=[xtb])


ALL_STAGES = ("memkv", "ffn1", "inproj", "attn", "ssd", "outcross", "ffn2")


def build(NB=4, stages=ALL_STAGES, dbg=()):
    nc = bass.Bass("TRN2", target_bir_lowering=False)
    c = Ctx()
    c.nc = nc
    c.s = Sch(nc)
    s = c.s
    NTP = NB * 2048
    NTOK = NTP + 128
    NBM = NB * 256
    NSEQ = NB + 16
    tiles = [(t * 512, 4) for t in range(NTP // 512)] + [(NTP, 1)]
    pb, bmask_np, negm_np = bias_consts()
    NPB = len(pb)

    def din(name, shape):
        return nc.dram_tensor(name, shape, F32, kind="ExternalInput").ap()

    def dout(name, shape):
        return nc.dram_tensor(name, shape, F32, kind="ExternalOutput").ap()

    def scr(name, shape, dt=F32):
        if name in dbg:
            return nc.dram_tensor(name, shape, dt, kind="ExternalOutput").ap()
        return nc.dram_tensor(name, shape, dt).ap()

    x_all = din("x_all", [NTOK, D])
    gcols_d = din("gcols", [128, 6, 8])
    gfin_d = din("gfin", [1, D])
    w1g, w1u, w1d = din("w1_gate", [128, 8, DFF]), din("w1_up", [128, 8, DFF]), din("w1_down", [128, 22, D])
    w2g, w2u, w2d = din("w2_gate", [128, 8, DFF]), din("w2_up", [128, 8, DFF]), din("w2_down", [128, 22, D])
    win_d = din("w_in", [128, 8, DIN])
    wout_d = din("w_out", [128, 12, D])
    wck_d, wcv_d, wcq_d, wco_d = (din(n, [128, 8, D]) for n in ("w_ck", "w_cv", "w_cq", "w_co"))
    mem_in = din("mem_in", [NBM, D])
    relb_d = din("rel_bias", [1, 256])
    bmask_d = din("bmask", [NPB, 128, 256])
    negm_d = din("negm", [3, 128, 256])
    convw_d = din("conv_w", [128, 12, 4])
    convb_d = din("conv_b", [128, 12])
    vec16_d = din("vec16", [1, 3, 16])
    gssm_d = din("g_ssm", [1, D])
    tri_d = din("tri", [128, 2, 128])
    ident_d = din("ident", [128, 128])
    cache_k = din("cache_k", [16, 2048, 512])
    cache_v = din("cache_v", [16, 2048, 512])
    cconv_d = din("cconv", [16, 128, 12, 3])
    state_d = din("state", [16, 128, 1024])
    cache_mk = din("cache_mk", [16, 256, D])
    cache_mv = din("cache_mv", [16, 256, D])

    y_out = dout("y_out", [NTOK, D])
    kT_out = dout("kT_out", [128, 4, NTOK])
    v_out = dout("v_out", [NTOK, 512])
    conv_out = dout("conv_out", [NSEQ, 128, 12, 3])
    ssm_out = dout("ssm_out", [NSEQ, 128, 1024])
    mk_out = dout("mk_out", [NBM, D])
    mv_out = dout("mv_out", [NBM, D])

    x1 = scr("x1", [NTOK, D])
    x3 = scr("x3", [NTOK, D])
    x4 = scr("x4", [NTOK, D])
    hT_scr = scr("hT_scr", [128, 8, NTOK], BF16)
    memKT = scr("memKT", [128, 8, NBM], BF16)
    memV = scr("memV", [NBM, D], BF16)
    qT = scr("qT", [128, 4, NTOK], BF16)
    kT = scr("kT", [128, 4, NTOK], BF16)
    v_bf = scr("v_bf", [NTOK, 512], BF16)
    z_scr = scr("z_scr", [NTOK, D])
    xbcT = scr("xbcT", [128, 12, NTOK])
    dt_scr = scr("dt_scr", [NTOK, 16])
    mixedT = scr("mixedT", [128, 12, NTOK], BF16)

    c.dbufs = {}

    def db(key):
        if key not in c.dbufs:
            c.dbufs[key] = Buf(str(key))
        return c.dbufs[key]
    c.db = db

    c.constb = Buf("const")
    c.ident_f = nc.alloc_sbuf_tensor("ident_f", [128, 128], F32)
    c.ident_bf = nc.alloc_sbuf_tensor("ident_bf", [128, 128], BF16)
    c.ones_bf = nc.alloc_sbuf_tensor("ones_bf", [128, 128], BF16)
    c.gcols = nc.alloc_sbuf_tensor("gcols_sb", [128, 6, 8], F32)
    c.gfin = nc.alloc_sbuf_tensor("gfin_sb", [128, D], F32)
    s.dma("sp", c.ident_f[:, :], ident_d[:, :], w=[c.constb])
    s.dma("sp", c.gcols[:, :, :], gcols_d[:, :, :], w=[c.constb])
    s.dma("sp", c.gfin[:, :], gfin_d[0:1, :].partition_broadcast(128), w=[c.constb])
    s.op("dve", lambda: nc.vector.tensor_copy(c.ident_bf[:, :], c.ident_f[:, :]), r=[c.constb], w=[c.constb])
    s.op("dve", lambda: nc.vector.memset(c.ones_bf[:, :], 1.0), w=[c.constb])
    s.barrier()

    if "memkv" in stages:
        memkv_stage(c, mem_in, c.gcols[:, 4, :], wck_d, wcv_d, mk_out, mv_out, memKT, memV, NBM)
    if "ffn1" in stages:
        ffn_stage(c, x_all, x1, c.gcols[:, 0, :], w1g, w1u, w1d, hT_scr, tiles, "f1")
    if "inproj" in stages:
        inproj_stage(c, x1, c.gcols[:, 1, :], win_d, qT, kT, kT_out, v_out, v_bf, z_scr, xbcT, dt_scr, tiles)
    if "attn" in stages:
        attn_stage(c, qT, kT, v_bf, mixedT, relb_d, bmask_d, negm_d, pb, cache_k, cache_v, NB, NTP)
    if "ssd" in stages:
        ssd_stage(c, xbcT, dt_scr, z_scr, mixedT, convw_d, convb_d, vec16_d, gssm_d, tri_d,
                  cconv_d, state_d, conv_out, ssm_out, NB, NTP)
    if "outcross" in stages:
        outcross_stage(c, x1, mixedT, c.gcols[:, 2, :], wout_d, wcq_d, wco_d, memKT, memV, cache_mk, cache_mv, x3, tiles, NTP)
    if "ffn2" in stages:
        ffn_stage(c, x3, x4, c.gcols[:, 3, :], w2g, w2u, w2d, hT_scr, tiles, "f2", final=(c.gfin[:, :], y_out))
    s.finish()
    return nc


def tile_w(w, kc):
    w = np.asarray(w, np.float32)
    K, F = w.shape
    return np.ascontiguousarray(w.reshape(kc, 128, F).transpose(1, 0, 2))


def gcol(g):
    return np.asarray(g, np.float32).reshape(8, 128).T


def tri_consts():
    j = np.arange(128)[:, None]
    l = np.arange(128)[None, :]
    return np.ascontiguousarray(np.stack([(j <= l), (j > l)], axis=1).astype(np.float32))


def shared_inputs(inp):
    f = lambda k: np.asarray(inp[k], np.float32)
    pb, bmask, negm = bias_consts()
    gc = np.zeros((128, 6, 8), np.float32)
    for n, k in enumerate(("g_ffn1", "g_mix", "g_cross", "g_ffn2", "g_mem")):
        gc[:, n, :] = gcol(f(k)[0])
    cw = f("conv_w")[0]
    sh = {
        "gcols": gc, "gfin": f("g_final").reshape(1, D),
        "w1_gate": tile_w(f("w1_gate")[0], 8), "w1_up": tile_w(f("w1_up")[0], 8), "w1_down": tile_w(f("w1_down")[0], 22),
        "w2_gate": tile_w(f("w2_gate")[0], 8), "w2_up": tile_w(f("w2_up")[0], 8), "w2_down": tile_w(f("w2_down")[0], 22),
        "w_in": tile_w(f("w_in")[0], 8), "w_out": tile_w(f("w_out")[0], 12),
        "w_ck": tile_w(f("w_ck")[0], 8), "w_cv": tile_w(f("w_cv")[0], 8),
        "w_cq": tile_w(f("w_cq")[0], 8), "w_co": tile_w(f("w_co")[0], 8),
        "rel_bias": f("rel_bias").reshape(1, 256),
        "bmask": bmask, "negm": negm,
        "conv_w": np.ascontiguousarray(cw.reshape(4, 12, 128).transpose(2, 1, 0)),
        "conv_b": np.ascontiguousarray(f("conv_b")[0].reshape(12, 128).T),
        "vec16": np.stack([f("dt_bias")[0], f("a_log")[0], f("d_skip")[0]])[None],
        "g_ssm": f("g_ssm").reshape(1, D),
        "tri": tri_consts(), "ident": np.eye(128, dtype=np.float32),
    }
    return sh


def core_inputs(inp, core, NB):
    f = lambda k: np.asarray(inp[k], np.float32)
    xp = f("x_prompt")[core * NB:(core + 1) * NB].reshape(NB * 2048, D)
    xs = f("x_sample")[core * 16:(core + 1) * 16].reshape(16, D)
    x_all = np.concatenate([xp, xs, np.zeros((112, D), np.float32)], 0)
    sl = slice(core * 16, (core + 1) * 16)
    cc = f("cache_conv")[0, sl]
    st = f("state_ssm")[0, sl]
    return {
        "x_all": x_all,
        "mem_in": f("mem_prompt")[core * NB:(core + 1) * NB].reshape(NB * 256, D),
        "cache_k": f("cache_win_k")[0, sl].reshape(16, 2048, 512),
        "cache_v": f("cache_win_v")[0, sl].reshape(16, 2048, 512),
        "cconv": np.ascontiguousarray(cc.reshape(16, 3, 12, 128).transpose(0, 3, 2, 1)),
        "state": np.ascontiguousarray(st.reshape(16, 1024, 128).transpose(0, 2, 1)),
        "cache_mk": f("cache_mem_k")[0, sl].reshape(16, 256, D),
        "cache_mv": f("cache_mem_v")[0, sl].reshape(16, 256, D),
    }


def assemble(results, NB, ncores):
    NTP = NB * 2048
    B = NB * ncores
    y_p = np.empty((B, 2048, D), np.float32)
    y_s = np.empty((16 * ncores, 1, D), np.float32)
    wk_p = np.empty((1, B, 2048, 8, 64), np.float32)
    wv_p = np.empty((1, B, 2048, 8, 64), np.float32)
    cv_p = np.empty((1, B, 3, 1536), np.float32)
    ss_p = np.empty((1, B, 16, 64, 128), np.float32)
    mk_p = np.empty((1, B, 256, 4, 256), np.float32)
    mv_p = np.empty((1, B, 256, 4, 256), np.float32)
    wk_s = np.empty((1, 16 * ncores, 1, 8, 64), np.float32)
    wv_s = np.empty((1, 16 * ncores, 1, 8, 64), np.float32)
    cv_s = np.empty((1, 16 * ncores, 3, 1536), np.float32)
    ss_s = np.empty((1, 16 * ncores, 16, 64, 128), np.float32)
    for cidx, r in enumerate(results):
        y = np.asarray(r["y_out"])
        ktok = np.asarray(r["kT_out"]).transpose(2, 1, 0).reshape(-1, 8, 64)
        vtok = np.asarray(r["v_out"]).reshape(-1, 8, 64)
        cv = np.asarray(r["conv_out"]).transpose(0, 3, 2, 1).reshape(-1, 3, 1536)
        ss = np.asarray(r["ssm_out"]).transpose(0, 2, 1).reshape(-1, 16, 64, 128)
        bs = slice(cidx * NB, (cidx + 1) * NB)
        ts = slice(cidx * 16, (cidx + 1) * 16)
        y_p[bs] = y[:NTP].reshape(NB, 2048, D)
        y_s[ts, 0] = y[NTP:NTP + 16]
        wk_p[0, bs] = ktok[:NTP].reshape(NB, 2048, 8, 64)
        wv_p[0, bs] = vtok[:NTP].reshape(NB, 2048, 8, 64)
        wk_s[0, ts, 0] = ktok[NTP:NTP + 16]
        wv_s[0, ts, 0] = vtok[NTP:NTP + 16]
        cv_p[0, bs] = cv[:NB]
        cv_s[0, ts] = cv[NB:]
        ss_p[0, bs] = ss[:NB]
        ss_s[0, ts] = ss[NB:]
        mk_p[0, bs] = np.asarray(r["mk_out"]).reshape(NB, 256, 4, 256)
        mv_p[0, bs] = np.asarray(r["mv_out"]).reshape(NB, 256, 4, 256)
    return (y_p, y_s, wk_p, wv_p, cv_p, ss_p, mk_p, mv_p, wk_s, wv_s, cv_s, ss_s)


def kernel(**inputs):
    NB = 4
    nc = build(NB=NB)
    sh = shared_inputs(inputs)
    in_maps = []
    for core in range(NCORES):
        m = dict(sh)
        m.update(core_inputs(inputs, core, NB))
        in_maps.append(m)
    res = run_bass_kernel_spmd(nc, in_maps, core_ids=list(range(NCORES)))
    return assemble(res.results, NB, NCORES)
```

```python
import numpy as np
import concourse.bass as bass
import concourse.mybir as mybir
from concourse.bass_utils import run_bass_kernel_spmd

F32 = mybir.dt.float32
BF16 = mybir.dt.bfloat16
AF = mybir.ActivationFunctionType
ALU = mybir.AluOpType
AX = mybir.AxisListType

D = 1024
DFF = 2816
NCORES = 8
EPS = 1e-6


class Buf:
    __slots__ = ("name", "w", "rs", "excl")

    def __init__(self, name="", excl=False):
        self.name = name
        self.w = None
        self.rs = {}
        self.excl = excl


class Sch:
    ND = 12

    def __init__(self, nc):
        self.nc = nc
        self.eng = {"pe": nc.tensor, "act": nc.scalar, "dve": nc.vector, "pool": nc.gpsimd, "sp": nc.sync}
        self.sem = {k: nc.alloc_semaphore("s_" + k) for k in self.eng}
        self.cnt = {k: 0 for k in self.eng}
        self.known = {k: {} for k in self.eng}
        self.dsem = {q: [[nc.alloc_semaphore(f"d_{q}{i}"), 0] for i in range(self.ND)]
                     for q in ("sp", "pool", "act")}
        self.dnext = {q: 0 for q in self.dsem}
        self.il = None

    def _deps(self, own, r, w):
        toks = []
        for b in r:
            if b.w is not None:
                toks.append(b.w)
        for b in w:
            if b.w is not None and b.w[0] is not own:
                toks.append(b.w)
            for sem, v in b.rs.items():
                if sem is not own:
                    toks.append((sem, v))
        return toks

    def _wait(self, e, toks):
        need = {}
        for sem, v in toks:
            if v > need.get(sem, 0):
                need[sem] = v
        kn = self.known[e]
        for sem, v in need.items():
            if kn.get(sem, 0) >= v:
                continue
            self.eng[e].wait_ge(sem, v)
            kn[sem] = v

    def _mark(self, tok, r, w):
        for b in r:
            if b.rs.get(tok[0], 0) < tok[1]:
                b.rs[tok[0]] = tok[1]
        for b in w:
            b.w = tok
            b.rs = {}

    def op(self, e, ins_fn, r=(), w=()):
        own = self.sem[e]
        if any(b.excl for b in r):
            w = list(w) + [b for b in r if b.excl]
            r = [b for b in r if not b.excl]
        self._wait(e, self._deps(own, r, w))
        ins = ins_fn()
        self.cnt[e] += 1
        ins.then_inc(own, 1)
        self._mark((own, self.cnt[e]), r, w)
        if self.il is not None:
            self.il.switch()
        return ins

    def dma(self, q, out, in_, r=(), w=(), slow=False):
        pool = self.dsem[q]
        i = self.dnext[q]
        self.dnext[q] = (i + 1) % len(pool)
        slot = pool[i]
        toks = self._deps(None, r, w)
        if slot[1] > 0:
            toks.append((slot[0], slot[1]))
        self._wait(q, toks)
        ins = (self.eng[q].dma_start(out=out, in_=in_, allow_slow_non_contiguous=True) if slow
               else self.eng[q].dma_start(out=out, in_=in_))
        slot[1] += 16
        ins.then_inc(slot[0], 16)
        self._mark((slot[0], slot[1]), r, w)
        if self.il is not None:
            self.il.switch()
        return ins

    def finish(self):
        toks = []
        for q in self.dsem:
            for sem, v in self.dsem[q]:
                if v > 0:
                    toks.append((sem, v))
        for k in self.eng:
            if self.cnt[k] > 0:
                toks.append((self.sem[k], self.cnt[k]))
        self._wait("sp", toks)


class Ring:
    def __init__(self, items):
        self.items = items
        self.i = 0

    def next(self):
        it = self.items[self.i]
        self.i = (self.i + 1) % len(self.items)
        return it


class Ctx:
    pass


def sb(nc, name, shape, dt):
    t = nc.alloc_sbuf_tensor(name, shape, dt)
    return t, Buf(name)


def sb_ring(nc, name, shape, dt, n):
    return Ring([sb(nc, f"{name}{i}", shape, dt) for i in range(n)])


import threading


class Interleave:
    def __init__(self, sch, fns):
        self.sch = sch
        self.fns = fns
        self.ev = [threading.Event() for _ in fns]
        self.alive = [True] * len(fns)
        self.idx = {}
        self.exc = None

    def _next(self, i):
        n = len(self.fns)
        for d in range(1, n + 1):
            j = (i + d) % n
            if j != i and self.alive[j]:
                return j
        return None

    def _wrap(self, i):
        self.idx[threading.get_ident()] = i
        self.ev[i].wait()
        self.ev[i].clear()
        try:
            if self.exc is None:
                self.fns[i]()
        except BaseException as e:
            self.exc = e
        finally:
            self.alive[i] = False
            j = self._next(i)
            if j is not None:
                self.ev[j].set()

    def switch(self):
        i = self.idx.get(threading.get_ident())
        if i is None:
            return
        if self.exc is not None:
            raise RuntimeError("sibling stream failed")
        j = self._next(i)
        if j is None:
            return
        self.ev[j].set()
        self.ev[i].wait()
        self.ev[i].clear()

    def run(self):
        ths = [threading.Thread(target=self._wrap, args=(i,)) for i in range(len(self.fns))]
        self.sch.il = self
        for t in ths:
            t.start()
        self.ev[0].set()
        for t in ths:
            t.join()
        self.sch.il = None
        if self.exc is not None:
            raise self.exc
from contextlib import ExitStack

DIN = 4112
PATTERNS = ((128, 1), (512, 4), (2048, 16))
NEG = -30000.0


def _barrier(self):
    toks = []
    for q in self.dsem:
        for sem, v in self.dsem[q]:
            if v > 0:
                toks.append((sem, v))
    for k in self.eng:
        if self.cnt[k] > 0:
            toks.append((self.sem[k], self.cnt[k]))
    for e in self.eng:
        self._wait(e, [t for t in toks if t[0] is not self.sem[e]])


Sch.barrier = _barrier


class Stage:
    _n = 0

    def __init__(self, c):
        self.c = c
        self.es = ExitStack()

    def __enter__(self):
        self.es.__enter__()
        return self

    def __exit__(self, *a):
        if a[0] is None:
            self.c.s.barrier()
        return self.es.__exit__(*a)

    def sb(self, name, shape, dt):
        Stage._n += 1
        t = self.es.enter_context(self.c.nc.sbuf_tensor(f"{name}_{Stage._n}", shape, dt))
        return t, Buf(name)

    def ring(self, name, shape, dt, n):
        return Ring([self.sb(f"{name}{i}", shape, dt) for i in range(n)])

    def ps(self, name, shape, dt=F32):
        Stage._n += 1
        t = self.es.enter_context(self.c.nc.psum_tensor(f"{name}_{Stage._n}", shape, dt))
        return t, Buf(name, excl=True)

    def psring(self, name, shape, dt, n):
        return Ring([self.ps(f"{name}{i}", shape, dt) for i in range(n)])


def load_weight(c, st, dram_w, dst, dst_buf, kc_n, f_n, piece=1024):
    s, nc = c.s, c.nc
    for kc in range(kc_n):
        for f0 in range(0, f_n, piece):
            fw = min(piece, f_n - f0)
            stg, stb = st.wstage.next()
            s.dma("sp", stg[:, 0:fw], dram_w[:, kc, f0:f0 + fw], w=[stb])
            s.op("pool", lambda: nc.gpsimd.tensor_copy(dst[:, kc, f0:f0 + fw], stg[:, 0:fw]),
                 r=[stb], w=[dst_buf])


def rstd_from_ss(c, ss, ssb, n):
    s, nc = c.s, c.nc
    s.op("dve", lambda: nc.vector.tensor_scalar(ss[:, 1:2], ss[:, 0:1], 1.0 / n, EPS, ALU.mult, ALU.add),
         r=[ssb], w=[ssb])
    s.op("act", lambda: nc.scalar.activation(out=ss[:, 3:4], in_=ss[:, 1:2], func=AF.Sqrt), r=[ssb], w=[ssb])
    s.op("dve", lambda: nc.vector.reciprocal(ss[:, 2:3], ss[:, 3:4]), r=[ssb], w=[ssb])


def norm_part(c, st, xt, xtb, nsub):
    s, nc = c.s, c.nc
    xns = []
    for sub in range(nsub):
        ss, ssb = st.stat.next()
        jk, jkb = st.junk.next()
        s.op("act", lambda: nc.scalar.activation(out=jk[:, :], in_=xt[:, sub, :], func=AF.Square,
                                                 accum_out=ss[:, 0:1]), r=[xtb], w=[jkb, ssb])
        rstd_from_ss(c, ss, ssb, D)
        xn, xnb = st.xn.next()
        s.op("act", lambda: nc.scalar.activation(out=xn[:, :], in_=xt[:, sub, :], func=AF.Copy,
                                                 scale=ss[:, 2:3]), r=[xtb, ssb], w=[xnb])
        xns.append((xn, xnb))
    return xns


def transpose_part(c, st, xns, gcol, hT, hTb):
    s, nc = c.s, c.nc
    for sub, (xn, xnb) in enumerate(xns):
        pt, ptb = st.pst.next()
        for kc in range(8):
            s.op("pe", lambda: nc.tensor.transpose(pt[:, kc, :], xn[:, kc * 128:(kc + 1) * 128], c.ident_bf[:, :]),
                 r=[xnb, c.constb], w=[ptb])
        s.op("dve", lambda: nc.vector.tensor_tensor(
            hT[:, :, sub * 128:(sub + 1) * 128], pt[:, :, :],
            gcol.unsqueeze(2).to_broadcast([128, 8, 128]), ALU.mult),
            r=[ptb, c.constb], w=[hTb])


def norm_transpose(c, st, xt, xtb, nsub, gcol, hT, hTb):
    transpose_part(c, st, norm_part(c, st, xt, xtb, nsub), gcol, hT, hTb)


def norm_bufs(st):
    st.stat = st.ring("stat", [128, 4], F32, 8)
    st.junk = st.ring("junk", [128, 1024], BF16, 2)
    st.xn = st.ring("xn", [128, 1024], BF16, 4)
    st.pst = st.psring("pst", [128, 8, 128], BF16, 2)


def mm_fm(c, W, Wb, f0, hT, hTb, N, ps, psb, KC=8):
    s, nc = c.s, c.nc
    for kc in range(KC):
        s.op("pe", lambda: nc.tensor.matmul(ps[:, 0:N], W[:, kc, f0:f0 + 128], hT[:, kc, 0:N],
                                            start=(kc == 0), stop=(kc == KC - 1)), r=[Wb, hTb], w=[psb])


def mm_tm(c, hT, hTb, sub, W, Wb, c0, ncols, ps, psb, KC=8):
    s, nc = c.s, c.nc
    for kc in range(KC):
        s.op("pe", lambda: nc.tensor.matmul(ps[:, 0:ncols], hT[:, kc, sub * 128:(sub + 1) * 128], W[:, kc, c0:c0 + ncols],
                                            start=(kc == 0), stop=(kc == KC - 1)), r=[Wb, hTb], w=[psb])


def evac(c, out, in_, r, w, scale=None):
    s, nc = c.s, c.nc
    c.flip = not getattr(c, "flip", False)
    if c.flip:
        if scale is None:
            s.op("act", lambda: nc.scalar.copy(out, in_), r=r, w=w)
        else:
            s.op("act", lambda: nc.scalar.mul(out, in_, scale), r=r, w=w)
    else:
        if scale is None:
            s.op("dve", lambda: nc.vector.tensor_copy(out, in_), r=r, w=w)
        else:
            s.op("dve", lambda: nc.vector.tensor_scalar(out, in_, scale, None, ALU.mult), r=r, w=w)


def ffn_stage(c, x_in, x_out, gcol, wg_d, wu_d, wd_d, hT_scr, tiles, tag, final=None):
    s, nc = c.s, c.nc
    H = DFF // 2
    HC = H // 128
    with Stage(c) as st:
        st.wstage = st.ring("wst", [128, 1024], F32, 6)
        wA, wAb = st.sb("wA", [128, 8, H], BF16)
        wB, wBb = st.sb("wB", [128, 8, H], BF16)
        wC, wCb = st.sb("wC", [128, HC, D], BF16)
        xtr = st.ring("xt", [128, 4, 1024], F32, 2)
        hTr = st.ring("hT", [128, 8, 512], BF16, 2)
        actr = st.ring("actT", [128, HC, 512], BF16, 2)
        sgr = st.ring("sg", [128, 512], F32, 2)
        norm_bufs(st)
        psr = st.psring("ps", [128, 512], F32, 6)
        for half in range(2):
            load_weight(c, st, wg_d[:, :, half * H:(half + 1) * H], wA, wAb, 8, H)
            load_weight(c, st, wu_d[:, :, half * H:(half + 1) * H], wB, wBb, 8, H)
            load_weight(c, st, wd_d[:, half * HC:(half + 1) * HC, :], wC, wCb, HC, D)
            src = x_in if half == 0 else x_out
            def prep1(tile):
                t0, nsub = tile
                N = nsub * 128
                xt, xtb = xtr.next()
                xob = c.db((tag, "xo", t0))
                rd = [xob] if half == 1 else [c.db((tag, "xi", t0))]
                s.dma("sp", xt[:, 0:nsub, :], src[t0:t0 + N, :].rearrange("(s p) d -> p s d", p=128), r=rd, w=[xtb])
                hT, hTb = hTr.next()
                xns = None
                if half == 0:
                    xns = norm_part(c, st, xt, xtb, nsub)
                else:
                    s.dma("sp", hT[:, :, 0:N], hT_scr[:, :, t0:t0 + N], r=[c.db((tag, "hs", t0))], w=[hTb])
                return (xt, xtb, hT, hTb, xob, xns, t0, N)

            def prep2(P):
                xt, xtb, hT, hTb, xob, xns, t0, N = P
                if half == 0:
                    transpose_part(c, st, xns, gcol, hT, hTb)
                    s.dma("pool", hT_scr[:, :, t0:t0 + N], hT[:, :, 0:N], r=[hTb], w=[c.db((tag, "hs", t0))])

            nxt = prep1(tiles[0])
            prep2(nxt)
            for ti, (t0, nsub) in enumerate(tiles):
                N = nsub * 128
                xt, xtb, hT, hTb, xob = nxt[0:5]
                if ti + 1 < len(tiles):
                    nxt = prep1(tiles[ti + 1])
                act, actb = actr.next()
                for j in range(HC):
                    pg, pgb = psr.next()
                    mm_fm(c, wA, wAb, j * 128, hT, hTb, N, pg, pgb)
                    pu, pub = psr.next()
                    mm_fm(c, wB, wBb, j * 128, hT, hTb, N, pu, pub)
                    sg, sgb = sgr.next()
                    s.op("act", lambda: nc.scalar.activation(out=sg[:, 0:N], in_=pg[:, 0:N], func=AF.Silu), r=[pgb], w=[sgb])
                    s.op("dve", lambda: nc.vector.tensor_tensor(act[:, j, 0:N], sg[:, 0:N], pu[:, 0:N], ALU.mult),
                         r=[sgb, pub], w=[actb])
                if ti + 1 < len(tiles):
                    prep2(nxt)
                for sub in range(nsub):
                    for hf in range(2):
                        pd, pdb = psr.next()
                        mm_tm(c, act, actb, sub, wC, wCb, hf * 512, 512, pd, pdb, KC=HC)
                        s.op("dve", lambda: nc.vector.scalar_tensor_tensor(
                            xt[:, sub, hf * 512:(hf + 1) * 512], pd[:, :], 0.5, xt[:, sub, hf * 512:(hf + 1) * 512],
                            ALU.mult, ALU.add), r=[pdb, xtb], w=[xtb])
                if half == 1 and final is not None:
                    gfin, y_out = final
                    for sub in range(nsub):
                        ss, ssb = st.stat.next()
                        jk, jkb = st.junk.next()
                        s.op("act", lambda: nc.scalar.activation(out=jk[:, :], in_=xt[:, sub, :], func=AF.Square,
                                                                 accum_out=ss[:, 0:1]), r=[xtb], w=[jkb, ssb])
                        rstd_from_ss(c, ss, ssb, D)
                        s.op("dve", lambda: nc.vector.scalar_tensor_tensor(
                            xt[:, sub, :], xt[:, sub, :], ss[:, 2:3], gfin, ALU.mult, ALU.mult),
                            r=[xtb, ssb, c.constb], w=[xtb])
                    s.dma("pool", y_out[t0:t0 + N, :].rearrange("(s p) d -> p s d", p=128), xt[:, 0:nsub, :], r=[xtb])
                else:
                    s.dma("pool", x_out[t0:t0 + N, :].rearrange("(s p) d -> p s d", p=128), xt[:, 0:nsub, :], r=[xtb], w=[xob])


def memkv_stage(c, mem_in, gcol, wck_d, wcv_d, mk_out, mv_out, memKT, memV, NBM):
    s, nc = c.s, c.nc
    with Stage(c) as st:
        st.wstage = st.ring("wst", [128, 1024], F32, 3)
        wK, wKb = st.sb("wK", [128, 8, D], BF16)
        wV, wVb = st.sb("wV", [128, 8, D], BF16)
        xtr = st.ring("xt", [128, 4, 1024], F32, 2)
        hTr = st.ring("hT", [128, 8, 512], BF16, 2)
        tmr = st.ring("tm", [128, 1024], F32, 3)
        tbr = st.ring("tb", [128, 1024], BF16, 2)
        fmr = st.ring("fm", [128, 512], BF16, 3)
        norm_bufs(st)
        psr = st.psring("ps", [128, 512], F32, 6)
        load_weight(c, st, wck_d, wK, wKb, 8, D)
        load_weight(c, st, wcv_d, wV, wVb, 8, D)
        tiles = []
        t0 = 0
        while t0 < NBM:
            n = min(4, (NBM - t0) // 128)
            tiles.append((t0, n))
            t0 += n * 128
        for (t0, nsub) in tiles:
            N = nsub * 128
            xt, xtb = xtr.next()
            s.dma("sp", xt[:, 0:nsub, :], mem_in[t0:t0 + N, :].rearrange("(s p) d -> p s d", p=128), w=[xtb])
            hT, hTb = hTr.next()
            norm_transpose(c, st, xt, xtb, nsub, gcol, hT, hTb)
            for sub in range(nsub):
                r0 = t0 + sub * 128
                for (W, Wb, outd, bfd) in ((wK, wKb, mk_out, None), (wV, wVb, mv_out, memV)):
                    tm, tmb = tmr.next()
                    for hf in range(2):
                        ps, psb = psr.next()
                        mm_tm(c, hT, hTb, sub, W, Wb, hf * 512, 512, ps, psb)
                        evac(c, tm[:, hf * 512:(hf + 1) * 512], ps[:, :], [psb], [tmb])
                    s.dma("pool", outd[r0:r0 + 128, :], tm[:, :], r=[tmb])
                    if bfd is not None:
                        tb, tbb = tbr.next()
                        s.op("pool", lambda: nc.gpsimd.tensor_copy(tb[:, :], tm[:, :]), r=[tmb], w=[tbb])
                        s.dma("pool", bfd[r0:r0 + 128, :], tb[:, :], r=[tbb])
            for j in range(8):
                ps, psb = psr.next()
                mm_fm(c, wK, wKb, j * 128, hT, hTb, N, ps, psb)
                fm, fmb = fmr.next()
                evac(c, fm[:, 0:N], ps[:, 0:N], [psb], [fmb])
                s.dma("pool", memKT[:, j, t0:t0 + N], fm[:, 0:N], r=[fmb])


def inproj_stage(c, x1, gcol, win_d, qT, kT, kT_out, v_out, v_bf, z_scr, xbcT, dt_scr, tiles):
    s, nc = c.s, c.nc
    with Stage(c) as st:
        st.wstage = st.ring("wst", [128, 1024], F32, 3)
        Wa, Wab = st.sb("winA", [128, 8, 2560], BF16)
        Wc, Wcb = st.sb("winB", [128, 8, DIN - 2560], BF16)
        xtr = st.ring("xt", [128, 4, 1024], F32, 2)
        hTr = st.ring("hT", [128, 8, 512], BF16, 2)
        f32r = st.ring("f32", [128, 512], F32, 6)
        bfr = st.ring("bf", [128, 512], BF16, 6)
        zr = st.ring("zt", [128, 1024], F32, 2)
        dtr = st.ring("dtt", [128, 16], F32, 3)
        norm_bufs(st)
        psr = st.psring("ps", [128, 512], F32, 6)
        load_weight(c, st, win_d[:, :, 0:2560], Wa, Wab, 8, 2560)
        load_weight(c, st, win_d[:, :, 2560:DIN], Wc, Wcb, 8, DIN - 2560)
        def prep1(tile):
            t0, nsub = tile
            N = nsub * 128
            xt, xtb = xtr.next()
            s.dma("sp", xt[:, 0:nsub, :], x1[t0:t0 + N, :].rearrange("(s p) d -> p s d", p=128), w=[xtb])
            hT, hTb = hTr.next()
            return (hT, hTb, norm_part(c, st, xt, xtb, nsub))

        nxt = prep1(tiles[0])
        transpose_part(c, st, nxt[2], gcol, nxt[0], nxt[1])
        for ti, (t0, nsub) in enumerate(tiles):
            N = nsub * 128
            hT, hTb = nxt[0], nxt[1]
            if ti + 1 < len(tiles):
                nxt = prep1(tiles[ti + 1])
            for j in range(20):
                f0 = j * 128 if j < 8 else 2560 + (j - 8) * 128
                ps, psb = psr.next()
                if f0 < 2560:
                    mm_fm(c, Wa, Wab, f0, hT, hTb, N, ps, psb)
                else:
                    mm_fm(c, Wc, Wcb, f0 - 2560, hT, hTb, N, ps, psb)
                if j < 4:
                    o, ob = bfr.next()
                    evac(c, o[:, 0:N], ps[:, 0:N], [psb], [ob], scale=0.125)
                    s.dma("pool", qT[:, j, t0:t0 + N], o[:, 0:N], r=[ob])
                elif j < 8:
                    o, ob = bfr.next()
                    evac(c, o[:, 0:N], ps[:, 0:N], [psb], [ob])
                    s.dma("pool", kT[:, j - 4, t0:t0 + N], o[:, 0:N], r=[ob])
                    o2, o2b = f32r.next()
                    evac(c, o2[:, 0:N], ps[:, 0:N], [psb], [o2b])
                    s.dma("pool", kT_out[:, j - 4, t0:t0 + N], o2[:, 0:N], r=[o2b])
                else:
                    o2, o2b = f32r.next()
                    evac(c, o2[:, 0:N], ps[:, 0:N], [psb], [o2b])
                    s.dma("pool", xbcT[:, j - 8, t0:t0 + N], o2[:, 0:N], r=[o2b])
            if ti + 1 < len(tiles):
                transpose_part(c, st, nxt[2], gcol, nxt[0], nxt[1])
            for sub in range(nsub):
                r0 = t0 + sub * 128
                ps, psb = psr.next()
                mm_tm(c, hT, hTb, sub, Wa, Wab, 1024, 512, ps, psb)
                o2, o2b = f32r.next()
                evac(c, o2[:, :], ps[:, :], [psb], [o2b])
                s.dma("pool", v_out[r0:r0 + 128, :], o2[:, :], r=[o2b])
                o, ob = bfr.next()
                evac(c, o[:, :], ps[:, :], [psb], [ob])
                s.dma("pool", v_bf[r0:r0 + 128, :], o[:, :], r=[ob])
                zt, ztb = zr.next()
                for hf in range(2):
                    ps, psb = psr.next()
                    mm_tm(c, hT, hTb, sub, Wa, Wab, 1536 + hf * 512, 512, ps, psb)
                    evac(c, zt[:, hf * 512:(hf + 1) * 512], ps[:, :], [psb], [ztb])
                s.dma("pool", z_scr[r0:r0 + 128, :], zt[:, :], r=[ztb])
                ps, psb = psr.next()
                mm_tm(c, hT, hTb, sub, Wc, Wcb, 4096 - 2560, 16, ps, psb)
                dtt, dtb = dtr.next()
                evac(c, dtt[:, :], ps[:, 0:16], [psb], [dtb])
                s.dma("pool", dt_scr[r0:r0 + 128, :], dtt[:, :], r=[dtb])


def t5_bucket_np(dist):
    d = np.maximum(np.asarray(dist, np.int64), 0)
    df = np.maximum(d, 1).astype(np.float32)
    large = 16 + (np.log(df / np.float32(16.0)) / np.float32(np.log(2048.0 / 16.0)) * np.float32(16.0)).astype(np.int32)
    large = np.minimum(large, 31)
    return np.where(d < 16, d, large).astype(np.int64)


def bias_consts():
    k = np.arange(128)[:, None, None]
    kb = np.arange(2)[None, :, None]
    q = np.arange(128)[None, None, :]
    delta = q + 128 * kb - k
    valid = (delta >= 0) & (delta <= 128)
    pb, masks, negm = [], [], []
    for pi, (wnd, dil) in enumerate(PATTERNS):
        bk = t5_bucket_np(delta * dil)
        negm.append(np.where(valid, 0.0, NEG).astype(np.float32).reshape(128, 256))
        for b in range(32):
            m = (valid & (bk == b))
            if m.any():
                pb.append((pi, b))
                masks.append(m.astype(np.float32).reshape(128, 256))
    return pb, np.stack(masks), np.stack(negm)


def attn_stage(c, qT, kT, v_bf, mixedT, relb_d, bmask_d, negm_d, pb, cache_k, cache_v, NB, NTP):
    s, nc = c.s, c.nc
    with Stage(c) as st:
        BT, BTb = st.sb("BT", [128, 24, 256], F32)
        rb, rbb = st.sb("rb", [128, 256], F32)
        mr = st.ring("bm", [128, 256], F32, 3)
        s.dma("sp", rb[:, :], relb_d[0:1, :].partition_broadcast(128), w=[rbb])
        for pi in range(3):
            for h in range(8):
                s.dma("sp", BT[:, pi * 8 + h, :], negm_d[pi], w=[BTb])
        for n, (pi, b) in enumerate(pb):
            m, mb = mr.next()
            s.dma("sp", m[:, :], bmask_d[n], w=[mb])
            for h in range(8):
                s.op("dve", lambda: nc.vector.scalar_tensor_tensor(
                    BT[:, pi * 8 + h, :], m[:, :], rb[:, b * 8 + h:b * 8 + h + 1], BT[:, pi * 8 + h, :],
                    ALU.mult, ALU.add), r=[mb, rbb, BTb], w=[BTb])
        BT4 = BT[:, :, :].rearrange("p n (kb q) -> p n kb q", kb=2)
        BTh, BThb = st.sb("BTh", [128, 24, 256], BF16)
        BTl, BTlb = st.sb("BTl", [128, 24, 256], BF16)
        btmp, btmpb = st.sb("btmp", [128, 8, 256], F32)
        for pi in range(3):
            ps_ = slice(pi * 8, pi * 8 + 8)
            s.op("dve", lambda: nc.vector.tensor_copy(BTh[:, ps_, :], BT[:, ps_, :]), r=[BTb], w=[BThb])
            s.op("dve", lambda: nc.vector.tensor_tensor(btmp[:, :, :], BT[:, ps_, :], BTh[:, ps_, :], ALU.subtract),
                 r=[BTb, BThb], w=[btmpb])
            s.op("dve", lambda: nc.vector.tensor_copy(BTl[:, ps_, :], btmp[:, :, :]), r=[btmpb], w=[BTlb])

        qTb, qTbb = st.sb("qTb", [128, 4, 2048], BF16)
        kTb, kTbb = st.sb("kTb", [128, 4, 2048], BF16)
        acc, accb = st.sb("acc", [128, 2, 4, 2048], F32)
        Vr = st.ring("Vt", [128, 512], BF16, 5)
        sbr = st.ring("sbs", [128, 2, 128], F32, 4)
        ptr = st.ring("PT", [128, 2, 128], BF16, 4)
        rcr = st.ring("rc", [128, 1024], F32, 1)
        obr = st.ring("ob", [128, 1024], BF16, 2)
        psS = st.psring("psS", [128, 512], F32, 3)
        psO = st.psring("psO", [128, 512], F32, 3)
        for b in range(NB):
            tok0 = b * 2048
            s.dma("sp", qTb[:, :, :], qT[:, :, tok0:tok0 + 2048], w=[qTbb])
            s.dma("sp", kTb[:, :, :], kT[:, :, tok0:tok0 + 2048], w=[kTbb])
            units = []
            for pi, (wnd, dil) in enumerate(PATTERNS):
                nblk = 2048 // dil // 128
                for r in range(dil):
                    for blk in range(nblk):
                        for h in range(8):
                            units.append((pi, dil, r, blk, h))
            vstate = {}

            def phaseA(u):
                pi, dil, r, blk, h = u
                cols = slice(r + dil * 128 * blk, r + dil * 128 * blk + dil * 127 + 1, dil)
                pcols = slice(r + dil * 128 * (blk - 1), r + dil * 128 * (blk - 1) + dil * 127 + 1, dil)
                if h == 0:
                    Vt, Vtb = Vr.next()
                    row0 = tok0 + r + dil * 128 * blk
                    s.dma("sp", Vt[:, :], v_bf[row0:row0 + 127 * dil + 1:dil, :], w=[Vtb])
                    prev = vstate.get((pi, r, blk - 1))
                    vstate[(pi, r, blk)] = (Vt, Vtb)
                    vstate[("cur", pi, r, blk)] = [(Vt, Vtb)] + ([prev] if blk > 0 else [])
                nkb = 2 if blk > 0 else 1
                pair = h // 2
                rows = slice(64 * (h % 2), 64 * (h % 2) + 64)
                Sp, Spb = psS.next()
                S = Sp[:, 0:256].rearrange("p (kb q) -> p kb q", kb=2)
                s.op("pe", lambda: nc.tensor.matmul(Sp[:, 0:nkb * 128], c.ident_bf[:, :], BTh[:, pi * 8 + h, 0:nkb * 128],
                                                    start=True, stop=False), r=[c.constb, BThb], w=[Spb])
                s.op("pe", lambda: nc.tensor.matmul(Sp[:, 0:nkb * 128], c.ident_bf[:, :], BTl[:, pi * 8 + h, 0:nkb * 128],
                                                    start=False, stop=False), r=[c.constb, BTlb], w=[Spb])
                s.op("pe", lambda: nc.tensor.matmul(S[:, 0, :], kTb[rows, pair, cols], qTb[rows, pair, cols],
                                                    start=False, stop=(nkb == 1)), r=[kTbb, qTbb], w=[Spb])
                if blk > 0:
                    s.op("pe", lambda: nc.tensor.matmul(S[:, 1, :], kTb[rows, pair, pcols], qTb[rows, pair, cols],
                                                        start=False, stop=True), r=[kTbb, qTbb], w=[Spb])
                PT, PTb = ptr.next()
                s.op("act", lambda: nc.scalar.activation(out=PT[:, 0:nkb, :], in_=S[:, 0:nkb, :], func=AF.Exp),
                     r=[Spb], w=[PTb])
                return (PT, PTb, cols, nkb, vstate[("cur", pi, r, blk)])

            def phaseB(u, A):
                pi, dil, r, blk, h = u
                PT, PTb, cols, nkb, vs = A
                pair = h // 2
                rows = slice(64 * (h % 2), 64 * (h % 2) + 64)
                Op, Opb = psO.next()
                OL = Op[:, 0:256].rearrange("p (a q) -> p a q", a=2)
                for kb, (vt, vtb) in enumerate(vs):
                    s.op("pe", lambda: nc.tensor.matmul(OL[:, 0, :], vt[:, pair * 128:(pair + 1) * 128], PT[:, kb, :],
                                                        start=(kb == 0), stop=(kb == nkb - 1)), r=[vtb, PTb], w=[Opb])
                for kb in range(nkb):
                    s.op("pe", lambda: nc.tensor.matmul(OL[:, 1, :], c.ones_bf[:, :], PT[:, kb, :],
                                                        start=(kb == 0), stop=(kb == nkb - 1)), r=[c.constb, PTb], w=[Opb])
                dst = acc[rows, :, pair, cols]
                if pi == 0:
                    s.op("dve", lambda: nc.vector.tensor_copy(dst, OL[rows, :, :]), r=[Opb], w=[accb])
                else:
                    s.op("dve", lambda: nc.vector.tensor_tensor(dst, dst, OL[rows, :, :], ALU.add),
                         r=[Opb, accb], w=[accb])

            Acur = phaseA(units[0])
            for k, u in enumerate(units):
                Anext = phaseA(units[k + 1]) if k + 1 < len(units) else None
                phaseB(u, Acur)
                Acur = Anext
            for pair in range(4):
                for hh in range(2):
                    ts_ = slice(hh * 1024, (hh + 1) * 1024)
                    rc, rcb = rcr.next()
                    s.op("dve", lambda: nc.vector.reciprocal(rc[:, :], acc[:, 1, pair, ts_]), r=[accb], w=[rcb])
                    ob, obb = obr.next()
                    s.op("dve", lambda: nc.vector.tensor_tensor(ob[:, :], acc[:, 0, pair, ts_], rc[:, :], ALU.mult),
                         r=[accb, rcb], w=[obb])
                    s.dma("pool", mixedT[:, pair, tok0 + hh * 1024:tok0 + (hh + 1) * 1024], ob[:, :], r=[obb])

        qS, qSb = st.sb("qS", [128, 4, 16], BF16)
        kS, kSb = st.sb("kS", [128, 4, 16], BF16)
        oS, oSb = st.sb("oS", [128, 4, 16], BF16)
        s.dma("sp", qS[:, :, :], qT[:, :, NTP:NTP + 16], w=[qSb])
        s.dma("sp", kS[:, :, :], kT[:, :, NTP:NTP + 16], w=[kSb])
        vrr = st.ring("vrow", [1, 512], BF16, 3)
        kgr = st.ring("Kg", [128, 512], F32, 3)
        vgr = st.ring("Vg", [128, 512], F32, 3)
        kgbr = st.ring("Kgb", [128, 512], BF16, 2)
        vgbr = st.ring("Vgb", [128, 512], BF16, 4)
        kgtr = st.ring("KgT", [128, 4, 128], BF16, 2)
        s8r = st.ring("s8", [128, 16], F32, 3)
        p8r = st.ring("p8", [128, 16], BF16, 3)
        t8r = st.ring("t8", [128, 16], F32, 2)
        pstr = st.psring("pstA", [128, 8, 128], BF16, 1)
        smp, smb = st.ps("psm", [128, 512], F32)
        Sgb = Sob = OSb = smb
        Sg = smp[:, 0:8]
        So = smp[0:1, 8:16]
        OS = smp[:, 16:32].rearrange("p (a h) -> p a h", a=2)
        for i in range(16):
            vrow, vrowb = vrr.next()
            s.dma("sp", vrow[0:1, :], v_bf[NTP + i:NTP + i + 1, :], w=[vrowb])
            keep = []
            for pi, (wnd, dil) in enumerate(PATTERNS):
                Kg, Kgb_ = kgr.next()
                Vg, Vgb_ = vgr.next()
                s.dma("sp", Kg[:, :], cache_k[i, 2048 - 128 * dil:2048:dil, :], w=[Kgb_])
                s.dma("sp", Vg[:, :], cache_v[i, 2048 - 128 * dil:2048:dil, :], w=[Vgb_])
                Kb, Kbb = kgbr.next()
                Vb, Vbb = vgbr.next()
                s.op("pool", lambda: nc.gpsimd.tensor_copy(Kb[:, :], Kg[:, :]), r=[Kgb_], w=[Kbb])
                s.op("pool", lambda: nc.gpsimd.tensor_copy(Vb[:, :], Vg[:, :]), r=[Vgb_], w=[Vbb])
                pt, ptb = pstr.next()
                for pr in range(4):
                    s.op("pe", lambda: nc.tensor.transpose(pt[:, pr, :], Kb[:, pr * 128:(pr + 1) * 128], c.ident_bf[:, :]),
                         r=[Kbb, c.constb], w=[ptb])
                KT_, KTb_ = kgtr.next()
                evac(c, KT_[:, :, :], pt[:, 0:4, :], [ptb], [KTb_])
                for h in range(8):
                    pair = h // 2
                    rows = slice(64 * (h % 2), 64 * (h % 2) + 64)
                    s.op("pe", lambda: nc.tensor.matmul(Sg[:, h:h + 1], KT_[rows, pair, :], qS[rows, pair, i:i + 1],
                                                        start=True, stop=True), r=[KTb_, qSb], w=[Sgb])
                    s.op("pe", lambda: nc.tensor.matmul(So[0:1, h:h + 1], kS[rows, pair, i:i + 1], qS[rows, pair, i:i + 1],
                                                        start=True, stop=True), r=[kSb, qSb], w=[Sob])
                s8, s8b = s8r.next()
                s.op("dve", lambda: nc.vector.tensor_tensor(s8[:, 0:8], Sg, BT4[:, pi * 8:(pi + 1) * 8, 1, 0], ALU.add),
                     r=[Sgb, BTb], w=[s8b])
                s.op("dve", lambda: nc.vector.tensor_tensor(s8[0:1, 8:16], So, BT4[0:1, pi * 8:(pi + 1) * 8, 0, 0], ALU.add),
                     r=[Sob, BTb], w=[s8b])
                p8, p8b = p8r.next()
                s.op("act", lambda: nc.scalar.activation(out=p8[:, 0:8], in_=s8[:, 0:8], func=AF.Exp), r=[s8b], w=[p8b])
                s.op("act", lambda: nc.scalar.activation(out=p8[0:1, 8:16], in_=s8[0:1, 8:16], func=AF.Exp), r=[s8b], w=[p8b])
                keep.append((Vb, Vbb, p8, p8b))
            for a_ in range(2):
                for h in range(8):
                    pair = h // 2
                    for pi, (Vb, Vbb, p8, p8b) in enumerate(keep):
                        lh = Vb[:, pair * 128:(pair + 1) * 128] if a_ == 0 else c.ones_bf[:, :]
                        lo = vrow[0:1, pair * 128:(pair + 1) * 128] if a_ == 0 else c.ones_bf[0:1, :]
                        s.op("pe", lambda: nc.tensor.matmul(OS[:, a_, h:h + 1], lh, p8[:, h:h + 1],
                                                            start=(pi == 0), stop=False), r=[Vbb, c.constb, p8b], w=[OSb])
                        s.op("pe", lambda: nc.tensor.matmul(OS[:, a_, h:h + 1], lo, p8[0:1, 8 + h:9 + h],
                                                            start=False, stop=(pi == 2)), r=[vrowb, c.constb, p8b], w=[OSb])
            t8, t8b = t8r.next()
            s.op("dve", lambda: nc.vector.reciprocal(t8[:, 8:16], OS[:, 1, :]), r=[OSb], w=[t8b])
            s.op("dve", lambda: nc.vector.tensor_tensor(t8[:, 0:8], OS[:, 0, :], t8[:, 8:16], ALU.mult), r=[OSb, t8b], w=[t8b])
            for half in range(2):
                rows = slice(64 * half, 64 * half + 64)
                s.op("dve", lambda: nc.vector.tensor_copy(oS[rows, :, i], t8[rows, half:8:2]), r=[t8b], w=[oSb])
        s.dma("pool", mixedT[:, 0:4, NTP:NTP + 16], oS[:, :, :], r=[oSb])


def ssd_stage(c, xbcT, dt_scr, z_scr, mixedT, convw_d, convb_d, vec16_d, gssm_d, tri_d,
              cconv_d, state_d, conv_out, ssm_out, NB, NTP):
    s, nc = c.s, c.nc
    NSTREAM = 2
    with Stage(c) as st:
        cw, cwb = st.sb("cw", [128, 12, 4], F32)
        cb, cbb = st.sb("cb", [128, 12], F32)
        v16, v16b = st.sb("v16", [128, 3, 16], F32)
        gss, gssb = st.sb("gss", [128, 1024], F32)
        tri, trib = st.sb("tri", [128, 2, 128], F32)
        onesf, onesfb = st.sb("onesf", [128, 128], F32)
        s.dma("sp", cw[:, :, :], convw_d[:, :, :], w=[cwb])
        s.dma("sp", cb[:, :], convb_d[:, :], w=[cbb])
        s.dma("sp", v16[:, :, :], vec16_d[0:1, :, :].partition_broadcast(128), w=[v16b])
        s.dma("sp", gss[:, :], gssm_d[0:1, :].partition_broadcast(128), w=[gssb])
        s.dma("sp", tri[:, :, :], tri_d[:, :, :], w=[trib])
        s.op("pool", lambda: nc.gpsimd.memset(onesf[:, :], 1.0), w=[onesfb])
        s.op("act", lambda: nc.scalar.activation(out=v16[:, 1, :], in_=v16[:, 1, :], func=AF.Exp), r=[v16b], w=[v16b])
        s.op("dve", lambda: nc.vector.tensor_scalar(v16[:, 1, :], v16[:, 1, :], -1.0, None, ALU.mult), r=[v16b], w=[v16b])
        dtb_bc, a_bc, dsk_bc = v16[:, 0, :], v16[:, 1, :], v16[:, 2, :]
        triU, strictL = tri[:, 0, :], tri[:, 1, :]

        def make_stream(k):
            S = Ctx()
            n = f"s{k}"
            S.xin = st.sb(n + "xin", [128, 12, 131], F32)
            S.cacc = st.sb(n + "cacc", [128, 12, 128], F32)
            S.ctmp = st.sb(n + "ctmp", [128, 12, 128], F32)
            S.c2 = st.sb(n + "c2", [128, 12, 128], F32)
            S.a32 = st.sb(n + "a32", [128, 12, 128], F32)
            S.BCT = st.sb(n + "BCT", [128, 4, 128], BF16)
            S.Btok = st.sb(n + "Btok", [128, 2, 128], BF16)
            S.xdt = st.sb(n + "xdt", [128, 16, 64], BF16)
            S.xdts = st.sb(n + "xdts", [128, 16, 64], BF16)
            S.dsk = st.sb(n + "dsk", [128, 16, 64], F32)
            S.dtt = st.sb(n + "dtt", [128, 6, 16], F32)
            S.acs = st.sb(n + "acs", [128, 5, 16], F32)
            S.H = st.sb(n + "H", [128, 16, 64], F32)
            S.Hb = st.sb(n + "Hb", [128, 16, 64], BF16)
            S.CBm = st.sb(n + "CBm", [128, 2, 128], F32)
            S.rhsD = st.ring(n + "rhsD", [128, 4, 128], F32, 2)
            S.Es = st.ring(n + "Es", [128, 4, 128], F32, 1)
            S.MT = st.ring(n + "MT", [128, 4, 128], BF16, 2)
            S.YT = st.sb(n + "YTsb", [128, 2, 16, 64], F32)
            S.yt = st.sb(n + "yt", [128, 16, 64], F32)
            S.zt = st.sb(n + "zt", [128, 1024], F32)
            S.yn = st.sb(n + "yn", [128, 1024], F32)
            S.yT = st.sb(n + "yT", [128, 8, 128], BF16)
            S.stat = st.ring(n + "stat", [128, 4], F32, 2)
            S.junk = st.sb(n + "junk", [128, 512], BF16)
            S.YTp = st.ps(n + "YTp", [128, 512], F32)
            S.Fp = st.ps(n + "Fp", [128, 512], F32)
            S.Mr = st.psring(n + "M", [128, 512], F32, 2)
            return S

        def front(S, item):
            kind, idx, tok0, ch, nchunk = item
            t0 = tok0 + ch * 128
            sidx = idx if kind == "p" else NB + idx
            Mr = S.Mr
            xin, xinb = S.xin
            if kind == "p":
                if ch == 0:
                    s.op("pool", lambda: nc.gpsimd.memset(xin[:, :, 0:3], 0.0), w=[xinb])
                    s.dma("sp", xin[:, :, 3:131], xbcT[:, :, t0:t0 + 128], w=[xinb])
                else:
                    s.dma("sp", xin[:, :, :], xbcT[:, :, t0 - 3:t0 + 128], w=[xinb])
            else:
                s.op("pool", lambda: nc.gpsimd.memset(xin[:, :, :], 0.0), w=[xinb])
                s.dma("sp", xin[:, :, 0:3], cconv_d[idx], w=[xinb])
                s.dma("sp", xin[:, :, 3:4], xbcT[:, :, t0:t0 + 1], w=[xinb], slow=True)
            dtt, dttb = S.dtt
            if kind == "p":
                s.dma("sp", dtt[:, 0, :], dt_scr[t0:t0 + 128, :], w=[dttb])
            else:
                s.op("pool", lambda: nc.gpsimd.memset(dtt[:, 0, :], 0.0), w=[dttb])
                s.dma("sp", dtt[0:1, 0, :], dt_scr[t0:t0 + 1, :], w=[dttb])
            zt, ztb = S.zt
            if kind == "p":
                s.dma("sp", zt[:, :], z_scr[t0:t0 + 128, :], w=[ztb])
            else:
                s.op("pool", lambda: nc.gpsimd.memset(zt[:, :], 0.0), w=[ztb])
                s.dma("sp", zt[0:1, :], z_scr[t0:t0 + 1, :], w=[ztb])
            if ch == nchunk - 1:
                lo = 128 if kind == "p" else 1
                s.dma("sp", conv_out[sidx], xin[:, :, lo:lo + 3], r=[xinb])
            cacc, caccb = S.cacc
            ctmp, ctmpb = S.ctmp
            c2, c2b = S.c2
            a32, a32b = S.a32

            def wbc(i):
                return cw[:, :, i:i + 1].to_broadcast([128, 12, 128])
            s.op("pool", lambda: nc.gpsimd.tensor_tensor(cacc[:, :, :], xin[:, :, 3:131], wbc(3), ALU.mult), r=[xinb, cwb], w=[caccb])
            s.op("dve", lambda: nc.vector.tensor_tensor(c2[:, :, :], xin[:, :, 1:129], wbc(1), ALU.mult), r=[xinb, cwb], w=[c2b])
            s.op("pool", lambda: nc.gpsimd.tensor_tensor(ctmp[:, :, :], xin[:, :, 2:130], wbc(2), ALU.mult), r=[xinb, cwb], w=[ctmpb])
            s.op("dve", lambda: nc.vector.tensor_tensor(a32[:, :, :], xin[:, :, 0:128], wbc(0), ALU.mult), r=[xinb, cwb], w=[a32b])
            s.op("pool", lambda: nc.gpsimd.tensor_tensor(cacc[:, :, :], cacc[:, :, :], ctmp[:, :, :], ALU.add), r=[ctmpb, caccb], w=[caccb])
            s.op("dve", lambda: nc.vector.tensor_tensor(c2[:, :, :], c2[:, :, :], a32[:, :, :], ALU.add), r=[c2b, a32b], w=[c2b])
            s.op("dve", lambda: nc.vector.tensor_tensor(c2[:, :, :], c2[:, :, :], cacc[:, :, :], ALU.add), r=[c2b, caccb], w=[c2b])
            for j in range(12):
                s.op("act", lambda: nc.scalar.activation(out=a32[:, j, :], in_=c2[:, j, :], func=AF.Silu, bias=cb[:, j:j + 1]),
                     r=[c2b, cbb], w=[a32b])
            BCT, BCTb = S.BCT
            s.op("act", lambda: nc.scalar.copy(BCT[:, :, :], a32[:, 8:12, :]), r=[a32b], w=[BCTb])
            s.op("dve", lambda: nc.vector.tensor_tensor(dtt[:, 1, :], dtt[:, 0, :], dtb_bc, ALU.add), r=[dttb, v16b], w=[dttb])
            s.op("act", lambda: nc.scalar.activation(out=dtt[:, 1, :], in_=dtt[:, 1, :], func=AF.Exp), r=[dttb], w=[dttb])
            s.op("dve", lambda: nc.vector.tensor_scalar(dtt[:, 1, :], dtt[:, 1, :], 1.0, None, ALU.add), r=[dttb], w=[dttb])
            s.op("act", lambda: nc.scalar.activation(out=dtt[:, 2, :], in_=dtt[:, 1, :], func=AF.Ln), r=[dttb], w=[dttb])
            if kind == "s":
                s.op("dve", lambda: nc.vector.tensor_scalar(dtt[:, 2, :], dtt[:, 2, :], c.ident_f[:, 0:1], None, ALU.mult),
                     r=[dttb, c.constb], w=[dttb])
            s.op("dve", lambda: nc.vector.tensor_tensor(dtt[:, 3, :], dtt[:, 2, :], a_bc, ALU.mult), r=[dttb, v16b], w=[dttb])
            dt_, la = dtt[:, 2, :], dtt[:, 3, :]
            Mc, Mcb = Mr.next()
            s.op("pe", lambda: nc.tensor.matmul(Mc[:, 0:16], triU, la, start=True, stop=True), r=[trib, dttb], w=[Mcb])
            s.op("pe", lambda: nc.tensor.matmul(Mc[:, 16:32], onesf[:, :], la, start=True, stop=True), r=[onesfb, dttb], w=[Mcb])
            acs, acsb = S.acs
            s.op("dve", lambda: nc.vector.tensor_copy(acs[:, 0:2, :], Mc[:, 0:32].rearrange("p (a e) -> p a e", a=2)),
                 r=[Mcb], w=[acsb])
            s.op("dve", lambda: nc.vector.tensor_tensor(acs[:, 3, :], acs[:, 1, :], acs[:, 0, :], ALU.subtract), r=[acsb], w=[acsb])
            s.op("act", lambda: nc.scalar.activation(out=acs[:, 2, :], in_=acs[:, 0, :], func=AF.Exp), r=[acsb], w=[acsb])
            s.op("act", lambda: nc.scalar.activation(out=acs[:, 3, :], in_=acs[:, 3, :], func=AF.Exp), r=[acsb], w=[acsb])
            s.op("act", lambda: nc.scalar.activation(out=acs[:, 4, :], in_=acs[:, 1, :], func=AF.Exp), r=[acsb], w=[acsb])
            s.op("dve", lambda: nc.vector.tensor_tensor(dtt[:, 4, :], dt_, acs[:, 3, :], ALU.mult), r=[dttb, acsb], w=[dttb])
            dtd = dtt[:, 4, :]
            xdt, xdtb = S.xdt
            xdts, xdtsb = S.xdts
            dsk, dskb = S.dsk
            for g in range(2):
                m, mb = Mr.next()
                for j in range(g * 4, g * 4 + 4):
                    s.op("pe", lambda: nc.tensor.transpose(m[:, (j % 4) * 128:(j % 4 + 1) * 128], a32[:, j, :], c.ident_f[:, :]),
                         r=[a32b, c.constb], w=[mb])
                mv = m[:, :].rearrange("p (e q) -> p e q", e=8)
                hs = slice(g * 8, g * 8 + 8)
                s.op("dve", lambda: nc.vector.tensor_tensor(xdt[:, hs, :], mv, dt_[:, hs].unsqueeze(2).to_broadcast([128, 8, 64]), ALU.mult),
                     r=[mb, dttb], w=[xdtb])
                s.op("dve", lambda: nc.vector.tensor_tensor(xdts[:, hs, :], mv, dtd[:, hs].unsqueeze(2).to_broadcast([128, 8, 64]), ALU.mult),
                     r=[mb, dttb], w=[xdtsb])
                s.op("dve", lambda: nc.vector.tensor_tensor(dsk[:, hs, :], mv, dsk_bc[:, hs].unsqueeze(2).to_broadcast([128, 8, 64]), ALU.mult),
                     r=[mb, v16b], w=[dskb])
            Mb_, Mbb = Mr.next()
            for g in range(2):
                s.op("pe", lambda: nc.tensor.transpose(Mb_[:, g * 128:(g + 1) * 128], a32[:, 8 + g, :], c.ident_f[:, :]),
                     r=[a32b, c.constb], w=[Mbb])
            Btok, Btokb = S.Btok
            s.op("act", lambda: nc.scalar.copy(Btok[:, :, :], Mb_[:, 0:256].rearrange("p (g n) -> p g n", g=2)), r=[Mbb], w=[Btokb])
            CBm, CBmb = S.CBm
            Mg, Mgb = Mr.next()
            for g in range(2):
                s.op("pe", lambda: nc.tensor.matmul(Mg[:, g * 128:(g + 1) * 128], BCT[:, g, :], BCT[:, 2 + g, :], start=True, stop=True),
                     r=[BCTb], w=[Mgb])
            s.op("dve", lambda: nc.vector.tensor_tensor(CBm[:, :, :], Mg[:, 0:256].rearrange("p (g l) -> p g l", g=2),
                                                        triU.unsqueeze(1).to_broadcast([128, 2, 128]), ALU.mult), r=[Mgb, trib], w=[CBmb])
            YT, YTb = S.YT
            YTp, YTpb = S.YTp
            for g in range(2):
                for q4 in range(2):
                    e0 = g * 8 + q4 * 4
                    rhsD, rhsDb = S.rhsD.next()
                    s.op("pool", lambda: nc.gpsimd.tensor_tensor(rhsD[:, :, :], triU.unsqueeze(1).to_broadcast([128, 4, 128]),
                                                                 la[:, e0:e0 + 4].unsqueeze(2).to_broadcast([128, 4, 128]), ALU.mult),
                         r=[trib, dttb], w=[rhsDb])
                    Md, Mdb = Mr.next()
                    s.op("pe", lambda: nc.tensor.matmul(Md[:, :], strictL, rhsD[:, :, :].rearrange("p e l -> p (e l)"), start=True, stop=True),
                         r=[trib, rhsDb], w=[Mdb])
                    Es, Esb = S.Es.next()
                    s.op("act", lambda: nc.scalar.activation(out=Es[:, :, :].rearrange("p e l -> p (e l)"), in_=Md[:, :], func=AF.Exp),
                         r=[Mdb], w=[Esb])
                    MT, MTb = S.MT.next()
                    s.op("dve", lambda: nc.vector.tensor_tensor(MT[:, :, :], Es[:, :, :],
                                                                CBm[:, g:g + 1, :].to_broadcast([128, 4, 128]), ALU.mult),
                         r=[Esb, CBmb], w=[MTb])
                    for e in range(e0, e0 + 4):
                        cs = slice((e - e0) * 64, (e - e0) * 64 + 64)
                        cs2 = slice(256 + (e - e0) * 64, 256 + (e - e0) * 64 + 64)
                        s.op("pe", lambda: nc.tensor.matmul(YTp[:, cs], MT[:, e - e0, :], xdt[:, e, :], start=True, stop=True),
                             r=[MTb, xdtb], w=[YTpb])
                        s.op("pe", lambda: nc.tensor.matmul(YTp[:, cs2], Btok[:, g, :], xdts[:, e, :], start=True, stop=True),
                             r=[Btokb, xdtsb], w=[YTpb])
                    s.op("act", lambda: nc.scalar.copy(YT[:, :, e0:e0 + 4, :], YTp[:, :].rearrange("p (a e q) -> p a e q", a=2, e=4)),
                         r=[YTpb], w=[YTb])

        def back(S, item):
            kind, idx, tok0, ch, nchunk = item
            t0 = tok0 + ch * 128
            sidx = idx if kind == "p" else NB + idx
            Mr = S.Mr
            H, Hb_ = S.H
            Hbc, Hbcb = S.Hb
            acs, acsb = S.acs
            BCT, BCTb = S.BCT
            YT, YTb = S.YT
            Fp, Fpb = S.Fp
            eacs, cdec = acs[:, 2, :], acs[:, 4, :]
            if ch == 0:
                if kind == "p":
                    s.op("dve", lambda: nc.vector.memset(H[:, :, :], 0.0), w=[Hb_])
                else:
                    s.dma("sp", H[:, :, :], state_d[idx].rearrange("n (e p) -> n e p", e=16), w=[Hb_])
                s.op("act", lambda: nc.scalar.copy(Hbc[:, :, :], H[:, :, :]), r=[Hb_], w=[Hbcb])
            yt, ytb = S.yt
            for g in range(2):
                hs = slice(g * 8, g * 8 + 8)
                for e in range(g * 8, g * 8 + 8):
                    cs = slice((e % 8) * 64, (e % 8) * 64 + 64)
                    s.op("pe", lambda: nc.tensor.matmul(Fp[:, cs], BCT[:, 2 + g, :], Hbc[:, e, :], start=True, stop=True),
                         r=[BCTb, Hbcb], w=[Fpb])
                fv = Fp[:, :].rearrange("p (e q) -> p e q", e=8)
                s.op("dve", lambda: nc.vector.tensor_tensor(yt[:, hs, :], fv, eacs[:, hs].unsqueeze(2).to_broadcast([128, 8, 64]), ALU.mult),
                     r=[Fpb, acsb], w=[ytb])
                s.op("dve", lambda: nc.vector.tensor_tensor(H[:, hs, :], H[:, hs, :], cdec[:, hs].unsqueeze(2).to_broadcast([128, 8, 64]), ALU.mult),
                     r=[Hb_, acsb], w=[Hb_])
                s.op("dve", lambda: nc.vector.tensor_tensor(H[:, hs, :], H[:, hs, :], YT[:, 1, hs, :], ALU.add), r=[Hb_, YTb], w=[Hb_])
            s.op("act", lambda: nc.scalar.copy(Hbc[:, :, :], H[:, :, :]), r=[Hb_], w=[Hbcb])
            s.op("pool", lambda: nc.gpsimd.tensor_tensor(yt[:, :, :], yt[:, :, :], YT[:, 0, :, :], ALU.add), r=[ytb, YTb], w=[ytb])
            s.op("pool", lambda: nc.gpsimd.tensor_tensor(yt[:, :, :], yt[:, :, :], S.dsk[0][:, :, :], ALU.add), r=[ytb, S.dsk[1]], w=[ytb])
            zt, ztb = S.zt
            s.op("act", lambda: nc.scalar.activation(out=zt[:, :], in_=zt[:, :], func=AF.Silu), r=[ztb], w=[ztb])
            ytf = yt[:, :, :].rearrange("p e q -> p (e q)")
            s.op("dve", lambda: nc.vector.tensor_tensor(ytf, ytf, zt[:, :], ALU.mult), r=[ytb, ztb], w=[ytb])
            yn, ynb = S.yn
            jk, jkb = S.junk
            for g in range(2):
                gs = slice(g * 512, (g + 1) * 512)
                ss, ssb = S.stat.next()
                s.op("act", lambda: nc.scalar.activation(out=jk[:, 0:512], in_=ytf[:, gs], func=AF.Square, accum_out=ss[:, 0:1]),
                     r=[ytb], w=[jkb, ssb])
                rstd_from_ss(c, ss, ssb, 512)
                s.op("dve", lambda: nc.vector.scalar_tensor_tensor(yn[:, gs], ytf[:, gs], ss[:, 2:3], gss[:, gs], ALU.mult, ALU.mult),
                     r=[ytb, ssb, gssb], w=[ynb])
            yT, yTb = S.yT
            for g in range(2):
                m, mb = Mr.next()
                for j in range(g * 4, g * 4 + 4):
                    s.op("pe", lambda: nc.tensor.transpose(m[:, (j % 4) * 128:(j % 4 + 1) * 128], yn[:, j * 128:(j + 1) * 128], c.ident_f[:, :]),
                         r=[ynb, c.constb], w=[mb])
                evac(c, yT[:, g * 4:(g + 1) * 4, :], m[:, :].rearrange("p (j t) -> p j t", j=4), [mb], [yTb])
            if kind == "p":
                s.dma("sp", mixedT[:, 4:12, t0:t0 + 128], yT[:, :, :], r=[yTb])
            else:
                s.dma("sp", mixedT[:, 4:12, t0:t0 + 1], yT[:, :, 0:1], r=[yTb], slow=True)
            if ch == nchunk - 1:
                s.dma("sp", ssm_out[sidx].rearrange("n (e p) -> n e p", e=16), H[:, :, :], r=[Hb_])

        streams = [make_stream(k) for k in range(NSTREAM)]
        work = [[] for _ in range(NSTREAM)]
        for b in range(NB):
            for ch in range(16):
                work[b % NSTREAM].append(("p", b, b * 2048, ch, 16))
        for i in range(16):
            work[(i + NB) % NSTREAM].append(("s", i, NTP + i, 0, 1))

        def runner(k):
            def run():
                for item in work[k]:
                    front(streams[k], item)
                    back(streams[k], item)
            return run
        Interleave(s, [runner(k) for k in range(NSTREAM)]).run()


def outcross_stage(c, x1, mixedT, gcol, wout_d, wcq_d, wco_d, memKT, memV, cache_mk, cache_mv, x3, tiles, NTP):
    s, nc = c.s, c.nc
    with Stage(c) as st:
        st.wstage = st.ring("wst", [128, 1024], F32, 3)
        wO, wOb = st.sb("wO", [128, 12, D], BF16)
        wQ, wQb = st.sb("wQ", [128, 8, D], BF16)
        wC, wCb = st.sb("wCo", [128, 8, D], BF16)
        xtr = st.ring("xt", [128, 4, 1024], F32, 2)
        mTr = st.ring("mT", [128, 12, 512], BF16, 1)
        hTr = st.ring("hT", [128, 8, 512], BF16, 2)
        qxr = st.ring("qx", [128, 8, 512], BF16, 1)
        oTr = st.ring("oTn", [128, 8, 512], BF16, 1)
        ptr = st.ring("PT", [128, 512], BF16, 4)
        rlr = st.ring("rl", [128, 512], F32, 2)
        ktr = st.ring("KTm", [128, 8, 256], BF16, 2)
        vmr = st.ring("Vm", [128, 2, 1024], BF16, 2)
        ckr = st.ring("ck", [128, 2, 1024], F32, 2)
        cbr = st.ring("ckb", [128, 2, 1024], BF16, 1)
        norm_bufs(st)
        psr = st.psring("ps", [128, 512], F32, 6)
        load_weight(c, st, wout_d, wO, wOb, 12, D)
        load_weight(c, st, wcq_d, wQ, wQb, 8, D)
        load_weight(c, st, wco_d, wC, wCb, 8, D)

        def cross_core(KTm, KTmb, Vm, Vmb, qx, qxb, oT, oTb, cols, n):
            for h in range(4):
                PTs = []
                for mb in range(2):
                    ps, psb = psr.next()
                    for dc in range(2):
                        s.op("pe", lambda: nc.tensor.matmul(ps[:, 0:n], KTm[:, 2 * h + dc, mb * 128:(mb + 1) * 128],
                                                            qx[:, 2 * h + dc, cols], start=(dc == 0), stop=(dc == 1)),
                             r=[KTmb, qxb], w=[psb])
                    PT, PTb = ptr.next()
                    s.op("act", lambda: nc.scalar.activation(out=PT[:, 0:n], in_=ps[:, 0:n], func=AF.Exp), r=[psb], w=[PTb])
                    PTs.append((PT, PTb))
                ps, psb = psr.next()
                for mb in range(2):
                    s.op("pe", lambda: nc.tensor.matmul(ps[:, 0:n], c.ones_bf[:, :], PTs[mb][0][:, 0:n],
                                                        start=(mb == 0), stop=(mb == 1)), r=[c.constb, PTs[mb][1]], w=[psb])
                rl, rlb = rlr.next()
                s.op("dve", lambda: nc.vector.reciprocal(rl[:, 0:n], ps[:, 0:n]), r=[psb], w=[rlb])
                for dc in range(2):
                    ps, psb = psr.next()
                    for mb in range(2):
                        s.op("pe", lambda: nc.tensor.matmul(ps[:, 0:n], Vm[:, mb, (2 * h + dc) * 128:(2 * h + dc + 1) * 128],
                                                            PTs[mb][0][:, 0:n], start=(mb == 0), stop=(mb == 1)),
                             r=[Vmb, PTs[mb][1]], w=[psb])
                    s.op("dve", lambda: nc.vector.tensor_tensor(oT[:, 2 * h + dc, cols], ps[:, 0:n], rl[:, 0:n], ALU.mult),
                         r=[psb, rlb], w=[oTb])

        cur_b = -1
        KTm = Vm = None
        for (t0, nsub) in tiles:
            N = nsub * 128
            xt, xtb = xtr.next()
            s.dma("sp", xt[:, 0:nsub, :], x1[t0:t0 + N, :].rearrange("(s p) d -> p s d", p=128), w=[xtb])
            mT, mTb = mTr.next()
            s.dma("sp", mT[:, :, 0:N], mixedT[:, :, t0:t0 + N], w=[mTb])
            for sub in range(nsub):
                for hf in range(2):
                    ps, psb = psr.next()
                    mm_tm(c, mT, mTb, sub, wO, wOb, hf * 512, 512, ps, psb, KC=12)
                    xs_ = xt[:, sub, hf * 512:(hf + 1) * 512]
                    s.op("dve", lambda: nc.vector.tensor_tensor(xs_, xs_, ps[:, :], ALU.add), r=[psb, xtb], w=[xtb])
            hT, hTb = hTr.next()
            norm_transpose(c, st, xt, xtb, nsub, gcol, hT, hTb)
            qx, qxb = qxr.next()
            for j in range(8):
                ps, psb = psr.next()
                mm_fm(c, wQ, wQb, j * 128, hT, hTb, N, ps, psb)
                evac(c, qx[:, j, 0:N], ps[:, 0:N], [psb], [qxb], scale=1.0 / 16.0)
            oT, oTb = oTr.next()
            if t0 < NTP:
                b = t0 // 2048
                if b != cur_b:
                    cur_b = b
                    KTm, KTmb = ktr.next()
                    Vm, Vmb = vmr.next()
                    s.dma("sp", KTm[:, :, :], memKT[:, :, b * 256:(b + 1) * 256], w=[KTmb])
                    s.dma("sp", Vm[:, :, :], memV[b * 256:(b + 1) * 256, :].rearrange("(mb m) d -> m mb d", m=128), w=[Vmb])
                cross_core(KTm, KTmb, Vm, Vmb, qx, qxb, oT, oTb, slice(0, N), N)
            else:
                s.op("pool", lambda: nc.gpsimd.memset(oT[:, :, 0:N], 0.0), w=[oTb])
                for i in range(16):
                    ck, ckb_ = ckr.next()
                    s.dma("sp", ck[:, :, :], cache_mk[i].rearrange("(mb m) d -> m mb d", m=128), w=[ckb_])
                    cb_, cbb_ = cbr.next()
                    s.op("pool", lambda: nc.gpsimd.tensor_copy(cb_[:, :, :], ck[:, :, :]), r=[ckb_], w=[cbb_])
                    KTs, KTsb = ktr.next()
                    for mb in range(2):
                        pt, ptb = st.pst.next()
                        for j in range(8):
                            s.op("pe", lambda: nc.tensor.transpose(pt[:, j, :], cb_[:, mb, j * 128:(j + 1) * 128], c.ident_bf[:, :]),
                                 r=[cbb_, c.constb], w=[ptb])
                        evac(c, KTs[:, :, mb * 128:(mb + 1) * 128], pt[:, :, :], [ptb], [KTsb])
                    cv, cvb_ = ckr.next()
                    s.dma("sp", cv[:, :, :], cache_mv[i].rearrange("(mb m) d -> m mb d", m=128), w=[cvb_])
                    Vs, Vsb = vmr.next()
                    s.op("pool", lambda: nc.gpsimd.tensor_copy(Vs[:, :, :], cv[:, :, :]), r=[cvb_], w=[Vsb])
                    cross_core(KTs, KTsb, Vs, Vsb, qx, qxb, oT, oTb, slice(i, i + 1), 1)
            for sub in range(nsub):
                for hf in range(2):
                    ps, psb = psr.next()
                    mm_tm(c, oT, oTb, sub, wC, wCb, hf * 512, 512, ps, psb)
                    xs_ = xt[:, sub, hf * 512:(hf + 1) * 512]
                    s.op("dve", lambda: nc.vector.tensor_tensor(xs_, xs_, ps[:, :], ALU.add), r=[psb, xtb], w=[xtb])
            s.dma("pool", x3[t0:t0 + N, :].rearrange("(s p) d -> p s d", p=128), xt[:, 0:nsub, :], r=[xtb])


ALL_STAGES = ("memkv", "ffn1", "inproj", "attn", "ssd", "outcross", "ffn2")


def build(NB=4, stages=ALL_STAGES, dbg=()):
    nc = bass.Bass("TRN2", target_bir_lowering=False)
    c = Ctx()
    c.nc = nc
    c.s = Sch(nc)
    s = c.s
    NTP = NB * 2048
    NTOK = NTP + 128
    NBM = NB * 256
    NSEQ = NB + 16
    tiles = [(t * 512, 4) for t in range(NTP // 512)] + [(NTP, 1)]
    pb, bmask_np, negm_np = bias_consts()
    NPB = len(pb)

    def din(name, shape):
        return nc.dram_tensor(name, shape, F32, kind="ExternalInput").ap()

    def dout(name, shape):
        return nc.dram_tensor(name, shape, F32, kind="ExternalOutput").ap()

    def scr(name, shape, dt=F32):
        if name in dbg:
            return nc.dram_tensor(name, shape, dt, kind="ExternalOutput").ap()
        return nc.dram_tensor(name, shape, dt).ap()

    x_all = din("x_all", [NTOK, D])
    gcols_d = din("gcols", [128, 6, 8])
    gfin_d = din("gfin", [1, D])
    w1g, w1u, w1d = din("w1_gate", [128, 8, DFF]), din("w1_up", [128, 8, DFF]), din("w1_down", [128, 22, D])
    w2g, w2u, w2d = din("w2_gate", [128, 8, DFF]), din("w2_up", [128, 8, DFF]), din("w2_down", [128, 22, D])
    win_d = din("w_in", [128, 8, DIN])
    wout_d = din("w_out", [128, 12, D])
    wck_d, wcv_d, wcq_d, wco_d = (din(n, [128, 8, D]) for n in ("w_ck", "w_cv", "w_cq", "w_co"))
    mem_in = din("mem_in", [NBM, D])
    relb_d = din("rel_bias", [1, 256])
    bmask_d = din("bmask", [NPB, 128, 256])
    negm_d = din("negm", [3, 128, 256])
    convw_d = din("conv_w", [128, 12, 4])
    convb_d = din("conv_b", [128, 12])
    vec16_d = din("vec16", [1, 3, 16])
    gssm_d = din("g_ssm", [1, D])
    tri_d = din("tri", [128, 2, 128])
    ident_d = din("ident", [128, 128])
    cache_k = din("cache_k", [16, 2048, 512])
    cache_v = din("cache_v", [16, 2048, 512])
    cconv_d = din("cconv", [16, 128, 12, 3])
    state_d = din("state", [16, 128, 1024])
    cache_mk = din("cache_mk", [16, 256, D])
    cache_mv = din("cache_mv", [16, 256, D])

    y_out = dout("y_out", [NTOK, D])
    kT_out = dout("kT_out", [128, 4, NTOK])
    v_out = dout("v_out", [NTOK, 512])
    conv_out = dout("conv_out", [NSEQ, 128, 12, 3])
    ssm_out = dout("ssm_out", [NSEQ, 128, 1024])
    mk_out = dout("mk_out", [NBM, D])
    mv_out = dout("mv_out", [NBM, D])

    x1 = scr("x1", [NTOK, D])
    x3 = scr("x3", [NTOK, D])
    x4 = scr("x4", [NTOK, D])
    hT_scr = scr("hT_scr", [128, 8, NTOK], BF16)
    memKT = scr("memKT", [128, 8, NBM], BF16)
    memV = scr("memV", [NBM, D], BF16)
    qT = scr("qT", [128, 4, NTOK], BF16)
    kT = scr("kT", [128, 4, NTOK], BF16)
    v_bf = scr("v_bf", [NTOK, 512], BF16)
    z_scr = scr("z_scr", [NTOK, D])
    xbcT = scr("xbcT", [128, 12, NTOK])
    dt_scr = scr("dt_scr", [NTOK, 16])
    mixedT = scr("mixedT", [128, 12, NTOK], BF16)

    c.dbufs = {}

    def db(key):
        if key not in c.dbufs:
            c.dbufs[key] = Buf(str(key))
        return c.dbufs[key]
    c.db = db

    c.constb = Buf("const")
    c.ident_f = nc.alloc_sbuf_tensor("ident_f", [128, 128], F32)
    c.ident_bf = nc.alloc_sbuf_tensor("ident_bf", [128, 128], BF16)
    c.ones_bf = nc.alloc_sbuf_tensor("ones_bf", [128, 128], BF16)
    c.gcols = nc.alloc_sbuf_tensor("gcols_sb", [128, 6, 8], F32)
    c.gfin = nc.alloc_sbuf_tensor("gfin_sb", [128, D], F32)
    s.dma("sp", c.ident_f[:, :], ident_d[:, :], w=[c.constb])
    s.dma("sp", c.gcols[:, :, :], gcols_d[:, :, :], w=[c.constb])
    s.dma("sp", c.gfin[:, :], gfin_d[0:1, :].partition_broadcast(128), w=[c.constb])
    s.op("dve", lambda: nc.vector.tensor_copy(c.ident_bf[:, :], c.ident_f[:, :]), r=[c.constb], w=[c.constb])
    s.op("dve", lambda: nc.vector.memset(c.ones_bf[:, :], 1.0), w=[c.constb])
    s.barrier()

    if "memkv" in stages:
        memkv_stage(c, mem_in, c.gcols[:, 4, :], wck_d, wcv_d, mk_out, mv_out, memKT, memV, NBM)
    if "ffn1" in stages:
        ffn_stage(c, x_all, x1, c.gcols[:, 0, :], w1g, w1u, w1d, hT_scr, tiles, "f1")
    if "inproj" in stages:
        inproj_stage(c, x1, c.gcols[:, 1, :], win_d, qT, kT, kT_out, v_out, v_bf, z_scr, xbcT, dt_scr, tiles)
    if "attn" in stages:
        attn_stage(c, qT, kT, v_bf, mixedT, relb_d, bmask_d, negm_d, pb, cache_k, cache_v, NB, NTP)
    if "ssd" in stages:
        ssd_stage(c, xbcT, dt_scr, z_scr, mixedT, convw_d, convb_d, vec16_d, gssm_d, tri_d,
                  cconv_d, state_d, conv_out, ssm_out, NB, NTP)
    if "outcross" in stages:
        outcross_stage(c, x1, mixedT, c.gcols[:, 2, :], wout_d, wcq_d, wco_d, memKT, memV, cache_mk, cache_mv, x3, tiles, NTP)
    if "ffn2" in stages:
        ffn_stage(c, x3, x4, c.gcols[:, 3, :], w2g, w2u, w2d, hT_scr, tiles, "f2", final=(c.gfin[:, :], y_out))
    s.finish()
    return nc


def tile_w(w, kc):
    w = np.asarray(w, np.float32)
    K, F = w.shape
    return np.ascontiguousarray(w.reshape(kc, 128, F).transpose(1, 0, 2))


def gcol(g):
    return np.asarray(g, np.float32).reshape(8, 128).T


def tri_consts():
    j = np.arange(128)[:, None]
    l = np.arange(128)[None, :]
    return np.ascontiguousarray(np.stack([(j <= l), (j > l)], axis=1).astype(np.float32))


def shared_inputs(inp):
    f = lambda k: np.asarray(inp[k], np.float32)
    pb, bmask, negm = bias_consts()
    gc = np.zeros((128, 6, 8), np.float32)
    for n, k in enumerate(("g_ffn1", "g_mix", "g_cross", "g_ffn2", "g_mem")):
        gc[:, n, :] = gcol(f(k)[0])
    cw = f("conv_w")[0]
    sh = {
        "gcols": gc, "gfin": f("g_final").reshape(1, D),
        "w1_gate": tile_w(f("w1_gate")[0], 8), "w1_up": tile_w(f("w1_up")[0], 8), "w1_down": tile_w(f("w1_down")[0], 22),
        "w2_gate": tile_w(f("w2_gate")[0], 8), "w2_up": tile_w(f("w2_up")[0], 8), "w2_down": tile_w(f("w2_down")[0], 22),
        "w_in": tile_w(f("w_in")[0], 8), "w_out": tile_w(f("w_out")[0], 12),
        "w_ck": tile_w(f("w_ck")[0], 8), "w_cv": tile_w(f("w_cv")[0], 8),
        "w_cq": tile_w(f("w_cq")[0], 8), "w_co": tile_w(f("w_co")[0], 8),
        "rel_bias": f("rel_bias").reshape(1, 256),
        "bmask": bmask, "negm": negm,
        "conv_w": np.ascontiguousarray(cw.reshape(4, 12, 128).transpose(2, 1, 0)),
        "conv_b": np.ascontiguousarray(f("conv_b")[0].reshape(12, 128).T),
        "vec16": np.stack([f("dt_bias")[0], f("a_log")[0], f("d_skip")[0]])[None],
        "g_ssm": f("g_ssm").reshape(1, D),
        "tri": tri_consts(), "ident": np.eye(128, dtype=np.float32),
    }
    return sh


def core_inputs(inp, core, NB):
    f = lambda k: np.asarray(inp[k], np.float32)
    xp = f("x_prompt")[core * NB:(core + 1) * NB].reshape(NB * 2048, D)
    xs = f("x_sample")[core * 16:(core + 1) * 16].reshape(16, D)
    x_all = np.concatenate([xp, xs, np.zeros((112, D), np.float32)], 0)
    sl = slice(core * 16, (core + 1) * 16)
    cc = f("cache_conv")[0, sl]
    st = f("state_ssm")[0, sl]
    return {
        "x_all": x_all,
        "mem_in": f("mem_prompt")[core * NB:(core + 1) * NB].reshape(NB * 256, D),
        "cache_k": f("cache_win_k")[0, sl].reshape(16, 2048, 512),
        "cache_v": f("cache_win_v")[0, sl].reshape(16, 2048, 512),
        "cconv": np.ascontiguousarray(cc.reshape(16, 3, 12, 128).transpose(0, 3, 2, 1)),
        "state": np.ascontiguousarray(st.reshape(16, 1024, 128).transpose(0, 2, 1)),
        "cache_mk": f("cache_mem_k")[0, sl].reshape(16, 256, D),
        "cache_mv": f("cache_mem_v")[0, sl].reshape(16, 256, D),
    }


def assemble(results, NB, ncores):
    NTP = NB * 2048
    B = NB * ncores
    y_p = np.empty((B, 2048, D), np.float32)
    y_s = np.empty((16 * ncores, 1, D), np.float32)
    wk_p = np.empty((1, B, 2048, 8, 64), np.float32)
    wv_p = np.empty((1, B, 2048, 8, 64), np.float32)
    cv_p = np.empty((1, B, 3, 1536), np.float32)
    ss_p = np.empty((1, B, 16, 64, 128), np.float32)
    mk_p = np.empty((1, B, 256, 4, 256), np.float32)
    mv_p = np.empty((1, B, 256, 4, 256), np.float32)
    wk_s = np.empty((1, 16 * ncores, 1, 8, 64), np.float32)
    wv_s = np.empty((1, 16 * ncores, 1, 8, 64), np.float32)
    cv_s = np.empty((1, 16 * ncores, 3, 1536), np.float32)
    ss_s = np.empty((1, 16 * ncores, 16, 64, 128), np.float32)
    for cidx, r in enumerate(results):
        y = np.asarray(r["y_out"])
        ktok = np.asarray(r["kT_out"]).transpose(2, 1, 0).reshape(-1, 8, 64)
        vtok = np.asarray(r["v_out"]).reshape(-1, 8, 64)
        cv = np.asarray(r["conv_out"]).transpose(0, 3, 2, 1).reshape(-1, 3, 1536)
        ss = np.asarray(r["ssm_out"]).transpose(0, 2, 1).reshape(-1, 16, 64, 128)
        bs = slice(cidx * NB, (cidx + 1) * NB)
        ts = slice(cidx * 16, (cidx + 1) * 16)
        y_p[bs] = y[:NTP].reshape(NB, 2048, D)
        y_s[ts, 0] = y[NTP:NTP + 16]
        wk_p[0, bs] = ktok[:NTP].reshape(NB, 2048, 8, 64)
        wv_p[0, bs] = vtok[:NTP].reshape(NB, 2048, 8, 64)
        wk_s[0, ts, 0] = ktok[NTP:NTP + 16]
        wv_s[0, ts, 0] = vtok[NTP:NTP + 16]
        cv_p[0, bs] = cv[:NB]
        cv_s[0, ts] = cv[NB:]
        ss_p[0, bs] = ss[:NB]
        ss_s[0, ts] = ss[NB:]
        mk_p[0, bs] = np.asarray(r["mk_out"]).reshape(NB, 256, 4, 256)
        mv_p[0, bs] = np.asarray(r["mv_out"]).reshape(NB, 256, 4, 256)
    return (y_p, y_s, wk_p, wv_p, cv_p, ss_p, mk_p, mv_p, wk_s, wv_s, cv_s, ss_s)


def kernel(**inputs):
    NB = 4
    nc = build(NB=NB)
    sh = shared_inputs(inputs)
    in_maps = []
    for core in range(NCORES):
        m = dict(sh)
        m.update(core_inputs(inputs, core, NB))
        in_maps.append(m)
    res = run_bass_kernel_spmd(nc, in_maps, core_ids=list(range(NCORES)))
    return assemble(res.results, NB, NCORES)
```

```python
import numpy as np
import concourse.bass as bass
import concourse.mybir as mybir
from concourse.bass_utils import run_bass_kernel_spmd

F32 = mybir.dt.float32
BF16 = mybir.dt.bfloat16
AF = mybir.ActivationFunctionType
ALU = mybir.AluOpType
AX = mybir.AxisListType

D = 1024
DFF = 2816
NCORES = 8
EPS = 1e-6


class Buf:
    __slots__ = ("name", "w", "rs", "excl")

    def __init__(self, name="", excl=False):
        self.name = name
        self.w = None
        self.rs = {}
        self.excl = excl


class Sch:
    ND = 12

    def __init__(self, nc):
        self.nc = nc
        self.eng = {"pe": nc.tensor, "act": nc.scalar, "dve": nc.vector, "pool": nc.gpsimd, "sp": nc.sync}
        self.sem = {k: nc.alloc_semaphore("s_" + k) for k in self.eng}
        self.cnt = {k: 0 for k in self.eng}
        self.known = {k: {} for k in self.eng}
        self.dsem = {q: [[nc.alloc_semaphore(f"d_{q}{i}"), 0] for i in range(self.ND)]
                     for q in ("sp", "pool", "act")}
        self.dnext = {q: 0 for q in self.dsem}
        self.il = None

    def _deps(self, own, r, w):
        toks = []
        for b in r:
            if b.w is not None:
                toks.append(b.w)
        for b in w:
            if b.w is not None and b.w[0] is not own:
                toks.append(b.w)
            for sem, v in b.rs.items():
                if sem is not own:
                    toks.append((sem, v))
        return toks

    def _wait(self, e, toks):
        need = {}
        for sem, v in toks:
            if v > need.get(sem, 0):
                need[sem] = v
        kn = self.known[e]
        for sem, v in need.items():
            if kn.get(sem, 0) >= v:
                continue
            self.eng[e].wait_ge(sem, v)
            kn[sem] = v

    def _mark(self, tok, r, w):
        for b in r:
            if b.rs.get(tok[0], 0) < tok[1]:
                b.rs[tok[0]] = tok[1]
        for b in w:
            b.w = tok
            b.rs = {}

    def op(self, e, ins_fn, r=(), w=()):
        own = self.sem[e]
        if any(b.excl for b in r):
            w = list(w) + [b for b in r if b.excl]
            r = [b for b in r if not b.excl]
        self._wait(e, self._deps(own, r, w))
        ins = ins_fn()
        self.cnt[e] += 1
        ins.then_inc(own, 1)
        self._mark((own, self.cnt[e]), r, w)
        if self.il is not None:
            self.il.switch()
        return ins

    def dma(self, q, out, in_, r=(), w=(), slow=False):
        pool = self.dsem[q]
        i = self.dnext[q]
        self.dnext[q] = (i + 1) % len(pool)
        slot = pool[i]
        toks = self._deps(None, r, w)
        if slot[1] > 0:
            toks.append((slot[0], slot[1]))
        self._wait(q, toks)
        ins = (self.eng[q].dma_start(out=out, in_=in_, allow_slow_non_contiguous=True) if slow
               else self.eng[q].dma_start(out=out, in_=in_))
        slot[1] += 16
        ins.then_inc(slot[0], 16)
        self._mark((slot[0], slot[1]), r, w)
        if self.il is not None:
            self.il.switch()
        return ins

    def finish(self):
        toks = []
        for q in self.dsem:
            for sem, v in self.dsem[q]:
                if v > 0:
                    toks.append((sem, v))
        for k in self.eng:
            if self.cnt[k] > 0:
                toks.append((self.sem[k], self.cnt[k]))
        self._wait("sp", toks)


class Ring:
    def __init__(self, items):
        self.items = items
        self.i = 0

    def next(self):
        it = self.items[self.i]
        self.i = (self.i + 1) % len(self.items)
        return it


class Ctx:
    pass


def sb(nc, name, shape, dt):
    t = nc.alloc_sbuf_tensor(name, shape, dt)
    return t, Buf(name)


def sb_ring(nc, name, shape, dt, n):
    return Ring([sb(nc, f"{name}{i}", shape, dt) for i in range(n)])


import threading


class Interleave:
    def __init__(self, sch, fns):
        self.sch = sch
        self.fns = fns
        self.ev = [threading.Event() for _ in fns]
        self.alive = [True] * len(fns)
        self.idx = {}
        self.exc = None

    def _next(self, i):
        n = len(self.fns)
        for d in range(1, n + 1):
            j = (i + d) % n
            if j != i and self.alive[j]:
                return j
        return None

    def _wrap(self, i):
        self.idx[threading.get_ident()] = i
        self.ev[i].wait()
        self.ev[i].clear()
        try:
            if self.exc is None:
                self.fns[i]()
        except BaseException as e:
            self.exc = e
        finally:
            self.alive[i] = False
            j = self._next(i)
            if j is not None:
                self.ev[j].set()

    def switch(self):
        i = self.idx.get(threading.get_ident())
        if i is None:
            return
        if self.exc is not None:
            raise RuntimeError("sibling stream failed")
        j = self._next(i)
        if j is None:
            return
        self.ev[j].set()
        self.ev[i].wait()
        self.ev[i].clear()

    def run(self):
        ths = [threading.Thread(target=self._wrap, args=(i,)) for i in range(len(self.fns))]
        self.sch.il = self
        for t in ths:
            t.start()
        self.ev[0].set()
        for t in ths:
            t.join()
        self.sch.il = None
        if self.exc is not None:
            raise self.exc
from contextlib import ExitStack

DIN = 4112
PATTERNS = ((128, 1), (512, 4), (2048, 16))
NEG = -30000.0


def _barrier(self):
    toks = []
    for q in self.dsem:
        for sem, v in self.dsem[q]:
            if v > 0:
                toks.append((sem, v))
    for k in self.eng:
        if self.cnt[k] > 0:
            toks.append((self.sem[k], self.cnt[k]))
    for e in self.eng:
        self._wait(e, [t for t in toks if t[0] is not self.sem[e]])


Sch.barrier = _barrier


class Stage:
    _n = 0

    def __init__(self, c):
        self.c = c
        self.es = ExitStack()

    def __enter__(self):
        self.es.__enter__()
        return self

    def __exit__(self, *a):
        if a[0] is None:
            self.c.s.barrier()
        return self.es.__exit__(*a)

    def sb(self, name, shape, dt):
        Stage._n += 1
        t = self.es.enter_context(self.c.nc.sbuf_tensor(f"{name}_{Stage._n}", shape, dt))
        return t, Buf(name)

    def ring(self, name, shape, dt, n):
        return Ring([self.sb(f"{name}{i}", shape, dt) for i in range(n)])

    def ps(self, name, shape, dt=F32):
        Stage._n += 1
        t = self.es.enter_context(self.c.nc.psum_tensor(f"{name}_{Stage._n}", shape, dt))
        return t, Buf(name, excl=True)

    def psring(self, name, shape, dt, n):
        return Ring([self.ps(f"{name}{i}", shape, dt) for i in range(n)])


def load_weight(c, st, dram_w, dst, dst_buf, kc_n, f_n, piece=1024):
    s, nc = c.s, c.nc
    for kc in range(kc_n):
        for f0 in range(0, f_n, piece):
            fw = min(piece, f_n - f0)
            stg, stb = st.wstage.next()
            s.dma("sp", stg[:, 0:fw], dram_w[:, kc, f0:f0 + fw], w=[stb])
            s.op("pool", lambda: nc.gpsimd.tensor_copy(dst[:, kc, f0:f0 + fw], stg[:, 0:fw]),
                 r=[stb], w=[dst_buf])


def rstd_from_ss(c, ss, ssb, n):
    s, nc = c.s, c.nc
    s.op("dve", lambda: nc.vector.tensor_scalar(ss[:, 1:2], ss[:, 0:1], 1.0 / n, EPS, ALU.mult, ALU.add),
         r=[ssb], w=[ssb])
    s.op("act", lambda: nc.scalar.activation(out=ss[:, 3:4], in_=ss[:, 1:2], func=AF.Sqrt), r=[ssb], w=[ssb])
    s.op("dve", lambda: nc.vector.reciprocal(ss[:, 2:3], ss[:, 3:4]), r=[ssb], w=[ssb])


def norm_part(c, st, xt, xtb, nsub):
    s, nc = c.s, c.nc
    xns = []
    for sub in range(nsub):
        ss, ssb = st.stat.next()
        jk, jkb = st.junk.next()
        s.op("act", lambda: nc.scalar.activation(out=jk[:, :], in_=xt[:, sub, :], func=AF.Square,
                                                 accum_out=ss[:, 0:1]), r=[xtb], w=[jkb, ssb])
        rstd_from_ss(c, ss, ssb, D)
        xn, xnb = st.xn.next()
        s.op("act", lambda: nc.scalar.activation(out=xn[:, :], in_=xt[:, sub, :], func=AF.Copy,
                                                 scale=ss[:, 2:3]), r=[xtb, ssb], w=[xnb])
        xns.append((xn, xnb))
    return xns


def transpose_part(c, st, xns, gcol, hT, hTb):
    s, nc = c.s, c.nc
    for sub, (xn, xnb) in enumerate(xns):
        pt, ptb = st.pst.next()
        for kc in range(8):
            s.op("pe", lambda: nc.tensor.transpose(pt[:, kc, :], xn[:, kc * 128:(kc + 1) * 128], c.ident_bf[:, :]),
                 r=[xnb, c.constb], w=[ptb])
        s.op("dve", lambda: nc.vector.tensor_tensor(
            hT[:, :, sub * 128:(sub + 1) * 128], pt[:, :, :],
            gcol.unsqueeze(2).to_broadcast([128, 8, 128]), ALU.mult),
            r=[ptb, c.constb], w=[hTb])


def norm_transpose(c, st, xt, xtb, nsub, gcol, hT, hTb):
    transpose_part(c, st, norm_part(c, st, xt, xtb, nsub), gcol, hT, hTb)


def norm_bufs(st):
    st.stat = st.ring("stat", [128, 4], F32, 8)
    st.junk = st.ring("junk", [128, 1024], BF16, 2)
    st.xn = st.ring("xn", [128, 1024], BF16, 4)
    st.pst = st.psring("pst", [128, 8, 128], BF16, 2)


def mm_fm(c, W, Wb, f0, hT, hTb, N, ps, psb, KC=8):
    s, nc = c.s, c.nc
    for kc in range(KC):
        s.op("pe", lambda: nc.tensor.matmul(ps[:, 0:N], W[:, kc, f0:f0 + 128], hT[:, kc, 0:N],
                                            start=(kc == 0), stop=(kc == KC - 1)), r=[Wb, hTb], w=[psb])


def mm_tm(c, hT, hTb, sub, W, Wb, c0, ncols, ps, psb, KC=8):
    s, nc = c.s, c.nc
    for kc in range(KC):
        s.op("pe", lambda: nc.tensor.matmul(ps[:, 0:ncols], hT[:, kc, sub * 128:(sub + 1) * 128], W[:, kc, c0:c0 + ncols],
                                            start=(kc == 0), stop=(kc == KC - 1)), r=[Wb, hTb], w=[psb])


def evac(c, out, in_, r, w, scale=None):
    s, nc = c.s, c.nc
    c.flip = not getattr(c, "flip", False)
    if c.flip:
        if scale is None:
            s.op("act", lambda: nc.scalar.copy(out, in_), r=r, w=w)
        else:
            s.op("act", lambda: nc.scalar.mul(out, in_, scale), r=r, w=w)
    else:
        if scale is None:
            s.op("dve", lambda: nc.vector.tensor_copy(out, in_), r=r, w=w)
        else:
            s.op("dve", lambda: nc.vector.tensor_scalar(out, in_, scale, None, ALU.mult), r=r, w=w)


def ffn_stage(c, x_in, x_out, gcol, wg_d, wu_d, wd_d, hT_scr, tiles, tag, final=None):
    s, nc = c.s, c.nc
    H = DFF // 2
    HC = H // 128
    with Stage(c) as st:
        st.wstage = st.ring("wst", [128, 1024], F32, 6)
        wA, wAb = st.sb("wA", [128, 8, H], BF16)
        wB, wBb = st.sb("wB", [128, 8, H], BF16)
        wC, wCb = st.sb("wC", [128, HC, D], BF16)
        xtr = st.ring("xt", [128, 4, 1024], F32, 3)
        hTr = st.ring("hT", [128, 8, 512], BF16, 2)
        actr = st.ring("actT", [128, HC, 512], BF16, 2)
        sgr = st.ring("sg", [128, 512], F32, 2)
        norm_bufs(st)
        psr = st.psring("ps", [128, 512], F32, 6)
        for half in range(2):
            load_weight(c, st, wg_d[:, :, half * H:(half + 1) * H], wA, wAb, 8, H)
            load_weight(c, st, wu_d[:, :, half * H:(half + 1) * H], wB, wBb, 8, H)
            load_weight(c, st, wd_d[:, half * HC:(half + 1) * HC, :], wC, wCb, HC, D)
            src = x_in if half == 0 else x_out
            def prep1(tile):
                t0, nsub = tile
                N = nsub * 128
                xt, xtb = xtr.next()
                xob = c.db((tag, "xo", t0))
                rd = [xob] if half == 1 else [c.db((tag, "xi", t0))]
                s.dma("sp", xt[:, 0:nsub, :], src[t0:t0 + N, :].rearrange("(s p) d -> p s d", p=128), r=rd, w=[xtb])
                hT, hTb = hTr.next()
                xns = None
                if half == 0:
                    xns = norm_part(c, st, xt, xtb, nsub)
                else:
                    s.dma("sp", hT[:, :, 0:N], hT_scr[:, :, t0:t0 + N], r=[c.db((tag, "hs", t0))], w=[hTb])
                return (xt, xtb, hT, hTb, xob, xns, t0, N)

            def prep2(P):
                xt, xtb, hT, hTb, xob, xns, t0, N = P
                if half == 0:
                    transpose_part(c, st, xns, gcol, hT, hTb)
                    s.dma("pool", hT_scr[:, :, t0:t0 + N], hT[:, :, 0:N], r=[hTb], w=[c.db((tag, "hs", t0))])

            nxt = prep1(tiles[0])
            prep2(nxt)
            for ti, (t0, nsub) in enumerate(tiles):
                N = nsub * 128
                xt, xtb, hT, hTb, xob = nxt[0:5]
                act, actb = actr.next()
                for j in range(HC):
                    if j == min(3, HC - 1) and ti + 1 < len(tiles):
                        nxt = prep1(tiles[ti + 1])
                    pg, pgb = psr.next()
                    mm_fm(c, wA, wAb, j * 128, hT, hTb, N, pg, pgb)
                    pu, pub = psr.next()
                    mm_fm(c, wB, wBb, j * 128, hT, hTb, N, pu, pub)
                    sg, sgb = sgr.next()
                    s.op("act", lambda: nc.scalar.activation(out=sg[:, 0:N], in_=pg[:, 0:N], func=AF.Silu), r=[pgb], w=[sgb])
                    s.op("dve", lambda: nc.vector.tensor_tensor(act[:, j, 0:N], sg[:, 0:N], pu[:, 0:N], ALU.mult),
                         r=[sgb, pub], w=[actb])
                if ti + 1 < len(tiles):
                    prep2(nxt)
                for sub in range(nsub):
                    for hf in range(2):
                        pd, pdb = psr.next()
                        mm_tm(c, act, actb, sub, wC, wCb, hf * 512, 512, pd, pdb, KC=HC)
                        s.op("dve", lambda: nc.vector.scalar_tensor_tensor(
                            xt[:, sub, hf * 512:(hf + 1) * 512], pd[:, :], 0.5, xt[:, sub, hf * 512:(hf + 1) * 512],
                            ALU.mult, ALU.add), r=[pdb, xtb], w=[xtb])
                if half == 1 and final is not None:
                    gfin, y_out = final
                    for sub in range(nsub):
                        ss, ssb = st.stat.next()
                        jk, jkb = st.junk.next()
                        s.op("act", lambda: nc.scalar.activation(out=jk[:, :], in_=xt[:, sub, :], func=AF.Square,
                                                                 accum_out=ss[:, 0:1]), r=[xtb], w=[jkb, ssb])
                        rstd_from_ss(c, ss, ssb, D)
                        s.op("dve", lambda: nc.vector.scalar_tensor_tensor(
                            xt[:, sub, :], xt[:, sub, :], ss[:, 2:3], gfin, ALU.mult, ALU.mult),
                            r=[xtb, ssb, c.constb], w=[xtb])
                    s.dma("pool", y_out[t0:t0 + N, :].rearrange("(s p) d -> p s d", p=128), xt[:, 0:nsub, :], r=[xtb])
                else:
                    s.dma("pool", x_out[t0:t0 + N, :].rearrange("(s p) d -> p s d", p=128), xt[:, 0:nsub, :], r=[xtb], w=[xob])


def memkv_stage(c, mem_in, gcol, wck_d, wcv_d, mk_out, mv_out, memKT, memV, NBM):
    s, nc = c.s, c.nc
    with Stage(c) as st:
        st.wstage = st.ring("wst", [128, 1024], F32, 3)
        wK, wKb = st.sb("wK", [128, 8, D], BF16)
        wV, wVb = st.sb("wV", [128, 8, D], BF16)
        xtr = st.ring("xt", [128, 4, 1024], F32, 2)
        hTr = st.ring("hT", [128, 8, 512], BF16, 2)
        tmr = st.ring("tm", [128, 1024], F32, 3)
        tbr = st.ring("tb", [128, 1024], BF16, 2)
        fmr = st.ring("fm", [128, 512], BF16, 3)
        norm_bufs(st)
        psr = st.psring("ps", [128, 512], F32, 6)
        load_weight(c, st, wck_d, wK, wKb, 8, D)
        load_weight(c, st, wcv_d, wV, wVb, 8, D)
        tiles = []
        t0 = 0
        while t0 < NBM:
            n = min(4, (NBM - t0) // 128)
            tiles.append((t0, n))
            t0 += n * 128
        for (t0, nsub) in tiles:
            N = nsub * 128
            xt, xtb = xtr.next()
            s.dma("sp", xt[:, 0:nsub, :], mem_in[t0:t0 + N, :].rearrange("(s p) d -> p s d", p=128), w=[xtb])
            hT, hTb = hTr.next()
            norm_transpose(c, st, xt, xtb, nsub, gcol, hT, hTb)
            for sub in range(nsub):
                r0 = t0 + sub * 128
                for (W, Wb, outd, bfd) in ((wK, wKb, mk_out, None), (wV, wVb, mv_out, memV)):
                    tm, tmb = tmr.next()
                    for hf in range(2):
                        ps, psb = psr.next()
                        mm_tm(c, hT, hTb, sub, W, Wb, hf * 512, 512, ps, psb)
                        evac(c, tm[:, hf * 512:(hf + 1) * 512], ps[:, :], [psb], [tmb])
                    s.dma("pool", outd[r0:r0 + 128, :], tm[:, :], r=[tmb])
                    if bfd is not None:
                        tb, tbb = tbr.next()
                        s.op("pool", lambda: nc.gpsimd.tensor_copy(tb[:, :], tm[:, :]), r=[tmb], w=[tbb])
                        s.dma("pool", bfd[r0:r0 + 128, :], tb[:, :], r=[tbb])
            for j in range(8):
                ps, psb = psr.next()
                mm_fm(c, wK, wKb, j * 128, hT, hTb, N, ps, psb)
                fm, fmb = fmr.next()
                evac(c, fm[:, 0:N], ps[:, 0:N], [psb], [fmb])
                s.dma("pool", memKT[:, j, t0:t0 + N], fm[:, 0:N], r=[fmb])


def inproj_stage(c, x1, gcol, win_d, qT, kT, kT_out, v_out, v_bf, z_scr, xbcT, dt_scr, tiles):
    s, nc = c.s, c.nc
    with Stage(c) as st:
        st.wstage = st.ring("wst", [128, 1024], F32, 3)
        Wa, Wab = st.sb("winA", [128, 8, 2560], BF16)
        Wc, Wcb = st.sb("winB", [128, 8, DIN - 2560], BF16)
        xtr = st.ring("xt", [128, 4, 1024], F32, 3)
        hTr = st.ring("hT", [128, 8, 512], BF16, 2)
        f32r = st.ring("f32", [128, 512], F32, 6)
        bfr = st.ring("bf", [128, 512], BF16, 6)
        zr = st.ring("zt", [128, 1024], F32, 2)
        dtr = st.ring("dtt", [128, 16], F32, 3)
        norm_bufs(st)
        psr = st.psring("ps", [128, 512], F32, 6)
        load_weight(c, st, win_d[:, :, 0:2560], Wa, Wab, 8, 2560)
        load_weight(c, st, win_d[:, :, 2560:DIN], Wc, Wcb, 8, DIN - 2560)
        def prep1(tile):
            t0, nsub = tile
            N = nsub * 128
            xt, xtb = xtr.next()
            s.dma("sp", xt[:, 0:nsub, :], x1[t0:t0 + N, :].rearrange("(s p) d -> p s d", p=128), w=[xtb])
            hT, hTb = hTr.next()
            return (hT, hTb, norm_part(c, st, xt, xtb, nsub))

        nxt = prep1(tiles[0])
        transpose_part(c, st, nxt[2], gcol, nxt[0], nxt[1])
        for ti, (t0, nsub) in enumerate(tiles):
            N = nsub * 128
            hT, hTb = nxt[0], nxt[1]
            for j in range(20):
                if j == 3 and ti + 1 < len(tiles):
                    nxt = prep1(tiles[ti + 1])
                f0 = j * 128 if j < 8 else 2560 + (j - 8) * 128
                ps, psb = psr.next()
                if f0 < 2560:
                    mm_fm(c, Wa, Wab, f0, hT, hTb, N, ps, psb)
                else:
                    mm_fm(c, Wc, Wcb, f0 - 2560, hT, hTb, N, ps, psb)
                if j < 4:
                    o, ob = bfr.next()
                    evac(c, o[:, 0:N], ps[:, 0:N], [psb], [ob], scale=0.125)
                    s.dma("pool", qT[:, j, t0:t0 + N], o[:, 0:N], r=[ob])
                elif j < 8:
                    o, ob = bfr.next()
                    evac(c, o[:, 0:N], ps[:, 0:N], [psb], [ob])
                    s.dma("pool", kT[:, j - 4, t0:t0 + N], o[:, 0:N], r=[ob])
                    o2, o2b = f32r.next()
                    evac(c, o2[:, 0:N], ps[:, 0:N], [psb], [o2b])
                    s.dma("pool", kT_out[:, j - 4, t0:t0 + N], o2[:, 0:N], r=[o2b])
                else:
                    o2, o2b = f32r.next()
                    evac(c, o2[:, 0:N], ps[:, 0:N], [psb], [o2b])
                    s.dma("pool", xbcT[:, j - 8, t0:t0 + N], o2[:, 0:N], r=[o2b])
            if ti + 1 < len(tiles):
                transpose_part(c, st, nxt[2], gcol, nxt[0], nxt[1])
            for sub in range(nsub):
                r0 = t0 + sub * 128
                ps, psb = psr.next()
                mm_tm(c, hT, hTb, sub, Wa, Wab, 1024, 512, ps, psb)
                o2, o2b = f32r.next()
                evac(c, o2[:, :], ps[:, :], [psb], [o2b])
                s.dma("pool", v_out[r0:r0 + 128, :], o2[:, :], r=[o2b])
                o, ob = bfr.next()
                evac(c, o[:, :], ps[:, :], [psb], [ob])
                s.dma("pool", v_bf[r0:r0 + 128, :], o[:, :], r=[ob])
                zt, ztb = zr.next()
                for hf in range(2):
                    ps, psb = psr.next()
                    mm_tm(c, hT, hTb, sub, Wa, Wab, 1536 + hf * 512, 512, ps, psb)
                    evac(c, zt[:, hf * 512:(hf + 1) * 512], ps[:, :], [psb], [ztb])
                s.dma("pool", z_scr[r0:r0 + 128, :], zt[:, :], r=[ztb])
                ps, psb = psr.next()
                mm_tm(c, hT, hTb, sub, Wc, Wcb, 4096 - 2560, 16, ps, psb)
                dtt, dtb = dtr.next()
                evac(c, dtt[:, :], ps[:, 0:16], [psb], [dtb])
                s.dma("pool", dt_scr[r0:r0 + 128, :], dtt[:, :], r=[dtb])


def t5_bucket_np(dist):
    d = np.maximum(np.asarray(dist, np.int64), 0)
    df = np.maximum(d, 1).astype(np.float32)
    large = 16 + (np.log(df / np.float32(16.0)) / np.float32(np.log(2048.0 / 16.0)) * np.float32(16.0)).astype(np.int32)
    large = np.minimum(large, 31)
    return np.where(d < 16, d, large).astype(np.int64)


def bias_consts():
    k = np.arange(128)[:, None, None]
    kb = np.arange(2)[None, :, None]
    q = np.arange(128)[None, None, :]
    delta = q + 128 * kb - k
    valid = (delta >= 0) & (delta <= 128)
    pb, masks, negm = [], [], []
    for pi, (wnd, dil) in enumerate(PATTERNS):
        bk = t5_bucket_np(delta * dil)
        negm.append(np.where(valid, 0.0, NEG).astype(np.float32).reshape(128, 256))
        for b in range(32):
            m = (valid & (bk == b))
            if m.any():
                pb.append((pi, b))
                masks.append(m.astype(np.float32).reshape(128, 256))
    return pb, np.stack(masks), np.stack(negm)


def attn_stage(c, qT, kT, v_bf, mixedT, relb_d, bmask_d, negm_d, pb, cache_k, cache_v, NB, NTP):
    s, nc = c.s, c.nc
    with Stage(c) as st:
        BT, BTb = st.sb("BT", [128, 24, 256], F32)
        rb, rbb = st.sb("rb", [128, 256], F32)
        mr = st.ring("bm", [128, 256], F32, 3)
        s.dma("sp", rb[:, :], relb_d[0:1, :].partition_broadcast(128), w=[rbb])
        for pi in range(3):
            for h in range(8):
                s.dma("sp", BT[:, pi * 8 + h, :], negm_d[pi], w=[BTb])
        for n, (pi, b) in enumerate(pb):
            m, mb = mr.next()
            s.dma("sp", m[:, :], bmask_d[n], w=[mb])
            for h in range(8):
                s.op("dve", lambda: nc.vector.scalar_tensor_tensor(
                    BT[:, pi * 8 + h, :], m[:, :], rb[:, b * 8 + h:b * 8 + h + 1], BT[:, pi * 8 + h, :],
                    ALU.mult, ALU.add), r=[mb, rbb, BTb], w=[BTb])
        BT4 = BT[:, :, :].rearrange("p n (kb q) -> p n kb q", kb=2)
        BTh, BThb = st.sb("BTh", [128, 24, 256], BF16)
        BTl, BTlb = st.sb("BTl", [128, 24, 256], BF16)
        btmp, btmpb = st.sb("btmp", [128, 8, 256], F32)
        for pi in range(3):
            ps_ = slice(pi * 8, pi * 8 + 8)
            s.op("dve", lambda: nc.vector.tensor_copy(BTh[:, ps_, :], BT[:, ps_, :]), r=[BTb], w=[BThb])
            s.op("dve", lambda: nc.vector.tensor_tensor(btmp[:, :, :], BT[:, ps_, :], BTh[:, ps_, :], ALU.subtract),
                 r=[BTb, BThb], w=[btmpb])
            s.op("dve", lambda: nc.vector.tensor_copy(BTl[:, ps_, :], btmp[:, :, :]), r=[btmpb], w=[BTlb])

        qTb, qTbb = st.sb("qTb", [128, 4, 2048], BF16)
        kTb, kTbb = st.sb("kTb", [128, 4, 2048], BF16)
        acc, accb = st.sb("acc", [128, 2, 4, 2048], F32)
        Vr = st.ring("Vt", [128, 512], BF16, 5)
        sbr = st.ring("sbs", [128, 2, 128], F32, 4)
        ptr = st.ring("PT", [128, 2, 128], BF16, 4)
        rcr = st.ring("rc", [128, 1024], F32, 1)
        obr = st.ring("ob", [128, 1024], BF16, 2)
        psS = st.psring("psS", [128, 512], F32, 3)
        psO = st.psring("psO", [128, 512], F32, 3)
        for b in range(NB):
            tok0 = b * 2048
            s.dma("sp", qTb[:, :, :], qT[:, :, tok0:tok0 + 2048], w=[qTbb])
            s.dma("sp", kTb[:, :, :], kT[:, :, tok0:tok0 + 2048], w=[kTbb])
            units = []
            for pi, (wnd, dil) in enumerate(PATTERNS):
                nblk = 2048 // dil // 128
                for r in range(dil):
                    for blk in range(nblk):
                        for h in range(8):
                            units.append((pi, dil, r, blk, h))
            vstate = {}

            def phaseA(u):
                pi, dil, r, blk, h = u
                cols = slice(r + dil * 128 * blk, r + dil * 128 * blk + dil * 127 + 1, dil)
                pcols = slice(r + dil * 128 * (blk - 1), r + dil * 128 * (blk - 1) + dil * 127 + 1, dil)
                if h == 0:
                    Vt, Vtb = Vr.next()
                    row0 = tok0 + r + dil * 128 * blk
                    s.dma("sp", Vt[:, :], v_bf[row0:row0 + 127 * dil + 1:dil, :], w=[Vtb])
                    prev = vstate.get((pi, r, blk - 1))
                    vstate[(pi, r, blk)] = (Vt, Vtb)
                    vstate[("cur", pi, r, blk)] = [(Vt, Vtb)] + ([prev] if blk > 0 else [])
                nkb = 2 if blk > 0 else 1
                pair = h // 2
                rows = slice(64 * (h % 2), 64 * (h % 2) + 64)
                Sp, Spb = psS.next()
                S = Sp[:, 0:256].rearrange("p (kb q) -> p kb q", kb=2)
                s.op("pe", lambda: nc.tensor.matmul(Sp[:, 0:nkb * 128], c.ident_bf[:, :], BTh[:, pi * 8 + h, 0:nkb * 128],
                                                    start=True, stop=False), r=[c.constb, BThb], w=[Spb])
                s.op("pe", lambda: nc.tensor.matmul(Sp[:, 0:nkb * 128], c.ident_bf[:, :], BTl[:, pi * 8 + h, 0:nkb * 128],
                                                    start=False, stop=False), r=[c.constb, BTlb], w=[Spb])
                s.op("pe", lambda: nc.tensor.matmul(S[:, 0, :], kTb[rows, pair, cols], qTb[rows, pair, cols],
                                                    start=False, stop=(nkb == 1)), r=[kTbb, qTbb], w=[Spb])
                if blk > 0:
                    s.op("pe", lambda: nc.tensor.matmul(S[:, 1, :], kTb[rows, pair, pcols], qTb[rows, pair, cols],
                                                        start=False, stop=True), r=[kTbb, qTbb], w=[Spb])
                PT, PTb = ptr.next()
                s.op("act", lambda: nc.scalar.activation(out=PT[:, 0:nkb, :], in_=S[:, 0:nkb, :], func=AF.Exp),
                     r=[Spb], w=[PTb])
                return (PT, PTb, cols, nkb, vstate[("cur", pi, r, blk)])

            def phaseB(u, A):
                pi, dil, r, blk, h = u
                PT, PTb, cols, nkb, vs = A
                pair = h // 2
                rows = slice(64 * (h % 2), 64 * (h % 2) + 64)
                Op, Opb = psO.next()
                OL = Op[:, 0:256].rearrange("p (a q) -> p a q", a=2)
                for kb, (vt, vtb) in enumerate(vs):
                    s.op("pe", lambda: nc.tensor.matmul(OL[:, 0, :], vt[:, pair * 128:(pair + 1) * 128], PT[:, kb, :],
                                                        start=(kb == 0), stop=(kb == nkb - 1)), r=[vtb, PTb], w=[Opb])
                for kb in range(nkb):
                    s.op("pe", lambda: nc.tensor.matmul(OL[:, 1, :], c.ones_bf[:, :], PT[:, kb, :],
                                                        start=(kb == 0), stop=(kb == nkb - 1)), r=[c.constb, PTb], w=[Opb])
                dst = acc[rows, :, pair, cols]
                if pi == 0:
                    s.op("dve", lambda: nc.vector.tensor_copy(dst, OL[rows, :, :]), r=[Opb], w=[accb])
                else:
                    s.op("dve", lambda: nc.vector.tensor_tensor(dst, dst, OL[rows, :, :], ALU.add),
                         r=[Opb, accb], w=[accb])

            Acur = phaseA(units[0])
            for k, u in enumerate(units):
                Anext = phaseA(units[k + 1]) if k + 1 < len(units) else None
                phaseB(u, Acur)
                Acur = Anext
            for pair in range(4):
                for hh in range(2):
                    ts_ = slice(hh * 1024, (hh + 1) * 1024)
                    rc, rcb = rcr.next()
                    s.op("dve", lambda: nc.vector.reciprocal(rc[:, :], acc[:, 1, pair, ts_]), r=[accb], w=[rcb])
                    ob, obb = obr.next()
                    s.op("dve", lambda: nc.vector.tensor_tensor(ob[:, :], acc[:, 0, pair, ts_], rc[:, :], ALU.mult),
                         r=[accb, rcb], w=[obb])
                    s.dma("pool", mixedT[:, pair, tok0 + hh * 1024:tok0 + (hh + 1) * 1024], ob[:, :], r=[obb])

        qS, qSb = st.sb("qS", [128, 4, 16], BF16)
        kS, kSb = st.sb("kS", [128, 4, 16], BF16)
        oS, oSb = st.sb("oS", [128, 4, 16], BF16)
        s.dma("sp", qS[:, :, :], qT[:, :, NTP:NTP + 16], w=[qSb])
        s.dma("sp", kS[:, :, :], kT[:, :, NTP:NTP + 16], w=[kSb])
        vrr = st.ring("vrow", [1, 512], BF16, 3)
        kgr = st.ring("Kg", [128, 512], F32, 3)
        vgr = st.ring("Vg", [128, 512], F32, 3)
        kgbr = st.ring("Kgb", [128, 512], BF16, 2)
        vgbr = st.ring("Vgb", [128, 512], BF16, 4)
        kgtr = st.ring("KgT", [128, 4, 128], BF16, 2)
        s8r = st.ring("s8", [128, 16], F32, 3)
        p8r = st.ring("p8", [128, 16], BF16, 3)
        t8r = st.ring("t8", [128, 16], F32, 2)
        pstr = st.psring("pstA", [128, 8, 128], BF16, 1)
        smp, smb = st.ps("psm", [128, 512], F32)
        Sgb = Sob = OSb = smb
        Sg = smp[:, 0:8]
        So = smp[0:1, 8:16]
        OS = smp[:, 16:32].rearrange("p (a h) -> p a h", a=2)
        for i in range(16):
            vrow, vrowb = vrr.next()
            s.dma("sp", vrow[0:1, :], v_bf[NTP + i:NTP + i + 1, :], w=[vrowb])
            keep = []
            for pi, (wnd, dil) in enumerate(PATTERNS):
                Kg, Kgb_ = kgr.next()
                Vg, Vgb_ = vgr.next()
                s.dma("sp", Kg[:, :], cache_k[i, 2048 - 128 * dil:2048:dil, :], w=[Kgb_])
                s.dma("sp", Vg[:, :], cache_v[i, 2048 - 128 * dil:2048:dil, :], w=[Vgb_])
                Kb, Kbb = kgbr.next()
                Vb, Vbb = vgbr.next()
                s.op("pool", lambda: nc.gpsimd.tensor_copy(Kb[:, :], Kg[:, :]), r=[Kgb_], w=[Kbb])
                s.op("pool", lambda: nc.gpsimd.tensor_copy(Vb[:, :], Vg[:, :]), r=[Vgb_], w=[Vbb])
                pt, ptb = pstr.next()
                for pr in range(4):
                    s.op("pe", lambda: nc.tensor.transpose(pt[:, pr, :], Kb[:, pr * 128:(pr + 1) * 128], c.ident_bf[:, :]),
                         r=[Kbb, c.constb], w=[ptb])
                KT_, KTb_ = kgtr.next()
                evac(c, KT_[:, :, :], pt[:, 0:4, :], [ptb], [KTb_])
                for h in range(8):
                    pair = h // 2
                    rows = slice(64 * (h % 2), 64 * (h % 2) + 64)
                    s.op("pe", lambda: nc.tensor.matmul(Sg[:, h:h + 1], KT_[rows, pair, :], qS[rows, pair, i:i + 1],
                                                        start=True, stop=True), r=[KTb_, qSb], w=[Sgb])
                    s.op("pe", lambda: nc.tensor.matmul(So[0:1, h:h + 1], kS[rows, pair, i:i + 1], qS[rows, pair, i:i + 1],
                                                        start=True, stop=True), r=[kSb, qSb], w=[Sob])
                s8, s8b = s8r.next()
                s.op("dve", lambda: nc.vector.tensor_tensor(s8[:, 0:8], Sg, BT4[:, pi * 8:(pi + 1) * 8, 1, 0], ALU.add),
                     r=[Sgb, BTb], w=[s8b])
                s.op("dve", lambda: nc.vector.tensor_tensor(s8[0:1, 8:16], So, BT4[0:1, pi * 8:(pi + 1) * 8, 0, 0], ALU.add),
                     r=[Sob, BTb], w=[s8b])
                p8, p8b = p8r.next()
                s.op("act", lambda: nc.scalar.activation(out=p8[:, 0:8], in_=s8[:, 0:8], func=AF.Exp), r=[s8b], w=[p8b])
                s.op("act", lambda: nc.scalar.activation(out=p8[0:1, 8:16], in_=s8[0:1, 8:16], func=AF.Exp), r=[s8b], w=[p8b])
                keep.append((Vb, Vbb, p8, p8b))
            for a_ in range(2):
                for h in range(8):
                    pair = h // 2
                    for pi, (Vb, Vbb, p8, p8b) in enumerate(keep):
                        lh = Vb[:, pair * 128:(pair + 1) * 128] if a_ == 0 else c.ones_bf[:, :]
                        lo = vrow[0:1, pair * 128:(pair + 1) * 128] if a_ == 0 else c.ones_bf[0:1, :]
                        s.op("pe", lambda: nc.tensor.matmul(OS[:, a_, h:h + 1], lh, p8[:, h:h + 1],
                                                            start=(pi == 0), stop=False), r=[Vbb, c.constb, p8b], w=[OSb])
                        s.op("pe", lambda: nc.tensor.matmul(OS[:, a_, h:h + 1], lo, p8[0:1, 8 + h:9 + h],
                                                            start=False, stop=(pi == 2)), r=[vrowb, c.constb, p8b], w=[OSb])
            t8, t8b = t8r.next()
            s.op("dve", lambda: nc.vector.reciprocal(t8[:, 8:16], OS[:, 1, :]), r=[OSb], w=[t8b])
            s.op("dve", lambda: nc.vector.tensor_tensor(t8[:, 0:8], OS[:, 0, :], t8[:, 8:16], ALU.mult), r=[OSb, t8b], w=[t8b])
            for half in range(2):
                rows = slice(64 * half, 64 * half + 64)
                s.op("dve", lambda: nc.vector.tensor_copy(oS[rows, :, i], t8[rows, half:8:2]), r=[t8b], w=[oSb])
        s.dma("pool", mixedT[:, 0:4, NTP:NTP + 16], oS[:, :, :], r=[oSb])


def ssd_stage(c, xbcT, dt_scr, z_scr, mixedT, convw_d, convb_d, vec16_d, gssm_d, tri_d,
              cconv_d, state_d, conv_out, ssm_out, NB, NTP):
    s, nc = c.s, c.nc
    NSTREAM = 2
    with Stage(c) as st:
        cw, cwb = st.sb("cw", [128, 12, 4], F32)
        cb, cbb = st.sb("cb", [128, 12], F32)
        v16, v16b = st.sb("v16", [128, 3, 16], F32)
        gss, gssb = st.sb("gss", [128, 1024], F32)
        tri, trib = st.sb("tri", [128, 2, 128], F32)
        onesf, onesfb = st.sb("onesf", [128, 128], F32)
        s.dma("sp", cw[:, :, :], convw_d[:, :, :], w=[cwb])
        s.dma("sp", cb[:, :], convb_d[:, :], w=[cbb])
        s.dma("sp", v16[:, :, :], vec16_d[0:1, :, :].partition_broadcast(128), w=[v16b])
        s.dma("sp", gss[:, :], gssm_d[0:1, :].partition_broadcast(128), w=[gssb])
        s.dma("sp", tri[:, :, :], tri_d[:, :, :], w=[trib])
        s.op("pool", lambda: nc.gpsimd.memset(onesf[:, :], 1.0), w=[onesfb])
        s.op("act", lambda: nc.scalar.activation(out=v16[:, 1, :], in_=v16[:, 1, :], func=AF.Exp), r=[v16b], w=[v16b])
        s.op("dve", lambda: nc.vector.tensor_scalar(v16[:, 1, :], v16[:, 1, :], -1.0, None, ALU.mult), r=[v16b], w=[v16b])
        dtb_bc, a_bc, dsk_bc = v16[:, 0, :], v16[:, 1, :], v16[:, 2, :]
        triU, strictL = tri[:, 0, :], tri[:, 1, :]

        def make_stream(k):
            S = Ctx()
            n = f"s{k}"
            S.xin = st.sb(n + "xin", [128, 12, 131], F32)
            S.cacc = st.sb(n + "cacc", [128, 12, 128], F32)
            S.ctmp = st.sb(n + "ctmp", [128, 12, 128], F32)
            S.c2 = st.sb(n + "c2", [128, 12, 128], F32)
            S.a32 = st.sb(n + "a32", [128, 12, 128], F32)
            S.BCT = st.sb(n + "BCT", [128, 4, 128], BF16)
            S.Btok = st.sb(n + "Btok", [128, 2, 128], BF16)
            S.xdt = st.sb(n + "xdt", [128, 16, 64], BF16)
            S.xdts = st.sb(n + "xdts", [128, 16, 64], BF16)
            S.dsk = st.sb(n + "dsk", [128, 16, 64], F32)
            S.dtt = st.sb(n + "dtt", [128, 6, 16], F32)
            S.acs = st.sb(n + "acs", [128, 5, 16], F32)
            S.H = st.sb(n + "H", [128, 16, 64], F32)
            S.Hb = st.sb(n + "Hb", [128, 16, 64], BF16)
            S.CBm = st.sb(n + "CBm", [128, 2, 128], F32)
            S.rhsD = st.ring(n + "rhsD", [128, 4, 128], F32, 2)
            S.Es = st.ring(n + "Es", [128, 4, 128], F32, 1)
            S.MT = st.ring(n + "MT", [128, 4, 128], BF16, 2)
            S.YT = st.sb(n + "YTsb", [128, 2, 16, 64], F32)
            S.yt = st.sb(n + "yt", [128, 16, 64], F32)
            S.zt = st.sb(n + "zt", [128, 1024], F32)
            S.yn = st.sb(n + "yn", [128, 1024], F32)
            S.yT = st.sb(n + "yT", [128, 8, 128], BF16)
            S.stat = st.ring(n + "stat", [128, 4], F32, 2)
            S.junk = st.sb(n + "junk", [128, 512], BF16)
            S.YTp = st.ps(n + "YTp", [128, 512], F32)
            S.Fp = st.ps(n + "Fp", [128, 512], F32)
            S.Mr = st.psring(n + "M", [128, 512], F32, 2)
            return S

        def front(S, item):
            kind, idx, tok0, ch, nchunk = item
            t0 = tok0 + ch * 128
            sidx = idx if kind == "p" else NB + idx
            Mr = S.Mr
            xin, xinb = S.xin
            if kind == "p":
                if ch == 0:
                    s.op("pool", lambda: nc.gpsimd.memset(xin[:, :, 0:3], 0.0), w=[xinb])
                    s.dma("sp", xin[:, :, 3:131], xbcT[:, :, t0:t0 + 128], w=[xinb])
                else:
                    s.dma("sp", xin[:, :, :], xbcT[:, :, t0 - 3:t0 + 128], w=[xinb])
            else:
                s.op("pool", lambda: nc.gpsimd.memset(xin[:, :, :], 0.0), w=[xinb])
                s.dma("sp", xin[:, :, 0:3], cconv_d[idx], w=[xinb])
                s.dma("sp", xin[:, :, 3:4], xbcT[:, :, t0:t0 + 1], w=[xinb], slow=True)
            dtt, dttb = S.dtt
            if kind == "p":
                s.dma("sp", dtt[:, 0, :], dt_scr[t0:t0 + 128, :], w=[dttb])
            else:
                s.op("pool", lambda: nc.gpsimd.memset(dtt[:, 0, :], 0.0), w=[dttb])
                s.dma("sp", dtt[0:1, 0, :], dt_scr[t0:t0 + 1, :], w=[dttb])
            zt, ztb = S.zt
            if kind == "p":
                s.dma("sp", zt[:, :], z_scr[t0:t0 + 128, :], w=[ztb])
            else:
                s.op("pool", lambda: nc.gpsimd.memset(zt[:, :], 0.0), w=[ztb])
                s.dma("sp", zt[0:1, :], z_scr[t0:t0 + 1, :], w=[ztb])
            if ch == nchunk - 1:
                lo = 128 if kind == "p" else 1
                s.dma("sp", conv_out[sidx], xin[:, :, lo:lo + 3], r=[xinb])
            cacc, caccb = S.cacc
            ctmp, ctmpb = S.ctmp
            c2, c2b = S.c2
            a32, a32b = S.a32

            def wbc(i):
                return cw[:, :, i:i + 1].to_broadcast([128, 12, 128])
            s.op("pool", lambda: nc.gpsimd.tensor_tensor(cacc[:, :, :], xin[:, :, 3:131], wbc(3), ALU.mult), r=[xinb, cwb], w=[caccb])
            s.op("dve", lambda: nc.vector.tensor_tensor(c2[:, :, :], xin[:, :, 1:129], wbc(1), ALU.mult), r=[xinb, cwb], w=[c2b])
            s.op("pool", lambda: nc.gpsimd.tensor_tensor(ctmp[:, :, :], xin[:, :, 2:130], wbc(2), ALU.mult), r=[xinb, cwb], w=[ctmpb])
            s.op("dve", lambda: nc.vector.tensor_tensor(a32[:, :, :], xin[:, :, 0:128], wbc(0), ALU.mult), r=[xinb, cwb], w=[a32b])
            s.op("pool", lambda: nc.gpsimd.tensor_tensor(cacc[:, :, :], cacc[:, :, :], ctmp[:, :, :], ALU.add), r=[ctmpb, caccb], w=[caccb])
            s.op("dve", lambda: nc.vector.tensor_tensor(c2[:, :, :], c2[:, :, :], a32[:, :, :], ALU.add), r=[c2b, a32b], w=[c2b])
            s.op("dve", lambda: nc.vector.tensor_tensor(c2[:, :, :], c2[:, :, :], cacc[:, :, :], ALU.add), r=[c2b, caccb], w=[c2b])
            for j in range(12):
                s.op("act", lambda: nc.scalar.activation(out=a32[:, j, :], in_=c2[:, j, :], func=AF.Silu, bias=cb[:, j:j + 1]),
                     r=[c2b, cbb], w=[a32b])
            BCT, BCTb = S.BCT
            s.op("act", lambda: nc.scalar.copy(BCT[:, :, :], a32[:, 8:12, :]), r=[a32b], w=[BCTb])
            s.op("dve", lambda: nc.vector.tensor_tensor(dtt[:, 1, :], dtt[:, 0, :], dtb_bc, ALU.add), r=[dttb, v16b], w=[dttb])
            s.op("act", lambda: nc.scalar.activation(out=dtt[:, 1, :], in_=dtt[:, 1, :], func=AF.Exp), r=[dttb], w=[dttb])
            s.op("dve", lambda: nc.vector.tensor_scalar(dtt[:, 1, :], dtt[:, 1, :], 1.0, None, ALU.add), r=[dttb], w=[dttb])
            s.op("act", lambda: nc.scalar.activation(out=dtt[:, 2, :], in_=dtt[:, 1, :], func=AF.Ln), r=[dttb], w=[dttb])
            if kind == "s":
                s.op("dve", lambda: nc.vector.tensor_scalar(dtt[:, 2, :], dtt[:, 2, :], c.ident_f[:, 0:1], None, ALU.mult),
                     r=[dttb, c.constb], w=[dttb])
            s.op("dve", lambda: nc.vector.tensor_tensor(dtt[:, 3, :], dtt[:, 2, :], a_bc, ALU.mult), r=[dttb, v16b], w=[dttb])
            dt_, la = dtt[:, 2, :], dtt[:, 3, :]
            Mc, Mcb = Mr.next()
            s.op("pe", lambda: nc.tensor.matmul(Mc[:, 0:16], triU, la, start=True, stop=True), r=[trib, dttb], w=[Mcb])
            s.op("pe", lambda: nc.tensor.matmul(Mc[:, 16:32], onesf[:, :], la, start=True, stop=True), r=[onesfb, dttb], w=[Mcb])
            acs, acsb = S.acs
            s.op("dve", lambda: nc.vector.tensor_copy(acs[:, 0:2, :], Mc[:, 0:32].rearrange("p (a e) -> p a e", a=2)),
                 r=[Mcb], w=[acsb])
            s.op("dve", lambda: nc.vector.tensor_tensor(acs[:, 3, :], acs[:, 1, :], acs[:, 0, :], ALU.subtract), r=[acsb], w=[acsb])
            s.op("act", lambda: nc.scalar.activation(out=acs[:, 2, :], in_=acs[:, 0, :], func=AF.Exp), r=[acsb], w=[acsb])
            s.op("act", lambda: nc.scalar.activation(out=acs[:, 3, :], in_=acs[:, 3, :], func=AF.Exp), r=[acsb], w=[acsb])
            s.op("act", lambda: nc.scalar.activation(out=acs[:, 4, :], in_=acs[:, 1, :], func=AF.Exp), r=[acsb], w=[acsb])
            s.op("dve", lambda: nc.vector.tensor_tensor(dtt[:, 4, :], dt_, acs[:, 3, :], ALU.mult), r=[dttb, acsb], w=[dttb])
            dtd = dtt[:, 4, :]
            xdt, xdtb = S.xdt
            xdts, xdtsb = S.xdts
            dsk, dskb = S.dsk
            for g in range(2):
                m, mb = Mr.next()
                for j in range(g * 4, g * 4 + 4):
                    s.op("pe", lambda: nc.tensor.transpose(m[:, (j % 4) * 128:(j % 4 + 1) * 128], a32[:, j, :], c.ident_f[:, :]),
                         r=[a32b, c.constb], w=[mb])
                mv = m[:, :].rearrange("p (e q) -> p e q", e=8)
                hs = slice(g * 8, g * 8 + 8)
                s.op("dve", lambda: nc.vector.tensor_tensor(xdt[:, hs, :], mv, dt_[:, hs].unsqueeze(2).to_broadcast([128, 8, 64]), ALU.mult),
                     r=[mb, dttb], w=[xdtb])
                s.op("dve", lambda: nc.vector.tensor_tensor(xdts[:, hs, :], mv, dtd[:, hs].unsqueeze(2).to_broadcast([128, 8, 64]), ALU.mult),
                     r=[mb, dttb], w=[xdtsb])
                s.op("dve", lambda: nc.vector.tensor_tensor(dsk[:, hs, :], mv, dsk_bc[:, hs].unsqueeze(2).to_broadcast([128, 8, 64]), ALU.mult),
                     r=[mb, v16b], w=[dskb])
            Mb_, Mbb = Mr.next()
            for g in range(2):
                s.op("pe", lambda: nc.tensor.transpose(Mb_[:, g * 128:(g + 1) * 128], a32[:, 8 + g, :], c.ident_f[:, :]),
                     r=[a32b, c.constb], w=[Mbb])
            Btok, Btokb = S.Btok
            s.op("act", lambda: nc.scalar.copy(Btok[:, :, :], Mb_[:, 0:256].rearrange("p (g n) -> p g n", g=2)), r=[Mbb], w=[Btokb])
            CBm, CBmb = S.CBm
            Mg, Mgb = Mr.next()
            for g in range(2):
                s.op("pe", lambda: nc.tensor.matmul(Mg[:, g * 128:(g + 1) * 128], BCT[:, g, :], BCT[:, 2 + g, :], start=True, stop=True),
                     r=[BCTb], w=[Mgb])
            s.op("dve", lambda: nc.vector.tensor_tensor(CBm[:, :, :], Mg[:, 0:256].rearrange("p (g l) -> p g l", g=2),
                                                        triU.unsqueeze(1).to_broadcast([128, 2, 128]), ALU.mult), r=[Mgb, trib], w=[CBmb])
            YT, YTb = S.YT
            YTp, YTpb = S.YTp
            for g in range(2):
                for q4 in range(2):
                    e0 = g * 8 + q4 * 4
                    rhsD, rhsDb = S.rhsD.next()
                    s.op("pool", lambda: nc.gpsimd.tensor_tensor(rhsD[:, :, :], triU.unsqueeze(1).to_broadcast([128, 4, 128]),
                                                                 la[:, e0:e0 + 4].unsqueeze(2).to_broadcast([128, 4, 128]), ALU.mult),
                         r=[trib, dttb], w=[rhsDb])
                    Md, Mdb = Mr.next()
                    s.op("pe", lambda: nc.tensor.matmul(Md[:, :], strictL, rhsD[:, :, :].rearrange("p e l -> p (e l)"), start=True, stop=True),
                         r=[trib, rhsDb], w=[Mdb])
                    Es, Esb = S.Es.next()
                    s.op("act", lambda: nc.scalar.activation(out=Es[:, :, :].rearrange("p e l -> p (e l)"), in_=Md[:, :], func=AF.Exp),
                         r=[Mdb], w=[Esb])
                    MT, MTb = S.MT.next()
                    s.op("dve", lambda: nc.vector.tensor_tensor(MT[:, :, :], Es[:, :, :],
                                                                CBm[:, g:g + 1, :].to_broadcast([128, 4, 128]), ALU.mult),
                         r=[Esb, CBmb], w=[MTb])
                    for e in range(e0, e0 + 4):
                        cs = slice((e - e0) * 64, (e - e0) * 64 + 64)
                        cs2 = slice(256 + (e - e0) * 64, 256 + (e - e0) * 64 + 64)
                        s.op("pe", lambda: nc.tensor.matmul(YTp[:, cs], MT[:, e - e0, :], xdt[:, e, :], start=True, stop=True),
                             r=[MTb, xdtb], w=[YTpb])
                        s.op("pe", lambda: nc.tensor.matmul(YTp[:, cs2], Btok[:, g, :], xdts[:, e, :], start=True, stop=True),
                             r=[Btokb, xdtsb], w=[YTpb])
                    s.op("act", lambda: nc.scalar.copy(YT[:, :, e0:e0 + 4, :], YTp[:, :].rearrange("p (a e q) -> p a e q", a=2, e=4)),
                         r=[YTpb], w=[YTb])

        def back(S, item):
            kind, idx, tok0, ch, nchunk = item
            t0 = tok0 + ch * 128
            sidx = idx if kind == "p" else NB + idx
            Mr = S.Mr
            H, Hb_ = S.H
            Hbc, Hbcb = S.Hb
            acs, acsb = S.acs
            BCT, BCTb = S.BCT
            YT, YTb = S.YT
            Fp, Fpb = S.Fp
            eacs, cdec = acs[:, 2, :], acs[:, 4, :]
            if ch == 0:
                if kind == "p":
                    s.op("dve", lambda: nc.vector.memset(H[:, :, :], 0.0), w=[Hb_])
                else:
                    s.dma("sp", H[:, :, :], state_d[idx].rearrange("n (e p) -> n e p", e=16), w=[Hb_])
                s.op("act", lambda: nc.scalar.copy(Hbc[:, :, :], H[:, :, :]), r=[Hb_], w=[Hbcb])
            yt, ytb = S.yt
            for g in range(2):
                hs = slice(g * 8, g * 8 + 8)
                for e in range(g * 8, g * 8 + 8):
                    cs = slice((e % 8) * 64, (e % 8) * 64 + 64)
                    s.op("pe", lambda: nc.tensor.matmul(Fp[:, cs], BCT[:, 2 + g, :], Hbc[:, e, :], start=True, stop=True),
                         r=[BCTb, Hbcb], w=[Fpb])
                fv = Fp[:, :].rearrange("p (e q) -> p e q", e=8)
                s.op("dve", lambda: nc.vector.tensor_tensor(yt[:, hs, :], fv, eacs[:, hs].unsqueeze(2).to_broadcast([128, 8, 64]), ALU.mult),
                     r=[Fpb, acsb], w=[ytb])
                s.op("dve", lambda: nc.vector.tensor_tensor(H[:, hs, :], H[:, hs, :], cdec[:, hs].unsqueeze(2).to_broadcast([128, 8, 64]), ALU.mult),
                     r=[Hb_, acsb], w=[Hb_])
                s.op("dve", lambda: nc.vector.tensor_tensor(H[:, hs, :], H[:, hs, :], YT[:, 1, hs, :], ALU.add), r=[Hb_, YTb], w=[Hb_])
            s.op("act", lambda: nc.scalar.copy(Hbc[:, :, :], H[:, :, :]), r=[Hb_], w=[Hbcb])
            s.op("pool", lambda: nc.gpsimd.tensor_tensor(yt[:, :, :], yt[:, :, :], YT[:, 0, :, :], ALU.add), r=[ytb, YTb], w=[ytb])
            s.op("pool", lambda: nc.gpsimd.tensor_tensor(yt[:, :, :], yt[:, :, :], S.dsk[0][:, :, :], ALU.add), r=[ytb, S.dsk[1]], w=[ytb])
            zt, ztb = S.zt
            s.op("act", lambda: nc.scalar.activation(out=zt[:, :], in_=zt[:, :], func=AF.Silu), r=[ztb], w=[ztb])
            ytf = yt[:, :, :].rearrange("p e q -> p (e q)")
            s.op("dve", lambda: nc.vector.tensor_tensor(ytf, ytf, zt[:, :], ALU.mult), r=[ytb, ztb], w=[ytb])
            yn, ynb = S.yn
            jk, jkb = S.junk
            for g in range(2):
                gs = slice(g * 512, (g + 1) * 512)
                ss, ssb = S.stat.next()
                s.op("act", lambda: nc.scalar.activation(out=jk[:, 0:512], in_=ytf[:, gs], func=AF.Square, accum_out=ss[:, 0:1]),
                     r=[ytb], w=[jkb, ssb])
                rstd_from_ss(c, ss, ssb, 512)
                s.op("dve", lambda: nc.vector.scalar_tensor_tensor(yn[:, gs], ytf[:, gs], ss[:, 2:3], gss[:, gs], ALU.mult, ALU.mult),
                     r=[ytb, ssb, gssb], w=[ynb])
            yT, yTb = S.yT
            for g in range(2):
                m, mb = Mr.next()
                for j in range(g * 4, g * 4 + 4):
                    s.op("pe", lambda: nc.tensor.transpose(m[:, (j % 4) * 128:(j % 4 + 1) * 128], yn[:, j * 128:(j + 1) * 128], c.ident_f[:, :]),
                         r=[ynb, c.constb], w=[mb])
                evac(c, yT[:, g * 4:(g + 1) * 4, :], m[:, :].rearrange("p (j t) -> p j t", j=4), [mb], [yTb])
            if kind == "p":
                s.dma("sp", mixedT[:, 4:12, t0:t0 + 128], yT[:, :, :], r=[yTb])
            else:
                s.dma("sp", mixedT[:, 4:12, t0:t0 + 1], yT[:, :, 0:1], r=[yTb], slow=True)
            if ch == nchunk - 1:
                s.dma("sp", ssm_out[sidx].rearrange("n (e p) -> n e p", e=16), H[:, :, :], r=[Hb_])

        streams = [make_stream(k) for k in range(NSTREAM)]
        work = [[] for _ in range(NSTREAM)]
        for b in range(NB):
            for ch in range(16):
                work[b % NSTREAM].append(("p", b, b * 2048, ch, 16))
        for i in range(16):
            work[(i + NB) % NSTREAM].append(("s", i, NTP + i, 0, 1))

        def runner(k):
            def run():
                for item in work[k]:
                    front(streams[k], item)
                    back(streams[k], item)
            return run
        Interleave(s, [runner(k) for k in range(NSTREAM)]).run()


def outcross_stage(c, x1, mixedT, gcol, wout_d, wcq_d, wco_d, memKT, memV, cache_mk, cache_mv, x3, tiles, NTP):
    s, nc = c.s, c.nc
    with Stage(c) as st:
        st.wstage = st.ring("wst", [128, 1024], F32, 3)
        wO, wOb = st.sb("wO", [128, 12, D], BF16)
        wQ, wQb = st.sb("wQ", [128, 8, D], BF16)
        wC, wCb = st.sb("wCo", [128, 8, D], BF16)
        xtr = st.ring("xt", [128, 4, 1024], F32, 2)
        mTr = st.ring("mT", [128, 12, 512], BF16, 1)
        hTr = st.ring("hT", [128, 8, 512], BF16, 2)
        qxr = st.ring("qx", [128, 8, 512], BF16, 1)
        oTr = st.ring("oTn", [128, 8, 512], BF16, 1)
        ptr = st.ring("PT", [128, 512], BF16, 4)
        rlr = st.ring("rl", [128, 512], F32, 2)
        ktr = st.ring("KTm", [128, 8, 256], BF16, 2)
        vmr = st.ring("Vm", [128, 2, 1024], BF16, 2)
        ckr = st.ring("ck", [128, 2, 1024], F32, 2)
        cbr = st.ring("ckb", [128, 2, 1024], BF16, 1)
        norm_bufs(st)
        psr = st.psring("ps", [128, 512], F32, 6)
        load_weight(c, st, wout_d, wO, wOb, 12, D)
        load_weight(c, st, wcq_d, wQ, wQb, 8, D)
        load_weight(c, st, wco_d, wC, wCb, 8, D)

        def cross_core(KTm, KTmb, Vm, Vmb, qx, qxb, oT, oTb, cols, n):
            for h in range(4):
                PTs = []
                for mb in range(2):
                    ps, psb = psr.next()
                    for dc in range(2):
                        s.op("pe", lambda: nc.tensor.matmul(ps[:, 0:n], KTm[:, 2 * h + dc, mb * 128:(mb + 1) * 128],
                                                            qx[:, 2 * h + dc, cols], start=(dc == 0), stop=(dc == 1)),
                             r=[KTmb, qxb], w=[psb])
                    PT, PTb = ptr.next()
                    s.op("act", lambda: nc.scalar.activation(out=PT[:, 0:n], in_=ps[:, 0:n], func=AF.Exp), r=[psb], w=[PTb])
                    PTs.append((PT, PTb))
                ps, psb = psr.next()
                for mb in range(2):
                    s.op("pe", lambda: nc.tensor.matmul(ps[:, 0:n], c.ones_bf[:, :], PTs[mb][0][:, 0:n],
                                                        start=(mb == 0), stop=(mb == 1)), r=[c.constb, PTs[mb][1]], w=[psb])
                rl, rlb = rlr.next()
                s.op("dve", lambda: nc.vector.reciprocal(rl[:, 0:n], ps[:, 0:n]), r=[psb], w=[rlb])
                for dc in range(2):
                    ps, psb = psr.next()
                    for mb in range(2):
                        s.op("pe", lambda: nc.tensor.matmul(ps[:, 0:n], Vm[:, mb, (2 * h + dc) * 128:(2 * h + dc + 1) * 128],
                                                            PTs[mb][0][:, 0:n], start=(mb == 0), stop=(mb == 1)),
                             r=[Vmb, PTs[mb][1]], w=[psb])
                    s.op("dve", lambda: nc.vector.tensor_tensor(oT[:, 2 * h + dc, cols], ps[:, 0:n], rl[:, 0:n], ALU.mult),
                         r=[psb, rlb], w=[oTb])

        cur_b = -1
        KTm = Vm = None
        for (t0, nsub) in tiles:
            N = nsub * 128
            xt, xtb = xtr.next()
            s.dma("sp", xt[:, 0:nsub, :], x1[t0:t0 + N, :].rearrange("(s p) d -> p s d", p=128), w=[xtb])
            mT, mTb = mTr.next()
            s.dma("sp", mT[:, :, 0:N], mixedT[:, :, t0:t0 + N], w=[mTb])
            for sub in range(nsub):
                for hf in range(2):
                    ps, psb = psr.next()
                    mm_tm(c, mT, mTb, sub, wO, wOb, hf * 512, 512, ps, psb, KC=12)
                    xs_ = xt[:, sub, hf * 512:(hf + 1) * 512]
                    s.op("dve", lambda: nc.vector.tensor_tensor(xs_, xs_, ps[:, :], ALU.add), r=[psb, xtb], w=[xtb])
            hT, hTb = hTr.next()
            norm_transpose(c, st, xt, xtb, nsub, gcol, hT, hTb)
            qx, qxb = qxr.next()
            for j in range(8):
                ps, psb = psr.next()
                mm_fm(c, wQ, wQb, j * 128, hT, hTb, N, ps, psb)
                evac(c, qx[:, j, 0:N], ps[:, 0:N], [psb], [qxb], scale=1.0 / 16.0)
            oT, oTb = oTr.next()
            if t0 < NTP:
                b = t0 // 2048
                if b != cur_b:
                    cur_b = b
                    KTm, KTmb = ktr.next()
                    Vm, Vmb = vmr.next()
                    s.dma("sp", KTm[:, :, :], memKT[:, :, b * 256:(b + 1) * 256], w=[KTmb])
                    s.dma("sp", Vm[:, :, :], memV[b * 256:(b + 1) * 256, :].rearrange("(mb m) d -> m mb d", m=128), w=[Vmb])
                cross_core(KTm, KTmb, Vm, Vmb, qx, qxb, oT, oTb, slice(0, N), N)
            else:
                s.op("pool", lambda: nc.gpsimd.memset(oT[:, :, 0:N], 0.0), w=[oTb])
                for i in range(16):
                    ck, ckb_ = ckr.next()
                    s.dma("sp", ck[:, :, :], cache_mk[i].rearrange("(mb m) d -> m mb d", m=128), w=[ckb_])
                    cb_, cbb_ = cbr.next()
                    s.op("pool", lambda: nc.gpsimd.tensor_copy(cb_[:, :, :], ck[:, :, :]), r=[ckb_], w=[cbb_])
                    KTs, KTsb = ktr.next()
                    for mb in range(2):
                        pt, ptb = st.pst.next()
                        for j in range(8):
                            s.op("pe", lambda: nc.tensor.transpose(pt[:, j, :], cb_[:, mb, j * 128:(j + 1) * 128], c.ident_bf[:, :]),
                                 r=[cbb_, c.constb], w=[ptb])
                        evac(c, KTs[:, :, mb * 128:(mb + 1) * 128], pt[:, :, :], [ptb], [KTsb])
                    cv, cvb_ = ckr.next()
                    s.dma("sp", cv[:, :, :], cache_mv[i].rearrange("(mb m) d -> m mb d", m=128), w=[cvb_])
                    Vs, Vsb = vmr.next()
                    s.op("pool", lambda: nc.gpsimd.tensor_copy(Vs[:, :, :], cv[:, :, :]), r=[cvb_], w=[Vsb])
                    cross_core(KTs, KTsb, Vs, Vsb, qx, qxb, oT, oTb, slice(i, i + 1), 1)
            for sub in range(nsub):
                for hf in range(2):
                    ps, psb = psr.next()
                    mm_tm(c, oT, oTb, sub, wC, wCb, hf * 512, 512, ps, psb)
                    xs_ = xt[:, sub, hf * 512:(hf + 1) * 512]
                    s.op("dve", lambda: nc.vector.tensor_tensor(xs_, xs_, ps[:, :], ALU.add), r=[psb, xtb], w=[xtb])
            s.dma("pool", x3[t0:t0 + N, :].rearrange("(s p) d -> p s d", p=128), xt[:, 0:nsub, :], r=[xtb])


ALL_STAGES = ("memkv", "ffn1", "inproj", "attn", "ssd", "outcross", "ffn2")


def build(NB=4, stages=ALL_STAGES, dbg=()):
    nc = bass.Bass("TRN2", target_bir_lowering=False)
    c = Ctx()
    c.nc = nc
    c.s = Sch(nc)
    s = c.s
    NTP = NB * 2048
    NTOK = NTP + 128
    NBM = NB * 256
    NSEQ = NB + 16
    tiles = [(t * 512, 4) for t in range(NTP // 512)] + [(NTP, 1)]
    pb, bmask_np, negm_np = bias_consts()
    NPB = len(pb)

    def din(name, shape):
        return nc.dram_tensor(name, shape, F32, kind="ExternalInput").ap()

    def dout(name, shape):
        return nc.dram_tensor(name, shape, F32, kind="ExternalOutput").ap()

    def scr(name, shape, dt=F32):
        if name in dbg:
            return nc.dram_tensor(name, shape, dt, kind="ExternalOutput").ap()
        return nc.dram_tensor(name, shape, dt).ap()

    x_all = din("x_all", [NTOK, D])
    gcols_d = din("gcols", [128, 6, 8])
    gfin_d = din("gfin", [1, D])
    w1g, w1u, w1d = din("w1_gate", [128, 8, DFF]), din("w1_up", [128, 8, DFF]), din("w1_down", [128, 22, D])
    w2g, w2u, w2d = din("w2_gate", [128, 8, DFF]), din("w2_up", [128, 8, DFF]), din("w2_down", [128, 22, D])
    win_d = din("w_in", [128, 8, DIN])
    wout_d = din("w_out", [128, 12, D])
    wck_d, wcv_d, wcq_d, wco_d = (din(n, [128, 8, D]) for n in ("w_ck", "w_cv", "w_cq", "w_co"))
    mem_in = din("mem_in", [NBM, D])
    relb_d = din("rel_bias", [1, 256])
    bmask_d = din("bmask", [NPB, 128, 256])
    negm_d = din("negm", [3, 128, 256])
    convw_d = din("conv_w", [128, 12, 4])
    convb_d = din("conv_b", [128, 12])
    vec16_d = din("vec16", [1, 3, 16])
    gssm_d = din("g_ssm", [1, D])
    tri_d = din("tri", [128, 2, 128])
    ident_d = din("ident", [128, 128])
    cache_k = din("cache_k", [16, 2048, 512])
    cache_v = din("cache_v", [16, 2048, 512])
    cconv_d = din("cconv", [16, 128, 12, 3])
    state_d = din("state", [16, 128, 1024])
    cache_mk = din("cache_mk", [16, 256, D])
    cache_mv = din("cache_mv", [16, 256, D])

    y_out = dout("y_out", [NTOK, D])
    kT_out = dout("kT_out", [128, 4, NTOK])
    v_out = dout("v_out", [NTOK, 512])
    conv_out = dout("conv_out", [NSEQ, 128, 12, 3])
    ssm_out = dout("ssm_out", [NSEQ, 128, 1024])
    mk_out = dout("mk_out", [NBM, D])
    mv_out = dout("mv_out", [NBM, D])

    x1 = scr("x1", [NTOK, D])
    x3 = scr("x3", [NTOK, D])
    x4 = scr("x4", [NTOK, D])
    hT_scr = scr("hT_scr", [128, 8, NTOK], BF16)
    memKT = scr("memKT", [128, 8, NBM], BF16)
    memV = scr("memV", [NBM, D], BF16)
    qT = scr("qT", [128, 4, NTOK], BF16)
    kT = scr("kT", [128, 4, NTOK], BF16)
    v_bf = scr("v_bf", [NTOK, 512], BF16)
    z_scr = scr("z_scr", [NTOK, D])
    xbcT = scr("xbcT", [128, 12, NTOK])
    dt_scr = scr("dt_scr", [NTOK, 16])
    mixedT = scr("mixedT", [128, 12, NTOK], BF16)

    c.dbufs = {}

    def db(key):
        if key not in c.dbufs:
            c.dbufs[key] = Buf(str(key))
        return c.dbufs[key]
    c.db = db

    c.constb = Buf("const")
    c.ident_f = nc.alloc_sbuf_tensor("ident_f", [128, 128], F32)
    c.ident_bf = nc.alloc_sbuf_tensor("ident_bf", [128, 128], BF16)
    c.ones_bf = nc.alloc_sbuf_tensor("ones_bf", [128, 128], BF16)
    c.gcols = nc.alloc_sbuf_tensor("gcols_sb", [128, 6, 8], F32)
    c.gfin = nc.alloc_sbuf_tensor("gfin_sb", [128, D], F32)
    s.dma("sp", c.ident_f[:, :], ident_d[:, :], w=[c.constb])
    s.dma("sp", c.gcols[:, :, :], gcols_d[:, :, :], w=[c.constb])
    s.dma("sp", c.gfin[:, :], gfin_d[0:1, :].partition_broadcast(128), w=[c.constb])
    s.op("dve", lambda: nc.vector.tensor_copy(c.ident_bf[:, :], c.ident_f[:, :]), r=[c.constb], w=[c.constb])
    s.op("dve", lambda: nc.vector.memset(c.ones_bf[:, :], 1.0), w=[c.constb])
    s.barrier()

    if "memkv" in stages:
        memkv_stage(c, mem_in, c.gcols[:, 4, :], wck_d, wcv_d, mk_out, mv_out, memKT, memV, NBM)
    if "ffn1" in stages:
        ffn_stage(c, x_all, x1, c.gcols[:, 0, :], w1g, w1u, w1d, hT_scr, tiles, "f1")
    if "inproj" in stages:
        inproj_stage(c, x1, c.gcols[:, 1, :], win_d, qT, kT, kT_out, v_out, v_bf, z_scr, xbcT, dt_scr, tiles)
    if "attn" in stages:
        attn_stage(c, qT, kT, v_bf, mixedT, relb_d, bmask_d, negm_d, pb, cache_k, cache_v, NB, NTP)
    if "ssd" in stages:
        ssd_stage(c, xbcT, dt_scr, z_scr, mixedT, convw_d, convb_d, vec16_d, gssm_d, tri_d,
                  cconv_d, state_d, conv_out, ssm_out, NB, NTP)
    if "outcross" in stages:
        outcross_stage(c, x1, mixedT, c.gcols[:, 2, :], wout_d, wcq_d, wco_d, memKT, memV, cache_mk, cache_mv, x3, tiles, NTP)
    if "ffn2" in stages:
        ffn_stage(c, x3, x4, c.gcols[:, 3, :], w2g, w2u, w2d, hT_scr, tiles, "f2", final=(c.gfin[:, :], y_out))
    s.finish()
    return nc


def tile_w(w, kc):
    w = np.asarray(w, np.float32)
    K, F = w.shape
    return np.ascontiguousarray(w.reshape(kc, 128, F).transpose(1, 0, 2))


def gcol(g):
    return np.asarray(g, np.float32).reshape(8, 128).T


def tri_consts():
    j = np.arange(128)[:, None]
    l = np.arange(128)[None, :]
    return np.ascontiguousarray(np.stack([(j <= l), (j > l)], axis=1).astype(np.float32))


def shared_inputs(inp):
    f = lambda k: np.asarray(inp[k], np.float32)
    pb, bmask, negm = bias_consts()
    gc = np.zeros((128, 6, 8), np.float32)
    for n, k in enumerate(("g_ffn1", "g_mix", "g_cross", "g_ffn2", "g_mem")):
        gc[:, n, :] = gcol(f(k)[0])
    cw = f("conv_w")[0]
    sh = {
        "gcols": gc, "gfin": f("g_final").reshape(1, D),
        "w1_gate": tile_w(f("w1_gate")[0], 8), "w1_up": tile_w(f("w1_up")[0], 8), "w1_down": tile_w(f("w1_down")[0], 22),
        "w2_gate": tile_w(f("w2_gate")[0], 8), "w2_up": tile_w(f("w2_up")[0], 8), "w2_down": tile_w(f("w2_down")[0], 22),
        "w_in": tile_w(f("w_in")[0], 8), "w_out": tile_w(f("w_out")[0], 12),
        "w_ck": tile_w(f("w_ck")[0], 8), "w_cv": tile_w(f("w_cv")[0], 8),
        "w_cq": tile_w(f("w_cq")[0], 8), "w_co": tile_w(f("w_co")[0], 8),
        "rel_bias": f("rel_bias").reshape(1, 256),
        "bmask": bmask, "negm": negm,
        "conv_w": np.ascontiguousarray(cw.reshape(4, 12, 128).transpose(2, 1, 0)),
        "conv_b": np.ascontiguousarray(f("conv_b")[0].reshape(12, 128).T),
        "vec16": np.stack([f("dt_bias")[0], f("a_log")[0], f("d_skip")[0]])[None],
        "g_ssm": f("g_ssm").reshape(1, D),
        "tri": tri_consts(), "ident": np.eye(128, dtype=np.float32),
    }
    return sh


def core_inputs(inp, core, NB):
    f = lambda k: np.asarray(inp[k], np.float32)
    xp = f("x_prompt")[core * NB:(core + 1) * NB].reshape(NB * 2048, D)
    xs = f("x_sample")[core * 16:(core + 1) * 16].reshape(16, D)
    x_all = np.concatenate([xp, xs, np.zeros((112, D), np.float32)], 0)
    sl = slice(core * 16, (core + 1) * 16)
    cc = f("cache_conv")[0, sl]
    st = f("state_ssm")[0, sl]
    return {
        "x_all": x_all,
        "mem_in": f("mem_prompt")[core * NB:(core + 1) * NB].reshape(NB * 256, D),
        "cache_k": f("cache_win_k")[0, sl].reshape(16, 2048, 512),
        "cache_v": f("cache_win_v")[0, sl].reshape(16, 2048, 512),
        "cconv": np.ascontiguousarray(cc.reshape(16, 3, 12, 128).transpose(0, 3, 2, 1)),
        "state": np.ascontiguousarray(st.reshape(16, 1024, 128).transpose(0, 2, 1)),
        "cache_mk": f("cache_mem_k")[0, sl].reshape(16, 256, D),
        "cache_mv": f("cache_mem_v")[0, sl].reshape(16, 256, D),
    }


def assemble(results, NB, ncores):
    NTP = NB * 2048
    B = NB * ncores
    y_p = np.empty((B, 2048, D), np.float32)
    y_s = np.empty((16 * ncores, 1, D), np.float32)
    wk_p = np.empty((1, B, 2048, 8, 64), np.float32)
    wv_p = np.empty((1, B, 2048, 8, 64), np.float32)
    cv_p = np.empty((1, B, 3, 1536), np.float32)
    ss_p = np.empty((1, B, 16, 64, 128), np.float32)
    mk_p = np.empty((1, B, 256, 4, 256), np.float32)
    mv_p = np.empty((1, B, 256, 4, 256), np.float32)
    wk_s = np.empty((1, 16 * ncores, 1, 8, 64), np.float32)
    wv_s = np.empty((1, 16 * ncores, 1, 8, 64), np.float32)
    cv_s = np.empty((1, 16 * ncores, 3, 1536), np.float32)
    ss_s = np.empty((1, 16 * ncores, 16, 64, 128), np.float32)
    for cidx, r in enumerate(results):
        y = np.asarray(r["y_out"])
        ktok = np.asarray(r["kT_out"]).transpose(2, 1, 0).reshape(-1, 8, 64)
        vtok = np.asarray(r["v_out"]).reshape(-1, 8, 64)
        cv = np.asarray(r["conv_out"]).transpose(0, 3, 2, 1).reshape(-1, 3, 1536)
        ss = np.asarray(r["ssm_out"]).transpose(0, 2, 1).reshape(-1, 16, 64, 128)
        bs = slice(cidx * NB, (cidx + 1) * NB)
        ts = slice(cidx * 16, (cidx + 1) * 16)
        y_p[bs] = y[:NTP].reshape(NB, 2048, D)
        y_s[ts, 0] = y[NTP:NTP + 16]
        wk_p[0, bs] = ktok[:NTP].reshape(NB, 2048, 8, 64)
        wv_p[0, bs] = vtok[:NTP].reshape(NB, 2048, 8, 64)
        wk_s[0, ts, 0] = ktok[NTP:NTP + 16]
        wv_s[0, ts, 0] = vtok[NTP:NTP + 16]
        cv_p[0, bs] = cv[:NB]
        cv_s[0, ts] = cv[NB:]
        ss_p[0, bs] = ss[:NB]
        ss_s[0, ts] = ss[NB:]
        mk_p[0, bs] = np.asarray(r["mk_out"]).reshape(NB, 256, 4, 256)
        mv_p[0, bs] = np.asarray(r["mv_out"]).reshape(NB, 256, 4, 256)
    return (y_p, y_s, wk_p, wv_p, cv_p, ss_p, mk_p, mv_p, wk_s, wv_s, cv_s, ss_s)


def kernel(**inputs):
    NB = 4
    nc = build(NB=NB)
    sh = shared_inputs(inputs)
    in_maps = []
    for core in range(NCORES):
        m = dict(sh)
        m.update(core_inputs(inputs, core, NB))
        in_maps.append(m)
    res = run_bass_kernel_spmd(nc, in_maps, core_ids=list(range(NCORES)))
    return assemble(res.results, NB, NCORES)
```

```python
import numpy as np
import concourse.bass as bass
import concourse.mybir as mybir
from concourse.bass_utils import run_bass_kernel_spmd

F32 = mybir.dt.float32
BF16 = mybir.dt.bfloat16
AF = mybir.ActivationFunctionType
ALU = mybir.AluOpType
AX = mybir.AxisListType

D = 1024
DFF = 2816
NCORES = 8
EPS = 1e-6


class Buf:
    __slots__ = ("name", "w", "rs", "excl")

    def __init__(self, name="", excl=False):
        self.name = name
        self.w = None
        self.rs = {}
        self.excl = excl


class Sch:
    ND = 12

    def __init__(self, nc):
        self.nc = nc
        self.eng = {"pe": nc.tensor, "act": nc.scalar, "dve": nc.vector, "pool": nc.gpsimd, "sp": nc.sync}
        self.sem = {k: nc.alloc_semaphore("s_" + k) for k in self.eng}
        self.cnt = {k: 0 for k in self.eng}
        self.known = {k: {} for k in self.eng}
        self.dsem = {q: [[nc.alloc_semaphore(f"d_{q}{i}"), 0] for i in range(self.ND)]
                     for q in ("sp", "pool", "act")}
        self.dnext = {q: 0 for q in self.dsem}
        self.il = None

    def _deps(self, own, r, w):
        toks = []
        for b in r:
            if b.w is not None:
                toks.append(b.w)
        for b in w:
            if b.w is not None and b.w[0] is not own:
                toks.append(b.w)
            for sem, v in b.rs.items():
                if sem is not own:
                    toks.append((sem, v))
        return toks

    def _wait(self, e, toks):
        need = {}
        for sem, v in toks:
            if v > need.get(sem, 0):
                need[sem] = v
        kn = self.known[e]
        for sem, v in need.items():
            if kn.get(sem, 0) >= v:
                continue
            self.eng[e].wait_ge(sem, v)
            kn[sem] = v

    def _mark(self, tok, r, w):
        for b in r:
            if b.rs.get(tok[0], 0) < tok[1]:
                b.rs[tok[0]] = tok[1]
        for b in w:
            b.w = tok
            b.rs = {}

    def op(self, e, ins_fn, r=(), w=()):
        own = self.sem[e]
        if any(b.excl for b in r):
            w = list(w) + [b for b in r if b.excl]
            r = [b for b in r if not b.excl]
        self._wait(e, self._deps(own, r, w))
        ins = ins_fn()
        self.cnt[e] += 1
        ins.then_inc(own, 1)
        self._mark((own, self.cnt[e]), r, w)
        if self.il is not None:
            self.il.switch()
        return ins

    def dma(self, q, out, in_, r=(), w=(), slow=False):
        pool = self.dsem[q]
        i = self.dnext[q]
        self.dnext[q] = (i + 1) % len(pool)
        slot = pool[i]
        toks = self._deps(None, r, w)
        if slot[1] > 0:
            toks.append((slot[0], slot[1]))
        self._wait(q, toks)
        ins = (self.eng[q].dma_start(out=out, in_=in_, allow_slow_non_contiguous=True) if slow
               else self.eng[q].dma_start(out=out, in_=in_))
        slot[1] += 16
        ins.then_inc(slot[0], 16)
        self._mark((slot[0], slot[1]), r, w)
        if self.il is not None:
            self.il.switch()
        return ins

    def finish(self):
        toks = []
        for q in self.dsem:
            for sem, v in self.dsem[q]:
                if v > 0:
                    toks.append((sem, v))
        for k in self.eng:
            if self.cnt[k] > 0:
                toks.append((self.sem[k], self.cnt[k]))
        self._wait("sp", toks)


class Ring:
    def __init__(self, items):
        self.items = items
        self.i = 0

    def next(self):
        it = self.items[self.i]
        self.i = (self.i + 1) % len(self.items)
        return it


class Ctx:
    pass


def sb(nc, name, shape, dt):
    t = nc.alloc_sbuf_tensor(name, shape, dt)
    return t, Buf(name)


def sb_ring(nc, name, shape, dt, n):
    return Ring([sb(nc, f"{name}{i}", shape, dt) for i in range(n)])


import threading


class Interleave:
    def __init__(self, sch, fns):
        self.sch = sch
        self.fns = fns
        self.ev = [threading.Event() for _ in fns]
        self.alive = [True] * len(fns)
        self.idx = {}
        self.exc = None

    def _next(self, i):
        n = len(self.fns)
        for d in range(1, n + 1):
            j = (i + d) % n
            if j != i and self.alive[j]:
                return j
        return None

    def _wrap(self, i):
        self.idx[threading.get_ident()] = i
        self.ev[i].wait()
        self.ev[i].clear()
        try:
            if self.exc is None:
                self.fns[i]()
        except BaseException as e:
            self.exc = e
        finally:
            self.alive[i] = False
            j = self._next(i)
            if j is not None:
                self.ev[j].set()

    def switch(self):
        i = self.idx.get(threading.get_ident())
        if i is None:
            return
        if self.exc is not None:
            raise RuntimeError("sibling stream failed")
        j = self._next(i)
        if j is None:
            return
        self.ev[j].set()
        self.ev[i].wait()
        self.ev[i].clear()

    def run(self):
        ths = [threading.Thread(target=self._wrap, args=(i,)) for i in range(len(self.fns))]
        self.sch.il = self
        for t in ths:
            t.start()
        self.ev[0].set()
        for t in ths:
            t.join()
        self.sch.il = None
        if self.exc is not None:
            raise self.exc
from contextlib import ExitStack

DIN = 4112
PATTERNS = ((128, 1), (512, 4), (2048, 16))
NEG = -30000.0


def _barrier(self):
    toks = []
    for q in self.dsem:
        for sem, v in self.dsem[q]:
            if v > 0:
                toks.append((sem, v))
    for k in self.eng:
        if self.cnt[k] > 0:
            toks.append((self.sem[k], self.cnt[k]))
    for e in self.eng:
        self._wait(e, [t for t in toks if t[0] is not self.sem[e]])


Sch.barrier = _barrier


class Stage:
    _n = 0

    def __init__(self, c):
        self.c = c
        self.es = ExitStack()

    def __enter__(self):
        self.es.__enter__()
        return self

    def __exit__(self, *a):
        if a[0] is None:
            self.c.s.barrier()
        return self.es.__exit__(*a)

    def sb(self, name, shape, dt):
        Stage._n += 1
        t = self.es.enter_context(self.c.nc.sbuf_tensor(f"{name}_{Stage._n}", shape, dt))
        return t, Buf(name)

    def ring(self, name, shape, dt, n):
        return Ring([self.sb(f"{name}{i}", shape, dt) for i in range(n)])

    def ps(self, name, shape, dt=F32):
        Stage._n += 1
        t = self.es.enter_context(self.c.nc.psum_tensor(f"{name}_{Stage._n}", shape, dt))
        return t, Buf(name, excl=True)

    def psring(self, name, shape, dt, n):
        return Ring([self.ps(f"{name}{i}", shape, dt) for i in range(n)])


def load_weight(c, st, dram_w, dst, dst_buf, kc_n, f_n, piece=1024):
    s, nc = c.s, c.nc
    for kc in range(kc_n):
        for f0 in range(0, f_n, piece):
            fw = min(piece, f_n - f0)
            stg, stb = st.wstage.next()
            s.dma("sp", stg[:, 0:fw], dram_w[:, kc, f0:f0 + fw], w=[stb])
            s.op("pool", lambda: nc.gpsimd.tensor_copy(dst[:, kc, f0:f0 + fw], stg[:, 0:fw]),
                 r=[stb], w=[dst_buf])


def rstd_from_ss(c, ss, ssb, n):
    s, nc = c.s, c.nc
    s.op("dve", lambda: nc.vector.tensor_scalar(ss[:, 1:2], ss[:, 0:1], 1.0 / n, EPS, ALU.mult, ALU.add),
         r=[ssb], w=[ssb])
    s.op("act", lambda: nc.scalar.activation(out=ss[:, 3:4], in_=ss[:, 1:2], func=AF.Sqrt), r=[ssb], w=[ssb])
    s.op("dve", lambda: nc.vector.reciprocal(ss[:, 2:3], ss[:, 3:4]), r=[ssb], w=[ssb])


def norm_part(c, st, xt, xtb, nsub):
    s, nc = c.s, c.nc
    xns = []
    for sub in range(nsub):
        ss, ssb = st.stat.next()
        jk, jkb = st.junk.next()
        s.op("act", lambda: nc.scalar.activation(out=jk[:, :], in_=xt[:, sub, :], func=AF.Square,
                                                 accum_out=ss[:, 0:1]), r=[xtb], w=[jkb, ssb])
        rstd_from_ss(c, ss, ssb, D)
        xn, xnb = st.xn.next()
        s.op("act", lambda: nc.scalar.activation(out=xn[:, :], in_=xt[:, sub, :], func=AF.Copy,
                                                 scale=ss[:, 2:3]), r=[xtb, ssb], w=[xnb])
        xns.append((xn, xnb))
    return xns


def transpose_part(c, st, xns, gcol, hT, hTb):
    s, nc = c.s, c.nc
    for sub, (xn, xnb) in enumerate(xns):
        pt, ptb = st.pst.next()
        for kc in range(8):
            s.op("pe", lambda: nc.tensor.transpose(pt[:, kc, :], xn[:, kc * 128:(kc + 1) * 128], c.ident_bf[:, :]),
                 r=[xnb, c.constb], w=[ptb])
        s.op("dve", lambda: nc.vector.tensor_tensor(
            hT[:, :, sub * 128:(sub + 1) * 128], pt[:, :, :],
            gcol.unsqueeze(2).to_broadcast([128, 8, 128]), ALU.mult),
            r=[ptb, c.constb], w=[hTb])


def norm_transpose(c, st, xt, xtb, nsub, gcol, hT, hTb):
    transpose_part(c, st, norm_part(c, st, xt, xtb, nsub), gcol, hT, hTb)


def norm_bufs(st):
    st.stat = st.ring("stat", [128, 4], F32, 8)
    st.junk = st.ring("junk", [128, 1024], BF16, 2)
    st.xn = st.ring("xn", [128, 1024], BF16, 4)
    st.pst = st.psring("pst", [128, 8, 128], BF16, 2)


def mm_fm(c, W, Wb, f0, hT, hTb, N, ps, psb, KC=8):
    s, nc = c.s, c.nc
    for kc in range(KC):
        s.op("pe", lambda: nc.tensor.matmul(ps[:, 0:N], W[:, kc, f0:f0 + 128], hT[:, kc, 0:N],
                                            start=(kc == 0), stop=(kc == KC - 1)), r=[Wb, hTb], w=[psb])


def mm_tm(c, hT, hTb, sub, W, Wb, c0, ncols, ps, psb, KC=8):
    s, nc = c.s, c.nc
    for kc in range(KC):
        s.op("pe", lambda: nc.tensor.matmul(ps[:, 0:ncols], hT[:, kc, sub * 128:(sub + 1) * 128], W[:, kc, c0:c0 + ncols],
                                            start=(kc == 0), stop=(kc == KC - 1)), r=[Wb, hTb], w=[psb])


def evac(c, out, in_, r, w, scale=None):
    s, nc = c.s, c.nc
    c.flip = not getattr(c, "flip", False)
    if c.flip:
        if scale is None:
            s.op("act", lambda: nc.scalar.copy(out, in_), r=r, w=w)
        else:
            s.op("act", lambda: nc.scalar.mul(out, in_, scale), r=r, w=w)
    else:
        if scale is None:
            s.op("dve", lambda: nc.vector.tensor_copy(out, in_), r=r, w=w)
        else:
            s.op("dve", lambda: nc.vector.tensor_scalar(out, in_, scale, None, ALU.mult), r=r, w=w)


def ffn_stage(c, x_in, x_out, gcol, wg_d, wu_d, wd_d, hT_scr, tiles, tag, final=None):
    s, nc = c.s, c.nc
    H = DFF // 2
    HC = H // 128
    with Stage(c) as st:
        st.wstage = st.ring("wst", [128, 1024], F32, 6)
        wA, wAb = st.sb("wA", [128, 8, H], BF16)
        wB, wBb = st.sb("wB", [128, 8, H], BF16)
        wC, wCb = st.sb("wC", [128, HC, D], BF16)
        xtr = st.ring("xt", [128, 4, 1024], F32, 3)
        hTr = st.ring("hT", [128, 8, 512], BF16, 2)
        actr = st.ring("actT", [128, HC, 512], BF16, 2)
        sgr = st.ring("sg", [128, 512], F32, 2)
        norm_bufs(st)
        psr = st.psring("ps", [128, 512], F32, 6)
        for half in range(2):
            load_weight(c, st, wg_d[:, :, half * H:(half + 1) * H], wA, wAb, 8, H)
            load_weight(c, st, wu_d[:, :, half * H:(half + 1) * H], wB, wBb, 8, H)
            load_weight(c, st, wd_d[:, half * HC:(half + 1) * HC, :], wC, wCb, HC, D)
            src = x_in if half == 0 else x_out
            def prep1(tile):
                t0, nsub = tile
                N = nsub * 128
                xt, xtb = xtr.next()
                xob = c.db((tag, "xo", t0))
                rd = [xob] if half == 1 else [c.db((tag, "xi", t0))]
                s.dma("sp", xt[:, 0:nsub, :], src[t0:t0 + N, :].rearrange("(s p) d -> p s d", p=128), r=rd, w=[xtb])
                hT, hTb = hTr.next()
                xns = None
                if half == 0:
                    xns = norm_part(c, st, xt, xtb, nsub)
                else:
                    s.dma("sp", hT[:, :, 0:N], hT_scr[:, :, t0:t0 + N], r=[c.db((tag, "hs", t0))], w=[hTb])
                return (xt, xtb, hT, hTb, xob, xns, t0, N)

            def prep2(P):
                xt, xtb, hT, hTb, xob, xns, t0, N = P
                if half == 0:
                    transpose_part(c, st, xns, gcol, hT, hTb)
                    s.dma("pool", hT_scr[:, :, t0:t0 + N], hT[:, :, 0:N], r=[hTb], w=[c.db((tag, "hs", t0))])

            nxt = prep1(tiles[0])
            prep2(nxt)
            for ti, (t0, nsub) in enumerate(tiles):
                N = nsub * 128
                xt, xtb, hT, hTb, xob = nxt[0:5]
                act, actb = actr.next()
                for j in range(HC):
                    if j == min(3, HC - 1) and ti + 1 < len(tiles):
                        nxt = prep1(tiles[ti + 1])
                    pg, pgb = psr.next()
                    mm_fm(c, wA, wAb, j * 128, hT, hTb, N, pg, pgb)
                    pu, pub = psr.next()
                    mm_fm(c, wB, wBb, j * 128, hT, hTb, N, pu, pub)
                    sg, sgb = sgr.next()
                    s.op("act", lambda: nc.scalar.activation(out=sg[:, 0:N], in_=pg[:, 0:N], func=AF.Silu), r=[pgb], w=[sgb])
                    s.op("dve", lambda: nc.vector.tensor_tensor(act[:, j, 0:N], sg[:, 0:N], pu[:, 0:N], ALU.mult),
                         r=[sgb, pub], w=[actb])
                if ti + 1 < len(tiles):
                    prep2(nxt)
                for sub in range(nsub):
                    for hf in range(2):
                        pd, pdb = psr.next()
                        mm_tm(c, act, actb, sub, wC, wCb, hf * 512, 512, pd, pdb, KC=HC)
                        s.op("dve", lambda: nc.vector.scalar_tensor_tensor(
                            xt[:, sub, hf * 512:(hf + 1) * 512], pd[:, :], 0.5, xt[:, sub, hf * 512:(hf + 1) * 512],
                            ALU.mult, ALU.add), r=[pdb, xtb], w=[xtb])
                if half == 1 and final is not None:
                    gfin, y_out = final
                    for sub in range(nsub):
                        ss, ssb = st.stat.next()
                        jk, jkb = st.junk.next()
                        s.op("act", lambda: nc.scalar.activation(out=jk[:, :], in_=xt[:, sub, :], func=AF.Square,
                                                                 accum_out=ss[:, 0:1]), r=[xtb], w=[jkb, ssb])
                        rstd_from_ss(c, ss, ssb, D)
                        s.op("dve", lambda: nc.vector.scalar_tensor_tensor(
                            xt[:, sub, :], xt[:, sub, :], ss[:, 2:3], gfin, ALU.mult, ALU.mult),
                            r=[xtb, ssb, c.constb], w=[xtb])
                    s.dma("pool", y_out[t0:t0 + N, :].rearrange("(s p) d -> p s d", p=128), xt[:, 0:nsub, :], r=[xtb])
                else:
                    s.dma("pool", x_out[t0:t0 + N, :].rearrange("(s p) d -> p s d", p=128), xt[:, 0:nsub, :], r=[xtb], w=[xob])


def memkv_stage(c, mem_in, gcol, wck_d, wcv_d, mk_out, mv_out, memKT, memV, NBM):
    s, nc = c.s, c.nc
    with Stage(c) as st:
        st.wstage = st.ring("wst", [128, 1024], F32, 3)
        wK, wKb = st.sb("wK", [128, 8, D], BF16)
        wV, wVb = st.sb("wV", [128, 8, D], BF16)
        xtr = st.ring("xt", [128, 4, 1024], F32, 2)
        hTr = st.ring("hT", [128, 8, 512], BF16, 2)
        tmr = st.ring("tm", [128, 1024], F32, 3)
        tbr = st.ring("tb", [128, 1024], BF16, 2)
        fmr = st.ring("fm", [128, 512], BF16, 3)
        norm_bufs(st)
        psr = st.psring("ps", [128, 512], F32, 6)
        load_weight(c, st, wck_d, wK, wKb, 8, D)
        load_weight(c, st, wcv_d, wV, wVb, 8, D)
        tiles = []
        t0 = 0
        while t0 < NBM:
            n = min(4, (NBM - t0) // 128)
            tiles.append((t0, n))
            t0 += n * 128
        for (t0, nsub) in tiles:
            N = nsub * 128
            xt, xtb = xtr.next()
            s.dma("sp", xt[:, 0:nsub, :], mem_in[t0:t0 + N, :].rearrange("(s p) d -> p s d", p=128), w=[xtb])
            hT, hTb = hTr.next()
            norm_transpose(c, st, xt, xtb, nsub, gcol, hT, hTb)
            for sub in range(nsub):
                r0 = t0 + sub * 128
                for (W, Wb, outd, bfd) in ((wK, wKb, mk_out, None), (wV, wVb, mv_out, memV)):
                    tm, tmb = tmr.next()
                    for hf in range(2):
                        ps, psb = psr.next()
                        mm_tm(c, hT, hTb, sub, W, Wb, hf * 512, 512, ps, psb)
                        evac(c, tm[:, hf * 512:(hf + 1) * 512], ps[:, :], [psb], [tmb])
                    s.dma("pool", outd[r0:r0 + 128, :], tm[:, :], r=[tmb])
                    if bfd is not None:
                        tb, tbb = tbr.next()
                        s.op("pool", lambda: nc.gpsimd.tensor_copy(tb[:, :], tm[:, :]), r=[tmb], w=[tbb])
                        s.dma("pool", bfd[r0:r0 + 128, :], tb[:, :], r=[tbb])
            for j in range(8):
                ps, psb = psr.next()
                mm_fm(c, wK, wKb, j * 128, hT, hTb, N, ps, psb)
                fm, fmb = fmr.next()
                evac(c, fm[:, 0:N], ps[:, 0:N], [psb], [fmb])
                s.dma("pool", memKT[:, j, t0:t0 + N], fm[:, 0:N], r=[fmb])


def inproj_stage(c, x1, gcol, win_d, qT, kT, kT_out, v_out, v_bf, z_scr, xbcT, dt_scr, tiles):
    s, nc = c.s, c.nc
    with Stage(c) as st:
        st.wstage = st.ring("wst", [128, 1024], F32, 3)
        Wa, Wab = st.sb("winA", [128, 8, 2560], BF16)
        Wc, Wcb = st.sb("winB", [128, 8, DIN - 2560], BF16)
        xtr = st.ring("xt", [128, 4, 1024], F32, 3)
        hTr = st.ring("hT", [128, 8, 512], BF16, 2)
        f32r = st.ring("f32", [128, 512], F32, 6)
        bfr = st.ring("bf", [128, 512], BF16, 6)
        zr = st.ring("zt", [128, 1024], F32, 2)
        dtr = st.ring("dtt", [128, 16], F32, 3)
        norm_bufs(st)
        psr = st.psring("ps", [128, 512], F32, 6)
        load_weight(c, st, win_d[:, :, 0:2560], Wa, Wab, 8, 2560)
        load_weight(c, st, win_d[:, :, 2560:DIN], Wc, Wcb, 8, DIN - 2560)
        def prep1(tile):
            t0, nsub = tile
            N = nsub * 128
            xt, xtb = xtr.next()
            s.dma("sp", xt[:, 0:nsub, :], x1[t0:t0 + N, :].rearrange("(s p) d -> p s d", p=128), w=[xtb])
            hT, hTb = hTr.next()
            return (hT, hTb, norm_part(c, st, xt, xtb, nsub))

        nxt = prep1(tiles[0])
        transpose_part(c, st, nxt[2], gcol, nxt[0], nxt[1])
        for ti, (t0, nsub) in enumerate(tiles):
            N = nsub * 128
            hT, hTb = nxt[0], nxt[1]
            for j in range(20):
                if j == 3 and ti + 1 < len(tiles):
                    nxt = prep1(tiles[ti + 1])
                f0 = j * 128 if j < 8 else 2560 + (j - 8) * 128
                ps, psb = psr.next()
                if f0 < 2560:
                    mm_fm(c, Wa, Wab, f0, hT, hTb, N, ps, psb)
                else:
                    mm_fm(c, Wc, Wcb, f0 - 2560, hT, hTb, N, ps, psb)
                if j < 4:
                    o, ob = bfr.next()
                    evac(c, o[:, 0:N], ps[:, 0:N], [psb], [ob], scale=0.125)
                    s.dma("pool", qT[:, j, t0:t0 + N], o[:, 0:N], r=[ob])
                elif j < 8:
                    o, ob = bfr.next()
                    evac(c, o[:, 0:N], ps[:, 0:N], [psb], [ob])
                    s.dma("pool", kT[:, j - 4, t0:t0 + N], o[:, 0:N], r=[ob])
                    o2, o2b = f32r.next()
                    evac(c, o2[:, 0:N], ps[:, 0:N], [psb], [o2b])
                    s.dma("pool", kT_out[:, j - 4, t0:t0 + N], o2[:, 0:N], r=[o2b])
                else:
                    o2, o2b = f32r.next()
                    evac(c, o2[:, 0:N], ps[:, 0:N], [psb], [o2b])
                    s.dma("pool", xbcT[:, j - 8, t0:t0 + N], o2[:, 0:N], r=[o2b])
            if ti + 1 < len(tiles):
                transpose_part(c, st, nxt[2], gcol, nxt[0], nxt[1])
            for sub in range(nsub):
                r0 = t0 + sub * 128
                ps, psb = psr.next()
                mm_tm(c, hT, hTb, sub, Wa, Wab, 1024, 512, ps, psb)
                o2, o2b = f32r.next()
                evac(c, o2[:, :], ps[:, :], [psb], [o2b])
                s.dma("pool", v_out[r0:r0 + 128, :], o2[:, :], r=[o2b])
                o, ob = bfr.next()
                evac(c, o[:, :], ps[:, :], [psb], [ob])
                s.dma("pool", v_bf[r0:r0 + 128, :], o[:, :], r=[ob])
                zt, ztb = zr.next()
                for hf in range(2):
                    ps, psb = psr.next()
                    mm_tm(c, hT, hTb, sub, Wa, Wab, 1536 + hf * 512, 512, ps, psb)
                    evac(c, zt[:, hf * 512:(hf + 1) * 512], ps[:, :], [psb], [ztb])
                s.dma("pool", z_scr[r0:r0 + 128, :], zt[:, :], r=[ztb])
                ps, psb = psr.next()
                mm_tm(c, hT, hTb, sub, Wc, Wcb, 4096 - 2560, 16, ps, psb)
                dtt, dtb = dtr.next()
                evac(c, dtt[:, :], ps[:, 0:16], [psb], [dtb])
                s.dma("pool", dt_scr[r0:r0 + 128, :], dtt[:, :], r=[dtb])


def t5_bucket_np(dist):
    d = np.maximum(np.asarray(dist, np.int64), 0)
    df = np.maximum(d, 1).astype(np.float32)
    large = 16 + (np.log(df / np.float32(16.0)) / np.float32(np.log(2048.0 / 16.0)) * np.float32(16.0)).astype(np.int32)
    large = np.minimum(large, 31)
    return np.where(d < 16, d, large).astype(np.int64)


def bias_consts():
    k = np.arange(128)[:, None, None]
    kb = np.arange(2)[None, :, None]
    q = np.arange(128)[None, None, :]
    delta = q + 128 * kb - k
    valid = (delta >= 0) & (delta <= 128)
    pb, masks, negm = [], [], []
    for pi, (wnd, dil) in enumerate(PATTERNS):
        bk = t5_bucket_np(delta * dil)
        negm.append(np.where(valid, 0.0, NEG).astype(np.float32).reshape(128, 256))
        for b in range(32):
            m = (valid & (bk == b))
            if m.any():
                pb.append((pi, b))
                masks.append(m.astype(np.float32).reshape(128, 256))
    return pb, np.stack(masks), np.stack(negm)


def attn_stage(c, qT, kT, v_bf, mixedT, relb_d, bmask_d, negm_d, pb, cache_k, cache_v, NB, NTP):
    s, nc = c.s, c.nc
    with Stage(c) as st:
        BT, BTb = st.sb("BT", [128, 24, 256], F32)
        rb, rbb = st.sb("rb", [128, 256], F32)
        mr = st.ring("bm", [128, 256], F32, 3)
        s.dma("sp", rb[:, :], relb_d[0:1, :].partition_broadcast(128), w=[rbb])
        for pi in range(3):
            for h in range(8):
                s.dma("sp", BT[:, pi * 8 + h, :], negm_d[pi], w=[BTb])
        for n, (pi, b) in enumerate(pb):
            m, mb = mr.next()
            s.dma("sp", m[:, :], bmask_d[n], w=[mb])
            for h in range(8):
                s.op("dve", lambda: nc.vector.scalar_tensor_tensor(
                    BT[:, pi * 8 + h, :], m[:, :], rb[:, b * 8 + h:b * 8 + h + 1], BT[:, pi * 8 + h, :],
                    ALU.mult, ALU.add), r=[mb, rbb, BTb], w=[BTb])
        BT4 = BT[:, :, :].rearrange("p n (kb q) -> p n kb q", kb=2)
        BTh, BThb = st.sb("BTh", [128, 24, 256], BF16)
        BTl, BTlb = st.sb("BTl", [128, 24, 256], BF16)
        btmp, btmpb = st.sb("btmp", [128, 8, 256], F32)
        for pi in range(3):
            ps_ = slice(pi * 8, pi * 8 + 8)
            s.op("dve", lambda: nc.vector.tensor_copy(BTh[:, ps_, :], BT[:, ps_, :]), r=[BTb], w=[BThb])
            s.op("dve", lambda: nc.vector.tensor_tensor(btmp[:, :, :], BT[:, ps_, :], BTh[:, ps_, :], ALU.subtract),
                 r=[BTb, BThb], w=[btmpb])
            s.op("dve", lambda: nc.vector.tensor_copy(BTl[:, ps_, :], btmp[:, :, :]), r=[btmpb], w=[BTlb])

        qTb, qTbb = st.sb("qTb", [128, 4, 2048], BF16)
        kTb, kTbb = st.sb("kTb", [128, 4, 2048], BF16)
        acc, accb = st.sb("acc", [128, 2, 4, 2048], F32)
        Vr = st.ring("Vt", [128, 512], BF16, 5)
        sbr = st.ring("sbs", [128, 2, 128], F32, 4)
        ptr = st.ring("PT", [128, 2, 128], BF16, 4)
        rcr = st.ring("rc", [128, 1024], F32, 1)
        obr = st.ring("ob", [128, 1024], BF16, 2)
        psS = st.psring("psS", [128, 512], F32, 3)
        psO = st.psring("psO", [128, 512], F32, 3)
        for b in range(NB):
            tok0 = b * 2048
            s.dma("sp", qTb[:, :, :], qT[:, :, tok0:tok0 + 2048], w=[qTbb])
            s.dma("sp", kTb[:, :, :], kT[:, :, tok0:tok0 + 2048], w=[kTbb])
            units = []
            for pi, (wnd, dil) in enumerate(PATTERNS):
                nblk = 2048 // dil // 128
                for r in range(dil):
                    for blk in range(nblk):
                        for h in range(8):
                            units.append((pi, dil, r, blk, h))
            vstate = {}

            def phaseA(u):
                pi, dil, r, blk, h = u
                cols = slice(r + dil * 128 * blk, r + dil * 128 * blk + dil * 127 + 1, dil)
                pcols = slice(r + dil * 128 * (blk - 1), r + dil * 128 * (blk - 1) + dil * 127 + 1, dil)
                if h == 0:
                    Vt, Vtb = Vr.next()
                    row0 = tok0 + r + dil * 128 * blk
                    s.dma("sp", Vt[:, :], v_bf[row0:row0 + 127 * dil + 1:dil, :], w=[Vtb])
                    prev = vstate.get((pi, r, blk - 1))
                    vstate[(pi, r, blk)] = (Vt, Vtb)
                    vstate[("cur", pi, r, blk)] = [(Vt, Vtb)] + ([prev] if blk > 0 else [])
                nkb = 2 if blk > 0 else 1
                pair = h // 2
                rows = slice(64 * (h % 2), 64 * (h % 2) + 64)
                Sp, Spb = psS.next()
                S = Sp[:, 0:256].rearrange("p (kb q) -> p kb q", kb=2)
                s.op("pe", lambda: nc.tensor.matmul(Sp[:, 0:nkb * 128], c.ident_bf[:, :], BTh[:, pi * 8 + h, 0:nkb * 128],
                                                    start=True, stop=False), r=[c.constb, BThb], w=[Spb])
                s.op("pe", lambda: nc.tensor.matmul(Sp[:, 0:nkb * 128], c.ident_bf[:, :], BTl[:, pi * 8 + h, 0:nkb * 128],
                                                    start=False, stop=False), r=[c.constb, BTlb], w=[Spb])
                s.op("pe", lambda: nc.tensor.matmul(S[:, 0, :], kTb[rows, pair, cols], qTb[rows, pair, cols],
                                                    start=False, stop=(nkb == 1)), r=[kTbb, qTbb], w=[Spb])
                if blk > 0:
                    s.op("pe", lambda: nc.tensor.matmul(S[:, 1, :], kTb[rows, pair, pcols], qTb[rows, pair, cols],
                                                        start=False, stop=True), r=[kTbb, qTbb], w=[Spb])
                PT, PTb = ptr.next()
                s.op("act", lambda: nc.scalar.activation(out=PT[:, 0:nkb, :], in_=S[:, 0:nkb, :], func=AF.Exp),
                     r=[Spb], w=[PTb])
                return (PT, PTb, cols, nkb, vstate[("cur", pi, r, blk)])

            def phaseB(u, A):
                pi, dil, r, blk, h = u
                PT, PTb, cols, nkb, vs = A
                pair = h // 2
                rows = slice(64 * (h % 2), 64 * (h % 2) + 64)
                Op, Opb = psO.next()
                OL = Op[:, 0:256].rearrange("p (a q) -> p a q", a=2)
                for kb, (vt, vtb) in enumerate(vs):
                    s.op("pe", lambda: nc.tensor.matmul(OL[:, 0, :], vt[:, pair * 128:(pair + 1) * 128], PT[:, kb, :],
                                                        start=(kb == 0), stop=(kb == nkb - 1)), r=[vtb, PTb], w=[Opb])
                for kb in range(nkb):
                    s.op("pe", lambda: nc.tensor.matmul(OL[:, 1, :], c.ones_bf[:, :], PT[:, kb, :],
                                                        start=(kb == 0), stop=(kb == nkb - 1)), r=[c.constb, PTb], w=[Opb])
                dst = acc[rows, :, pair, cols]
                if pi == 0:
                    s.op("dve", lambda: nc.vector.tensor_copy(dst, OL[rows, :, :]), r=[Opb], w=[accb])
                else:
                    s.op("dve", lambda: nc.vector.tensor_tensor(dst, dst, OL[rows, :, :], ALU.add),
                         r=[Opb, accb], w=[accb])

            Acur = phaseA(units[0])
            for k, u in enumerate(units):
                Anext = phaseA(units[k + 1]) if k + 1 < len(units) else None
                phaseB(u, Acur)
                Acur = Anext
            for pair in range(4):
                for hh in range(2):
                    ts_ = slice(hh * 1024, (hh + 1) * 1024)
                    rc, rcb = rcr.next()
                    s.op("dve", lambda: nc.vector.reciprocal(rc[:, :], acc[:, 1, pair, ts_]), r=[accb], w=[rcb])
                    ob, obb = obr.next()
                    s.op("dve", lambda: nc.vector.tensor_tensor(ob[:, :], acc[:, 0, pair, ts_], rc[:, :], ALU.mult),
                         r=[accb, rcb], w=[obb])
                    s.dma("pool", mixedT[:, pair, tok0 + hh * 1024:tok0 + (hh + 1) * 1024], ob[:, :], r=[obb])

        qS, qSb = st.sb("qS", [128, 4, 16], BF16)
        kS, kSb = st.sb("kS", [128, 4, 16], BF16)
        oS, oSb = st.sb("oS", [128, 4, 16], BF16)
        s.dma("sp", qS[:, :, :], qT[:, :, NTP:NTP + 16], w=[qSb])
        s.dma("sp", kS[:, :, :], kT[:, :, NTP:NTP + 16], w=[kSb])
        vrr = st.ring("vrow", [1, 512], BF16, 3)
        kgr = st.ring("Kg", [128, 512], F32, 3)
        vgr = st.ring("Vg", [128, 512], F32, 3)
        kgbr = st.ring("Kgb", [128, 512], BF16, 2)
        vgbr = st.ring("Vgb", [128, 512], BF16, 4)
        kgtr = st.ring("KgT", [128, 4, 128], BF16, 2)
        s8r = st.ring("s8", [128, 16], F32, 3)
        p8r = st.ring("p8", [128, 16], BF16, 3)
        t8r = st.ring("t8", [128, 16], F32, 2)
        pstr = st.psring("pstA", [128, 8, 128], BF16, 1)
        smp, smb = st.ps("psm", [128, 512], F32)
        Sgb = Sob = OSb = smb
        Sg = smp[:, 0:8]
        So = smp[0:1, 8:16]
        OS = smp[:, 16:32].rearrange("p (a h) -> p a h", a=2)
        for i in range(16):
            vrow, vrowb = vrr.next()
            s.dma("sp", vrow[0:1, :], v_bf[NTP + i:NTP + i + 1, :], w=[vrowb])
            keep = []
            for pi, (wnd, dil) in enumerate(PATTERNS):
                Kg, Kgb_ = kgr.next()
                Vg, Vgb_ = vgr.next()
                s.dma("sp", Kg[:, :], cache_k[i, 2048 - 128 * dil:2048:dil, :], w=[Kgb_])
                s.dma("sp", Vg[:, :], cache_v[i, 2048 - 128 * dil:2048:dil, :], w=[Vgb_])
                Kb, Kbb = kgbr.next()
                Vb, Vbb = vgbr.next()
                s.op("pool", lambda: nc.gpsimd.tensor_copy(Kb[:, :], Kg[:, :]), r=[Kgb_], w=[Kbb])
                s.op("pool", lambda: nc.gpsimd.tensor_copy(Vb[:, :], Vg[:, :]), r=[Vgb_], w=[Vbb])
                pt, ptb = pstr.next()
                for pr in range(4):
                    s.op("pe", lambda: nc.tensor.transpose(pt[:, pr, :], Kb[:, pr * 128:(pr + 1) * 128], c.ident_bf[:, :]),
                         r=[Kbb, c.constb], w=[ptb])
                KT_, KTb_ = kgtr.next()
                evac(c, KT_[:, :, :], pt[:, 0:4, :], [ptb], [KTb_])
                for h in range(8):
                    pair = h // 2
                    rows = slice(64 * (h % 2), 64 * (h % 2) + 64)
                    s.op("pe", lambda: nc.tensor.matmul(Sg[:, h:h + 1], KT_[rows, pair, :], qS[rows, pair, i:i + 1],
                                                        start=True, stop=True), r=[KTb_, qSb], w=[Sgb])
                    s.op("pe", lambda: nc.tensor.matmul(So[0:1, h:h + 1], kS[rows, pair, i:i + 1], qS[rows, pair, i:i + 1],
                                                        start=True, stop=True), r=[kSb, qSb], w=[Sob])
                s8, s8b = s8r.next()
                s.op("dve", lambda: nc.vector.tensor_tensor(s8[:, 0:8], Sg, BT4[:, pi * 8:(pi + 1) * 8, 1, 0], ALU.add),
                     r=[Sgb, BTb], w=[s8b])
                s.op("dve", lambda: nc.vector.tensor_tensor(s8[0:1, 8:16], So, BT4[0:1, pi * 8:(pi + 1) * 8, 0, 0], ALU.add),
                     r=[Sob, BTb], w=[s8b])
                p8, p8b = p8r.next()
                s.op("act", lambda: nc.scalar.activation(out=p8[:, 0:8], in_=s8[:, 0:8], func=AF.Exp), r=[s8b], w=[p8b])
                s.op("act", lambda: nc.scalar.activation(out=p8[0:1, 8:16], in_=s8[0:1, 8:16], func=AF.Exp), r=[s8b], w=[p8b])
                keep.append((Vb, Vbb, p8, p8b))
            for a_ in range(2):
                for h in range(8):
                    pair = h // 2
                    for pi, (Vb, Vbb, p8, p8b) in enumerate(keep):
                        lh = Vb[:, pair * 128:(pair + 1) * 128] if a_ == 0 else c.ones_bf[:, :]
                        lo = vrow[0:1, pair * 128:(pair + 1) * 128] if a_ == 0 else c.ones_bf[0:1, :]
                        s.op("pe", lambda: nc.tensor.matmul(OS[:, a_, h:h + 1], lh, p8[:, h:h + 1],
                                                            start=(pi == 0), stop=False), r=[Vbb, c.constb, p8b], w=[OSb])
                        s.op("pe", lambda: nc.tensor.matmul(OS[:, a_, h:h + 1], lo, p8[0:1, 8 + h:9 + h],
                                                            start=False, stop=(pi == 2)), r=[vrowb, c.constb, p8b], w=[OSb])
            t8, t8b = t8r.next()
            s.op("dve", lambda: nc.vector.reciprocal(t8[:, 8:16], OS[:, 1, :]), r=[OSb], w=[t8b])
            s.op("dve", lambda: nc.vector.tensor_tensor(t8[:, 0:8], OS[:, 0, :], t8[:, 8:16], ALU.mult), r=[OSb, t8b], w=[t8b])
            for half in range(2):
                rows = slice(64 * half, 64 * half + 64)
                s.op("dve", lambda: nc.vector.tensor_copy(oS[rows, :, i], t8[rows, half:8:2]), r=[t8b], w=[oSb])
        s.dma("pool", mixedT[:, 0:4, NTP:NTP + 16], oS[:, :, :], r=[oSb])


def ssd_stage(c, xbcT, dt_scr, z_scr, mixedT, convw_d, convb_d, vec16_d, gssm_d, tri_d,
              cconv_d, state_d, conv_out, ssm_out, NB, NTP):
    s, nc = c.s, c.nc
    NSTREAM = 2
    with Stage(c) as st:
        cw, cwb = st.sb("cw", [128, 12, 4], F32)
        cb, cbb = st.sb("cb", [128, 12], F32)
        v16, v16b = st.sb("v16", [128, 3, 16], F32)
        gss, gssb = st.sb("gss", [128, 1024], F32)
        tri, trib = st.sb("tri", [128, 2, 128], F32)
        onesf, onesfb = st.sb("onesf", [128, 128], F32)
        s.dma("sp", cw[:, :, :], convw_d[:, :, :], w=[cwb])
        s.dma("sp", cb[:, :], convb_d[:, :], w=[cbb])
        s.dma("sp", v16[:, :, :], vec16_d[0:1, :, :].partition_broadcast(128), w=[v16b])
        s.dma("sp", gss[:, :], gssm_d[0:1, :].partition_broadcast(128), w=[gssb])
        s.dma("sp", tri[:, :, :], tri_d[:, :, :], w=[trib])
        s.op("pool", lambda: nc.gpsimd.memset(onesf[:, :], 1.0), w=[onesfb])
        s.op("act", lambda: nc.scalar.activation(out=v16[:, 1, :], in_=v16[:, 1, :], func=AF.Exp), r=[v16b], w=[v16b])
        s.op("dve", lambda: nc.vector.tensor_scalar(v16[:, 1, :], v16[:, 1, :], -1.0, None, ALU.mult), r=[v16b], w=[v16b])
        dtb_bc, a_bc, dsk_bc = v16[:, 0, :], v16[:, 1, :], v16[:, 2, :]
        triU, strictL = tri[:, 0, :], tri[:, 1, :]
        dg, dgb = st.sb("dg", [128, 48, 128], F32)
        for j in range(12):
            for i in range(4):
                s.op("dve", lambda: nc.vector.tensor_scalar(dg[:, j * 4 + i, :], c.ident_f[:, :], cw[:, j, i:i + 1], None, ALU.mult),
                     r=[c.constb, cwb], w=[dgb])

        def make_stream(k):
            S = Ctx()
            n = f"s{k}"
            S.xin = st.sb(n + "xin", [128, 12, 131], F32)
            S.a32 = st.sb(n + "a32", [128, 12, 128], F32)
            S.BCT = st.sb(n + "BCT", [128, 4, 128], BF16)
            S.Btok = st.sb(n + "Btok", [128, 2, 128], BF16)
            S.xdt = st.sb(n + "xdt", [128, 16, 64], BF16)
            S.xdts = st.sb(n + "xdts", [128, 16, 64], BF16)
            S.dsk = st.sb(n + "dsk", [128, 16, 64], F32)
            S.dtt = st.sb(n + "dtt", [128, 6, 16], F32)
            S.acs = st.sb(n + "acs", [128, 5, 16], F32)
            S.H = st.sb(n + "H", [128, 16, 64], F32)
            S.Hb = st.sb(n + "Hb", [128, 16, 64], BF16)
            S.CBm = st.sb(n + "CBm", [128, 2, 128], F32)
            S.rhsD = st.ring(n + "rhsD", [128, 4, 128], F32, 2)
            S.Es = st.ring(n + "Es", [128, 4, 128], F32, 1)
            S.MT = st.ring(n + "MT", [128, 4, 128], BF16, 2)
            S.YT = st.sb(n + "YTsb", [128, 2, 16, 64], F32)
            S.yt = st.sb(n + "yt", [128, 16, 64], F32)
            S.zt = st.sb(n + "zt", [128, 1024], F32)
            S.yn = st.sb(n + "yn", [128, 1024], F32)
            S.yT = st.sb(n + "yT", [128, 8, 128], BF16)
            S.stat = st.ring(n + "stat", [128, 4], F32, 2)
            S.junk = st.sb(n + "junk", [128, 512], BF16)
            S.YTp = st.ps(n + "YTp", [128, 512], F32)
            S.Fp = st.ps(n + "Fp", [128, 512], F32)
            S.Mr = st.psring(n + "M", [128, 512], F32, 2)
            return S

        def front(S, item):
            kind, idx, tok0, ch, nchunk = item
            t0 = tok0 + ch * 128
            sidx = idx if kind == "p" else NB + idx
            Mr = S.Mr
            xin, xinb = S.xin
            if kind == "p":
                if ch == 0:
                    s.op("pool", lambda: nc.gpsimd.memset(xin[:, :, 0:3], 0.0), w=[xinb])
                    s.dma("sp", xin[:, :, 3:131], xbcT[:, :, t0:t0 + 128], w=[xinb])
                else:
                    s.dma("sp", xin[:, :, :], xbcT[:, :, t0 - 3:t0 + 128], w=[xinb])
            else:
                s.op("pool", lambda: nc.gpsimd.memset(xin[:, :, :], 0.0), w=[xinb])
                s.dma("sp", xin[:, :, 0:3], cconv_d[idx], w=[xinb])
                s.dma("sp", xin[:, :, 3:4], xbcT[:, :, t0:t0 + 1], w=[xinb], slow=True)
            dtt, dttb = S.dtt
            if kind == "p":
                s.dma("sp", dtt[:, 0, :], dt_scr[t0:t0 + 128, :], w=[dttb])
            else:
                s.op("pool", lambda: nc.gpsimd.memset(dtt[:, 0, :], 0.0), w=[dttb])
                s.dma("sp", dtt[0:1, 0, :], dt_scr[t0:t0 + 1, :], w=[dttb])
            zt, ztb = S.zt
            if kind == "p":
                s.dma("sp", zt[:, :], z_scr[t0:t0 + 128, :], w=[ztb])
            else:
                s.op("pool", lambda: nc.gpsimd.memset(zt[:, :], 0.0), w=[ztb])
                s.dma("sp", zt[0:1, :], z_scr[t0:t0 + 1, :], w=[ztb])
            if ch == nchunk - 1:
                lo = 128 if kind == "p" else 1
                s.dma("sp", conv_out[sidx], xin[:, :, lo:lo + 3], r=[xinb])
            a32, a32b = S.a32
            for jg in range(3):
                m, mb = Mr.next()
                for jj in range(4):
                    j = jg * 4 + jj
                    for i in range(4):
                        s.op("pe", lambda: nc.tensor.matmul(m[:, jj * 128:(jj + 1) * 128], dg[:, j * 4 + i, :], xin[:, j, i:i + 128],
                                                            start=(i == 0), stop=(i == 3)), r=[dgb, xinb], w=[mb])
                for jj in range(4):
                    j = jg * 4 + jj
                    s.op("act", lambda: nc.scalar.activation(out=a32[:, j, :], in_=m[:, jj * 128:(jj + 1) * 128], func=AF.Silu,
                                                             bias=cb[:, j:j + 1]), r=[mb, cbb], w=[a32b])
            BCT, BCTb = S.BCT
            s.op("act", lambda: nc.scalar.copy(BCT[:, :, :], a32[:, 8:12, :]), r=[a32b], w=[BCTb])
            s.op("dve", lambda: nc.vector.tensor_tensor(dtt[:, 1, :], dtt[:, 0, :], dtb_bc, ALU.add), r=[dttb, v16b], w=[dttb])
            s.op("act", lambda: nc.scalar.activation(out=dtt[:, 1, :], in_=dtt[:, 1, :], func=AF.Exp), r=[dttb], w=[dttb])
            s.op("dve", lambda: nc.vector.tensor_scalar(dtt[:, 1, :], dtt[:, 1, :], 1.0, None, ALU.add), r=[dttb], w=[dttb])
            s.op("act", lambda: nc.scalar.activation(out=dtt[:, 2, :], in_=dtt[:, 1, :], func=AF.Ln), r=[dttb], w=[dttb])
            if kind == "s":
                s.op("dve", lambda: nc.vector.tensor_scalar(dtt[:, 2, :], dtt[:, 2, :], c.ident_f[:, 0:1], None, ALU.mult),
                     r=[dttb, c.constb], w=[dttb])
            s.op("dve", lambda: nc.vector.tensor_tensor(dtt[:, 3, :], dtt[:, 2, :], a_bc, ALU.mult), r=[dttb, v16b], w=[dttb])
            dt_, la = dtt[:, 2, :], dtt[:, 3, :]
            Mc, Mcb = Mr.next()
            s.op("pe", lambda: nc.tensor.matmul(Mc[:, 0:16], triU, la, start=True, stop=True), r=[trib, dttb], w=[Mcb])
            s.op("pe", lambda: nc.tensor.matmul(Mc[:, 16:32], onesf[:, :], la, start=True, stop=True), r=[onesfb, dttb], w=[Mcb])
            acs, acsb = S.acs
            s.op("dve", lambda: nc.vector.tensor_copy(acs[:, 0:2, :], Mc[:, 0:32].rearrange("p (a e) -> p a e", a=2)),
                 r=[Mcb], w=[acsb])
            s.op("dve", lambda: nc.vector.tensor_tensor(acs[:, 3, :], acs[:, 1, :], acs[:, 0, :], ALU.subtract), r=[acsb], w=[acsb])
            s.op("act", lambda: nc.scalar.activation(out=acs[:, 2, :], in_=acs[:, 0, :], func=AF.Exp), r=[acsb], w=[acsb])
            s.op("act", lambda: nc.scalar.activation(out=acs[:, 3, :], in_=acs[:, 3, :], func=AF.Exp), r=[acsb], w=[acsb])
            s.op("act", lambda: nc.scalar.activation(out=acs[:, 4, :], in_=acs[:, 1, :], func=AF.Exp), r=[acsb], w=[acsb])
            s.op("dve", lambda: nc.vector.tensor_tensor(dtt[:, 4, :], dt_, acs[:, 3, :], ALU.mult), r=[dttb, acsb], w=[dttb])
            dtd = dtt[:, 4, :]
            xdt, xdtb = S.xdt
            xdts, xdtsb = S.xdts
            dsk, dskb = S.dsk
            for g in range(2):
                m, mb = Mr.next()
                for j in range(g * 4, g * 4 + 4):
                    s.op("pe", lambda: nc.tensor.transpose(m[:, (j % 4) * 128:(j % 4 + 1) * 128], a32[:, j, :], c.ident_f[:, :]),
                         r=[a32b, c.constb], w=[mb])
                mv = m[:, :].rearrange("p (e q) -> p e q", e=8)
                hs = slice(g * 8, g * 8 + 8)
                s.op("dve", lambda: nc.vector.tensor_tensor(xdt[:, hs, :], mv, dt_[:, hs].unsqueeze(2).to_broadcast([128, 8, 64]), ALU.mult),
                     r=[mb, dttb], w=[xdtb])
                s.op("dve", lambda: nc.vector.tensor_tensor(xdts[:, hs, :], mv, dtd[:, hs].unsqueeze(2).to_broadcast([128, 8, 64]), ALU.mult),
                     r=[mb, dttb], w=[xdtsb])
                s.op("dve", lambda: nc.vector.tensor_tensor(dsk[:, hs, :], mv, dsk_bc[:, hs].unsqueeze(2).to_broadcast([128, 8, 64]), ALU.mult),
                     r=[mb, v16b], w=[dskb])
            Mb_, Mbb = Mr.next()
            for g in range(2):
                s.op("pe", lambda: nc.tensor.transpose(Mb_[:, g * 128:(g + 1) * 128], a32[:, 8 + g, :], c.ident_f[:, :]),
                     r=[a32b, c.constb], w=[Mbb])
            Btok, Btokb = S.Btok
            s.op("act", lambda: nc.scalar.copy(Btok[:, :, :], Mb_[:, 0:256].rearrange("p (g n) -> p g n", g=2)), r=[Mbb], w=[Btokb])
            CBm, CBmb = S.CBm
            Mg, Mgb = Mr.next()
            for g in range(2):
                s.op("pe", lambda: nc.tensor.matmul(Mg[:, g * 128:(g + 1) * 128], BCT[:, g, :], BCT[:, 2 + g, :], start=True, stop=True),
                     r=[BCTb], w=[Mgb])
            s.op("dve", lambda: nc.vector.tensor_tensor(CBm[:, :, :], Mg[:, 0:256].rearrange("p (g l) -> p g l", g=2),
                                                        triU.unsqueeze(1).to_broadcast([128, 2, 128]), ALU.mult), r=[Mgb, trib], w=[CBmb])
            YT, YTb = S.YT
            YTp, YTpb = S.YTp
            for g in range(2):
                for q4 in range(2):
                    e0 = g * 8 + q4 * 4
                    rhsD, rhsDb = S.rhsD.next()
                    s.op("pool", lambda: nc.gpsimd.tensor_tensor(rhsD[:, :, :], triU.unsqueeze(1).to_broadcast([128, 4, 128]),
                                                                 la[:, e0:e0 + 4].unsqueeze(2).to_broadcast([128, 4, 128]), ALU.mult),
                         r=[trib, dttb], w=[rhsDb])
                    Md, Mdb = Mr.next()
                    s.op("pe", lambda: nc.tensor.matmul(Md[:, :], strictL, rhsD[:, :, :].rearrange("p e l -> p (e l)"), start=True, stop=True),
                         r=[trib, rhsDb], w=[Mdb])
                    Es, Esb = S.Es.next()
                    s.op("act", lambda: nc.scalar.activation(out=Es[:, :, :].rearrange("p e l -> p (e l)"), in_=Md[:, :], func=AF.Exp),
                         r=[Mdb], w=[Esb])
                    MT, MTb = S.MT.next()
                    s.op("dve", lambda: nc.vector.tensor_tensor(MT[:, :, :], Es[:, :, :],
                                                                CBm[:, g:g + 1, :].to_broadcast([128, 4, 128]), ALU.mult),
                         r=[Esb, CBmb], w=[MTb])
                    for e in range(e0, e0 + 4):
                        cs = slice((e - e0) * 64, (e - e0) * 64 + 64)
                        cs2 = slice(256 + (e - e0) * 64, 256 + (e - e0) * 64 + 64)
                        s.op("pe", lambda: nc.tensor.matmul(YTp[:, cs], MT[:, e - e0, :], xdt[:, e, :], start=True, stop=True),
                             r=[MTb, xdtb], w=[YTpb])
                        s.op("pe", lambda: nc.tensor.matmul(YTp[:, cs2], Btok[:, g, :], xdts[:, e, :], start=True, stop=True),
                             r=[Btokb, xdtsb], w=[YTpb])
                    s.op("act", lambda: nc.scalar.copy(YT[:, :, e0:e0 + 4, :], YTp[:, :].rearrange("p (a e q) -> p a e q", a=2, e=4)),
                         r=[YTpb], w=[YTb])

        def back(S, item):
            kind, idx, tok0, ch, nchunk = item
            t0 = tok0 + ch * 128
            sidx = idx if kind == "p" else NB + idx
            Mr = S.Mr
            H, Hb_ = S.H
            Hbc, Hbcb = S.Hb
            acs, acsb = S.acs
            BCT, BCTb = S.BCT
            YT, YTb = S.YT
            Fp, Fpb = S.Fp
            eacs, cdec = acs[:, 2, :], acs[:, 4, :]
            if ch == 0:
                if kind == "p":
                    s.op("dve", lambda: nc.vector.memset(H[:, :, :], 0.0), w=[Hb_])
                else:
                    s.dma("sp", H[:, :, :], state_d[idx].rearrange("n (e p) -> n e p", e=16), w=[Hb_])
                s.op("act", lambda: nc.scalar.copy(Hbc[:, :, :], H[:, :, :]), r=[Hb_], w=[Hbcb])
            yt, ytb = S.yt
            for g in range(2):
                hs = slice(g * 8, g * 8 + 8)
                for e in range(g * 8, g * 8 + 8):
                    cs = slice((e % 8) * 64, (e % 8) * 64 + 64)
                    s.op("pe", lambda: nc.tensor.matmul(Fp[:, cs], BCT[:, 2 + g, :], Hbc[:, e, :], start=True, stop=True),
                         r=[BCTb, Hbcb], w=[Fpb])
                fv = Fp[:, :].rearrange("p (e q) -> p e q", e=8)
                s.op("dve", lambda: nc.vector.tensor_tensor(yt[:, hs, :], fv, eacs[:, hs].unsqueeze(2).to_broadcast([128, 8, 64]), ALU.mult),
                     r=[Fpb, acsb], w=[ytb])
                s.op("dve", lambda: nc.vector.tensor_tensor(H[:, hs, :], H[:, hs, :], cdec[:, hs].unsqueeze(2).to_broadcast([128, 8, 64]), ALU.mult),
                     r=[Hb_, acsb], w=[Hb_])
                s.op("dve", lambda: nc.vector.tensor_tensor(H[:, hs, :], H[:, hs, :], YT[:, 1, hs, :], ALU.add), r=[Hb_, YTb], w=[Hb_])
            s.op("act", lambda: nc.scalar.copy(Hbc[:, :, :], H[:, :, :]), r=[Hb_], w=[Hbcb])
            s.op("pool", lambda: nc.gpsimd.tensor_tensor(yt[:, :, :], yt[:, :, :], YT[:, 0, :, :], ALU.add), r=[ytb, YTb], w=[ytb])
            s.op("pool", lambda: nc.gpsimd.tensor_tensor(yt[:, :, :], yt[:, :, :], S.dsk[0][:, :, :], ALU.add), r=[ytb, S.dsk[1]], w=[ytb])
            zt, ztb = S.zt
            s.op("act", lambda: nc.scalar.activation(out=zt[:, :], in_=zt[:, :], func=AF.Silu), r=[ztb], w=[ztb])
            ytf = yt[:, :, :].rearrange("p e q -> p (e q)")
            s.op("dve", lambda: nc.vector.tensor_tensor(ytf, ytf, zt[:, :], ALU.mult), r=[ytb, ztb], w=[ytb])
            yn, ynb = S.yn
            jk, jkb = S.junk
            for g in range(2):
                gs = slice(g * 512, (g + 1) * 512)
                ss, ssb = S.stat.next()
                s.op("act", lambda: nc.scalar.activation(out=jk[:, 0:512], in_=ytf[:, gs], func=AF.Square, accum_out=ss[:, 0:1]),
                     r=[ytb], w=[jkb, ssb])
                rstd_from_ss(c, ss, ssb, 512)
                s.op("dve", lambda: nc.vector.scalar_tensor_tensor(yn[:, gs], ytf[:, gs], ss[:, 2:3], gss[:, gs], ALU.mult, ALU.mult),
                     r=[ytb, ssb, gssb], w=[ynb])
            yT, yTb = S.yT
            for g in range(2):
                m, mb = Mr.next()
                for j in range(g * 4, g * 4 + 4):
                    s.op("pe", lambda: nc.tensor.transpose(m[:, (j % 4) * 128:(j % 4 + 1) * 128], yn[:, j * 128:(j + 1) * 128], c.ident_f[:, :]),
                         r=[ynb, c.constb], w=[mb])
                evac(c, yT[:, g * 4:(g + 1) * 4, :], m[:, :].rearrange("p (j t) -> p j t", j=4), [mb], [yTb])
            if kind == "p":
                s.dma("sp", mixedT[:, 4:12, t0:t0 + 128], yT[:, :, :], r=[yTb])
            else:
                s.dma("sp", mixedT[:, 4:12, t0:t0 + 1], yT[:, :, 0:1], r=[yTb], slow=True)
            if ch == nchunk - 1:
                s.dma("sp", ssm_out[sidx].rearrange("n (e p) -> n e p", e=16), H[:, :, :], r=[Hb_])

        streams = [make_stream(k) for k in range(NSTREAM)]
        work = [[] for _ in range(NSTREAM)]
        for b in range(NB):
            for ch in range(16):
                work[b % NSTREAM].append(("p", b, b * 2048, ch, 16))
        for i in range(16):
            work[(i + NB) % NSTREAM].append(("s", i, NTP + i, 0, 1))

        def runner(k):
            def run():
                for item in work[k]:
                    front(streams[k], item)
                    back(streams[k], item)
            return run
        Interleave(s, [runner(k) for k in range(NSTREAM)]).run()


def outcross_stage(c, x1, mixedT, gcol, wout_d, wcq_d, wco_d, memKT, memV, cache_mk, cache_mv, x3, tiles, NTP):
    s, nc = c.s, c.nc
    with Stage(c) as st:
        st.wstage = st.ring("wst", [128, 1024], F32, 3)
        wO, wOb = st.sb("wO", [128, 12, D], BF16)
        wQ, wQb = st.sb("wQ", [128, 8, D], BF16)
        wC, wCb = st.sb("wCo", [128, 8, D], BF16)
        xtr = st.ring("xt", [128, 4, 1024], F32, 2)
        mTr = st.ring("mT", [128, 12, 512], BF16, 1)
        hTr = st.ring("hT", [128, 8, 512], BF16, 2)
        qxr = st.ring("qx", [128, 8, 512], BF16, 1)
        oTr = st.ring("oTn", [128, 8, 512], BF16, 1)
        ptr = st.ring("PT", [128, 512], BF16, 4)
        rlr = st.ring("rl", [128, 512], F32, 2)
        ktr = st.ring("KTm", [128, 8, 256], BF16, 2)
        vmr = st.ring("Vm", [128, 2, 1024], BF16, 2)
        ckr = st.ring("ck", [128, 2, 1024], F32, 2)
        cbr = st.ring("ckb", [128, 2, 1024], BF16, 1)
        norm_bufs(st)
        psr = st.psring("ps", [128, 512], F32, 6)
        load_weight(c, st, wout_d, wO, wOb, 12, D)
        load_weight(c, st, wcq_d, wQ, wQb, 8, D)
        load_weight(c, st, wco_d, wC, wCb, 8, D)

        def cross_core(KTm, KTmb, Vm, Vmb, qx, qxb, oT, oTb, cols, n):
            for h in range(4):
                PTs = []
                for mb in range(2):
                    ps, psb = psr.next()
                    for dc in range(2):
                        s.op("pe", lambda: nc.tensor.matmul(ps[:, 0:n], KTm[:, 2 * h + dc, mb * 128:(mb + 1) * 128],
                                                            qx[:, 2 * h + dc, cols], start=(dc == 0), stop=(dc == 1)),
                             r=[KTmb, qxb], w=[psb])
                    PT, PTb = ptr.next()
                    s.op("act", lambda: nc.scalar.activation(out=PT[:, 0:n], in_=ps[:, 0:n], func=AF.Exp), r=[psb], w=[PTb])
                    PTs.append((PT, PTb))
                ps, psb = psr.next()
                for mb in range(2):
                    s.op("pe", lambda: nc.tensor.matmul(ps[:, 0:n], c.ones_bf[:, :], PTs[mb][0][:, 0:n],
                                                        start=(mb == 0), stop=(mb == 1)), r=[c.constb, PTs[mb][1]], w=[psb])
                rl, rlb = rlr.next()
                s.op("dve", lambda: nc.vector.reciprocal(rl[:, 0:n], ps[:, 0:n]), r=[psb], w=[rlb])
                for dc in range(2):
                    ps, psb = psr.next()
                    for mb in range(2):
                        s.op("pe", lambda: nc.tensor.matmul(ps[:, 0:n], Vm[:, mb, (2 * h + dc) * 128:(2 * h + dc + 1) * 128],
                                                            PTs[mb][0][:, 0:n], start=(mb == 0), stop=(mb == 1)),
                             r=[Vmb, PTs[mb][1]], w=[psb])
                    s.op("dve", lambda: nc.vector.tensor_tensor(oT[:, 2 * h + dc, cols], ps[:, 0:n], rl[:, 0:n], ALU.mult),
                         r=[psb, rlb], w=[oTb])

        cur_b = -1
        KTm = Vm = None
        for (t0, nsub) in tiles:
            N = nsub * 128
            xt, xtb = xtr.next()
            s.dma("sp", xt[:, 0:nsub, :], x1[t0:t0 + N, :].rearrange("(s p) d -> p s d", p=128), w=[xtb])
            mT, mTb = mTr.next()
            s.dma("sp", mT[:, :, 0:N], mixedT[:, :, t0:t0 + N], w=[mTb])
            for sub in range(nsub):
                for hf in range(2):
                    ps, psb = psr.next()
                    mm_tm(c, mT, mTb, sub, wO, wOb, hf * 512, 512, ps, psb, KC=12)
                    xs_ = xt[:, sub, hf * 512:(hf + 1) * 512]
                    s.op("dve", lambda: nc.vector.tensor_tensor(xs_, xs_, ps[:, :], ALU.add), r=[psb, xtb], w=[xtb])
            hT, hTb = hTr.next()
            norm_transpose(c, st, xt, xtb, nsub, gcol, hT, hTb)
            qx, qxb = qxr.next()
            for j in range(8):
                ps, psb = psr.next()
                mm_fm(c, wQ, wQb, j * 128, hT, hTb, N, ps, psb)
                evac(c, qx[:, j, 0:N], ps[:, 0:N], [psb], [qxb], scale=1.0 / 16.0)
            oT, oTb = oTr.next()
            if t0 < NTP:
                b = t0 // 2048
                if b != cur_b:
                    cur_b = b
                    KTm, KTmb = ktr.next()
                    Vm, Vmb = vmr.next()
                    s.dma("sp", KTm[:, :, :], memKT[:, :, b * 256:(b + 1) * 256], w=[KTmb])
                    s.dma("sp", Vm[:, :, :], memV[b * 256:(b + 1) * 256, :].rearrange("(mb m) d -> m mb d", m=128), w=[Vmb])
                cross_core(KTm, KTmb, Vm, Vmb, qx, qxb, oT, oTb, slice(0, N), N)
            else:
                s.op("pool", lambda: nc.gpsimd.memset(oT[:, :, 0:N], 0.0), w=[oTb])
                for i in range(16):
                    ck, ckb_ = ckr.next()
                    s.dma("sp", ck[:, :, :], cache_mk[i].rearrange("(mb m) d -> m mb d", m=128), w=[ckb_])
                    cb_, cbb_ = cbr.next()
                    s.op("pool", lambda: nc.gpsimd.tensor_copy(cb_[:, :, :], ck[:, :, :]), r=[ckb_], w=[cbb_])
                    KTs, KTsb = ktr.next()
                    for mb in range(2):
                        pt, ptb = st.pst.next()
                        for j in range(8):
                            s.op("pe", lambda: nc.tensor.transpose(pt[:, j, :], cb_[:, mb, j * 128:(j + 1) * 128], c.ident_bf[:, :]),
                                 r=[cbb_, c.constb], w=[ptb])
                        evac(c, KTs[:, :, mb * 128:(mb + 1) * 128], pt[:, :, :], [ptb], [KTsb])
                    cv, cvb_ = ckr.next()
                    s.dma("sp", cv[:, :, :], cache_mv[i].rearrange("(mb m) d -> m mb d", m=128), w=[cvb_])
                    Vs, Vsb = vmr.next()
                    s.op("pool", lambda: nc.gpsimd.tensor_copy(Vs[:, :, :], cv[:, :, :]), r=[cvb_], w=[Vsb])
                    cross_core(KTs, KTsb, Vs, Vsb, qx, qxb, oT, oTb, slice(i, i + 1), 1)
            for sub in range(nsub):
                for hf in range(2):
                    ps, psb = psr.next()
                    mm_tm(c, oT, oTb, sub, wC, wCb, hf * 512, 512, ps, psb)
                    xs_ = xt[:, sub, hf * 512:(hf + 1) * 512]
                    s.op("dve", lambda: nc.vector.tensor_tensor(xs_, xs_, ps[:, :], ALU.add), r=[psb, xtb], w=[xtb])
            s.dma("pool", x3[t0:t0 + N, :].rearrange("(s p) d -> p s d", p=128), xt[:, 0:nsub, :], r=[xtb])


ALL_STAGES = ("memkv", "ffn1", "inproj", "attn", "ssd", "outcross", "ffn2")


def build(NB=4, stages=ALL_STAGES, dbg=()):
    nc = bass.Bass("TRN2", target_bir_lowering=False)
    c = Ctx()
    c.nc = nc
    c.s = Sch(nc)
    s = c.s
    NTP = NB * 2048
    NTOK = NTP + 128
    NBM = NB * 256
    NSEQ = NB + 16
    tiles = [(t * 512, 4) for t in range(NTP // 512)] + [(NTP, 1)]
    pb, bmask_np, negm_np = bias_consts()
    NPB = len(pb)

    def din(name, shape):
        return nc.dram_tensor(name, shape, F32, kind="ExternalInput").ap()

    def dout(name, shape):
        return nc.dram_tensor(name, shape, F32, kind="ExternalOutput").ap()

    def scr(name, shape, dt=F32):
        if name in dbg:
            return nc.dram_tensor(name, shape, dt, kind="ExternalOutput").ap()
        return nc.dram_tensor(name, shape, dt).ap()

    x_all = din("x_all", [NTOK, D])
    gcols_d = din("gcols", [128, 6, 8])
    gfin_d = din("gfin", [1, D])
    w1g, w1u, w1d = din("w1_gate", [128, 8, DFF]), din("w1_up", [128, 8, DFF]), din("w1_down", [128, 22, D])
    w2g, w2u, w2d = din("w2_gate", [128, 8, DFF]), din("w2_up", [128, 8, DFF]), din("w2_down", [128, 22, D])
    win_d = din("w_in", [128, 8, DIN])
    wout_d = din("w_out", [128, 12, D])
    wck_d, wcv_d, wcq_d, wco_d = (din(n, [128, 8, D]) for n in ("w_ck", "w_cv", "w_cq", "w_co"))
    mem_in = din("mem_in", [NBM, D])
    relb_d = din("rel_bias", [1, 256])
    bmask_d = din("bmask", [NPB, 128, 256])
    negm_d = din("negm", [3, 128, 256])
    convw_d = din("conv_w", [128, 12, 4])
    convb_d = din("conv_b", [128, 12])
    vec16_d = din("vec16", [1, 3, 16])
    gssm_d = din("g_ssm", [1, D])
    tri_d = din("tri", [128, 2, 128])
    ident_d = din("ident", [128, 128])
    cache_k = din("cache_k", [16, 2048, 512])
    cache_v = din("cache_v", [16, 2048, 512])
    cconv_d = din("cconv", [16, 128, 12, 3])
    state_d = din("state", [16, 128, 1024])
    cache_mk = din("cache_mk", [16, 256, D])
    cache_mv = din("cache_mv", [16, 256, D])

    y_out = dout("y_out", [NTOK, D])
    kT_out = dout("kT_out", [128, 4, NTOK])
    v_out = dout("v_out", [NTOK, 512])
    conv_out = dout("conv_out", [NSEQ, 128, 12, 3])
    ssm_out = dout("ssm_out", [NSEQ, 128, 1024])
    mk_out = dout("mk_out", [NBM, D])
    mv_out = dout("mv_out", [NBM, D])

    x1 = scr("x1", [NTOK, D])
    x3 = scr("x3", [NTOK, D])
    x4 = scr("x4", [NTOK, D])
    hT_scr = scr("hT_scr", [128, 8, NTOK], BF16)
    memKT = scr("memKT", [128, 8, NBM], BF16)
    memV = scr("memV", [NBM, D], BF16)
    qT = scr("qT", [128, 4, NTOK], BF16)
    kT = scr("kT", [128, 4, NTOK], BF16)
    v_bf = scr("v_bf", [NTOK, 512], BF16)
    z_scr = scr("z_scr", [NTOK, D])
    xbcT = scr("xbcT", [128, 12, NTOK])
    dt_scr = scr("dt_scr", [NTOK, 16])
    mixedT = scr("mixedT", [128, 12, NTOK], BF16)

    c.dbufs = {}

    def db(key):
        if key not in c.dbufs:
            c.dbufs[key] = Buf(str(key))
        return c.dbufs[key]
    c.db = db

    c.constb = Buf("const")
    c.ident_f = nc.alloc_sbuf_tensor("ident_f", [128, 128], F32)
    c.ident_bf = nc.alloc_sbuf_tensor("ident_bf", [128, 128], BF16)
    c.ones_bf = nc.alloc_sbuf_tensor("ones_bf", [128, 128], BF16)
    c.gcols = nc.alloc_sbuf_tensor("gcols_sb", [128, 6, 8], F32)
    c.gfin = nc.alloc_sbuf_tensor("gfin_sb", [128, D], F32)
    s.dma("sp", c.ident_f[:, :], ident_d[:, :], w=[c.constb])
    s.dma("sp", c.gcols[:, :, :], gcols_d[:, :, :], w=[c.constb])
    s.dma("sp", c.gfin[:, :], gfin_d[0:1, :].partition_broadcast(128), w=[c.constb])
    s.op("dve", lambda: nc.vector.tensor_copy(c.ident_bf[:, :], c.ident_f[:, :]), r=[c.constb], w=[c.constb])
    s.op("dve", lambda: nc.vector.memset(c.ones_bf[:, :], 1.0), w=[c.constb])
    s.barrier()

    if "memkv" in stages:
        memkv_stage(c, mem_in, c.gcols[:, 4, :], wck_d, wcv_d, mk_out, mv_out, memKT, memV, NBM)
    if "ffn1" in stages:
        ffn_stage(c, x_all, x1, c.gcols[:, 0, :], w1g, w1u, w1d, hT_scr, tiles, "f1")
    if "inproj" in stages:
        inproj_stage(c, x1, c.gcols[:, 1, :], win_d, qT, kT, kT_out, v_out, v_bf, z_scr, xbcT, dt_scr, tiles)
    if "attn" in stages:
        attn_stage(c, qT, kT, v_bf, mixedT, relb_d, bmask_d, negm_d, pb, cache_k, cache_v, NB, NTP)
    if "ssd" in stages:
        ssd_stage(c, xbcT, dt_scr, z_scr, mixedT, convw_d, convb_d, vec16_d, gssm_d, tri_d,
                  cconv_d, state_d, conv_out, ssm_out, NB, NTP)
    if "outcross" in stages:
        outcross_stage(c, x1, mixedT, c.gcols[:, 2, :], wout_d, wcq_d, wco_d, memKT, memV, cache_mk, cache_mv, x3, tiles, NTP)
    if "ffn2" in stages:
        ffn_stage(c, x3, x4, c.gcols[:, 3, :], w2g, w2u, w2d, hT_scr, tiles, "f2", final=(c.gfin[:, :], y_out))
    s.finish()
    return nc


def tile_w(w, kc):
    w = np.asarray(w, np.float32)
    K, F = w.shape
    return np.ascontiguousarray(w.reshape(kc, 128, F).transpose(1, 0, 2))


def gcol(g):
    return np.asarray(g, np.float32).reshape(8, 128).T


def tri_consts():
    j = np.arange(128)[:, None]
    l = np.arange(128)[None, :]
    return np.ascontiguousarray(np.stack([(j <= l), (j > l)], axis=1).astype(np.float32))


def shared_inputs(inp):
    f = lambda k: np.asarray(inp[k], np.float32)
    pb, bmask, negm = bias_consts()
    gc = np.zeros((128, 6, 8), np.float32)
    for n, k in enumerate(("g_ffn1", "g_mix", "g_cross", "g_ffn2", "g_mem")):
        gc[:, n, :] = gcol(f(k)[0])
    cw = f("conv_w")[0]
    sh = {
        "gcols": gc, "gfin": f("g_final").reshape(1, D),
        "w1_gate": tile_w(f("w1_gate")[0], 8), "w1_up": tile_w(f("w1_up")[0], 8), "w1_down": tile_w(f("w1_down")[0], 22),
        "w2_gate": tile_w(f("w2_gate")[0], 8), "w2_up": tile_w(f("w2_up")[0], 8), "w2_down": tile_w(f("w2_down")[0], 22),
        "w_in": tile_w(f("w_in")[0], 8), "w_out": tile_w(f("w_out")[0], 12),
        "w_ck": tile_w(f("w_ck")[0], 8), "w_cv": tile_w(f("w_cv")[0], 8),
        "w_cq": tile_w(f("w_cq")[0], 8), "w_co": tile_w(f("w_co")[0], 8),
        "rel_bias": f("rel_bias").reshape(1, 256),
        "bmask": bmask, "negm": negm,
        "conv_w": np.ascontiguousarray(cw.reshape(4, 12, 128).transpose(2, 1, 0)),
        "conv_b": np.ascontiguousarray(f("conv_b")[0].reshape(12, 128).T),
        "vec16": np.stack([f("dt_bias")[0], f("a_log")[0], f("d_skip")[0]])[None],
        "g_ssm": f("g_ssm").reshape(1, D),
        "tri": tri_consts(), "ident": np.eye(128, dtype=np.float32),
    }
    return sh


def core_inputs(inp, core, NB):
    f = lambda k: np.asarray(inp[k], np.float32)
    xp = f("x_prompt")[core * NB:(core + 1) * NB].reshape(NB * 2048, D)
    xs = f("x_sample")[core * 16:(core + 1) * 16].reshape(16, D)
    x_all = np.concatenate([xp, xs, np.zeros((112, D), np.float32)], 0)
    sl = slice(core * 16, (core + 1) * 16)
    cc = f("cache_conv")[0, sl]
    st = f("state_ssm")[0, sl]
    return {
        "x_all": x_all,
        "mem_in": f("mem_prompt")[core * NB:(core + 1) * NB].reshape(NB * 256, D),
        "cache_k": f("cache_win_k")[0, sl].reshape(16, 2048, 512),
        "cache_v": f("cache_win_v")[0, sl].reshape(16, 2048, 512),
        "cconv": np.ascontiguousarray(cc.reshape(16, 3, 12, 128).transpose(0, 3, 2, 1)),
        "state": np.ascontiguousarray(st.reshape(16, 1024, 128).transpose(0, 2, 1)),
        "cache_mk": f("cache_mem_k")[0, sl].reshape(16, 256, D),
        "cache_mv": f("cache_mem_v")[0, sl].reshape(16, 256, D),
    }


def assemble(results, NB, ncores):
    NTP = NB * 2048
    B = NB * ncores
    y_p = np.empty((B, 2048, D), np.float32)
    y_s = np.empty((16 * ncores, 1, D), np.float32)
    wk_p = np.empty((1, B, 2048, 8, 64), np.float32)
    wv_p = np.empty((1, B, 2048, 8, 64), np.float32)
    cv_p = np.empty((1, B, 3, 1536), np.float32)
    ss_p = np.empty((1, B, 16, 64, 128), np.float32)
    mk_p = np.empty((1, B, 256, 4, 256), np.float32)
    mv_p = np.empty((1, B, 256, 4, 256), np.float32)
    wk_s = np.empty((1, 16 * ncores, 1, 8, 64), np.float32)
    wv_s = np.empty((1, 16 * ncores, 1, 8, 64), np.float32)
    cv_s = np.empty((1, 16 * ncores, 3, 1536), np.float32)
    ss_s = np.empty((1, 16 * ncores, 16, 64, 128), np.float32)
    for cidx, r in enumerate(results):
        y = np.asarray(r["y_out"])
        ktok = np.asarray(r["kT_out"]).transpose(2, 1, 0).reshape(-1, 8, 64)
        vtok = np.asarray(r["v_out"]).reshape(-1, 8, 64)
        cv = np.asarray(r["conv_out"]).transpose(0, 3, 2, 1).reshape(-1, 3, 1536)
        ss = np.asarray(r["ssm_out"]).transpose(0, 2, 1).reshape(-1, 16, 64, 128)
        bs = slice(cidx * NB, (cidx + 1) * NB)
        ts = slice(cidx * 16, (cidx + 1) * 16)
        y_p[bs] = y[:NTP].reshape(NB, 2048, D)
        y_s[ts, 0] = y[NTP:NTP + 16]
        wk_p[0, bs] = ktok[:NTP].reshape(NB, 2048, 8, 64)
        wv_p[0, bs] = vtok[:NTP].reshape(NB, 2048, 8, 64)
        wk_s[0, ts, 0] = ktok[NTP:NTP + 16]
        wv_s[0, ts, 0] = vtok[NTP:NTP + 16]
        cv_p[0, bs] = cv[:NB]
        cv_s[0, ts] = cv[NB:]
        ss_p[0, bs] = ss[:NB]
        ss_s[0, ts] = ss[NB:]
        mk_p[0, bs] = np.asarray(r["mk_out"]).reshape(NB, 256, 4, 256)
        mv_p[0, bs] = np.asarray(r["mv_out"]).reshape(NB, 256, 4, 256)
    return (y_p, y_s, wk_p, wv_p, cv_p, ss_p, mk_p, mv_p, wk_s, wv_s, cv_s, ss_s)


def kernel(**inputs):
    NB = 4
    nc = build(NB=NB)
    sh = shared_inputs(inputs)
    in_maps = []
    for core in range(NCORES):
        m = dict(sh)
        m.update(core_inputs(inputs, core, NB))
        in_maps.append(m)
    res = run_bass_kernel_spmd(nc, in_maps, core_ids=list(range(NCORES)))
    return assemble(res.results, NB, NCORES)
```

```python
import numpy as np
import concourse.bass as bass
import concourse.mybir as mybir
from concourse.bass_utils import run_bass_kernel_spmd

F32 = mybir.dt.float32
BF16 = mybir.dt.bfloat16
AF = mybir.ActivationFunctionType
ALU = mybir.AluOpType
AX = mybir.AxisListType

D = 1024
DFF = 2816
NCORES = 8
EPS = 1e-6


class Buf:
    __slots__ = ("name", "w", "rs", "excl")

    def __init__(self, name="", excl=False):
        self.name = name
        self.w = None
        self.rs = {}
        self.excl = excl


class Sch:
    ND = 12

    def __init__(self, nc):
        self.nc = nc
        self.eng = {"pe": nc.tensor, "act": nc.scalar, "dve": nc.vector, "pool": nc.gpsimd, "sp": nc.sync}
        self.sem = {k: nc.alloc_semaphore("s_" + k) for k in self.eng}
        self.cnt = {k: 0 for k in self.eng}
        self.known = {k: {} for k in self.eng}
        self.dsem = {q: [[nc.alloc_semaphore(f"d_{q}{i}"), 0] for i in range(self.ND)]
                     for q in ("sp", "pool", "act")}
        self.dnext = {q: 0 for q in self.dsem}
        self.il = None

    def _deps(self, own, r, w):
        toks = []
        for b in r:
            if b.w is not None:
                toks.append(b.w)
        for b in w:
            if b.w is not None and b.w[0] is not own:
                toks.append(b.w)
            for sem, v in b.rs.items():
                if sem is not own:
                    toks.append((sem, v))
        return toks

    def _wait(self, e, toks):
        need = {}
        for sem, v in toks:
            if v > need.get(sem, 0):
                need[sem] = v
        kn = self.known[e]
        for sem, v in need.items():
            if kn.get(sem, 0) >= v:
                continue
            self.eng[e].wait_ge(sem, v)
            kn[sem] = v

    def _mark(self, tok, r, w):
        for b in r:
            if b.rs.get(tok[0], 0) < tok[1]:
                b.rs[tok[0]] = tok[1]
        for b in w:
            b.w = tok
            b.rs = {}

    def op(self, e, ins_fn, r=(), w=()):
        own = self.sem[e]
        if any(b.excl for b in r):
            w = list(w) + [b for b in r if b.excl]
            r = [b for b in r if not b.excl]
        self._wait(e, self._deps(own, r, w))
        ins = ins_fn()
        self.cnt[e] += 1
        ins.then_inc(own, 1)
        self._mark((own, self.cnt[e]), r, w)
        if self.il is not None:
            self.il.switch()
        return ins

    def dma(self, q, out, in_, r=(), w=(), slow=False):
        pool = self.dsem[q]
        i = self.dnext[q]
        self.dnext[q] = (i + 1) % len(pool)
        slot = pool[i]
        toks = self._deps(None, r, w)
        if slot[1] > 0:
            toks.append((slot[0], slot[1]))
        self._wait(q, toks)
        ins = (self.eng[q].dma_start(out=out, in_=in_, allow_slow_non_contiguous=True) if slow
               else self.eng[q].dma_start(out=out, in_=in_))
        slot[1] += 16
        ins.then_inc(slot[0], 16)
        self._mark((slot[0], slot[1]), r, w)
        if self.il is not None:
            self.il.switch()
        return ins

    def finish(self):
        toks = []
        for q in self.dsem:
            for sem, v in self.dsem[q]:
                if v > 0:
                    toks.append((sem, v))
        for k in self.eng:
            if self.cnt[k] > 0:
                toks.append((self.sem[k], self.cnt[k]))
        self._wait("sp", toks)


class Ring:
    def __init__(self, items):
        self.items = items
        self.i = 0

    def next(self):
        it = self.items[self.i]
        self.i = (self.i + 1) % len(self.items)
        return it


class Ctx:
    pass


def sb(nc, name, shape, dt):
    t = nc.alloc_sbuf_tensor(name, shape, dt)
    return t, Buf(name)


def sb_ring(nc, name, shape, dt, n):
    return Ring([sb(nc, f"{name}{i}", shape, dt) for i in range(n)])


import threading


class Interleave:
    def __init__(self, sch, fns):
        self.sch = sch
        self.fns = fns
        self.ev = [threading.Event() for _ in fns]
        self.alive = [True] * len(fns)
        self.idx = {}
        self.exc = None

    def _next(self, i):
        n = len(self.fns)
        for d in range(1, n + 1):
            j = (i + d) % n
            if j != i and self.alive[j]:
                return j
        return None

    def _wrap(self, i):
        self.idx[threading.get_ident()] = i
        self.ev[i].wait()
        self.ev[i].clear()
        try:
            if self.exc is None:
                self.fns[i]()
        except BaseException as e:
            self.exc = e
        finally:
            self.alive[i] = False
            j = self._next(i)
            if j is not None:
                self.ev[j].set()

    def switch(self):
        i = self.idx.get(threading.get_ident())
        if i is None:
            return
        if self.exc is not None:
            raise RuntimeError("sibling stream failed")
        j = self._next(i)
        if j is None:
            return
        self.ev[j].set()
        self.ev[i].wait()
        self.ev[i].clear()

    def run(self):
        ths = [threading.Thread(target=self._wrap, args=(i,)) for i in range(len(self.fns))]
        self.sch.il = self
        for t in ths:
            t.start()
        self.ev[0].set()
        for t in ths:
            t.join()
        self.sch.il = None
        if self.exc is not None:
            raise self.exc
from contextlib import ExitStack

DIN = 4112
PATTERNS = ((128, 1), (512, 4), (2048, 16))
NEG = -30000.0


def _barrier(self):
    toks = []
    for q in self.dsem:
        for sem, v in self.dsem[q]:
            if v > 0:
                toks.append((sem, v))
    for k in self.eng:
        if self.cnt[k] > 0:
            toks.append((self.sem[k], self.cnt[k]))
    for e in self.eng:
        self._wait(e, [t for t in toks if t[0] is not self.sem[e]])


Sch.barrier = _barrier


class Stage:
    _n = 0

    def __init__(self, c):
        self.c = c
        self.es = ExitStack()

    def __enter__(self):
        self.es.__enter__()
        return self

    def __exit__(self, *a):
        if a[0] is None:
            self.c.s.barrier()
        return self.es.__exit__(*a)

    def sb(self, name, shape, dt):
        Stage._n += 1
        t = self.es.enter_context(self.c.nc.sbuf_tensor(f"{name}_{Stage._n}", shape, dt))
        return t, Buf(name)

    def ring(self, name, shape, dt, n):
        return Ring([self.sb(f"{name}{i}", shape, dt) for i in range(n)])

    def ps(self, name, shape, dt=F32):
        Stage._n += 1
        t = self.es.enter_context(self.c.nc.psum_tensor(f"{name}_{Stage._n}", shape, dt))
        return t, Buf(name, excl=True)

    def psring(self, name, shape, dt, n):
        return Ring([self.ps(f"{name}{i}", shape, dt) for i in range(n)])


def load_weight(c, st, dram_w, dst, dst_buf, kc_n, f_n, piece=1024):
    s, nc = c.s, c.nc
    for kc in range(kc_n):
        for f0 in range(0, f_n, piece):
            fw = min(piece, f_n - f0)
            stg, stb = st.wstage.next()
            s.dma("sp", stg[:, 0:fw], dram_w[:, kc, f0:f0 + fw], w=[stb])
            s.op("pool", lambda: nc.gpsimd.tensor_copy(dst[:, kc, f0:f0 + fw], stg[:, 0:fw]),
                 r=[stb], w=[dst_buf])


def rstd_from_ss(c, ss, ssb, n):
    s, nc = c.s, c.nc
    s.op("dve", lambda: nc.vector.tensor_scalar(ss[:, 1:2], ss[:, 0:1], 1.0 / n, EPS, ALU.mult, ALU.add),
         r=[ssb], w=[ssb])
    s.op("act", lambda: nc.scalar.activation(out=ss[:, 3:4], in_=ss[:, 1:2], func=AF.Sqrt), r=[ssb], w=[ssb])
    s.op("dve", lambda: nc.vector.reciprocal(ss[:, 2:3], ss[:, 3:4]), r=[ssb], w=[ssb])


def norm_part(c, st, xt, xtb, nsub):
    s, nc = c.s, c.nc
    xns = []
    for sub in range(nsub):
        ss, ssb = st.stat.next()
        jk, jkb = st.junk.next()
        s.op("act", lambda: nc.scalar.activation(out=jk[:, :], in_=xt[:, sub, :], func=AF.Square,
                                                 accum_out=ss[:, 0:1]), r=[xtb], w=[jkb, ssb])
        rstd_from_ss(c, ss, ssb, D)
        xn, xnb = st.xn.next()
        s.op("act", lambda: nc.scalar.activation(out=xn[:, :], in_=xt[:, sub, :], func=AF.Copy,
                                                 scale=ss[:, 2:3]), r=[xtb, ssb], w=[xnb])
        xns.append((xn, xnb))
    return xns


def transpose_part(c, st, xns, gcol, hT, hTb):
    s, nc = c.s, c.nc
    for sub, (xn, xnb) in enumerate(xns):
        pt, ptb = st.pst.next()
        for kc in range(8):
            s.op("pe", lambda: nc.tensor.transpose(pt[:, kc, :], xn[:, kc * 128:(kc + 1) * 128], c.ident_bf[:, :]),
                 r=[xnb, c.constb], w=[ptb])
        s.op("dve", lambda: nc.vector.tensor_tensor(
            hT[:, :, sub * 128:(sub + 1) * 128], pt[:, :, :],
            gcol.unsqueeze(2).to_broadcast([128, 8, 128]), ALU.mult),
            r=[ptb, c.constb], w=[hTb])


def norm_transpose(c, st, xt, xtb, nsub, gcol, hT, hTb):
    transpose_part(c, st, norm_part(c, st, xt, xtb, nsub), gcol, hT, hTb)


def norm_bufs(st):
    st.stat = st.ring("stat", [128, 4], F32, 8)
    st.junk = st.ring("junk", [128, 1024], BF16, 2)
    st.xn = st.ring("xn", [128, 1024], BF16, 4)
    st.pst = st.psring("pst", [128, 8, 128], BF16, 2)


def mm_fm(c, W, Wb, f0, hT, hTb, N, ps, psb, KC=8):
    s, nc = c.s, c.nc
    for kc in range(KC):
        s.op("pe", lambda: nc.tensor.matmul(ps[:, 0:N], W[:, kc, f0:f0 + 128], hT[:, kc, 0:N],
                                            start=(kc == 0), stop=(kc == KC - 1)), r=[Wb, hTb], w=[psb])


def mm_tm(c, hT, hTb, sub, W, Wb, c0, ncols, ps, psb, KC=8):
    s, nc = c.s, c.nc
    for kc in range(KC):
        s.op("pe", lambda: nc.tensor.matmul(ps[:, 0:ncols], hT[:, kc, sub * 128:(sub + 1) * 128], W[:, kc, c0:c0 + ncols],
                                            start=(kc == 0), stop=(kc == KC - 1)), r=[Wb, hTb], w=[psb])


def evac(c, out, in_, r, w, scale=None):
    s, nc = c.s, c.nc
    c.flip = not getattr(c, "flip", False)
    if c.flip:
        if scale is None:
            s.op("act", lambda: nc.scalar.copy(out, in_), r=r, w=w)
        else:
            s.op("act", lambda: nc.scalar.mul(out, in_, scale), r=r, w=w)
    else:
        if scale is None:
            s.op("dve", lambda: nc.vector.tensor_copy(out, in_), r=r, w=w)
        else:
            s.op("dve", lambda: nc.vector.tensor_scalar(out, in_, scale, None, ALU.mult), r=r, w=w)


def ffn_stage(c, x_in, x_out, gcol, wg_d, wu_d, wd_d, hT_scr, tiles, tag, final=None):
    s, nc = c.s, c.nc
    H = DFF // 2
    HC = H // 128
    with Stage(c) as st:
        st.wstage = st.ring("wst", [128, 1024], F32, 6)
        wA, wAb = st.sb("wA", [128, 8, H], BF16)
        wB, wBb = st.sb("wB", [128, 8, H], BF16)
        wC, wCb = st.sb("wC", [128, HC, D], BF16)
        xtr = st.ring("xt", [128, 4, 1024], F32, 3)
        hTr = st.ring("hT", [128, 8, 512], BF16, 2)
        actr = st.ring("actT", [128, HC, 512], BF16, 2)
        sgr = st.ring("sg", [128, 512], F32, 2)
        norm_bufs(st)
        psr = st.psring("ps", [128, 512], F32, 6)
        for half in range(2):
            load_weight(c, st, wg_d[:, :, half * H:(half + 1) * H], wA, wAb, 8, H)
            load_weight(c, st, wu_d[:, :, half * H:(half + 1) * H], wB, wBb, 8, H)
            load_weight(c, st, wd_d[:, half * HC:(half + 1) * HC, :], wC, wCb, HC, D)
            src = x_in if half == 0 else x_out
            def prep1(tile):
                t0, nsub = tile
                N = nsub * 128
                xt, xtb = xtr.next()
                xob = c.db((tag, "xo", t0))
                rd = [xob] if half == 1 else [c.db((tag, "xi", t0))]
                s.dma("sp", xt[:, 0:nsub, :], src[t0:t0 + N, :].rearrange("(s p) d -> p s d", p=128), r=rd, w=[xtb])
                hT, hTb = hTr.next()
                xns = None
                if half == 0:
                    xns = norm_part(c, st, xt, xtb, nsub)
                else:
                    s.dma("sp", hT[:, :, 0:N], hT_scr[:, :, t0:t0 + N], r=[c.db((tag, "hs", t0))], w=[hTb])
                return (xt, xtb, hT, hTb, xob, xns, t0, N)

            def prep2(P):
                xt, xtb, hT, hTb, xob, xns, t0, N = P
                if half == 0:
                    transpose_part(c, st, xns, gcol, hT, hTb)
                    s.dma("pool", hT_scr[:, :, t0:t0 + N], hT[:, :, 0:N], r=[hTb], w=[c.db((tag, "hs", t0))])

            nxt = prep1(tiles[0])
            prep2(nxt)
            for ti, (t0, nsub) in enumerate(tiles):
                N = nsub * 128
                xt, xtb, hT, hTb, xob = nxt[0:5]
                act, actb = actr.next()
                for j in range(HC):
                    if j == min(3, HC - 1) and ti + 1 < len(tiles):
                        nxt = prep1(tiles[ti + 1])
                    pg, pgb = psr.next()
                    mm_fm(c, wA, wAb, j * 128, hT, hTb, N, pg, pgb)
                    pu, pub = psr.next()
                    mm_fm(c, wB, wBb, j * 128, hT, hTb, N, pu, pub)
                    sg, sgb = sgr.next()
                    s.op("act", lambda: nc.scalar.activation(out=sg[:, 0:N], in_=pg[:, 0:N], func=AF.Silu), r=[pgb], w=[sgb])
                    s.op("dve", lambda: nc.vector.tensor_tensor(act[:, j, 0:N], sg[:, 0:N], pu[:, 0:N], ALU.mult),
                         r=[sgb, pub], w=[actb])
                if ti + 1 < len(tiles):
                    prep2(nxt)
                for sub in range(nsub):
                    for hf in range(2):
                        pd, pdb = psr.next()
                        mm_tm(c, act, actb, sub, wC, wCb, hf * 512, 512, pd, pdb, KC=HC)
                        s.op("dve", lambda: nc.vector.scalar_tensor_tensor(
                            xt[:, sub, hf * 512:(hf + 1) * 512], pd[:, :], 0.5, xt[:, sub, hf * 512:(hf + 1) * 512],
                            ALU.mult, ALU.add), r=[pdb, xtb], w=[xtb])
                if half == 1 and final is not None:
                    gfin, y_out = final
                    for sub in range(nsub):
                        ss, ssb = st.stat.next()
                        jk, jkb = st.junk.next()
                        s.op("act", lambda: nc.scalar.activation(out=jk[:, :], in_=xt[:, sub, :], func=AF.Square,
                                                                 accum_out=ss[:, 0:1]), r=[xtb], w=[jkb, ssb])
                        rstd_from_ss(c, ss, ssb, D)
                        s.op("dve", lambda: nc.vector.scalar_tensor_tensor(
                            xt[:, sub, :], xt[:, sub, :], ss[:, 2:3], gfin, ALU.mult, ALU.mult),
                            r=[xtb, ssb, c.constb], w=[xtb])
                    s.dma("pool", y_out[t0:t0 + N, :].rearrange("(s p) d -> p s d", p=128), xt[:, 0:nsub, :], r=[xtb])
                else:
                    s.dma("pool", x_out[t0:t0 + N, :].rearrange("(s p) d -> p s d", p=128), xt[:, 0:nsub, :], r=[xtb], w=[xob])


def memkv_stage(c, mem_in, gcol, wck_d, wcv_d, mk_out, mv_out, memKT, memV, NBM):
    s, nc = c.s, c.nc
    with Stage(c) as st:
        st.wstage = st.ring("wst", [128, 1024], F32, 3)
        wK, wKb = st.sb("wK", [128, 8, D], BF16)
        wV, wVb = st.sb("wV", [128, 8, D], BF16)
        xtr = st.ring("xt", [128, 4, 1024], F32, 2)
        hTr = st.ring("hT", [128, 8, 512], BF16, 2)
        tmr = st.ring("tm", [128, 1024], F32, 3)
        tbr = st.ring("tb", [128, 1024], BF16, 2)
        fmr = st.ring("fm", [128, 512], BF16, 3)
        norm_bufs(st)
        psr = st.psring("ps", [128, 512], F32, 6)
        load_weight(c, st, wck_d, wK, wKb, 8, D)
        load_weight(c, st, wcv_d, wV, wVb, 8, D)
        tiles = []
        t0 = 0
        while t0 < NBM:
            n = min(4, (NBM - t0) // 128)
            tiles.append((t0, n))
            t0 += n * 128
        for (t0, nsub) in tiles:
            N = nsub * 128
            xt, xtb = xtr.next()
            s.dma("sp", xt[:, 0:nsub, :], mem_in[t0:t0 + N, :].rearrange("(s p) d -> p s d", p=128), w=[xtb])
            hT, hTb = hTr.next()
            norm_transpose(c, st, xt, xtb, nsub, gcol, hT, hTb)
            for sub in range(nsub):
                r0 = t0 + sub * 128
                for (W, Wb, outd, bfd) in ((wK, wKb, mk_out, None), (wV, wVb, mv_out, memV)):
                    tm, tmb = tmr.next()
                    for hf in range(2):
                        ps, psb = psr.next()
                        mm_tm(c, hT, hTb, sub, W, Wb, hf * 512, 512, ps, psb)
                        evac(c, tm[:, hf * 512:(hf + 1) * 512], ps[:, :], [psb], [tmb])
                    s.dma("pool", outd[r0:r0 + 128, :], tm[:, :], r=[tmb])
                    if bfd is not None:
                        tb, tbb = tbr.next()
                        s.op("pool", lambda: nc.gpsimd.tensor_copy(tb[:, :], tm[:, :]), r=[tmb], w=[tbb])
                        s.dma("pool", bfd[r0:r0 + 128, :], tb[:, :], r=[tbb])
            for j in range(8):
                ps, psb = psr.next()
                mm_fm(c, wK, wKb, j * 128, hT, hTb, N, ps, psb)
                fm, fmb = fmr.next()
                evac(c, fm[:, 0:N], ps[:, 0:N], [psb], [fmb])
                s.dma("pool", memKT[:, j, t0:t0 + N], fm[:, 0:N], r=[fmb])


def inproj_stage(c, x1, gcol, win_d, qT, kT, kT_out, v_out, v_bf, z_scr, xbcT, dt_scr, tiles):
    s, nc = c.s, c.nc
    with Stage(c) as st:
        st.wstage = st.ring("wst", [128, 1024], F32, 3)
        Wa, Wab = st.sb("winA", [128, 8, 2560], BF16)
        Wc, Wcb = st.sb("winB", [128, 8, DIN - 2560], BF16)
        xtr = st.ring("xt", [128, 4, 1024], F32, 3)
        hTr = st.ring("hT", [128, 8, 512], BF16, 2)
        f32r = st.ring("f32", [128, 512], F32, 6)
        bfr = st.ring("bf", [128, 512], BF16, 6)
        zr = st.ring("zt", [128, 1024], F32, 2)
        dtr = st.ring("dtt", [128, 16], F32, 3)
        norm_bufs(st)
        psr = st.psring("ps", [128, 512], F32, 6)
        load_weight(c, st, win_d[:, :, 0:2560], Wa, Wab, 8, 2560)
        load_weight(c, st, win_d[:, :, 2560:DIN], Wc, Wcb, 8, DIN - 2560)
        def prep1(tile):
            t0, nsub = tile
            N = nsub * 128
            xt, xtb = xtr.next()
            s.dma("sp", xt[:, 0:nsub, :], x1[t0:t0 + N, :].rearrange("(s p) d -> p s d", p=128), w=[xtb])
            hT, hTb = hTr.next()
            return (hT, hTb, norm_part(c, st, xt, xtb, nsub))

        nxt = prep1(tiles[0])
        transpose_part(c, st, nxt[2], gcol, nxt[0], nxt[1])
        for ti, (t0, nsub) in enumerate(tiles):
            N = nsub * 128
            hT, hTb = nxt[0], nxt[1]
            for j in range(20):
                if j == 3 and ti + 1 < len(tiles):
                    nxt = prep1(tiles[ti + 1])
                f0 = j * 128 if j < 8 else 2560 + (j - 8) * 128
                ps, psb = psr.next()
                if f0 < 2560:
                    mm_fm(c, Wa, Wab, f0, hT, hTb, N, ps, psb)
                else:
                    mm_fm(c, Wc, Wcb, f0 - 2560, hT, hTb, N, ps, psb)
                if j < 4:
                    o, ob = bfr.next()
                    evac(c, o[:, 0:N], ps[:, 0:N], [psb], [ob], scale=0.125)
                    s.dma("pool", qT[:, j, t0:t0 + N], o[:, 0:N], r=[ob])
                elif j < 8:
                    o, ob = bfr.next()
                    evac(c, o[:, 0:N], ps[:, 0:N], [psb], [ob])
                    s.dma("pool", kT[:, j - 4, t0:t0 + N], o[:, 0:N], r=[ob])
                    o2, o2b = f32r.next()
                    evac(c, o2[:, 0:N], ps[:, 0:N], [psb], [o2b])
                    s.dma("pool", kT_out[:, j - 4, t0:t0 + N], o2[:, 0:N], r=[o2b])
                else:
                    o2, o2b = f32r.next()
                    evac(c, o2[:, 0:N], ps[:, 0:N], [psb], [o2b])
                    s.dma("pool", xbcT[:, j - 8, t0:t0 + N], o2[:, 0:N], r=[o2b])
            if ti + 1 < len(tiles):
                transpose_part(c, st, nxt[2], gcol, nxt[0], nxt[1])
            for sub in range(nsub):
                r0 = t0 + sub * 128
                ps, psb = psr.next()
                mm_tm(c, hT, hTb, sub, Wa, Wab, 1024, 512, ps, psb)
                o2, o2b = f32r.next()
                evac(c, o2[:, :], ps[:, :], [psb], [o2b])
                s.dma("pool", v_out[r0:r0 + 128, :], o2[:, :], r=[o2b])
                o, ob = bfr.next()
                evac(c, o[:, :], ps[:, :], [psb], [ob])
                s.dma("pool", v_bf[r0:r0 + 128, :], o[:, :], r=[ob])
                zt, ztb = zr.next()
                for hf in range(2):
                    ps, psb = psr.next()
                    mm_tm(c, hT, hTb, sub, Wa, Wab, 1536 + hf * 512, 512, ps, psb)
                    evac(c, zt[:, hf * 512:(hf + 1) * 512], ps[:, :], [psb], [ztb])
                s.dma("pool", z_scr[r0:r0 + 128, :], zt[:, :], r=[ztb])
                ps, psb = psr.next()
                mm_tm(c, hT, hTb, sub, Wc, Wcb, 4096 - 2560, 16, ps, psb)
                dtt, dtb = dtr.next()
                evac(c, dtt[:, :], ps[:, 0:16], [psb], [dtb])
                s.dma("pool", dt_scr[r0:r0 + 128, :], dtt[:, :], r=[dtb])


def t5_bucket_np(dist):
    d = np.maximum(np.asarray(dist, np.int64), 0)
    df = np.maximum(d, 1).astype(np.float32)
    large = 16 + (np.log(df / np.float32(16.0)) / np.float32(np.log(2048.0 / 16.0)) * np.float32(16.0)).astype(np.int32)
    large = np.minimum(large, 31)
    return np.where(d < 16, d, large).astype(np.int64)


def bias_consts():
    k = np.arange(128)[:, None, None]
    kb = np.arange(2)[None, :, None]
    q = np.arange(128)[None, None, :]
    delta = q + 128 * kb - k
    valid = (delta >= 0) & (delta <= 128)
    pb, masks, negm = [], [], []
    for pi, (wnd, dil) in enumerate(PATTERNS):
        bk = t5_bucket_np(delta * dil)
        negm.append(np.where(valid, 0.0, NEG).astype(np.float32).reshape(128, 256))
        for b in range(32):
            m = (valid & (bk == b))
            if m.any():
                pb.append((pi, b))
                masks.append(m.astype(np.float32).reshape(128, 256))
    return pb, np.stack(masks), np.stack(negm)


def attn_stage(c, qT, kT, v_bf, mixedT, relb_d, bmask_d, negm_d, pb, cache_k, cache_v, NB, NTP):
    s, nc = c.s, c.nc
    with Stage(c) as st:
        BT, BTb = st.sb("BT", [128, 24, 256], F32)
        rb, rbb = st.sb("rb", [128, 256], F32)
        mr = st.ring("bm", [128, 256], F32, 3)
        s.dma("sp", rb[:, :], relb_d[0:1, :].partition_broadcast(128), w=[rbb])
        for pi in range(3):
            for h in range(8):
                s.dma("sp", BT[:, pi * 8 + h, :], negm_d[pi], w=[BTb])
        for n, (pi, b) in enumerate(pb):
            m, mb = mr.next()
            s.dma("sp", m[:, :], bmask_d[n], w=[mb])
            for h in range(8):
                s.op("dve", lambda: nc.vector.scalar_tensor_tensor(
                    BT[:, pi * 8 + h, :], m[:, :], rb[:, b * 8 + h:b * 8 + h + 1], BT[:, pi * 8 + h, :],
                    ALU.mult, ALU.add), r=[mb, rbb, BTb], w=[BTb])
        BT4 = BT[:, :, :].rearrange("p n (kb q) -> p n kb q", kb=2)
        BTh, BThb = st.sb("BTh", [128, 24, 256], BF16)
        BTl, BTlb = st.sb("BTl", [128, 24, 256], BF16)
        btmp, btmpb = st.sb("btmp", [128, 8, 256], F32)
        for pi in range(3):
            ps_ = slice(pi * 8, pi * 8 + 8)
            s.op("dve", lambda: nc.vector.tensor_copy(BTh[:, ps_, :], BT[:, ps_, :]), r=[BTb], w=[BThb])
            s.op("dve", lambda: nc.vector.tensor_tensor(btmp[:, :, :], BT[:, ps_, :], BTh[:, ps_, :], ALU.subtract),
                 r=[BTb, BThb], w=[btmpb])
            s.op("dve", lambda: nc.vector.tensor_copy(BTl[:, ps_, :], btmp[:, :, :]), r=[btmpb], w=[BTlb])

        qTb, qTbb = st.sb("qTb", [128, 4, 2048], BF16)
        kTb, kTbb = st.sb("kTb", [128, 4, 2048], BF16)
        acc, accb = st.sb("acc", [128, 2, 4, 2048], F32)
        Vr = st.ring("Vt", [128, 512], BF16, 5)
        sbr = st.ring("sbs", [128, 2, 128], F32, 4)
        ptr = st.ring("PT", [128, 2, 128], BF16, 4)
        rcr = st.ring("rc", [128, 1024], F32, 1)
        obr = st.ring("ob", [128, 1024], BF16, 2)
        psS = st.psring("psS", [128, 512], F32, 3)
        psO = st.psring("psO", [128, 512], F32, 3)
        for b in range(NB):
            tok0 = b * 2048
            s.dma("sp", qTb[:, :, :], qT[:, :, tok0:tok0 + 2048], w=[qTbb])
            s.dma("sp", kTb[:, :, :], kT[:, :, tok0:tok0 + 2048], w=[kTbb])
            units = []
            for pi, (wnd, dil) in enumerate(PATTERNS):
                nblk = 2048 // dil // 128
                for r in range(dil):
                    for blk in range(nblk):
                        for h in range(8):
                            units.append((pi, dil, r, blk, h))
            vstate = {}

            def phaseA(u):
                pi, dil, r, blk, h = u
                cols = slice(r + dil * 128 * blk, r + dil * 128 * blk + dil * 127 + 1, dil)
                pcols = slice(r + dil * 128 * (blk - 1), r + dil * 128 * (blk - 1) + dil * 127 + 1, dil)
                if h == 0:
                    Vt, Vtb = Vr.next()
                    row0 = tok0 + r + dil * 128 * blk
                    s.dma("sp", Vt[:, :], v_bf[row0:row0 + 127 * dil + 1:dil, :], w=[Vtb])
                    prev = vstate.get((pi, r, blk - 1))
                    vstate[(pi, r, blk)] = (Vt, Vtb)
                    vstate[("cur", pi, r, blk)] = [(Vt, Vtb)] + ([prev] if blk > 0 else [])
                nkb = 2 if blk > 0 else 1
                pair = h // 2
                rows = slice(64 * (h % 2), 64 * (h % 2) + 64)
                Sp, Spb = psS.next()
                S = Sp[:, 0:256].rearrange("p (kb q) -> p kb q", kb=2)
                s.op("pe", lambda: nc.tensor.matmul(Sp[:, 0:nkb * 128], c.ident_bf[:, :], BTh[:, pi * 8 + h, 0:nkb * 128],
                                                    start=True, stop=False), r=[c.constb, BThb], w=[Spb])
                s.op("pe", lambda: nc.tensor.matmul(Sp[:, 0:nkb * 128], c.ident_bf[:, :], BTl[:, pi * 8 + h, 0:nkb * 128],
                                                    start=False, stop=False), r=[c.constb, BTlb], w=[Spb])
                s.op("pe", lambda: nc.tensor.matmul(S[:, 0, :], kTb[rows, pair, cols], qTb[rows, pair, cols],
                                                    start=False, stop=(nkb == 1)), r=[kTbb, qTbb], w=[Spb])
                if blk > 0:
                    s.op("pe", lambda: nc.tensor.matmul(S[:, 1, :], kTb[rows, pair, pcols], qTb[rows, pair, cols],
                                                        start=False, stop=True), r=[kTbb, qTbb], w=[Spb])
                PT, PTb = ptr.next()
                s.op("act", lambda: nc.scalar.activation(out=PT[:, 0:nkb, :], in_=S[:, 0:nkb, :], func=AF.Exp),
                     r=[Spb], w=[PTb])
                return (PT, PTb, cols, nkb, vstate[("cur", pi, r, blk)])

            def phaseB(u, A):
                pi, dil, r, blk, h = u
                PT, PTb, cols, nkb, vs = A
                pair = h // 2
                rows = slice(64 * (h % 2), 64 * (h % 2) + 64)
                Op, Opb = psO.next()
                OL = Op[:, 0:256].rearrange("p (a q) -> p a q", a=2)
                for kb, (vt, vtb) in enumerate(vs):
                    s.op("pe", lambda: nc.tensor.matmul(OL[:, 0, :], vt[:, pair * 128:(pair + 1) * 128], PT[:, kb, :],
                                                        start=(kb == 0), stop=(kb == nkb - 1)), r=[vtb, PTb], w=[Opb])
                for kb in range(nkb):
                    s.op("pe", lambda: nc.tensor.matmul(OL[:, 1, :], c.ones_bf[:, :], PT[:, kb, :],
                                                        start=(kb == 0), stop=(kb == nkb - 1)), r=[c.constb, PTb], w=[Opb])
                dst = acc[rows, :, pair, cols]
                if pi == 0:
                    s.op("dve", lambda: nc.vector.tensor_copy(dst, OL[rows, :, :]), r=[Opb], w=[accb])
                else:
                    s.op("dve", lambda: nc.vector.tensor_tensor(dst, dst, OL[rows, :, :], ALU.add),
                         r=[Opb, accb], w=[accb])

            Acur = phaseA(units[0])
            for k, u in enumerate(units):
                Anext = phaseA(units[k + 1]) if k + 1 < len(units) else None
                phaseB(u, Acur)
                Acur = Anext
            for pair in range(4):
                for hh in range(2):
                    ts_ = slice(hh * 1024, (hh + 1) * 1024)
                    rc, rcb = rcr.next()
                    s.op("dve", lambda: nc.vector.reciprocal(rc[:, :], acc[:, 1, pair, ts_]), r=[accb], w=[rcb])
                    ob, obb = obr.next()
                    s.op("dve", lambda: nc.vector.tensor_tensor(ob[:, :], acc[:, 0, pair, ts_], rc[:, :], ALU.mult),
                         r=[accb, rcb], w=[obb])
                    s.dma("pool", mixedT[:, pair, tok0 + hh * 1024:tok0 + (hh + 1) * 1024], ob[:, :], r=[obb])

        qS, qSb = st.sb("qS", [128, 4, 16], BF16)
        kS, kSb = st.sb("kS", [128, 4, 16], BF16)
        oS, oSb = st.sb("oS", [128, 4, 16], BF16)
        s.dma("sp", qS[:, :, :], qT[:, :, NTP:NTP + 16], w=[qSb])
        s.dma("sp", kS[:, :, :], kT[:, :, NTP:NTP + 16], w=[kSb])
        vrr = st.ring("vrow", [1, 512], BF16, 3)
        kgr = st.ring("Kg", [128, 512], F32, 3)
        vgr = st.ring("Vg", [128, 512], F32, 3)
        kgbr = st.ring("Kgb", [128, 512], BF16, 2)
        vgbr = st.ring("Vgb", [128, 512], BF16, 4)
        kgtr = st.ring("KgT", [128, 4, 128], BF16, 2)
        s8r = st.ring("s8", [128, 16], F32, 3)
        p8r = st.ring("p8", [128, 16], BF16, 3)
        t8r = st.ring("t8", [128, 16], F32, 2)
        pstr = st.psring("pstA", [128, 8, 128], BF16, 1)
        smp, smb = st.ps("psm", [128, 512], F32)
        Sgb = Sob = OSb = smb
        Sg = smp[:, 0:8]
        So = smp[0:1, 8:16]
        OS = smp[:, 16:32].rearrange("p (a h) -> p a h", a=2)
        for i in range(16):
            vrow, vrowb = vrr.next()
            s.dma("sp", vrow[0:1, :], v_bf[NTP + i:NTP + i + 1, :], w=[vrowb])
            keep = []
            for pi, (wnd, dil) in enumerate(PATTERNS):
                Kg, Kgb_ = kgr.next()
                Vg, Vgb_ = vgr.next()
                s.dma("sp", Kg[:, :], cache_k[i, 2048 - 128 * dil:2048:dil, :], w=[Kgb_])
                s.dma("sp", Vg[:, :], cache_v[i, 2048 - 128 * dil:2048:dil, :], w=[Vgb_])
                Kb, Kbb = kgbr.next()
                Vb, Vbb = vgbr.next()
                s.op("pool", lambda: nc.gpsimd.tensor_copy(Kb[:, :], Kg[:, :]), r=[Kgb_], w=[Kbb])
                s.op("pool", lambda: nc.gpsimd.tensor_copy(Vb[:, :], Vg[:, :]), r=[Vgb_], w=[Vbb])
                pt, ptb = pstr.next()
                for pr in range(4):
                    s.op("pe", lambda: nc.tensor.transpose(pt[:, pr, :], Kb[:, pr * 128:(pr + 1) * 128], c.ident_bf[:, :]),
                         r=[Kbb, c.constb], w=[ptb])
                KT_, KTb_ = kgtr.next()
                evac(c, KT_[:, :, :], pt[:, 0:4, :], [ptb], [KTb_])
                for h in range(8):
                    pair = h // 2
                    rows = slice(64 * (h % 2), 64 * (h % 2) + 64)
                    s.op("pe", lambda: nc.tensor.matmul(Sg[:, h:h + 1], KT_[rows, pair, :], qS[rows, pair, i:i + 1],
                                                        start=True, stop=True), r=[KTb_, qSb], w=[Sgb])
                    s.op("pe", lambda: nc.tensor.matmul(So[0:1, h:h + 1], kS[rows, pair, i:i + 1], qS[rows, pair, i:i + 1],
                                                        start=True, stop=True), r=[kSb, qSb], w=[Sob])
                s8, s8b = s8r.next()
                s.op("dve", lambda: nc.vector.tensor_tensor(s8[:, 0:8], Sg, BT4[:, pi * 8:(pi + 1) * 8, 1, 0], ALU.add),
                     r=[Sgb, BTb], w=[s8b])
                s.op("dve", lambda: nc.vector.tensor_tensor(s8[0:1, 8:16], So, BT4[0:1, pi * 8:(pi + 1) * 8, 0, 0], ALU.add),
                     r=[Sob, BTb], w=[s8b])
                p8, p8b = p8r.next()
                s.op("act", lambda: nc.scalar.activation(out=p8[:, 0:8], in_=s8[:, 0:8], func=AF.Exp), r=[s8b], w=[p8b])
                s.op("act", lambda: nc.scalar.activation(out=p8[0:1, 8:16], in_=s8[0:1, 8:16], func=AF.Exp), r=[s8b], w=[p8b])
                keep.append((Vb, Vbb, p8, p8b))
            for a_ in range(2):
                for h in range(8):
                    pair = h // 2
                    for pi, (Vb, Vbb, p8, p8b) in enumerate(keep):
                        lh = Vb[:, pair * 128:(pair + 1) * 128] if a_ == 0 else c.ones_bf[:, :]
                        lo = vrow[0:1, pair * 128:(pair + 1) * 128] if a_ == 0 else c.ones_bf[0:1, :]
                        s.op("pe", lambda: nc.tensor.matmul(OS[:, a_, h:h + 1], lh, p8[:, h:h + 1],
                                                            start=(pi == 0), stop=False), r=[Vbb, c.constb, p8b], w=[OSb])
                        s.op("pe", lambda: nc.tensor.matmul(OS[:, a_, h:h + 1], lo, p8[0:1, 8 + h:9 + h],
                                                            start=False, stop=(pi == 2)), r=[vrowb, c.constb, p8b], w=[OSb])
            t8, t8b = t8r.next()
            s.op("dve", lambda: nc.vector.reciprocal(t8[:, 8:16], OS[:, 1, :]), r=[OSb], w=[t8b])
            s.op("dve", lambda: nc.vector.tensor_tensor(t8[:, 0:8], OS[:, 0, :], t8[:, 8:16], ALU.mult), r=[OSb, t8b], w=[t8b])
            for half in range(2):
                rows = slice(64 * half, 64 * half + 64)
                s.op("dve", lambda: nc.vector.tensor_copy(oS[rows, :, i], t8[rows, half:8:2]), r=[t8b], w=[oSb])
        s.dma("pool", mixedT[:, 0:4, NTP:NTP + 16], oS[:, :, :], r=[oSb])


def ssd_stage(c, xbcT, dt_scr, z_scr, mixedT, convw_d, convb_d, vec16_d, gssm_d, tri_d,
              cconv_d, state_d, conv_out, ssm_out, NB, NTP):
    s, nc = c.s, c.nc
    NSTREAM = 2
    with Stage(c) as st:
        cw, cwb = st.sb("cw", [128, 12, 4], F32)
        cb, cbb = st.sb("cb", [128, 12], F32)
        v16, v16b = st.sb("v16", [128, 3, 16], F32)
        gss, gssb = st.sb("gss", [128, 1024], F32)
        tri, trib = st.sb("tri", [128, 2, 128], F32)
        onesf, onesfb = st.sb("onesf", [128, 128], F32)
        s.dma("sp", cw[:, :, :], convw_d[:, :, :], w=[cwb])
        s.dma("sp", cb[:, :], convb_d[:, :], w=[cbb])
        s.dma("sp", v16[:, :, :], vec16_d[0:1, :, :].partition_broadcast(128), w=[v16b])
        s.dma("sp", gss[:, :], gssm_d[0:1, :].partition_broadcast(128), w=[gssb])
        s.dma("sp", tri[:, :, :], tri_d[:, :, :], w=[trib])
        s.op("pool", lambda: nc.gpsimd.memset(onesf[:, :], 1.0), w=[onesfb])
        s.op("act", lambda: nc.scalar.activation(out=v16[:, 1, :], in_=v16[:, 1, :], func=AF.Exp), r=[v16b], w=[v16b])
        s.op("dve", lambda: nc.vector.tensor_scalar(v16[:, 1, :], v16[:, 1, :], -1.0, None, ALU.mult), r=[v16b], w=[v16b])
        dtb_bc, a_bc, dsk_bc = v16[:, 0, :], v16[:, 1, :], v16[:, 2, :]
        triU, strictL = tri[:, 0, :], tri[:, 1, :]
        dg, dgb = st.sb("dg", [128, 48, 128], F32)
        for j in range(12):
            for i in range(4):
                s.op("dve", lambda: nc.vector.tensor_scalar(dg[:, j * 4 + i, :], c.ident_f[:, :], cw[:, j, i:i + 1], None, ALU.mult),
                     r=[c.constb, cwb], w=[dgb])

        def make_stream(k):
            S = Ctx()
            n = f"s{k}"
            S.xin = st.sb(n + "xin", [128, 12, 131], F32)
            S.a32 = st.sb(n + "a32", [128, 12, 128], F32)
            S.BCTr = st.ring(n + "BCT", [128, 4, 128], BF16, 2)
            S.Btok = st.sb(n + "Btok", [128, 2, 128], BF16)
            S.xdt = st.sb(n + "xdt", [128, 16, 64], BF16)
            S.xdts = st.sb(n + "xdts", [128, 16, 64], BF16)
            S.dskr = st.ring(n + "dsk", [128, 16, 64], F32, 2)
            S.dtt = st.sb(n + "dtt", [128, 6, 16], F32)
            S.acsr = st.ring(n + "acs", [128, 5, 16], F32, 2)
            S.H = st.sb(n + "H", [128, 16, 64], F32)
            S.Hb = st.sb(n + "Hb", [128, 16, 64], BF16)
            S.CBm = st.sb(n + "CBm", [128, 2, 128], F32)
            S.rhsD = st.ring(n + "rhsD", [128, 4, 128], F32, 2)
            S.Es = st.ring(n + "Es", [128, 4, 128], F32, 1)
            S.MT = st.ring(n + "MT", [128, 4, 128], BF16, 2)
            S.YTr = st.ring(n + "YTsb", [128, 2, 16, 64], F32, 2)
            S.yt = st.sb(n + "yt", [128, 16, 64], F32)
            S.ztr = st.ring(n + "zt", [128, 1024], F32, 2)
            S.yn = st.sb(n + "yn", [128, 1024], F32)
            S.yT = st.sb(n + "yT", [128, 8, 128], BF16)
            S.stat = st.ring(n + "stat", [128, 4], F32, 2)
            S.junk = st.sb(n + "junk", [128, 512], BF16)
            S.YTp = st.ps(n + "YTp", [128, 512], F32)
            S.Fp = st.ps(n + "Fp", [128, 512], F32)
            S.Mr = st.psring(n + "M", [128, 512], F32, 2)
            return S

        def front(S, item):
            kind, idx, tok0, ch, nchunk = item
            t0 = tok0 + ch * 128
            sidx = idx if kind == "p" else NB + idx
            Mr = S.Mr
            F = Ctx()
            F.BCT, F.dsk, F.acs, F.zt, F.YT = S.BCTr.next(), S.dskr.next(), S.acsr.next(), S.ztr.next(), S.YTr.next()
            xin, xinb = S.xin
            if kind == "p":
                if ch == 0:
                    s.op("pool", lambda: nc.gpsimd.memset(xin[:, :, 0:3], 0.0), w=[xinb])
                    s.dma("sp", xin[:, :, 3:131], xbcT[:, :, t0:t0 + 128], w=[xinb])
                else:
                    s.dma("sp", xin[:, :, :], xbcT[:, :, t0 - 3:t0 + 128], w=[xinb])
            else:
                s.op("pool", lambda: nc.gpsimd.memset(xin[:, :, :], 0.0), w=[xinb])
                s.dma("sp", xin[:, :, 0:3], cconv_d[idx], w=[xinb])
                s.dma("sp", xin[:, :, 3:4], xbcT[:, :, t0:t0 + 1], w=[xinb], slow=True)
            dtt, dttb = S.dtt
            if kind == "p":
                s.dma("sp", dtt[:, 0, :], dt_scr[t0:t0 + 128, :], w=[dttb])
            else:
                s.op("pool", lambda: nc.gpsimd.memset(dtt[:, 0, :], 0.0), w=[dttb])
                s.dma("sp", dtt[0:1, 0, :], dt_scr[t0:t0 + 1, :], w=[dttb])
            zt, ztb = F.zt
            if kind == "p":
                s.dma("sp", zt[:, :], z_scr[t0:t0 + 128, :], w=[ztb])
            else:
                s.op("pool", lambda: nc.gpsimd.memset(zt[:, :], 0.0), w=[ztb])
                s.dma("sp", zt[0:1, :], z_scr[t0:t0 + 1, :], w=[ztb])
            if ch == nchunk - 1:
                lo = 128 if kind == "p" else 1
                s.dma("sp", conv_out[sidx], xin[:, :, lo:lo + 3], r=[xinb])
            a32, a32b = S.a32
            for jg in range(3):
                m, mb = Mr.next()
                for jj in range(4):
                    j = jg * 4 + jj
                    for i in range(4):
                        s.op("pe", lambda: nc.tensor.matmul(m[:, jj * 128:(jj + 1) * 128], dg[:, j * 4 + i, :], xin[:, j, i:i + 128],
                                                            start=(i == 0), stop=(i == 3)), r=[dgb, xinb], w=[mb])
                for jj in range(4):
                    j = jg * 4 + jj
                    s.op("act", lambda: nc.scalar.activation(out=a32[:, j, :], in_=m[:, jj * 128:(jj + 1) * 128], func=AF.Silu,
                                                             bias=cb[:, j:j + 1]), r=[mb, cbb], w=[a32b])
            BCT, BCTb = F.BCT
            s.op("act", lambda: nc.scalar.copy(BCT[:, :, :], a32[:, 8:12, :]), r=[a32b], w=[BCTb])
            s.op("dve", lambda: nc.vector.tensor_tensor(dtt[:, 1, :], dtt[:, 0, :], dtb_bc, ALU.add), r=[dttb, v16b], w=[dttb])
            s.op("act", lambda: nc.scalar.activation(out=dtt[:, 1, :], in_=dtt[:, 1, :], func=AF.Exp), r=[dttb], w=[dttb])
            s.op("dve", lambda: nc.vector.tensor_scalar(dtt[:, 1, :], dtt[:, 1, :], 1.0, None, ALU.add), r=[dttb], w=[dttb])
            s.op("act", lambda: nc.scalar.activation(out=dtt[:, 2, :], in_=dtt[:, 1, :], func=AF.Ln), r=[dttb], w=[dttb])
            if kind == "s":
                s.op("dve", lambda: nc.vector.tensor_scalar(dtt[:, 2, :], dtt[:, 2, :], c.ident_f[:, 0:1], None, ALU.mult),
                     r=[dttb, c.constb], w=[dttb])
            s.op("dve", lambda: nc.vector.tensor_tensor(dtt[:, 3, :], dtt[:, 2, :], a_bc, ALU.mult), r=[dttb, v16b], w=[dttb])
            dt_, la = dtt[:, 2, :], dtt[:, 3, :]
            Mc, Mcb = Mr.next()
            s.op("pe", lambda: nc.tensor.matmul(Mc[:, 0:16], triU, la, start=True, stop=True), r=[trib, dttb], w=[Mcb])
            s.op("pe", lambda: nc.tensor.matmul(Mc[:, 16:32], onesf[:, :], la, start=True, stop=True), r=[onesfb, dttb], w=[Mcb])
            acs, acsb = F.acs
            s.op("dve", lambda: nc.vector.tensor_copy(acs[:, 0:2, :], Mc[:, 0:32].rearrange("p (a e) -> p a e", a=2)),
                 r=[Mcb], w=[acsb])
            s.op("dve", lambda: nc.vector.tensor_tensor(acs[:, 3, :], acs[:, 1, :], acs[:, 0, :], ALU.subtract), r=[acsb], w=[acsb])
            s.op("act", lambda: nc.scalar.activation(out=acs[:, 2, :], in_=acs[:, 0, :], func=AF.Exp), r=[acsb], w=[acsb])
            s.op("act", lambda: nc.scalar.activation(out=acs[:, 3, :], in_=acs[:, 3, :], func=AF.Exp), r=[acsb], w=[acsb])
            s.op("act", lambda: nc.scalar.activation(out=acs[:, 4, :], in_=acs[:, 1, :], func=AF.Exp), r=[acsb], w=[acsb])
            s.op("dve", lambda: nc.vector.tensor_tensor(dtt[:, 4, :], dt_, acs[:, 3, :], ALU.mult), r=[dttb, acsb], w=[dttb])
            dtd = dtt[:, 4, :]
            xdt, xdtb = S.xdt
            xdts, xdtsb = S.xdts
            dsk, dskb = F.dsk
            for g in range(2):
                m, mb = Mr.next()
                for j in range(g * 4, g * 4 + 4):
                    s.op("pe", lambda: nc.tensor.transpose(m[:, (j % 4) * 128:(j % 4 + 1) * 128], a32[:, j, :], c.ident_f[:, :]),
                         r=[a32b, c.constb], w=[mb])
                mv = m[:, :].rearrange("p (e q) -> p e q", e=8)
                hs = slice(g * 8, g * 8 + 8)
                s.op("dve", lambda: nc.vector.tensor_tensor(xdt[:, hs, :], mv, dt_[:, hs].unsqueeze(2).to_broadcast([128, 8, 64]), ALU.mult),
                     r=[mb, dttb], w=[xdtb])
                s.op("dve", lambda: nc.vector.tensor_tensor(xdts[:, hs, :], mv, dtd[:, hs].unsqueeze(2).to_broadcast([128, 8, 64]), ALU.mult),
                     r=[mb, dttb], w=[xdtsb])
                s.op("dve", lambda: nc.vector.tensor_tensor(dsk[:, hs, :], mv, dsk_bc[:, hs].unsqueeze(2).to_broadcast([128, 8, 64]), ALU.mult),
                     r=[mb, v16b], w=[dskb])
            Mb_, Mbb = Mr.next()
            for g in range(2):
                s.op("pe", lambda: nc.tensor.transpose(Mb_[:, g * 128:(g + 1) * 128], a32[:, 8 + g, :], c.ident_f[:, :]),
                     r=[a32b, c.constb], w=[Mbb])
            Btok, Btokb = S.Btok
            s.op("act", lambda: nc.scalar.copy(Btok[:, :, :], Mb_[:, 0:256].rearrange("p (g n) -> p g n", g=2)), r=[Mbb], w=[Btokb])
            CBm, CBmb = S.CBm
            Mg, Mgb = Mr.next()
            for g in range(2):
                s.op("pe", lambda: nc.tensor.matmul(Mg[:, g * 128:(g + 1) * 128], BCT[:, g, :], BCT[:, 2 + g, :], start=True, stop=True),
                     r=[BCTb], w=[Mgb])
            s.op("dve", lambda: nc.vector.tensor_tensor(CBm[:, :, :], Mg[:, 0:256].rearrange("p (g l) -> p g l", g=2),
                                                        triU.unsqueeze(1).to_broadcast([128, 2, 128]), ALU.mult), r=[Mgb, trib], w=[CBmb])
            YT, YTb = F.YT
            YTp, YTpb = S.YTp
            for g in range(2):
                for q4 in range(2):
                    e0 = g * 8 + q4 * 4
                    rhsD, rhsDb = S.rhsD.next()
                    s.op("pool", lambda: nc.gpsimd.tensor_tensor(rhsD[:, :, :], triU.unsqueeze(1).to_broadcast([128, 4, 128]),
                                                                 la[:, e0:e0 + 4].unsqueeze(2).to_broadcast([128, 4, 128]), ALU.mult),
                         r=[trib, dttb], w=[rhsDb])
                    Md, Mdb = Mr.next()
                    s.op("pe", lambda: nc.tensor.matmul(Md[:, :], strictL, rhsD[:, :, :].rearrange("p e l -> p (e l)"), start=True, stop=True),
                         r=[trib, rhsDb], w=[Mdb])
                    Es, Esb = S.Es.next()
                    s.op("act", lambda: nc.scalar.activation(out=Es[:, :, :].rearrange("p e l -> p (e l)"), in_=Md[:, :], func=AF.Exp),
                         r=[Mdb], w=[Esb])
                    MT, MTb = S.MT.next()
                    s.op("dve", lambda: nc.vector.tensor_tensor(MT[:, :, :], Es[:, :, :],
                                                                CBm[:, g:g + 1, :].to_broadcast([128, 4, 128]), ALU.mult),
                         r=[Esb, CBmb], w=[MTb])
                    for e in range(e0, e0 + 4):
                        cs = slice((e - e0) * 64, (e - e0) * 64 + 64)
                        cs2 = slice(256 + (e - e0) * 64, 256 + (e - e0) * 64 + 64)
                        s.op("pe", lambda: nc.tensor.matmul(YTp[:, cs], MT[:, e - e0, :], xdt[:, e, :], start=True, stop=True),
                             r=[MTb, xdtb], w=[YTpb])
                        s.op("pe", lambda: nc.tensor.matmul(YTp[:, cs2], Btok[:, g, :], xdts[:, e, :], start=True, stop=True),
                             r=[Btokb, xdtsb], w=[YTpb])
                    s.op("act", lambda: nc.scalar.copy(YT[:, :, e0:e0 + 4, :], YTp[:, :].rearrange("p (a e q) -> p a e q", a=2, e=4)),
                         r=[YTpb], w=[YTb])

            return F

        def back(S, item, F):
            kind, idx, tok0, ch, nchunk = item
            t0 = tok0 + ch * 128
            sidx = idx if kind == "p" else NB + idx
            Mr = S.Mr
            H, Hb_ = S.H
            Hbc, Hbcb = S.Hb
            acs, acsb = F.acs
            BCT, BCTb = F.BCT
            YT, YTb = F.YT
            Fp, Fpb = S.Fp
            eacs, cdec = acs[:, 2, :], acs[:, 4, :]
            if ch == 0:
                if kind == "p":
                    s.op("dve", lambda: nc.vector.memset(H[:, :, :], 0.0), w=[Hb_])
                else:
                    s.dma("sp", H[:, :, :], state_d[idx].rearrange("n (e p) -> n e p", e=16), w=[Hb_])
                s.op("act", lambda: nc.scalar.copy(Hbc[:, :, :], H[:, :, :]), r=[Hb_], w=[Hbcb])
            yt, ytb = S.yt
            for g in range(2):
                hs = slice(g * 8, g * 8 + 8)
                for e in range(g * 8, g * 8 + 8):
                    cs = slice((e % 8) * 64, (e % 8) * 64 + 64)
                    s.op("pe", lambda: nc.tensor.matmul(Fp[:, cs], BCT[:, 2 + g, :], Hbc[:, e, :], start=True, stop=True),
                         r=[BCTb, Hbcb], w=[Fpb])
                fv = Fp[:, :].rearrange("p (e q) -> p e q", e=8)
                s.op("dve", lambda: nc.vector.tensor_tensor(yt[:, hs, :], fv, eacs[:, hs].unsqueeze(2).to_broadcast([128, 8, 64]), ALU.mult),
                     r=[Fpb, acsb], w=[ytb])
                s.op("dve", lambda: nc.vector.tensor_tensor(H[:, hs, :], H[:, hs, :], cdec[:, hs].unsqueeze(2).to_broadcast([128, 8, 64]), ALU.mult),
                     r=[Hb_, acsb], w=[Hb_])
                s.op("dve", lambda: nc.vector.tensor_tensor(H[:, hs, :], H[:, hs, :], YT[:, 1, hs, :], ALU.add), r=[Hb_, YTb], w=[Hb_])
            s.op("act", lambda: nc.scalar.copy(Hbc[:, :, :], H[:, :, :]), r=[Hb_], w=[Hbcb])
            s.op("pool", lambda: nc.gpsimd.tensor_tensor(yt[:, :, :], yt[:, :, :], YT[:, 0, :, :], ALU.add), r=[ytb, YTb], w=[ytb])
            s.op("pool", lambda: nc.gpsimd.tensor_tensor(yt[:, :, :], yt[:, :, :], F.dsk[0][:, :, :], ALU.add), r=[ytb, F.dsk[1]], w=[ytb])
            zt, ztb = F.zt
            s.op("act", lambda: nc.scalar.activation(out=zt[:, :], in_=zt[:, :], func=AF.Silu), r=[ztb], w=[ztb])
            ytf = yt[:, :, :].rearrange("p e q -> p (e q)")
            s.op("dve", lambda: nc.vector.tensor_tensor(ytf, ytf, zt[:, :], ALU.mult), r=[ytb, ztb], w=[ytb])
            yn, ynb = S.yn
            jk, jkb = S.junk
            for g in range(2):
                gs = slice(g * 512, (g + 1) * 512)
                ss, ssb = S.stat.next()
                s.op("act", lambda: nc.scalar.activation(out=jk[:, 0:512], in_=ytf[:, gs], func=AF.Square, accum_out=ss[:, 0:1]),
                     r=[ytb], w=[jkb, ssb])
                rstd_from_ss(c, ss, ssb, 512)
                s.op("dve", lambda: nc.vector.scalar_tensor_tensor(yn[:, gs], ytf[:, gs], ss[:, 2:3], gss[:, gs], ALU.mult, ALU.mult),
                     r=[ytb, ssb, gssb], w=[ynb])
            yT, yTb = S.yT
            for g in range(2):
                m, mb = Mr.next()
                for j in range(g * 4, g * 4 + 4):
                    s.op("pe", lambda: nc.tensor.transpose(m[:, (j % 4) * 128:(j % 4 + 1) * 128], yn[:, j * 128:(j + 1) * 128], c.ident_f[:, :]),
                         r=[ynb, c.constb], w=[mb])
                evac(c, yT[:, g * 4:(g + 1) * 4, :], m[:, :].rearrange("p (j t) -> p j t", j=4), [mb], [yTb])
            if kind == "p":
                s.dma("sp", mixedT[:, 4:12, t0:t0 + 128], yT[:, :, :], r=[yTb])
            else:
                s.dma("sp", mixedT[:, 4:12, t0:t0 + 1], yT[:, :, 0:1], r=[yTb], slow=True)
            if ch == nchunk - 1:
                s.dma("sp", ssm_out[sidx].rearrange("n (e p) -> n e p", e=16), H[:, :, :], r=[Hb_])

        streams = [make_stream(k) for k in range(NSTREAM)]
        work = [[] for _ in range(NSTREAM)]
        for b in range(NB):
            for ch in range(16):
                work[b % NSTREAM].append(("p", b, b * 2048, ch, 16))
        for i in range(16):
            work[(i + NB) % NSTREAM].append(("s", i, NTP + i, 0, 1))

        def runner(k):
            def run():
                items = work[k]
                if not items:
                    return
                Fc = front(streams[k], items[0])
                for n_, item in enumerate(items):
                    Fn = front(streams[k], items[n_ + 1]) if n_ + 1 < len(items) else None
                    back(streams[k], item, Fc)
                    Fc = Fn
            return run
        Interleave(s, [runner(k) for k in range(NSTREAM)]).run()


def outcross_stage(c, x1, mixedT, gcol, wout_d, wcq_d, wco_d, memKT, memV, cache_mk, cache_mv, x3, tiles, NTP):
    s, nc = c.s, c.nc
    with Stage(c) as st:
        st.wstage = st.ring("wst", [128, 1024], F32, 3)
        wO, wOb = st.sb("wO", [128, 12, D], BF16)
        wQ, wQb = st.sb("wQ", [128, 8, D], BF16)
        wC, wCb = st.sb("wCo", [128, 8, D], BF16)
        xtr = st.ring("xt", [128, 4, 1024], F32, 2)
        mTr = st.ring("mT", [128, 12, 512], BF16, 1)
        hTr = st.ring("hT", [128, 8, 512], BF16, 2)
        qxr = st.ring("qx", [128, 8, 512], BF16, 1)
        oTr = st.ring("oTn", [128, 8, 512], BF16, 1)
        ptr = st.ring("PT", [128, 512], BF16, 4)
        rlr = st.ring("rl", [128, 512], F32, 2)
        ktr = st.ring("KTm", [128, 8, 256], BF16, 2)
        vmr = st.ring("Vm", [128, 2, 1024], BF16, 2)
        ckr = st.ring("ck", [128, 2, 1024], F32, 2)
        cbr = st.ring("ckb", [128, 2, 1024], BF16, 1)
        norm_bufs(st)
        psr = st.psring("ps", [128, 512], F32, 6)
        load_weight(c, st, wout_d, wO, wOb, 12, D)
        load_weight(c, st, wcq_d, wQ, wQb, 8, D)
        load_weight(c, st, wco_d, wC, wCb, 8, D)

        def cross_core(KTm, KTmb, Vm, Vmb, qx, qxb, oT, oTb, cols, n):
            for h in range(4):
                PTs = []
                for mb in range(2):
                    ps, psb = psr.next()
                    for dc in range(2):
                        s.op("pe", lambda: nc.tensor.matmul(ps[:, 0:n], KTm[:, 2 * h + dc, mb * 128:(mb + 1) * 128],
                                                            qx[:, 2 * h + dc, cols], start=(dc == 0), stop=(dc == 1)),
                             r=[KTmb, qxb], w=[psb])
                    PT, PTb = ptr.next()
                    s.op("act", lambda: nc.scalar.activation(out=PT[:, 0:n], in_=ps[:, 0:n], func=AF.Exp), r=[psb], w=[PTb])
                    PTs.append((PT, PTb))
                ps, psb = psr.next()
                for mb in range(2):
                    s.op("pe", lambda: nc.tensor.matmul(ps[:, 0:n], c.ones_bf[:, :], PTs[mb][0][:, 0:n],
                                                        start=(mb == 0), stop=(mb == 1)), r=[c.constb, PTs[mb][1]], w=[psb])
                rl, rlb = rlr.next()
                s.op("dve", lambda: nc.vector.reciprocal(rl[:, 0:n], ps[:, 0:n]), r=[psb], w=[rlb])
                for dc in range(2):
                    ps, psb = psr.next()
                    for mb in range(2):
                        s.op("pe", lambda: nc.tensor.matmul(ps[:, 0:n], Vm[:, mb, (2 * h + dc) * 128:(2 * h + dc + 1) * 128],
                                                            PTs[mb][0][:, 0:n], start=(mb == 0), stop=(mb == 1)),
                             r=[Vmb, PTs[mb][1]], w=[psb])
                    s.op("dve", lambda: nc.vector.tensor_tensor(oT[:, 2 * h + dc, cols], ps[:, 0:n], rl[:, 0:n], ALU.mult),
                         r=[psb, rlb], w=[oTb])

        cur_b = -1
        KTm = Vm = None
        for (t0, nsub) in tiles:
            N = nsub * 128
            xt, xtb = xtr.next()
            s.dma("sp", xt[:, 0:nsub, :], x1[t0:t0 + N, :].rearrange("(s p) d -> p s d", p=128), w=[xtb])
            mT, mTb = mTr.next()
            s.dma("sp", mT[:, :, 0:N], mixedT[:, :, t0:t0 + N], w=[mTb])
            for sub in range(nsub):
                for hf in range(2):
                    ps, psb = psr.next()
                    mm_tm(c, mT, mTb, sub, wO, wOb, hf * 512, 512, ps, psb, KC=12)
                    xs_ = xt[:, sub, hf * 512:(hf + 1) * 512]
                    s.op("dve", lambda: nc.vector.tensor_tensor(xs_, xs_, ps[:, :], ALU.add), r=[psb, xtb], w=[xtb])
            hT, hTb = hTr.next()
            norm_transpose(c, st, xt, xtb, nsub, gcol, hT, hTb)
            qx, qxb = qxr.next()
            for j in range(8):
                ps, psb = psr.next()
                mm_fm(c, wQ, wQb, j * 128, hT, hTb, N, ps, psb)
                evac(c, qx[:, j, 0:N], ps[:, 0:N], [psb], [qxb], scale=1.0 / 16.0)
            oT, oTb = oTr.next()
            if t0 < NTP:
                b = t0 // 2048
                if b != cur_b:
                    cur_b = b
                    KTm, KTmb = ktr.next()
                    Vm, Vmb = vmr.next()
                    s.dma("sp", KTm[:, :, :], memKT[:, :, b * 256:(b + 1) * 256], w=[KTmb])
                    s.dma("sp", Vm[:, :, :], memV[b * 256:(b + 1) * 256, :].rearrange("(mb m) d -> m mb d", m=128), w=[Vmb])
                cross_core(KTm, KTmb, Vm, Vmb, qx, qxb, oT, oTb, slice(0, N), N)
            else:
                s.op("pool", lambda: nc.gpsimd.memset(oT[:, :, 0:N], 0.0), w=[oTb])
                for i in range(16):
                    ck, ckb_ = ckr.next()
                    s.dma("sp", ck[:, :, :], cache_mk[i].rearrange("(mb m) d -> m mb d", m=128), w=[ckb_])
                    cb_, cbb_ = cbr.next()
                    s.op("pool", lambda: nc.gpsimd.tensor_copy(cb_[:, :, :], ck[:, :, :]), r=[ckb_], w=[cbb_])
                    KTs, KTsb = ktr.next()
                    for mb in range(2):
                        pt, ptb = st.pst.next()
                        for j in range(8):
                            s.op("pe", lambda: nc.tensor.transpose(pt[:, j, :], cb_[:, mb, j * 128:(j + 1) * 128], c.ident_bf[:, :]),
                                 r=[cbb_, c.constb], w=[ptb])
                        evac(c, KTs[:, :, mb * 128:(mb + 1) * 128], pt[:, :, :], [ptb], [KTsb])
                    cv, cvb_ = ckr.next()
                    s.dma("sp", cv[:, :, :], cache_mv[i].rearrange("(mb m) d -> m mb d", m=128), w=[cvb_])
                    Vs, Vsb = vmr.next()
                    s.op("pool", lambda: nc.gpsimd.tensor_copy(Vs[:, :, :], cv[:, :, :]), r=[cvb_], w=[Vsb])
                    cross_core(KTs, KTsb, Vs, Vsb, qx, qxb, oT, oTb, slice(i, i + 1), 1)
            for sub in range(nsub):
                for hf in range(2):
                    ps, psb = psr.next()
                    mm_tm(c, oT, oTb, sub, wC, wCb, hf * 512, 512, ps, psb)
                    xs_ = xt[:, sub, hf * 512:(hf + 1) * 512]
                    s.op("dve", lambda: nc.vector.tensor_tensor(xs_, xs_, ps[:, :], ALU.add), r=[psb, xtb], w=[xtb])
            s.dma("pool", x3[t0:t0 + N, :].rearrange("(s p) d -> p s d", p=128), xt[:, 0:nsub, :], r=[xtb])


ALL_STAGES = ("memkv", "ffn1", "inproj", "attn", "ssd", "outcross", "ffn2")


def build(NB=4, stages=ALL_STAGES, dbg=()):
    nc = bass.Bass("TRN2", target_bir_lowering=False)
    c = Ctx()
    c.nc = nc
    c.s = Sch(nc)
    s = c.s
    NTP = NB * 2048
    NTOK = NTP + 128
    NBM = NB * 256
    NSEQ = NB + 16
    tiles = [(t * 512, 4) for t in range(NTP // 512)] + [(NTP, 1)]
    pb, bmask_np, negm_np = bias_consts()
    NPB = len(pb)

    def din(name, shape):
        return nc.dram_tensor(name, shape, F32, kind="ExternalInput").ap()

    def dout(name, shape):
        return nc.dram_tensor(name, shape, F32, kind="ExternalOutput").ap()

    def scr(name, shape, dt=F32):
        if name in dbg:
            return nc.dram_tensor(name, shape, dt, kind="ExternalOutput").ap()
        return nc.dram_tensor(name, shape, dt).ap()

    x_all = din("x_all", [NTOK, D])
    gcols_d = din("gcols", [128, 6, 8])
    gfin_d = din("gfin", [1, D])
    w1g, w1u, w1d = din("w1_gate", [128, 8, DFF]), din("w1_up", [128, 8, DFF]), din("w1_down", [128, 22, D])
    w2g, w2u, w2d = din("w2_gate", [128, 8, DFF]), din("w2_up", [128, 8, DFF]), din("w2_down", [128, 22, D])
    win_d = din("w_in", [128, 8, DIN])
    wout_d = din("w_out", [128, 12, D])
    wck_d, wcv_d, wcq_d, wco_d = (din(n, [128, 8, D]) for n in ("w_ck", "w_cv", "w_cq", "w_co"))
    mem_in = din("mem_in", [NBM, D])
    relb_d = din("rel_bias", [1, 256])
    bmask_d = din("bmask", [NPB, 128, 256])
    negm_d = din("negm", [3, 128, 256])
    convw_d = din("conv_w", [128, 12, 4])
    convb_d = din("conv_b", [128, 12])
    vec16_d = din("vec16", [1, 3, 16])
    gssm_d = din("g_ssm", [1, D])
    tri_d = din("tri", [128, 2, 128])
    ident_d = din("ident", [128, 128])
    cache_k = din("cache_k", [16, 2048, 512])
    cache_v = din("cache_v", [16, 2048, 512])
    cconv_d = din("cconv", [16, 128, 12, 3])
    state_d = din("state", [16, 128, 1024])
    cache_mk = din("cache_mk", [16, 256, D])
    cache_mv = din("cache_mv", [16, 256, D])

    y_out = dout("y_out", [NTOK, D])
    kT_out = dout("kT_out", [128, 4, NTOK])
    v_out = dout("v_out", [NTOK, 512])
    conv_out = dout("conv_out", [NSEQ, 128, 12, 3])
    ssm_out = dout("ssm_out", [NSEQ, 128, 1024])
    mk_out = dout("mk_out", [NBM, D])
    mv_out = dout("mv_out", [NBM, D])

    x1 = scr("x1", [NTOK, D])
    x3 = scr("x3", [NTOK, D])
    x4 = scr("x4", [NTOK, D])
    hT_scr = scr("hT_scr", [128, 8, NTOK], BF16)
    memKT = scr("memKT", [128, 8, NBM], BF16)
    memV = scr("memV", [NBM, D], BF16)
    qT = scr("qT", [128, 4, NTOK], BF16)
    kT = scr("kT", [128, 4, NTOK], BF16)
    v_bf = scr("v_bf", [NTOK, 512], BF16)
    z_scr = scr("z_scr", [NTOK, D])
    xbcT = scr("xbcT", [128, 12, NTOK])
    dt_scr = scr("dt_scr", [NTOK, 16])
    mixedT = scr("mixedT", [128, 12, NTOK], BF16)

    c.dbufs = {}

    def db(key):
        if key not in c.dbufs:
            c.dbufs[key] = Buf(str(key))
        return c.dbufs[key]
    c.db = db

    c.constb = Buf("const")
    c.ident_f = nc.alloc_sbuf_tensor("ident_f", [128, 128], F32)
    c.ident_bf = nc.alloc_sbuf_tensor("ident_bf", [128, 128], BF16)
    c.ones_bf = nc.alloc_sbuf_tensor("ones_bf", [128, 128], BF16)
    c.gcols = nc.alloc_sbuf_tensor("gcols_sb", [128, 6, 8], F32)
    c.gfin = nc.alloc_sbuf_tensor("gfin_sb", [128, D], F32)
    s.dma("sp", c.ident_f[:, :], ident_d[:, :], w=[c.constb])
    s.dma("sp", c.gcols[:, :, :], gcols_d[:, :, :], w=[c.constb])
    s.dma("sp", c.gfin[:, :], gfin_d[0:1, :].partition_broadcast(128), w=[c.constb])
    s.op("dve", lambda: nc.vector.tensor_copy(c.ident_bf[:, :], c.ident_f[:, :]), r=[c.constb], w=[c.constb])
    s.op("dve", lambda: nc.vector.memset(c.ones_bf[:, :], 1.0), w=[c.constb])
    s.barrier()

    if "memkv" in stages:
        memkv_stage(c, mem_in, c.gcols[:, 4, :], wck_d, wcv_d, mk_out, mv_out, memKT, memV, NBM)
    if "ffn1" in stages:
        ffn_stage(c, x_all, x1, c.gcols[:, 0, :], w1g, w1u, w1d, hT_scr, tiles, "f1")
    if "inproj" in stages:
        inproj_stage(c, x1, c.gcols[:, 1, :], win_d, qT, kT, kT_out, v_out, v_bf, z_scr, xbcT, dt_scr, tiles)
    if "attn" in stages:
        attn_stage(c, qT, kT, v_bf, mixedT, relb_d, bmask_d, negm_d, pb, cache_k, cache_v, NB, NTP)
    if "ssd" in stages:
        ssd_stage(c, xbcT, dt_scr, z_scr, mixedT, convw_d, convb_d, vec16_d, gssm_d, tri_d,
                  cconv_d, state_d, conv_out, ssm_out, NB, NTP)
    if "outcross" in stages:
        outcross_stage(c, x1, mixedT, c.gcols[:, 2, :], wout_d, wcq_d, wco_d, memKT, memV, cache_mk, cache_mv, x3, tiles, NTP)
    if "ffn2" in stages:
        ffn_stage(c, x3, x4, c.gcols[:, 3, :], w2g, w2u, w2d, hT_scr, tiles, "f2", final=(c.gfin[:, :], y_out))
    s.finish()
    return nc


def tile_w(w, kc):
    w = np.asarray(w, np.float32)
    K, F = w.shape
    return np.ascontiguousarray(w.reshape(kc, 128, F).transpose(1, 0, 2))


def gcol(g):
    return np.asarray(g, np.float32).reshape(8, 128).T


def tri_consts():
    j = np.arange(128)[:, None]
    l = np.arange(128)[None, :]
    return np.ascontiguousarray(np.stack([(j <= l), (j > l)], axis=1).astype(np.float32))


def shared_inputs(inp):
    f = lambda k: np.asarray(inp[k], np.float32)
    pb, bmask, negm = bias_consts()
    gc = np.zeros((128, 6, 8), np.float32)
    for n, k in enumerate(("g_ffn1", "g_mix", "g_cross", "g_ffn2", "g_mem")):
        gc[:, n, :] = gcol(f(k)[0])
    cw = f("conv_w")[0]
    sh = {
        "gcols": gc, "gfin": f("g_final").reshape(1, D),
        "w1_gate": tile_w(f("w1_gate")[0], 8), "w1_up": tile_w(f("w1_up")[0], 8), "w1_down": tile_w(f("w1_down")[0], 22),
        "w2_gate": tile_w(f("w2_gate")[0], 8), "w2_up": tile_w(f("w2_up")[0], 8), "w2_down": tile_w(f("w2_down")[0], 22),
        "w_in": tile_w(f("w_in")[0], 8), "w_out": tile_w(f("w_out")[0], 12),
        "w_ck": tile_w(f("w_ck")[0], 8), "w_cv": tile_w(f("w_cv")[0], 8),
        "w_cq": tile_w(f("w_cq")[0], 8), "w_co": tile_w(f("w_co")[0], 8),
        "rel_bias": f("rel_bias").reshape(1, 256),
        "bmask": bmask, "negm": negm,
        "conv_w": np.ascontiguousarray(cw.reshape(4, 12, 128).transpose(2, 1, 0)),
        "conv_b": np.ascontiguousarray(f("conv_b")[0].reshape(12, 128).T),
        "vec16": np.stack([f("dt_bias")[0], f("a_log")[0], f("d_skip")[0]])[None],
        "g_ssm": f("g_ssm").reshape(1, D),
        "tri": tri_consts(), "ident": np.eye(128, dtype=np.float32),
    }
    return sh


def core_inputs(inp, core, NB):
    f = lambda k: np.asarray(inp[k], np.float32)
    xp = f("x_prompt")[core * NB:(core + 1) * NB].reshape(NB * 2048, D)
    xs = f("x_sample")[core * 16:(core + 1) * 16].reshape(16, D)
    x_all = np.concatenate([xp, xs, np.zeros((112, D), np.float32)], 0)
    sl = slice(core * 16, (core + 1) * 16)
    cc = f("cache_conv")[0, sl]
    st = f("state_ssm")[0, sl]
    return {
        "x_all": x_all,
        "mem_in": f("mem_prompt")[core * NB:(core + 1) * NB].reshape(NB * 256, D),
        "cache_k": f("cache_win_k")[0, sl].reshape(16, 2048, 512),
        "cache_v": f("cache_win_v")[0, sl].reshape(16, 2048, 512),
        "cconv": np.ascontiguousarray(cc.reshape(16, 3, 12, 128).transpose(0, 3, 2, 1)),
        "state": np.ascontiguousarray(st.reshape(16, 1024, 128).transpose(0, 2, 1)),
        "cache_mk": f("cache_mem_k")[0, sl].reshape(16, 256, D),
        "cache_mv": f("cache_mem_v")[0, sl].reshape(16, 256, D),
    }


def assemble(results, NB, ncores):
    NTP = NB * 2048
    B = NB * ncores
    y_p = np.empty((B, 2048, D), np.float32)
    y_s = np.empty((16 * ncores, 1, D), np.float32)
    wk_p = np.empty((1, B, 2048, 8, 64), np.float32)
    wv_p = np.empty((1, B, 2048, 8, 64), np.float32)
    cv_p = np.empty((1, B, 3, 1536), np.float32)
    ss_p = np.empty((1, B, 16, 64, 128), np.float32)
    mk_p = np.empty((1, B, 256, 4, 256), np.float32)
    mv_p = np.empty((1, B, 256, 4, 256), np.float32)
    wk_s = np.empty((1, 16 * ncores, 1, 8, 64), np.float32)
    wv_s = np.empty((1, 16 * ncores, 1, 8, 64), np.float32)
    cv_s = np.empty((1, 16 * ncores, 3, 1536), np.float32)
    ss_s = np.empty((1, 16 * ncores, 16, 64, 128), np.float32)
    for cidx, r in enumerate(results):
        y = np.asarray(r["y_out"])
        ktok = np.asarray(r["kT_out"]).transpose(2, 1, 0).reshape(-1, 8, 64)
        vtok = np.asarray(r["v_out"]).reshape(-1, 8, 64)
        cv = np.asarray(r["conv_out"]).transpose(0, 3, 2, 1).reshape(-1, 3, 1536)
        ss = np.asarray(r["ssm_out"]).transpose(0, 2, 1).reshape(-1, 16, 64, 128)
        bs = slice(cidx * NB, (cidx + 1) * NB)
        ts = slice(cidx * 16, (cidx + 1) * 16)
        y_p[bs] = y[:NTP].reshape(NB, 2048, D)
        y_s[ts, 0] = y[NTP:NTP + 16]
        wk_p[0, bs] = ktok[:NTP].reshape(NB, 2048, 8, 64)
        wv_p[0, bs] = vtok[:NTP].reshape(NB, 2048, 8, 64)
        wk_s[0, ts, 0] = ktok[NTP:NTP + 16]
        wv_s[0, ts, 0] = vtok[NTP:NTP + 16]
        cv_p[0, bs] = cv[:NB]
        cv_s[0, ts] = cv[NB:]
        ss_p[0, bs] = ss[:NB]
        ss_s[0, ts] = ss[NB:]
        mk_p[0, bs] = np.asarray(r["mk_out"]).reshape(NB, 256, 4, 256)
        mv_p[0, bs] = np.asarray(r["mv_out"]).reshape(NB, 256, 4, 256)
    return (y_p, y_s, wk_p, wv_p, cv_p, ss_p, mk_p, mv_p, wk_s, wv_s, cv_s, ss_s)


def kernel(**inputs):
    NB = 4
    nc = build(NB=NB)
    sh = shared_inputs(inputs)
    in_maps = []
    for core in range(NCORES):
        m = dict(sh)
        m.update(core_inputs(inputs, core, NB))
        in_maps.append(m)
    res = run_bass_kernel_spmd(nc, in_maps, core_ids=list(range(NCORES)))
    return assemble(res.results, NB, NCORES)
```

```python
import numpy as np
import concourse.bass as bass
import concourse.mybir as mybir
from concourse.bass_utils import run_bass_kernel_spmd

F32 = mybir.dt.float32
BF16 = mybir.dt.bfloat16
AF = mybir.ActivationFunctionType
ALU = mybir.AluOpType
AX = mybir.AxisListType

D = 1024
DFF = 2816
NCORES = 8
EPS = 1e-6


class Buf:
    __slots__ = ("name", "w", "rs", "excl")

    def __init__(self, name="", excl=False):
        self.name = name
        self.w = None
        self.rs = {}
        self.excl = excl


class Sch:
    ND = 12

    def __init__(self, nc):
        self.nc = nc
        self.eng = {"pe": nc.tensor, "act": nc.scalar, "dve": nc.vector, "pool": nc.gpsimd, "sp": nc.sync}
        self.sem = {k: nc.alloc_semaphore("s_" + k) for k in self.eng}
        self.cnt = {k: 0 for k in self.eng}
        self.known = {k: {} for k in self.eng}
        self.dsem = {q: [[nc.alloc_semaphore(f"d_{q}{i}"), 0] for i in range(self.ND)]
                     for q in ("sp", "pool", "act")}
        self.dnext = {q: 0 for q in self.dsem}
        self.il = None

    def _deps(self, own, r, w):
        toks = []
        for b in r:
            if b.w is not None:
                toks.append(b.w)
        for b in w:
            if b.w is not None and b.w[0] is not own:
                toks.append(b.w)
            for sem, v in b.rs.items():
                if sem is not own:
                    toks.append((sem, v))
        return toks

    def _wait(self, e, toks):
        need = {}
        for sem, v in toks:
            if v > need.get(sem, 0):
                need[sem] = v
        kn = self.known[e]
        for sem, v in need.items():
            if kn.get(sem, 0) >= v:
                continue
            self.eng[e].wait_ge(sem, v)
            kn[sem] = v

    def _mark(self, tok, r, w):
        for b in r:
            if b.rs.get(tok[0], 0) < tok[1]:
                b.rs[tok[0]] = tok[1]
        for b in w:
            b.w = tok
            b.rs = {}

    def op(self, e, ins_fn, r=(), w=()):
        own = self.sem[e]
        if any(b.excl for b in r):
            w = list(w) + [b for b in r if b.excl]
            r = [b for b in r if not b.excl]
        self._wait(e, self._deps(own, r, w))
        ins = ins_fn()
        self.cnt[e] += 1
        ins.then_inc(own, 1)
        self._mark((own, self.cnt[e]), r, w)
        if self.il is not None:
            self.il.switch()
        return ins

    def dma(self, q, out, in_, r=(), w=(), slow=False):
        pool = self.dsem[q]
        i = self.dnext[q]
        self.dnext[q] = (i + 1) % len(pool)
        slot = pool[i]
        toks = self._deps(None, r, w)
        if slot[1] > 0:
            toks.append((slot[0], slot[1]))
        self._wait(q, toks)
        ins = (self.eng[q].dma_start(out=out, in_=in_, allow_slow_non_contiguous=True) if slow
               else self.eng[q].dma_start(out=out, in_=in_))
        slot[1] += 16
        ins.then_inc(slot[0], 16)
        self._mark((slot[0], slot[1]), r, w)
        if self.il is not None:
            self.il.switch()
        return ins

    def finish(self):
        toks = []
        for q in self.dsem:
            for sem, v in self.dsem[q]:
                if v > 0:
                    toks.append((sem, v))
        for k in self.eng:
            if self.cnt[k] > 0:
                toks.append((self.sem[k], self.cnt[k]))
        self._wait("sp", toks)


class Ring:
    def __init__(self, items):
        self.items = items
        self.i = 0

    def next(self):
        it = self.items[self.i]
        self.i = (self.i + 1) % len(self.items)
        return it


class Ctx:
    pass


def sb(nc, name, shape, dt):
    t = nc.alloc_sbuf_tensor(name, shape, dt)
    return t, Buf(name)


def sb_ring(nc, name, shape, dt, n):
    return Ring([sb(nc, f"{name}{i}", shape, dt) for i in range(n)])


import threading


class Interleave:
    def __init__(self, sch, fns):
        self.sch = sch
        self.fns = fns
        self.ev = [threading.Event() for _ in fns]
        self.alive = [True] * len(fns)
        self.idx = {}
        self.exc = None

    def _next(self, i):
        n = len(self.fns)
        for d in range(1, n + 1):
            j = (i + d) % n
            if j != i and self.alive[j]:
                return j
        return None

    def _wrap(self, i):
        self.idx[threading.get_ident()] = i
        self.ev[i].wait()
        self.ev[i].clear()
        try:
            if self.exc is None:
                self.fns[i]()
        except BaseException as e:
            self.exc = e
        finally:
            self.alive[i] = False
            j = self._next(i)
            if j is not None:
                self.ev[j].set()

    def switch(self):
        i = self.idx.get(threading.get_ident())
        if i is None:
            return
        if self.exc is not None:
            raise RuntimeError("sibling stream failed")
        j = self._next(i)
        if j is None:
            return
        self.ev[j].set()
        self.ev[i].wait()
        self.ev[i].clear()

    def run(self):
        ths = [threading.Thread(target=self._wrap, args=(i,)) for i in range(len(self.fns))]
        self.sch.il = self
        for t in ths:
            t.start()
        self.ev[0].set()
        for t in ths:
            t.join()
        self.sch.il = None
        if self.exc is not None:
            raise self.exc
from contextlib import ExitStack

DIN = 4112
PATTERNS = ((128, 1), (512, 4), (2048, 16))
NEG = -30000.0


def _barrier(self):
    toks = []
    for q in self.dsem:
        for sem, v in self.dsem[q]:
            if v > 0:
                toks.append((sem, v))
    for k in self.eng:
        if self.cnt[k] > 0:
            toks.append((self.sem[k], self.cnt[k]))
    for e in self.eng:
        self._wait(e, [t for t in toks if t[0] is not self.sem[e]])


Sch.barrier = _barrier


class Stage:
    _n = 0

    def __init__(self, c):
        self.c = c
        self.es = ExitStack()

    def __enter__(self):
        self.es.__enter__()
        return self

    def __exit__(self, *a):
        if a[0] is None:
            self.c.s.barrier()
        return self.es.__exit__(*a)

    def sb(self, name, shape, dt):
        Stage._n += 1
        t = self.es.enter_context(self.c.nc.sbuf_tensor(f"{name}_{Stage._n}", shape, dt))
        return t, Buf(name)

    def ring(self, name, shape, dt, n):
        return Ring([self.sb(f"{name}{i}", shape, dt) for i in range(n)])

    def ps(self, name, shape, dt=F32):
        Stage._n += 1
        t = self.es.enter_context(self.c.nc.psum_tensor(f"{name}_{Stage._n}", shape, dt))
        return t, Buf(name, excl=True)

    def psring(self, name, shape, dt, n):
        return Ring([self.ps(f"{name}{i}", shape, dt) for i in range(n)])


def load_weight(c, st, dram_w, dst, dst_buf, kc_n, f_n, piece=1024):
    s, nc = c.s, c.nc
    for kc in range(kc_n):
        for f0 in range(0, f_n, piece):
            fw = min(piece, f_n - f0)
            stg, stb = st.wstage.next()
            s.dma("sp", stg[:, 0:fw], dram_w[:, kc, f0:f0 + fw], w=[stb])
            s.op("pool", lambda: nc.gpsimd.tensor_copy(dst[:, kc, f0:f0 + fw], stg[:, 0:fw]),
                 r=[stb], w=[dst_buf])


def rstd_from_ss(c, ss, ssb, n):
    s, nc = c.s, c.nc
    s.op("dve", lambda: nc.vector.tensor_scalar(ss[:, 1:2], ss[:, 0:1], 1.0 / n, EPS, ALU.mult, ALU.add),
         r=[ssb], w=[ssb])
    s.op("act", lambda: nc.scalar.activation(out=ss[:, 3:4], in_=ss[:, 1:2], func=AF.Sqrt), r=[ssb], w=[ssb])
    s.op("dve", lambda: nc.vector.reciprocal(ss[:, 2:3], ss[:, 3:4]), r=[ssb], w=[ssb])


def norm_part(c, st, xt, xtb, nsub):
    s, nc = c.s, c.nc
    xns = []
    for sub in range(nsub):
        ss, ssb = st.stat.next()
        jk, jkb = st.junk.next()
        s.op("act", lambda: nc.scalar.activation(out=jk[:, :], in_=xt[:, sub, :], func=AF.Square,
                                                 accum_out=ss[:, 0:1]), r=[xtb], w=[jkb, ssb])
        rstd_from_ss(c, ss, ssb, D)
        xn, xnb = st.xn.next()
        s.op("act", lambda: nc.scalar.activation(out=xn[:, :], in_=xt[:, sub, :], func=AF.Copy,
                                                 scale=ss[:, 2:3]), r=[xtb, ssb], w=[xnb])
        xns.append((xn, xnb))
    return xns


def transpose_part(c, st, xns, gcol, hT, hTb):
    s, nc = c.s, c.nc
    for sub, (xn, xnb) in enumerate(xns):
        pt, ptb = st.pst.next()
        for kc in range(8):
            s.op("pe", lambda: nc.tensor.transpose(pt[:, kc, :], xn[:, kc * 128:(kc + 1) * 128], c.ident_bf[:, :]),
                 r=[xnb, c.constb], w=[ptb])
        s.op("dve", lambda: nc.vector.tensor_tensor(
            hT[:, :, sub * 128:(sub + 1) * 128], pt[:, :, :],
            gcol.unsqueeze(2).to_broadcast([128, 8, 128]), ALU.mult),
            r=[ptb, c.constb], w=[hTb])


def norm_transpose(c, st, xt, xtb, nsub, gcol, hT, hTb):
    transpose_part(c, st, norm_part(c, st, xt, xtb, nsub), gcol, hT, hTb)


def norm_bufs(st):
    st.stat = st.ring("stat", [128, 4], F32, 8)
    st.junk = st.ring("junk", [128, 1024], BF16, 2)
    st.xn = st.ring("xn", [128, 1024], BF16, 4)
    st.pst = st.psring("pst", [128, 8, 128], BF16, 2)


def mm_fm(c, W, Wb, f0, hT, hTb, N, ps, psb, KC=8):
    s, nc = c.s, c.nc
    for kc in range(KC):
        s.op("pe", lambda: nc.tensor.matmul(ps[:, 0:N], W[:, kc, f0:f0 + 128], hT[:, kc, 0:N],
                                            start=(kc == 0), stop=(kc == KC - 1)), r=[Wb, hTb], w=[psb])


def mm_tm(c, hT, hTb, sub, W, Wb, c0, ncols, ps, psb, KC=8):
    s, nc = c.s, c.nc
    for kc in range(KC):
        s.op("pe", lambda: nc.tensor.matmul(ps[:, 0:ncols], hT[:, kc, sub * 128:(sub + 1) * 128], W[:, kc, c0:c0 + ncols],
                                            start=(kc == 0), stop=(kc == KC - 1)), r=[Wb, hTb], w=[psb])


def evac(c, out, in_, r, w, scale=None):
    s, nc = c.s, c.nc
    c.flip = not getattr(c, "flip", False)
    if c.flip:
        if scale is None:
            s.op("act", lambda: nc.scalar.copy(out, in_), r=r, w=w)
        else:
            s.op("act", lambda: nc.scalar.mul(out, in_, scale), r=r, w=w)
    else:
        if scale is None:
            s.op("dve", lambda: nc.vector.tensor_copy(out, in_), r=r, w=w)
        else:
            s.op("dve", lambda: nc.vector.tensor_scalar(out, in_, scale, None, ALU.mult), r=r, w=w)


def ffn_stage(c, x_in, x_out, gcol, wg_d, wu_d, wd_d, hT_scr, tiles, tag, final=None):
    s, nc = c.s, c.nc
    H = DFF // 2
    HC = H // 128
    with Stage(c) as st:
        st.wstage = st.ring("wst", [128, 1024], F32, 6)
        wA, wAb = st.sb("wA", [128, 8, H], BF16)
        wB, wBb = st.sb("wB", [128, 8, H], BF16)
        wC, wCb = st.sb("wC", [128, HC, D], BF16)
        xtr = st.ring("xt", [128, 4, 1024], F32, 3)
        hTr = st.ring("hT", [128, 8, 512], BF16, 2)
        actr = st.ring("actT", [128, HC, 512], BF16, 2)
        sgr = st.ring("sg", [128, 512], F32, 2)
        norm_bufs(st)
        psr = st.psring("ps", [128, 512], F32, 6)
        for half in range(2):
            load_weight(c, st, wg_d[:, :, half * H:(half + 1) * H], wA, wAb, 8, H)
            load_weight(c, st, wu_d[:, :, half * H:(half + 1) * H], wB, wBb, 8, H)
            load_weight(c, st, wd_d[:, half * HC:(half + 1) * HC, :], wC, wCb, HC, D)
            src = x_in if half == 0 else x_out
            def prep1(tile):
                t0, nsub = tile
                N = nsub * 128
                xt, xtb = xtr.next()
                xob = c.db((tag, "xo", t0))
                rd = [xob] if half == 1 else [c.db((tag, "xi", t0))]
                s.dma("sp", xt[:, 0:nsub, :], src[t0:t0 + N, :].rearrange("(s p) d -> p s d", p=128), r=rd, w=[xtb])
                hT, hTb = hTr.next()
                xns = None
                if half == 0:
                    xns = norm_part(c, st, xt, xtb, nsub)
                else:
                    s.dma("sp", hT[:, :, 0:N], hT_scr[:, :, t0:t0 + N], r=[c.db((tag, "hs", t0))], w=[hTb])
                return (xt, xtb, hT, hTb, xob, xns, t0, N)

            def prep2(P):
                xt, xtb, hT, hTb, xob, xns, t0, N = P
                if half == 0:
                    transpose_part(c, st, xns, gcol, hT, hTb)
                    s.dma("pool", hT_scr[:, :, t0:t0 + N], hT[:, :, 0:N], r=[hTb], w=[c.db((tag, "hs", t0))])

            nxt = prep1(tiles[0])
            prep2(nxt)
            for ti, (t0, nsub) in enumerate(tiles):
                N = nsub * 128
                xt, xtb, hT, hTb, xob = nxt[0:5]
                act, actb = actr.next()
                for j in range(HC):
                    if j == min(3, HC - 1) and ti + 1 < len(tiles):
                        nxt = prep1(tiles[ti + 1])
                    pg, pgb = psr.next()
                    mm_fm(c, wA, wAb, j * 128, hT, hTb, N, pg, pgb)
                    pu, pub = psr.next()
                    mm_fm(c, wB, wBb, j * 128, hT, hTb, N, pu, pub)
                    sg, sgb = sgr.next()
                    s.op("act", lambda: nc.scalar.activation(out=sg[:, 0:N], in_=pg[:, 0:N], func=AF.Silu), r=[pgb], w=[sgb])
                    s.op("dve", lambda: nc.vector.tensor_tensor(act[:, j, 0:N], sg[:, 0:N], pu[:, 0:N], ALU.mult),
                         r=[sgb, pub], w=[actb])
                if ti + 1 < len(tiles):
                    prep2(nxt)
                for sub in range(nsub):
                    for hf in range(2):
                        pd, pdb = psr.next()
                        mm_tm(c, act, actb, sub, wC, wCb, hf * 512, 512, pd, pdb, KC=HC)
                        s.op("dve", lambda: nc.vector.scalar_tensor_tensor(
                            xt[:, sub, hf * 512:(hf + 1) * 512], pd[:, :], 0.5, xt[:, sub, hf * 512:(hf + 1) * 512],
                            ALU.mult, ALU.add), r=[pdb, xtb], w=[xtb])
                if half == 1 and final is not None:
                    gfin, y_out = final
                    for sub in range(nsub):
                        ss, ssb = st.stat.next()
                        jk, jkb = st.junk.next()
                        s.op("act", lambda: nc.scalar.activation(out=jk[:, :], in_=xt[:, sub, :], func=AF.Square,
                                                                 accum_out=ss[:, 0:1]), r=[xtb], w=[jkb, ssb])
                        rstd_from_ss(c, ss, ssb, D)
                        s.op("dve", lambda: nc.vector.scalar_tensor_tensor(
                            xt[:, sub, :], xt[:, sub, :], ss[:, 2:3], gfin, ALU.mult, ALU.mult),
                            r=[xtb, ssb, c.constb], w=[xtb])
                    s.dma("pool", y_out[t0:t0 + N, :].rearrange("(s p) d -> p s d", p=128), xt[:, 0:nsub, :], r=[xtb])
                else:
                    s.dma("pool", x_out[t0:t0 + N, :].rearrange("(s p) d -> p s d", p=128), xt[:, 0:nsub, :], r=[xtb], w=[xob])


def memkv_stage(c, mem_in, gcol, wck_d, wcv_d, mk_out, mv_out, memKT, memV, NBM):
    s, nc = c.s, c.nc
    with Stage(c) as st:
        st.wstage = st.ring("wst", [128, 1024], F32, 3)
        wK, wKb = st.sb("wK", [128, 8, D], BF16)
        wV, wVb = st.sb("wV", [128, 8, D], BF16)
        xtr = st.ring("xt", [128, 4, 1024], F32, 2)
        hTr = st.ring("hT", [128, 8, 512], BF16, 2)
        tmr = st.ring("tm", [128, 1024], F32, 3)
        tbr = st.ring("tb", [128, 1024], BF16, 2)
        fmr = st.ring("fm", [128, 512], BF16, 3)
        norm_bufs(st)
        psr = st.psring("ps", [128, 512], F32, 6)
        load_weight(c, st, wck_d, wK, wKb, 8, D)
        load_weight(c, st, wcv_d, wV, wVb, 8, D)
        tiles = []
        t0 = 0
        while t0 < NBM:
            n = min(4, (NBM - t0) // 128)
            tiles.append((t0, n))
            t0 += n * 128
        for (t0, nsub) in tiles:
            N = nsub * 128
            xt, xtb = xtr.next()
            s.dma("sp", xt[:, 0:nsub, :], mem_in[t0:t0 + N, :].rearrange("(s p) d -> p s d", p=128), w=[xtb])
            hT, hTb = hTr.next()
            norm_transpose(c, st, xt, xtb, nsub, gcol, hT, hTb)
            for sub in range(nsub):
                r0 = t0 + sub * 128
                for (W, Wb, outd, bfd) in ((wK, wKb, mk_out, None), (wV, wVb, mv_out, memV)):
                    tm, tmb = tmr.next()
                    for hf in range(2):
                        ps, psb = psr.next()
                        mm_tm(c, hT, hTb, sub, W, Wb, hf * 512, 512, ps, psb)
                        evac(c, tm[:, hf * 512:(hf + 1) * 512], ps[:, :], [psb], [tmb])
                    s.dma("pool", outd[r0:r0 + 128, :], tm[:, :], r=[tmb])
                    if bfd is not None:
                        tb, tbb = tbr.next()
                        s.op("pool", lambda: nc.gpsimd.tensor_copy(tb[:, :], tm[:, :]), r=[tmb], w=[tbb])
                        s.dma("pool", bfd[r0:r0 + 128, :], tb[:, :], r=[tbb])
            for j in range(8):
                ps, psb = psr.next()
                mm_fm(c, wK, wKb, j * 128, hT, hTb, N, ps, psb)
                fm, fmb = fmr.next()
                evac(c, fm[:, 0:N], ps[:, 0:N], [psb], [fmb])
                s.dma("pool", memKT[:, j, t0:t0 + N], fm[:, 0:N], r=[fmb])


def inproj_stage(c, x1, gcol, win_d, qT, kT, kT_out, v_out, v_bf, z_scr, xbcT, dt_scr, tiles):
    s, nc = c.s, c.nc
    with Stage(c) as st:
        st.wstage = st.ring("wst", [128, 1024], F32, 3)
        Wa, Wab = st.sb("winA", [128, 8, 2560], BF16)
        Wc, Wcb = st.sb("winB", [128, 8, DIN - 2560], BF16)
        xtr = st.ring("xt", [128, 4, 1024], F32, 3)
        hTr = st.ring("hT", [128, 8, 512], BF16, 2)
        f32r = st.ring("f32", [128, 512], F32, 6)
        bfr = st.ring("bf", [128, 512], BF16, 6)
        zr = st.ring("zt", [128, 1024], F32, 2)
        dtr = st.ring("dtt", [128, 16], F32, 3)
        norm_bufs(st)
        psr = st.psring("ps", [128, 512], F32, 6)
        load_weight(c, st, win_d[:, :, 0:2560], Wa, Wab, 8, 2560)
        load_weight(c, st, win_d[:, :, 2560:DIN], Wc, Wcb, 8, DIN - 2560)
        def prep1(tile):
            t0, nsub = tile
            N = nsub * 128
            xt, xtb = xtr.next()
            s.dma("sp", xt[:, 0:nsub, :], x1[t0:t0 + N, :].rearrange("(s p) d -> p s d", p=128), w=[xtb])
            hT, hTb = hTr.next()
            return (hT, hTb, norm_part(c, st, xt, xtb, nsub))

        nxt = prep1(tiles[0])
        transpose_part(c, st, nxt[2], gcol, nxt[0], nxt[1])
        for ti, (t0, nsub) in enumerate(tiles):
            N = nsub * 128
            hT, hTb = nxt[0], nxt[1]
            for j in range(20):
                if j == 3 and ti + 1 < len(tiles):
                    nxt = prep1(tiles[ti + 1])
                f0 = j * 128 if j < 8 else 2560 + (j - 8) * 128
                ps, psb = psr.next()
                if f0 < 2560:
                    mm_fm(c, Wa, Wab, f0, hT, hTb, N, ps, psb)
                else:
                    mm_fm(c, Wc, Wcb, f0 - 2560, hT, hTb, N, ps, psb)
                if j < 4:
                    o, ob = bfr.next()
                    evac(c, o[:, 0:N], ps[:, 0:N], [psb], [ob], scale=0.125)
                    s.dma("pool", qT[:, j, t0:t0 + N], o[:, 0:N], r=[ob])
                elif j < 8:
                    o, ob = bfr.next()
                    evac(c, o[:, 0:N], ps[:, 0:N], [psb], [ob])
                    s.dma("pool", kT[:, j - 4, t0:t0 + N], o[:, 0:N], r=[ob])
                    o2, o2b = f32r.next()
                    evac(c, o2[:, 0:N], ps[:, 0:N], [psb], [o2b])
                    s.dma("pool", kT_out[:, j - 4, t0:t0 + N], o2[:, 0:N], r=[o2b])
                else:
                    o2, o2b = f32r.next()
                    evac(c, o2[:, 0:N], ps[:, 0:N], [psb], [o2b])
                    s.dma("pool", xbcT[:, j - 8, t0:t0 + N], o2[:, 0:N], r=[o2b])
            if ti + 1 < len(tiles):
                transpose_part(c, st, nxt[2], gcol, nxt[0], nxt[1])
            for sub in range(nsub):
                r0 = t0 + sub * 128
                ps, psb = psr.next()
                mm_tm(c, hT, hTb, sub, Wa, Wab, 1024, 512, ps, psb)
                o2, o2b = f32r.next()
                evac(c, o2[:, :], ps[:, :], [psb], [o2b])
                s.dma("pool", v_out[r0:r0 + 128, :], o2[:, :], r=[o2b])
                o, ob = bfr.next()
                evac(c, o[:, :], ps[:, :], [psb], [ob])
                s.dma("pool", v_bf[r0:r0 + 128, :], o[:, :], r=[ob])
                zt, ztb = zr.next()
                for hf in range(2):
                    ps, psb = psr.next()
                    mm_tm(c, hT, hTb, sub, Wa, Wab, 1536 + hf * 512, 512, ps, psb)
                    evac(c, zt[:, hf * 512:(hf + 1) * 512], ps[:, :], [psb], [ztb])
                s.dma("pool", z_scr[r0:r0 + 128, :], zt[:, :], r=[ztb])
                ps, psb = psr.next()
                mm_tm(c, hT, hTb, sub, Wc, Wcb, 4096 - 2560, 16, ps, psb)
                dtt, dtb = dtr.next()
                evac(c, dtt[:, :], ps[:, 0:16], [psb], [dtb])
                s.dma("pool", dt_scr[r0:r0 + 128, :], dtt[:, :], r=[dtb])


def t5_bucket_np(dist):
    d = np.maximum(np.asarray(dist, np.int64), 0)
    df = np.maximum(d, 1).astype(np.float32)
    large = 16 + (np.log(df / np.float32(16.0)) / np.float32(np.log(2048.0 / 16.0)) * np.float32(16.0)).astype(np.int32)
    large = np.minimum(large, 31)
    return np.where(d < 16, d, large).astype(np.int64)


def bias_consts():
    k = np.arange(128)[:, None, None]
    kb = np.arange(2)[None, :, None]
    q = np.arange(128)[None, None, :]
    delta = q + 128 * kb - k
    valid = (delta >= 0) & (delta <= 128)
    pb, masks, negm = [], [], []
    for pi, (wnd, dil) in enumerate(PATTERNS):
        bk = t5_bucket_np(delta * dil)
        negm.append(np.where(valid, 0.0, NEG).astype(np.float32).reshape(128, 256))
        for b in range(32):
            m = (valid & (bk == b))
            if m.any():
                pb.append((pi, b))
                masks.append(m.astype(np.float32).reshape(128, 256))
    return pb, np.stack(masks), np.stack(negm)


def attn_stage(c, qT, kT, v_bf, mixedT, relb_d, bmask_d, negm_d, pb, cache_k, cache_v, NB, NTP):
    s, nc = c.s, c.nc
    with Stage(c) as st:
        BT, BTb = st.sb("BT", [128, 24, 256], F32)
        rb, rbb = st.sb("rb", [128, 256], F32)
        mr = st.ring("bm", [128, 256], F32, 3)
        s.dma("sp", rb[:, :], relb_d[0:1, :].partition_broadcast(128), w=[rbb])
        for pi in range(3):
            for h in range(8):
                s.dma("sp", BT[:, pi * 8 + h, :], negm_d[pi], w=[BTb])
        for n, (pi, b) in enumerate(pb):
            m, mb = mr.next()
            s.dma("sp", m[:, :], bmask_d[n], w=[mb])
            for h in range(8):
                s.op("dve", lambda: nc.vector.scalar_tensor_tensor(
                    BT[:, pi * 8 + h, :], m[:, :], rb[:, b * 8 + h:b * 8 + h + 1], BT[:, pi * 8 + h, :],
                    ALU.mult, ALU.add), r=[mb, rbb, BTb], w=[BTb])
        BT4 = BT[:, :, :].rearrange("p n (kb q) -> p n kb q", kb=2)
        BTh, BThb = st.sb("BTh", [128, 24, 256], BF16)
        BTl, BTlb = st.sb("BTl", [128, 24, 256], BF16)
        btmp, btmpb = st.sb("btmp", [128, 8, 256], F32)
        for pi in range(3):
            ps_ = slice(pi * 8, pi * 8 + 8)
            s.op("dve", lambda: nc.vector.tensor_copy(BTh[:, ps_, :], BT[:, ps_, :]), r=[BTb], w=[BThb])
            s.op("dve", lambda: nc.vector.tensor_tensor(btmp[:, :, :], BT[:, ps_, :], BTh[:, ps_, :], ALU.subtract),
                 r=[BTb, BThb], w=[btmpb])
            s.op("dve", lambda: nc.vector.tensor_copy(BTl[:, ps_, :], btmp[:, :, :]), r=[btmpb], w=[BTlb])

        qTb, qTbb = st.sb("qTb", [128, 4, 2048], BF16)
        kTb, kTbb = st.sb("kTb", [128, 4, 2048], BF16)
        acc, accb = st.sb("acc", [128, 2, 4, 2048], F32)
        Vr = st.ring("Vt", [128, 512], BF16, 5)
        sbr = st.ring("sbs", [128, 2, 128], F32, 4)
        ptr = st.ring("PT", [128, 2, 128], BF16, 4)
        rcr = st.ring("rc", [128, 1024], F32, 1)
        obr = st.ring("ob", [128, 1024], BF16, 2)
        psS = st.psring("psS", [128, 512], F32, 3)
        psO = st.psring("psO", [128, 512], F32, 3)
        for b in range(NB):
            tok0 = b * 2048
            s.dma("sp", qTb[:, :, :], qT[:, :, tok0:tok0 + 2048], w=[qTbb])
            s.dma("sp", kTb[:, :, :], kT[:, :, tok0:tok0 + 2048], w=[kTbb])
            units = []
            for pi, (wnd, dil) in enumerate(PATTERNS):
                nblk = 2048 // dil // 128
                for r in range(dil):
                    for blk in range(nblk):
                        for h in range(8):
                            units.append((pi, dil, r, blk, h))
            vstate = {}

            def phaseA(u):
                pi, dil, r, blk, h = u
                cols = slice(r + dil * 128 * blk, r + dil * 128 * blk + dil * 127 + 1, dil)
                pcols = slice(r + dil * 128 * (blk - 1), r + dil * 128 * (blk - 1) + dil * 127 + 1, dil)
                if h == 0:
                    Vt, Vtb = Vr.next()
                    row0 = tok0 + r + dil * 128 * blk
                    s.dma("sp", Vt[:, :], v_bf[row0:row0 + 127 * dil + 1:dil, :], w=[Vtb])
                    prev = vstate.get((pi, r, blk - 1))
                    vstate[(pi, r, blk)] = (Vt, Vtb)
                    vstate[("cur", pi, r, blk)] = [(Vt, Vtb)] + ([prev] if blk > 0 else [])
                nkb = 2 if blk > 0 else 1
                pair = h // 2
                rows = slice(64 * (h % 2), 64 * (h % 2) + 64)
                Sp, Spb = psS.next()
                S = Sp[:, 0:256].rearrange("p (kb q) -> p kb q", kb=2)
                s.op("pe", lambda: nc.tensor.matmul(Sp[:, 0:nkb * 128], c.ident_bf[:, :], BTh[:, pi * 8 + h, 0:nkb * 128],
                                                    start=True, stop=False), r=[c.constb, BThb], w=[Spb])
                s.op("pe", lambda: nc.tensor.matmul(Sp[:, 0:nkb * 128], c.ident_bf[:, :], BTl[:, pi * 8 + h, 0:nkb * 128],
                                                    start=False, stop=False), r=[c.constb, BTlb], w=[Spb])
                s.op("pe", lambda: nc.tensor.matmul(S[:, 0, :], kTb[rows, pair, cols], qTb[rows, pair, cols],
                                                    start=False, stop=(nkb == 1)), r=[kTbb, qTbb], w=[Spb])
                if blk > 0:
                    s.op("pe", lambda: nc.tensor.matmul(S[:, 1, :], kTb[rows, pair, pcols], qTb[rows, pair, cols],
                                                        start=False, stop=True), r=[kTbb, qTbb], w=[Spb])
                PT, PTb = ptr.next()
                s.op("act", lambda: nc.scalar.activation(out=PT[:, 0:nkb, :], in_=S[:, 0:nkb, :], func=AF.Exp),
                     r=[Spb], w=[PTb])
                return (PT, PTb, cols, nkb, vstate[("cur", pi, r, blk)])

            def phaseB(u, A):
                pi, dil, r, blk, h = u
                PT, PTb, cols, nkb, vs = A
                pair = h // 2
                rows = slice(64 * (h % 2), 64 * (h % 2) + 64)
                Op, Opb = psO.next()
                OL = Op[:, 0:256].rearrange("p (a q) -> p a q", a=2)
                for kb, (vt, vtb) in enumerate(vs):
                    s.op("pe", lambda: nc.tensor.matmul(OL[:, 0, :], vt[:, pair * 128:(pair + 1) * 128], PT[:, kb, :],
                                                        start=(kb == 0), stop=(kb == nkb - 1)), r=[vtb, PTb], w=[Opb])
                for kb in range(nkb):
                    s.op("pe", lambda: nc.tensor.matmul(OL[:, 1, :], c.ones_bf[:, :], PT[:, kb, :],
                                                        start=(kb == 0), stop=(kb == nkb - 1)), r=[c.constb, PTb], w=[Opb])
                dst = acc[rows, :, pair, cols]
                if pi == 0:
                    s.op("dve", lambda: nc.vector.tensor_copy(dst, OL[rows, :, :]), r=[Opb], w=[accb])
                else:
                    s.op("dve", lambda: nc.vector.tensor_tensor(dst, dst, OL[rows, :, :], ALU.add),
                         r=[Opb, accb], w=[accb])

            Acur = phaseA(units[0])
            for k, u in enumerate(units):
                Anext = phaseA(units[k + 1]) if k + 1 < len(units) else None
                phaseB(u, Acur)
                Acur = Anext
            for pair in range(4):
                for hh in range(2):
                    ts_ = slice(hh * 1024, (hh + 1) * 1024)
                    rc, rcb = rcr.next()
                    s.op("dve", lambda: nc.vector.reciprocal(rc[:, :], acc[:, 1, pair, ts_]), r=[accb], w=[rcb])
                    ob, obb = obr.next()
                    s.op("dve", lambda: nc.vector.tensor_tensor(ob[:, :], acc[:, 0, pair, ts_], rc[:, :], ALU.mult),
                         r=[accb, rcb], w=[obb])
                    s.dma("pool", mixedT[:, pair, tok0 + hh * 1024:tok0 + (hh + 1) * 1024], ob[:, :], r=[obb])

        qS, qSb = st.sb("qS", [128, 4, 16], BF16)
        kS, kSb = st.sb("kS", [128, 4, 16], BF16)
        oS, oSb = st.sb("oS", [128, 4, 16], BF16)
        s.dma("sp", qS[:, :, :], qT[:, :, NTP:NTP + 16], w=[qSb])
        s.dma("sp", kS[:, :, :], kT[:, :, NTP:NTP + 16], w=[kSb])
        vrr = st.ring("vrow", [1, 512], BF16, 3)
        kgr = st.ring("Kg", [128, 512], F32, 3)
        vgr = st.ring("Vg", [128, 512], F32, 3)
        kgbr = st.ring("Kgb", [128, 512], BF16, 2)
        vgbr = st.ring("Vgb", [128, 512], BF16, 4)
        kgtr = st.ring("KgT", [128, 4, 128], BF16, 2)
        s8r = st.ring("s8", [128, 16], F32, 3)
        p8r = st.ring("p8", [128, 16], BF16, 3)
        t8r = st.ring("t8", [128, 16], F32, 2)
        pstr = st.psring("pstA", [128, 8, 128], BF16, 1)
        smp, smb = st.ps("psm", [128, 512], F32)
        Sgb = Sob = OSb = smb
        Sg = smp[:, 0:8]
        So = smp[0:1, 8:16]
        OS = smp[:, 16:32].rearrange("p (a h) -> p a h", a=2)
        for i in range(16):
            vrow, vrowb = vrr.next()
            s.dma("sp", vrow[0:1, :], v_bf[NTP + i:NTP + i + 1, :], w=[vrowb])
            keep = []
            for pi, (wnd, dil) in enumerate(PATTERNS):
                Kg, Kgb_ = kgr.next()
                Vg, Vgb_ = vgr.next()
                s.dma("sp", Kg[:, :], cache_k[i, 2048 - 128 * dil:2048:dil, :], w=[Kgb_])
                s.dma("sp", Vg[:, :], cache_v[i, 2048 - 128 * dil:2048:dil, :], w=[Vgb_])
                Kb, Kbb = kgbr.next()
                Vb, Vbb = vgbr.next()
                s.op("pool", lambda: nc.gpsimd.tensor_copy(Kb[:, :], Kg[:, :]), r=[Kgb_], w=[Kbb])
                s.op("pool", lambda: nc.gpsimd.tensor_copy(Vb[:, :], Vg[:, :]), r=[Vgb_], w=[Vbb])
                pt, ptb = pstr.next()
                for pr in range(4):
                    s.op("pe", lambda: nc.tensor.transpose(pt[:, pr, :], Kb[:, pr * 128:(pr + 1) * 128], c.ident_bf[:, :]),
                         r=[Kbb, c.constb], w=[ptb])
                KT_, KTb_ = kgtr.next()
                evac(c, KT_[:, :, :], pt[:, 0:4, :], [ptb], [KTb_])
                for h in range(8):
                    pair = h // 2
                    rows = slice(64 * (h % 2), 64 * (h % 2) + 64)
                    s.op("pe", lambda: nc.tensor.matmul(Sg[:, h:h + 1], KT_[rows, pair, :], qS[rows, pair, i:i + 1],
                                                        start=True, stop=True), r=[KTb_, qSb], w=[Sgb])
                    s.op("pe", lambda: nc.tensor.matmul(So[0:1, h:h + 1], kS[rows, pair, i:i + 1], qS[rows, pair, i:i + 1],
                                                        start=True, stop=True), r=[kSb, qSb], w=[Sob])
                s8, s8b = s8r.next()
                s.op("dve", lambda: nc.vector.tensor_tensor(s8[:, 0:8], Sg, BT4[:, pi * 8:(pi + 1) * 8, 1, 0], ALU.add),
                     r=[Sgb, BTb], w=[s8b])
                s.op("dve", lambda: nc.vector.tensor_tensor(s8[0:1, 8:16], So, BT4[0:1, pi * 8:(pi + 1) * 8, 0, 0], ALU.add),
                     r=[Sob, BTb], w=[s8b])
                p8, p8b = p8r.next()
                s.op("act", lambda: nc.scalar.activation(out=p8[:, 0:8], in_=s8[:, 0:8], func=AF.Exp), r=[s8b], w=[p8b])
                s.op("act", lambda: nc.scalar.activation(out=p8[0:1, 8:16], in_=s8[0:1, 8:16], func=AF.Exp), r=[s8b], w=[p8b])
                keep.append((Vb, Vbb, p8, p8b))
            for a_ in range(2):
                for h in range(8):
                    pair = h // 2
                    for pi, (Vb, Vbb, p8, p8b) in enumerate(keep):
                        lh = Vb[:, pair * 128:(pair + 1) * 128] if a_ == 0 else c.ones_bf[:, :]
                        lo = vrow[0:1, pair * 128:(pair + 1) * 128] if a_ == 0 else c.ones_bf[0:1, :]
                        s.op("pe", lambda: nc.tensor.matmul(OS[:, a_, h:h + 1], lh, p8[:, h:h + 1],
                                                            start=(pi == 0), stop=False), r=[Vbb, c.constb, p8b], w=[OSb])
                        s.op("pe", lambda: nc.tensor.matmul(OS[:, a_, h:h + 1], lo, p8[0:1, 8 + h:9 + h],
                                                            start=False, stop=(pi == 2)), r=[vrowb, c.constb, p8b], w=[OSb])
            t8, t8b = t8r.next()
            s.op("dve", lambda: nc.vector.reciprocal(t8[:, 8:16], OS[:, 1, :]), r=[OSb], w=[t8b])
            s.op("dve", lambda: nc.vector.tensor_tensor(t8[:, 0:8], OS[:, 0, :], t8[:, 8:16], ALU.mult), r=[OSb, t8b], w=[t8b])
            for half in range(2):
                rows = slice(64 * half, 64 * half + 64)
                s.op("dve", lambda: nc.vector.tensor_copy(oS[rows, :, i], t8[rows, half:8:2]), r=[t8b], w=[oSb])
        s.dma("pool", mixedT[:, 0:4, NTP:NTP + 16], oS[:, :, :], r=[oSb])


def ssd_stage(c, xbcT, dt_scr, z_scr, mixedT, convw_d, convb_d, vec16_d, gssm_d, tri_d,
              cconv_d, state_d, conv_out, ssm_out, NB, NTP):
    s, nc = c.s, c.nc
    NSTREAM = 2
    with Stage(c) as st:
        cw, cwb = st.sb("cw", [128, 12, 4], F32)
        cb, cbb = st.sb("cb", [128, 12], F32)
        v16, v16b = st.sb("v16", [128, 3, 16], F32)
        gss, gssb = st.sb("gss", [128, 1024], F32)
        tri, trib = st.sb("tri", [128, 2, 128], F32)
        onesf, onesfb = st.sb("onesf", [128, 128], F32)
        s.dma("sp", cw[:, :, :], convw_d[:, :, :], w=[cwb])
        s.dma("sp", cb[:, :], convb_d[:, :], w=[cbb])
        s.dma("sp", v16[:, :, :], vec16_d[0:1, :, :].partition_broadcast(128), w=[v16b])
        s.dma("sp", gss[:, :], gssm_d[0:1, :].partition_broadcast(128), w=[gssb])
        s.dma("sp", tri[:, :, :], tri_d[:, :, :], w=[trib])
        s.op("pool", lambda: nc.gpsimd.memset(onesf[:, :], 1.0), w=[onesfb])
        s.op("act", lambda: nc.scalar.activation(out=v16[:, 1, :], in_=v16[:, 1, :], func=AF.Exp), r=[v16b], w=[v16b])
        s.op("dve", lambda: nc.vector.tensor_scalar(v16[:, 1, :], v16[:, 1, :], -1.0, None, ALU.mult), r=[v16b], w=[v16b])
        dtb_bc, a_bc, dsk_bc = v16[:, 0, :], v16[:, 1, :], v16[:, 2, :]
        triU, strictL = tri[:, 0, :], tri[:, 1, :]
        dg, dgb = st.sb("dg", [128, 48, 128], F32)
        for j in range(12):
            for i in range(4):
                s.op("dve", lambda: nc.vector.tensor_scalar(dg[:, j * 4 + i, :], c.ident_f[:, :], cw[:, j, i:i + 1], None, ALU.mult),
                     r=[c.constb, cwb], w=[dgb])

        def make_stream(k):
            S = Ctx()
            n = f"s{k}"
            S.xin = st.sb(n + "xin", [128, 12, 131], F32)
            S.a32 = st.sb(n + "a32", [128, 12, 128], F32)
            S.BCTr = st.ring(n + "BCT", [128, 4, 128], BF16, 2)
            S.Btok = st.sb(n + "Btok", [128, 2, 128], BF16)
            S.xdt = st.sb(n + "xdt", [128, 16, 64], BF16)
            S.xdts = st.sb(n + "xdts", [128, 16, 64], BF16)
            S.dskr = st.ring(n + "dsk", [128, 16, 64], F32, 2)
            S.dtt = st.sb(n + "dtt", [128, 6, 16], F32)
            S.acsr = st.ring(n + "acs", [128, 5, 16], F32, 2)
            S.H = st.sb(n + "H", [128, 16, 64], F32)
            S.Hb = st.sb(n + "Hb", [128, 16, 64], BF16)
            S.CBm = st.sb(n + "CBm", [128, 2, 128], F32)
            S.rhsD = st.ring(n + "rhsD", [128, 4, 128], F32, 2)
            S.Es = st.ring(n + "Es", [128, 4, 128], F32, 1)
            S.MT = st.ring(n + "MT", [128, 4, 128], BF16, 2)
            S.YTr = st.ring(n + "YTsb", [128, 2, 16, 64], F32, 2)
            S.yt = st.sb(n + "yt", [128, 16, 64], F32)
            S.ztr = st.ring(n + "zt", [128, 1024], F32, 2)
            S.yn = st.sb(n + "yn", [128, 1024], F32)
            S.yT = st.sb(n + "yT", [128, 8, 128], BF16)
            S.stat = st.ring(n + "stat", [128, 4], F32, 2)
            S.junk = st.sb(n + "junk", [128, 512], BF16)
            S.YTp = st.ps(n + "YTp", [128, 512], F32)
            S.Fp = st.ps(n + "Fp", [128, 512], F32)
            S.Mr = st.psring(n + "M", [128, 512], F32, 2)
            return S

        def front(S, item):
            kind, idx, tok0, ch, nchunk = item
            t0 = tok0 + ch * 128
            sidx = idx if kind == "p" else NB + idx
            Mr = S.Mr
            F = Ctx()
            F.BCT, F.dsk, F.acs, F.zt, F.YT = S.BCTr.next(), S.dskr.next(), S.acsr.next(), S.ztr.next(), S.YTr.next()
            xin, xinb = S.xin
            if kind == "p":
                if ch == 0:
                    s.op("dve", lambda: nc.vector.memset(xin[:, :, 0:3], 0.0), w=[xinb])
                    s.dma("sp", xin[:, :, 3:131], xbcT[:, :, t0:t0 + 128], w=[xinb])
                else:
                    s.dma("sp", xin[:, :, :], xbcT[:, :, t0 - 3:t0 + 128], w=[xinb])
            else:
                s.op("dve", lambda: nc.vector.memset(xin[:, :, :], 0.0), w=[xinb])
                s.dma("sp", xin[:, :, 0:3], cconv_d[idx], w=[xinb])
                s.dma("sp", xin[:, :, 3:4], xbcT[:, :, t0:t0 + 1], w=[xinb], slow=True)
            dtt, dttb = S.dtt
            if kind == "p":
                s.dma("sp", dtt[:, 0, :], dt_scr[t0:t0 + 128, :], w=[dttb])
            else:
                s.op("dve", lambda: nc.vector.memset(dtt[:, 0, :], 0.0), w=[dttb])
                s.dma("sp", dtt[0:1, 0, :], dt_scr[t0:t0 + 1, :], w=[dttb])
            zt, ztb = F.zt
            if kind == "p":
                s.dma("sp", zt[:, :], z_scr[t0:t0 + 128, :], w=[ztb])
            else:
                s.op("dve", lambda: nc.vector.memset(zt[:, :], 0.0), w=[ztb])
                s.dma("sp", zt[0:1, :], z_scr[t0:t0 + 1, :], w=[ztb])
            if ch == nchunk - 1:
                lo = 128 if kind == "p" else 1
                s.dma("pool", conv_out[sidx], xin[:, :, lo:lo + 3], r=[xinb])
            a32, a32b = S.a32
            for jg in range(3):
                m, mb = Mr.next()
                for jj in range(4):
                    j = jg * 4 + jj
                    for i in range(4):
                        s.op("pe", lambda: nc.tensor.matmul(m[:, jj * 128:(jj + 1) * 128], dg[:, j * 4 + i, :], xin[:, j, i:i + 128],
                                                            start=(i == 0), stop=(i == 3)), r=[dgb, xinb], w=[mb])
                for jj in range(4):
                    j = jg * 4 + jj
                    s.op("act", lambda: nc.scalar.activation(out=a32[:, j, :], in_=m[:, jj * 128:(jj + 1) * 128], func=AF.Silu,
                                                             bias=cb[:, j:j + 1]), r=[mb, cbb], w=[a32b])
            BCT, BCTb = F.BCT
            s.op("act", lambda: nc.scalar.copy(BCT[:, :, :], a32[:, 8:12, :]), r=[a32b], w=[BCTb])
            s.op("dve", lambda: nc.vector.tensor_tensor(dtt[:, 1, :], dtt[:, 0, :], dtb_bc, ALU.add), r=[dttb, v16b], w=[dttb])
            s.op("act", lambda: nc.scalar.activation(out=dtt[:, 1, :], in_=dtt[:, 1, :], func=AF.Exp), r=[dttb], w=[dttb])
            s.op("dve", lambda: nc.vector.tensor_scalar(dtt[:, 1, :], dtt[:, 1, :], 1.0, None, ALU.add), r=[dttb], w=[dttb])
            s.op("act", lambda: nc.scalar.activation(out=dtt[:, 2, :], in_=dtt[:, 1, :], func=AF.Ln), r=[dttb], w=[dttb])
            if kind == "s":
                s.op("dve", lambda: nc.vector.tensor_scalar(dtt[:, 2, :], dtt[:, 2, :], c.ident_f[:, 0:1], None, ALU.mult),
                     r=[dttb, c.constb], w=[dttb])
            s.op("dve", lambda: nc.vector.tensor_tensor(dtt[:, 3, :], dtt[:, 2, :], a_bc, ALU.mult), r=[dttb, v16b], w=[dttb])
            dt_, la = dtt[:, 2, :], dtt[:, 3, :]
            Mc, Mcb = Mr.next()
            s.op("pe", lambda: nc.tensor.matmul(Mc[:, 0:16], triU, la, start=True, stop=True), r=[trib, dttb], w=[Mcb])
            s.op("pe", lambda: nc.tensor.matmul(Mc[:, 16:32], onesf[:, :], la, start=True, stop=True), r=[onesfb, dttb], w=[Mcb])
            acs, acsb = F.acs
            s.op("dve", lambda: nc.vector.tensor_copy(acs[:, 0:2, :], Mc[:, 0:32].rearrange("p (a e) -> p a e", a=2)),
                 r=[Mcb], w=[acsb])
            s.op("dve", lambda: nc.vector.tensor_tensor(acs[:, 3, :], acs[:, 1, :], acs[:, 0, :], ALU.subtract), r=[acsb], w=[acsb])
            s.op("act", lambda: nc.scalar.activation(out=acs[:, 2, :], in_=acs[:, 0, :], func=AF.Exp), r=[acsb], w=[acsb])
            s.op("act", lambda: nc.scalar.activation(out=acs[:, 3, :], in_=acs[:, 3, :], func=AF.Exp), r=[acsb], w=[acsb])
            s.op("act", lambda: nc.scalar.activation(out=acs[:, 4, :], in_=acs[:, 1, :], func=AF.Exp), r=[acsb], w=[acsb])
            s.op("dve", lambda: nc.vector.tensor_tensor(dtt[:, 4, :], dt_, acs[:, 3, :], ALU.mult), r=[dttb, acsb], w=[dttb])
            dtd = dtt[:, 4, :]
            xdt, xdtb = S.xdt
            xdts, xdtsb = S.xdts
            dsk, dskb = F.dsk
            for g in range(2):
                m, mb = Mr.next()
                for j in range(g * 4, g * 4 + 4):
                    s.op("pe", lambda: nc.tensor.transpose(m[:, (j % 4) * 128:(j % 4 + 1) * 128], a32[:, j, :], c.ident_f[:, :]),
                         r=[a32b, c.constb], w=[mb])
                mv = m[:, :].rearrange("p (e q) -> p e q", e=8)
                hs = slice(g * 8, g * 8 + 8)
                s.op("dve", lambda: nc.vector.tensor_tensor(xdt[:, hs, :], mv, dt_[:, hs].unsqueeze(2).to_broadcast([128, 8, 64]), ALU.mult),
                     r=[mb, dttb], w=[xdtb])
                s.op("dve", lambda: nc.vector.tensor_tensor(xdts[:, hs, :], mv, dtd[:, hs].unsqueeze(2).to_broadcast([128, 8, 64]), ALU.mult),
                     r=[mb, dttb], w=[xdtsb])
                s.op("dve", lambda: nc.vector.tensor_tensor(dsk[:, hs, :], mv, dsk_bc[:, hs].unsqueeze(2).to_broadcast([128, 8, 64]), ALU.mult),
                     r=[mb, v16b], w=[dskb])
            Mb_, Mbb = Mr.next()
            for g in range(2):
                s.op("pe", lambda: nc.tensor.transpose(Mb_[:, g * 128:(g + 1) * 128], a32[:, 8 + g, :], c.ident_f[:, :]),
                     r=[a32b, c.constb], w=[Mbb])
            Btok, Btokb = S.Btok
            s.op("act", lambda: nc.scalar.copy(Btok[:, :, :], Mb_[:, 0:256].rearrange("p (g n) -> p g n", g=2)), r=[Mbb], w=[Btokb])
            CBm, CBmb = S.CBm
            Mg, Mgb = Mr.next()
            for g in range(2):
                s.op("pe", lambda: nc.tensor.matmul(Mg[:, g * 128:(g + 1) * 128], BCT[:, g, :], BCT[:, 2 + g, :], start=True, stop=True),
                     r=[BCTb], w=[Mgb])
            s.op("dve", lambda: nc.vector.tensor_tensor(CBm[:, :, :], Mg[:, 0:256].rearrange("p (g l) -> p g l", g=2),
                                                        triU.unsqueeze(1).to_broadcast([128, 2, 128]), ALU.mult), r=[Mgb, trib], w=[CBmb])
            YT, YTb = F.YT
            YTp, YTpb = S.YTp
            for g in range(2):
                for q4 in range(2):
                    e0 = g * 8 + q4 * 4
                    rhsD, rhsDb = S.rhsD.next()
                    s.op("dve", lambda: nc.vector.tensor_tensor(rhsD[:, :, :], triU.unsqueeze(1).to_broadcast([128, 4, 128]),
                                                                 la[:, e0:e0 + 4].unsqueeze(2).to_broadcast([128, 4, 128]), ALU.mult),
                         r=[trib, dttb], w=[rhsDb])
                    Md, Mdb = Mr.next()
                    s.op("pe", lambda: nc.tensor.matmul(Md[:, :], strictL, rhsD[:, :, :].rearrange("p e l -> p (e l)"), start=True, stop=True),
                         r=[trib, rhsDb], w=[Mdb])
                    Es, Esb = S.Es.next()
                    s.op("act", lambda: nc.scalar.activation(out=Es[:, :, :].rearrange("p e l -> p (e l)"), in_=Md[:, :], func=AF.Exp),
                         r=[Mdb], w=[Esb])
                    MT, MTb = S.MT.next()
                    s.op("dve", lambda: nc.vector.tensor_tensor(MT[:, :, :], Es[:, :, :],
                                                                CBm[:, g:g + 1, :].to_broadcast([128, 4, 128]), ALU.mult),
                         r=[Esb, CBmb], w=[MTb])
                    for e in range(e0, e0 + 4):
                        cs = slice((e - e0) * 64, (e - e0) * 64 + 64)
                        cs2 = slice(256 + (e - e0) * 64, 256 + (e - e0) * 64 + 64)
                        s.op("pe", lambda: nc.tensor.matmul(YTp[:, cs], MT[:, e - e0, :], xdt[:, e, :], start=True, stop=True),
                             r=[MTb, xdtb], w=[YTpb])
                        s.op("pe", lambda: nc.tensor.matmul(YTp[:, cs2], Btok[:, g, :], xdts[:, e, :], start=True, stop=True),
                             r=[Btokb, xdtsb], w=[YTpb])
                    s.op("act", lambda: nc.scalar.copy(YT[:, :, e0:e0 + 4, :], YTp[:, :].rearrange("p (a e q) -> p a e q", a=2, e=4)),
                         r=[YTpb], w=[YTb])

            return F

        def back(S, item, F):
            kind, idx, tok0, ch, nchunk = item
            t0 = tok0 + ch * 128
            sidx = idx if kind == "p" else NB + idx
            Mr = S.Mr
            H, Hb_ = S.H
            Hbc, Hbcb = S.Hb
            acs, acsb = F.acs
            BCT, BCTb = F.BCT
            YT, YTb = F.YT
            Fp, Fpb = S.Fp
            eacs, cdec = acs[:, 2, :], acs[:, 4, :]
            if ch == 0:
                if kind == "p":
                    s.op("dve", lambda: nc.vector.memset(H[:, :, :], 0.0), w=[Hb_])
                else:
                    s.dma("sp", H[:, :, :], state_d[idx].rearrange("n (e p) -> n e p", e=16), w=[Hb_])
                s.op("act", lambda: nc.scalar.copy(Hbc[:, :, :], H[:, :, :]), r=[Hb_], w=[Hbcb])
            yt, ytb = S.yt
            for g in range(2):
                hs = slice(g * 8, g * 8 + 8)
                for e in range(g * 8, g * 8 + 8):
                    cs = slice((e % 8) * 64, (e % 8) * 64 + 64)
                    s.op("pe", lambda: nc.tensor.matmul(Fp[:, cs], BCT[:, 2 + g, :], Hbc[:, e, :], start=True, stop=True),
                         r=[BCTb, Hbcb], w=[Fpb])
                fv = Fp[:, :].rearrange("p (e q) -> p e q", e=8)
                s.op("dve", lambda: nc.vector.tensor_tensor(yt[:, hs, :], fv, eacs[:, hs].unsqueeze(2).to_broadcast([128, 8, 64]), ALU.mult),
                     r=[Fpb, acsb], w=[ytb])
                s.op("dve", lambda: nc.vector.tensor_tensor(H[:, hs, :], H[:, hs, :], cdec[:, hs].unsqueeze(2).to_broadcast([128, 8, 64]), ALU.mult),
                     r=[Hb_, acsb], w=[Hb_])
                s.op("dve", lambda: nc.vector.tensor_tensor(H[:, hs, :], H[:, hs, :], YT[:, 1, hs, :], ALU.add), r=[Hb_, YTb], w=[Hb_])
            s.op("act", lambda: nc.scalar.copy(Hbc[:, :, :], H[:, :, :]), r=[Hb_], w=[Hbcb])
            s.op("dve", lambda: nc.vector.tensor_tensor(yt[:, :, :], yt[:, :, :], YT[:, 0, :, :], ALU.add), r=[ytb, YTb], w=[ytb])
            s.op("dve", lambda: nc.vector.tensor_tensor(yt[:, :, :], yt[:, :, :], F.dsk[0][:, :, :], ALU.add), r=[ytb, F.dsk[1]], w=[ytb])
            zt, ztb = F.zt
            s.op("act", lambda: nc.scalar.activation(out=zt[:, :], in_=zt[:, :], func=AF.Silu), r=[ztb], w=[ztb])
            ytf = yt[:, :, :].rearrange("p e q -> p (e q)")
            s.op("dve", lambda: nc.vector.tensor_tensor(ytf, ytf, zt[:, :], ALU.mult), r=[ytb, ztb], w=[ytb])
            yn, ynb = S.yn
            jk, jkb = S.junk
            for g in range(2):
                gs = slice(g * 512, (g + 1) * 512)
                ss, ssb = S.stat.next()
                s.op("act", lambda: nc.scalar.activation(out=jk[:, 0:512], in_=ytf[:, gs], func=AF.Square, accum_out=ss[:, 0:1]),
                     r=[ytb], w=[jkb, ssb])
                rstd_from_ss(c, ss, ssb, 512)
                s.op("dve", lambda: nc.vector.scalar_tensor_tensor(yn[:, gs], ytf[:, gs], ss[:, 2:3], gss[:, gs], ALU.mult, ALU.mult),
                     r=[ytb, ssb, gssb], w=[ynb])
            yT, yTb = S.yT
            for g in range(2):
                m, mb = Mr.next()
                for j in range(g * 4, g * 4 + 4):
                    s.op("pe", lambda: nc.tensor.transpose(m[:, (j % 4) * 128:(j % 4 + 1) * 128], yn[:, j * 128:(j + 1) * 128], c.ident_f[:, :]),
                         r=[ynb, c.constb], w=[mb])
                evac(c, yT[:, g * 4:(g + 1) * 4, :], m[:, :].rearrange("p (j t) -> p j t", j=4), [mb], [yTb])
            if kind == "p":
                s.dma("pool", mixedT[:, 4:12, t0:t0 + 128], yT[:, :, :], r=[yTb])
            else:
                s.dma("pool", mixedT[:, 4:12, t0:t0 + 1], yT[:, :, 0:1], r=[yTb], slow=True)
            if ch == nchunk - 1:
                s.dma("pool", ssm_out[sidx].rearrange("n (e p) -> n e p", e=16), H[:, :, :], r=[Hb_])

        streams = [make_stream(k) for k in range(NSTREAM)]
        work = [[] for _ in range(NSTREAM)]
        for b in range(NB):
            for ch in range(16):
                work[b % NSTREAM].append(("p", b, b * 2048, ch, 16))
        for i in range(16):
            work[(i + NB) % NSTREAM].append(("s", i, NTP + i, 0, 1))

        def runner(k):
            def run():
                items = work[k]
                if not items:
                    return
                Fc = front(streams[k], items[0])
                for n_, item in enumerate(items):
                    Fn = front(streams[k], items[n_ + 1]) if n_ + 1 < len(items) else None
                    back(streams[k], item, Fc)
                    Fc = Fn
            return run
        Interleave(s, [runner(k) for k in range(NSTREAM)]).run()


def outcross_stage(c, x1, mixedT, gcol, wout_d, wcq_d, wco_d, memKT, memV, cache_mk, cache_mv, x3, tiles, NTP):
    s, nc = c.s, c.nc
    with Stage(c) as st:
        st.wstage = st.ring("wst", [128, 1024], F32, 3)
        wO, wOb = st.sb("wO", [128, 12, D], BF16)
        wQ, wQb = st.sb("wQ", [128, 8, D], BF16)
        wC, wCb = st.sb("wCo", [128, 8, D], BF16)
        xtr = st.ring("xt", [128, 4, 1024], F32, 2)
        mTr = st.ring("mT", [128, 12, 512], BF16, 1)
        hTr = st.ring("hT", [128, 8, 512], BF16, 2)
        qxr = st.ring("qx", [128, 8, 512], BF16, 1)
        oTr = st.ring("oTn", [128, 8, 512], BF16, 1)
        ptr = st.ring("PT", [128, 512], BF16, 4)
        rlr = st.ring("rl", [128, 512], F32, 2)
        ktr = st.ring("KTm", [128, 8, 256], BF16, 2)
        vmr = st.ring("Vm", [128, 2, 1024], BF16, 2)
        ckr = st.ring("ck", [128, 2, 1024], F32, 2)
        cbr = st.ring("ckb", [128, 2, 1024], BF16, 1)
        norm_bufs(st)
        psr = st.psring("ps", [128, 512], F32, 6)
        load_weight(c, st, wout_d, wO, wOb, 12, D)
        load_weight(c, st, wcq_d, wQ, wQb, 8, D)
        load_weight(c, st, wco_d, wC, wCb, 8, D)

        def cross_core(KTm, KTmb, Vm, Vmb, qx, qxb, oT, oTb, cols, n):
            for h in range(4):
                PTs = []
                for mb in range(2):
                    ps, psb = psr.next()
                    for dc in range(2):
                        s.op("pe", lambda: nc.tensor.matmul(ps[:, 0:n], KTm[:, 2 * h + dc, mb * 128:(mb + 1) * 128],
                                                            qx[:, 2 * h + dc, cols], start=(dc == 0), stop=(dc == 1)),
                             r=[KTmb, qxb], w=[psb])
                    PT, PTb = ptr.next()
                    s.op("act", lambda: nc.scalar.activation(out=PT[:, 0:n], in_=ps[:, 0:n], func=AF.Exp), r=[psb], w=[PTb])
                    PTs.append((PT, PTb))
                ps, psb = psr.next()
                for mb in range(2):
                    s.op("pe", lambda: nc.tensor.matmul(ps[:, 0:n], c.ones_bf[:, :], PTs[mb][0][:, 0:n],
                                                        start=(mb == 0), stop=(mb == 1)), r=[c.constb, PTs[mb][1]], w=[psb])
                rl, rlb = rlr.next()
                s.op("dve", lambda: nc.vector.reciprocal(rl[:, 0:n], ps[:, 0:n]), r=[psb], w=[rlb])
                for dc in range(2):
                    ps, psb = psr.next()
                    for mb in range(2):
                        s.op("pe", lambda: nc.tensor.matmul(ps[:, 0:n], Vm[:, mb, (2 * h + dc) * 128:(2 * h + dc + 1) * 128],
                                                            PTs[mb][0][:, 0:n], start=(mb == 0), stop=(mb == 1)),
                             r=[Vmb, PTs[mb][1]], w=[psb])
                    s.op("dve", lambda: nc.vector.tensor_tensor(oT[:, 2 * h + dc, cols], ps[:, 0:n], rl[:, 0:n], ALU.mult),
                         r=[psb, rlb], w=[oTb])

        cur_b = -1
        KTm = Vm = None
        for (t0, nsub) in tiles:
            N = nsub * 128
            xt, xtb = xtr.next()
            s.dma("sp", xt[:, 0:nsub, :], x1[t0:t0 + N, :].rearrange("(s p) d -> p s d", p=128), w=[xtb])
            mT, mTb = mTr.next()
            s.dma("sp", mT[:, :, 0:N], mixedT[:, :, t0:t0 + N], w=[mTb])
            for sub in range(nsub):
                for hf in range(2):
                    ps, psb = psr.next()
                    mm_tm(c, mT, mTb, sub, wO, wOb, hf * 512, 512, ps, psb, KC=12)
                    xs_ = xt[:, sub, hf * 512:(hf + 1) * 512]
                    s.op("dve", lambda: nc.vector.tensor_tensor(xs_, xs_, ps[:, :], ALU.add), r=[psb, xtb], w=[xtb])
            hT, hTb = hTr.next()
            norm_transpose(c, st, xt, xtb, nsub, gcol, hT, hTb)
            qx, qxb = qxr.next()
            for j in range(8):
                ps, psb = psr.next()
                mm_fm(c, wQ, wQb, j * 128, hT, hTb, N, ps, psb)
                evac(c, qx[:, j, 0:N], ps[:, 0:N], [psb], [qxb], scale=1.0 / 16.0)
            oT, oTb = oTr.next()
            if t0 < NTP:
                b = t0 // 2048
                if b != cur_b:
                    cur_b = b
                    KTm, KTmb = ktr.next()
                    Vm, Vmb = vmr.next()
                    s.dma("sp", KTm[:, :, :], memKT[:, :, b * 256:(b + 1) * 256], w=[KTmb])
                    s.dma("sp", Vm[:, :, :], memV[b * 256:(b + 1) * 256, :].rearrange("(mb m) d -> m mb d", m=128), w=[Vmb])
                cross_core(KTm, KTmb, Vm, Vmb, qx, qxb, oT, oTb, slice(0, N), N)
            else:
                s.op("pool", lambda: nc.gpsimd.memset(oT[:, :, 0:N], 0.0), w=[oTb])
                for i in range(16):
                    ck, ckb_ = ckr.next()
                    s.dma("sp", ck[:, :, :], cache_mk[i].rearrange("(mb m) d -> m mb d", m=128), w=[ckb_])
                    cb_, cbb_ = cbr.next()
                    s.op("pool", lambda: nc.gpsimd.tensor_copy(cb_[:, :, :], ck[:, :, :]), r=[ckb_], w=[cbb_])
                    KTs, KTsb = ktr.next()
                    for mb in range(2):
                        pt, ptb = st.pst.next()
                        for j in range(8):
                            s.op("pe", lambda: nc.tensor.transpose(pt[:, j, :], cb_[:, mb, j * 128:(j + 1) * 128], c.ident_bf[:, :]),
                                 r=[cbb_, c.constb], w=[ptb])
                        evac(c, KTs[:, :, mb * 128:(mb + 1) * 128], pt[:, :, :], [ptb], [KTsb])
                    cv, cvb_ = ckr.next()
                    s.dma("sp", cv[:, :, :], cache_mv[i].rearrange("(mb m) d -> m mb d", m=128), w=[cvb_])
                    Vs, Vsb = vmr.next()
                    s.op("pool", lambda: nc.gpsimd.tensor_copy(Vs[:, :, :], cv[:, :, :]), r=[cvb_], w=[Vsb])
                    cross_core(KTs, KTsb, Vs, Vsb, qx, qxb, oT, oTb, slice(i, i + 1), 1)
            for sub in range(nsub):
                for hf in range(2):
                    ps, psb = psr.next()
                    mm_tm(c, oT, oTb, sub, wC, wCb, hf * 512, 512, ps, psb)
                    xs_ = xt[:, sub, hf * 512:(hf + 1) * 512]
                    s.op("dve", lambda: nc.vector.tensor_tensor(xs_, xs_, ps[:, :], ALU.add), r=[psb, xtb], w=[xtb])
            s.dma("pool", x3[t0:t0 + N, :].rearrange("(s p) d -> p s d", p=128), xt[:, 0:nsub, :], r=[xtb])


ALL_STAGES = ("memkv", "ffn1", "inproj", "attn", "ssd", "outcross", "ffn2")


def build(NB=4, stages=ALL_STAGES, dbg=()):
    nc = bass.Bass("TRN2", target_bir_lowering=False)
    c = Ctx()
    c.nc = nc
    c.s = Sch(nc)
    s = c.s
    NTP = NB * 2048
    NTOK = NTP + 128
    NBM = NB * 256
    NSEQ = NB + 16
    tiles = [(t * 512, 4) for t in range(NTP // 512)] + [(NTP, 1)]
    pb, bmask_np, negm_np = bias_consts()
    NPB = len(pb)

    def din(name, shape):
        return nc.dram_tensor(name, shape, F32, kind="ExternalInput").ap()

    def dout(name, shape):
        return nc.dram_tensor(name, shape, F32, kind="ExternalOutput").ap()

    def scr(name, shape, dt=F32):
        if name in dbg:
            return nc.dram_tensor(name, shape, dt, kind="ExternalOutput").ap()
        return nc.dram_tensor(name, shape, dt).ap()

    x_all = din("x_all", [NTOK, D])
    gcols_d = din("gcols", [128, 6, 8])
    gfin_d = din("gfin", [1, D])
    w1g, w1u, w1d = din("w1_gate", [128, 8, DFF]), din("w1_up", [128, 8, DFF]), din("w1_down", [128, 22, D])
    w2g, w2u, w2d = din("w2_gate", [128, 8, DFF]), din("w2_up", [128, 8, DFF]), din("w2_down", [128, 22, D])
    win_d = din("w_in", [128, 8, DIN])
    wout_d = din("w_out", [128, 12, D])
    wck_d, wcv_d, wcq_d, wco_d = (din(n, [128, 8, D]) for n in ("w_ck", "w_cv", "w_cq", "w_co"))
    mem_in = din("mem_in", [NBM, D])
    relb_d = din("rel_bias", [1, 256])
    bmask_d = din("bmask", [NPB, 128, 256])
    negm_d = din("negm", [3, 128, 256])
    convw_d = din("conv_w", [128, 12, 4])
    convb_d = din("conv_b", [128, 12])
    vec16_d = din("vec16", [1, 3, 16])
    gssm_d = din("g_ssm", [1, D])
    tri_d = din("tri", [128, 2, 128])
    ident_d = din("ident", [128, 128])
    cache_k = din("cache_k", [16, 2048, 512])
    cache_v = din("cache_v", [16, 2048, 512])
    cconv_d = din("cconv", [16, 128, 12, 3])
    state_d = din("state", [16, 128, 1024])
    cache_mk = din("cache_mk", [16, 256, D])
    cache_mv = din("cache_mv", [16, 256, D])

    y_out = dout("y_out", [NTOK, D])
    kT_out = dout("kT_out", [128, 4, NTOK])
    v_out = dout("v_out", [NTOK, 512])
    conv_out = dout("conv_out", [NSEQ, 128, 12, 3])
    ssm_out = dout("ssm_out", [NSEQ, 128, 1024])
    mk_out = dout("mk_out", [NBM, D])
    mv_out = dout("mv_out", [NBM, D])

    x1 = scr("x1", [NTOK, D])
    x3 = scr("x3", [NTOK, D])
    x4 = scr("x4", [NTOK, D])
    hT_scr = scr("hT_scr", [128, 8, NTOK], BF16)
    memKT = scr("memKT", [128, 8, NBM], BF16)
    memV = scr("memV", [NBM, D], BF16)
    qT = scr("qT", [128, 4, NTOK], BF16)
    kT = scr("kT", [128, 4, NTOK], BF16)
    v_bf = scr("v_bf", [NTOK, 512], BF16)
    z_scr = scr("z_scr", [NTOK, D])
    xbcT = scr("xbcT", [128, 12, NTOK])
    dt_scr = scr("dt_scr", [NTOK, 16])
    mixedT = scr("mixedT", [128, 12, NTOK], BF16)

    c.dbufs = {}

    def db(key):
        if key not in c.dbufs:
            c.dbufs[key] = Buf(str(key))
        return c.dbufs[key]
    c.db = db

    c.constb = Buf("const")
    c.ident_f = nc.alloc_sbuf_tensor("ident_f", [128, 128], F32)
    c.ident_bf = nc.alloc_sbuf_tensor("ident_bf", [128, 128], BF16)
    c.ones_bf = nc.alloc_sbuf_tensor("ones_bf", [128, 128], BF16)
    c.gcols = nc.alloc_sbuf_tensor("gcols_sb", [128, 6, 8], F32)
    c.gfin = nc.alloc_sbuf_tensor("gfin_sb", [128, D], F32)
    s.dma("sp", c.ident_f[:, :], ident_d[:, :], w=[c.constb])
    s.dma("sp", c.gcols[:, :, :], gcols_d[:, :, :], w=[c.constb])
    s.dma("sp", c.gfin[:, :], gfin_d[0:1, :].partition_broadcast(128), w=[c.constb])
    s.op("dve", lambda: nc.vector.tensor_copy(c.ident_bf[:, :], c.ident_f[:, :]), r=[c.constb], w=[c.constb])
    s.op("dve", lambda: nc.vector.memset(c.ones_bf[:, :], 1.0), w=[c.constb])
    s.barrier()

    if "memkv" in stages:
        memkv_stage(c, mem_in, c.gcols[:, 4, :], wck_d, wcv_d, mk_out, mv_out, memKT, memV, NBM)
    if "ffn1" in stages:
        ffn_stage(c, x_all, x1, c.gcols[:, 0, :], w1g, w1u, w1d, hT_scr, tiles, "f1")
    if "inproj" in stages:
        inproj_stage(c, x1, c.gcols[:, 1, :], win_d, qT, kT, kT_out, v_out, v_bf, z_scr, xbcT, dt_scr, tiles)
    if "attn" in stages:
        attn_stage(c, qT, kT, v_bf, mixedT, relb_d, bmask_d, negm_d, pb, cache_k, cache_v, NB, NTP)
    if "ssd" in stages:
        ssd_stage(c, xbcT, dt_scr, z_scr, mixedT, convw_d, convb_d, vec16_d, gssm_d, tri_d,
                  cconv_d, state_d, conv_out, ssm_out, NB, NTP)
    if "outcross" in stages:
        outcross_stage(c, x1, mixedT, c.gcols[:, 2, :], wout_d, wcq_d, wco_d, memKT, memV, cache_mk, cache_mv, x3, tiles, NTP)
    if "ffn2" in stages:
        ffn_stage(c, x3, x4, c.gcols[:, 3, :], w2g, w2u, w2d, hT_scr, tiles, "f2", final=(c.gfin[:, :], y_out))
    s.finish()
    return nc


def tile_w(w, kc):
    w = np.asarray(w, np.float32)
    K, F = w.shape
    return np.ascontiguousarray(w.reshape(kc, 128, F).transpose(1, 0, 2))


def gcol(g):
    return np.asarray(g, np.float32).reshape(8, 128).T


def tri_consts():
    j = np.arange(128)[:, None]
    l = np.arange(128)[None, :]
    return np.ascontiguousarray(np.stack([(j <= l), (j > l)], axis=1).astype(np.float32))


def shared_inputs(inp):
    f = lambda k: np.asarray(inp[k], np.float32)
    pb, bmask, negm = bias_consts()
    gc = np.zeros((128, 6, 8), np.float32)
    for n, k in enumerate(("g_ffn1", "g_mix", "g_cross", "g_ffn2", "g_mem")):
        gc[:, n, :] = gcol(f(k)[0])
    cw = f("conv_w")[0]
    sh = {
        "gcols": gc, "gfin": f("g_final").reshape(1, D),
        "w1_gate": tile_w(f("w1_gate")[0], 8), "w1_up": tile_w(f("w1_up")[0], 8), "w1_down": tile_w(f("w1_down")[0], 22),
        "w2_gate": tile_w(f("w2_gate")[0], 8), "w2_up": tile_w(f("w2_up")[0], 8), "w2_down": tile_w(f("w2_down")[0], 22),
        "w_in": tile_w(f("w_in")[0], 8), "w_out": tile_w(f("w_out")[0], 12),
        "w_ck": tile_w(f("w_ck")[0], 8), "w_cv": tile_w(f("w_cv")[0], 8),
        "w_cq": tile_w(f("w_cq")[0], 8), "w_co": tile_w(f("w_co")[0], 8),
        "rel_bias": f("rel_bias").reshape(1, 256),
        "bmask": bmask, "negm": negm,
        "conv_w": np.ascontiguousarray(cw.reshape(4, 12, 128).transpose(2, 1, 0)),
        "conv_b": np.ascontiguousarray(f("conv_b")[0].reshape(12, 128).T),
        "vec16": np.stack([f("dt_bias")[0], f("a_log")[0], f("d_skip")[0]])[None],
        "g_ssm": f("g_ssm").reshape(1, D),
        "tri": tri_consts(), "ident": np.eye(128, dtype=np.float32),
    }
    return sh


def core_inputs(inp, core, NB):
    f = lambda k: np.asarray(inp[k], np.float32)
    xp = f("x_prompt")[core * NB:(core + 1) * NB].reshape(NB * 2048, D)
    xs = f("x_sample")[core * 16:(core + 1) * 16].reshape(16, D)
    x_all = np.concatenate([xp, xs, np.zeros((112, D), np.float32)], 0)
    sl = slice(core * 16, (core + 1) * 16)
    cc = f("cache_conv")[0, sl]
    st = f("state_ssm")[0, sl]
    return {
        "x_all": x_all,
        "mem_in": f("mem_prompt")[core * NB:(core + 1) * NB].reshape(NB * 256, D),
        "cache_k": f("cache_win_k")[0, sl].reshape(16, 2048, 512),
        "cache_v": f("cache_win_v")[0, sl].reshape(16, 2048, 512),
        "cconv": np.ascontiguousarray(cc.reshape(16, 3, 12, 128).transpose(0, 3, 2, 1)),
        "state": np.ascontiguousarray(st.reshape(16, 1024, 128).transpose(0, 2, 1)),
        "cache_mk": f("cache_mem_k")[0, sl].reshape(16, 256, D),
        "cache_mv": f("cache_mem_v")[0, sl].reshape(16, 256, D),
    }


def assemble(results, NB, ncores):
    NTP = NB * 2048
    B = NB * ncores
    y_p = np.empty((B, 2048, D), np.float32)
    y_s = np.empty((16 * ncores, 1, D), np.float32)
    wk_p = np.empty((1, B, 2048, 8, 64), np.float32)
    wv_p = np.empty((1, B, 2048, 8, 64), np.float32)
    cv_p = np.empty((1, B, 3, 1536), np.float32)
    ss_p = np.empty((1, B, 16, 64, 128), np.float32)
    mk_p = np.empty((1, B, 256, 4, 256), np.float32)
    mv_p = np.empty((1, B, 256, 4, 256), np.float32)
    wk_s = np.empty((1, 16 * ncores, 1, 8, 64), np.float32)
    wv_s = np.empty((1, 16 * ncores, 1, 8, 64), np.float32)
    cv_s = np.empty((1, 16 * ncores, 3, 1536), np.float32)
    ss_s = np.empty((1, 16 * ncores, 16, 64, 128), np.float32)
    for cidx, r in enumerate(results):
        y = np.asarray(r["y_out"])
        ktok = np.asarray(r["kT_out"]).transpose(2, 1, 0).reshape(-1, 8, 64)
        vtok = np.asarray(r["v_out"]).reshape(-1, 8, 64)
        cv = np.asarray(r["conv_out"]).transpose(0, 3, 2, 1).reshape(-1, 3, 1536)
        ss = np.asarray(r["ssm_out"]).transpose(0, 2, 1).reshape(-1, 16, 64, 128)
        bs = slice(cidx * NB, (cidx + 1) * NB)
        ts = slice(cidx * 16, (cidx + 1) * 16)
        y_p[bs] = y[:NTP].reshape(NB, 2048, D)
        y_s[ts, 0] = y[NTP:NTP + 16]
        wk_p[0, bs] = ktok[:NTP].reshape(NB, 2048, 8, 64)
        wv_p[0, bs] = vtok[:NTP].reshape(NB, 2048, 8, 64)
        wk_s[0, ts, 0] = ktok[NTP:NTP + 16]
        wv_s[0, ts, 0] = vtok[NTP:NTP + 16]
        cv_p[0, bs] = cv[:NB]
        cv_s[0, ts] = cv[NB:]
        ss_p[0, bs] = ss[:NB]
        ss_s[0, ts] = ss[NB:]
        mk_p[0, bs] = np.asarray(r["mk_out"]).reshape(NB, 256, 4, 256)
        mv_p[0, bs] = np.asarray(r["mv_out"]).reshape(NB, 256, 4, 256)
    return (y_p, y_s, wk_p, wv_p, cv_p, ss_p, mk_p, mv_p, wk_s, wv_s, cv_s, ss_s)


def kernel(**inputs):
    NB = 4
    nc = build(NB=NB)
    sh = shared_inputs(inputs)
    in_maps = []
    for core in range(NCORES):
        m = dict(sh)
        m.update(core_inputs(inputs, core, NB))
        in_maps.append(m)
    res = run_bass_kernel_spmd(nc, in_maps, core_ids=list(range(NCORES)))
    return assemble(res.results, NB, NCORES)
```
